# Optimizing a Trainium2 kernel written in Bass

```python
import jax, jax.numpy as jnp
from jax import lax
import numpy as np


D_MODEL = 1024
BATCH = 8
SEQ = 2048
DEPTH = 2
DEC_BATCH = 4
DEC_SEQ = 4096
PAST_LEN = 128

N_MIXERS = 2
N_ATTN_LAYERS = (DEPTH + 1) // 2
N_RET_LAYERS = DEPTH // 2
HEAD_DIM = 64
N_HEADS = D_MODEL // HEAD_DIM
N_KV_HEADS = 4
GROUP = N_HEADS // N_KV_HEADS
WINDOW = 128
BLOCK = 128
ROT_DIM = HEAD_DIM // 4
ROPE_THETA = 500000.0
Q_DIM = N_HEADS * HEAD_DIM
KV_DIM = N_KV_HEADS * HEAD_DIM
QKV_DIM = Q_DIM + 2 * KV_DIM
RET_HEADS = 4
RET_QK_DIM = D_MODEL // RET_HEADS
RET_V_TOTAL = 2 * D_MODEL
RET_V_DIM = RET_V_TOTAL // RET_HEADS
RET_IN_DIM = 2 * D_MODEL + 2 * RET_V_TOTAL
RET_THETA = 10000.0
CHUNK = 128
D_FF = 2816
FFN_RESIDUAL = 0.5
NORM_EPS = 1e-6

kernel_name = 'hybrid_bidir_swa_retention_macaron'


def rmsnorm(x, g):
    x32 = x.astype(jnp.float32)
    y = x32 * lax.rsqrt(jnp.mean(x32 * x32, axis=-1, keepdims=True) + NORM_EPS)
    return (y * g.astype(jnp.float32)).astype(x.dtype)


def rotary(x, n_rot, theta):
    L = x.shape[1]
    half = n_rot // 2
    inv_freq = theta ** (-jnp.arange(half, dtype=jnp.float32) / half)
    ang = jnp.arange(L, dtype=jnp.float32)[:, None] * inv_freq[None, :]
    cos = jnp.cos(ang)[None, :, None, :]
    sin = jnp.sin(ang)[None, :, None, :]
    xr = x[..., :n_rot].astype(jnp.float32)
    x1, x2 = xr[..., :half], xr[..., half:]
    rot = jnp.concatenate([x1 * cos - x2 * sin, x2 * cos + x1 * sin], axis=-1)
    return jnp.concatenate([rot.astype(x.dtype), x[..., n_rot:]], axis=-1)


def swiglu(h, w_in, w_out):
    gate, up = jnp.split(h @ w_in, 2, axis=-1)
    return (jax.nn.silu(gate) * up) @ w_out


def windowed_gqa(h, w_qkv, w_o, sink):
    B, L, _ = h.shape
    nb = L // BLOCK
    qkv = h @ w_qkv
    q = qkv[..., :Q_DIM].reshape(B, L, N_HEADS, HEAD_DIM)
    k = qkv[..., Q_DIM:Q_DIM + KV_DIM].reshape(B, L, N_KV_HEADS, HEAD_DIM)
    v = qkv[..., Q_DIM + KV_DIM:].reshape(B, L, N_KV_HEADS, HEAD_DIM)
    q = rotary(q, ROT_DIM, ROPE_THETA)
    k = rotary(k, ROT_DIM, ROPE_THETA)
    qb = q.reshape(B, nb, BLOCK, N_KV_HEADS, GROUP, HEAD_DIM)
    pad = ((0, 0), (BLOCK, BLOCK), (0, 0), (0, 0))
    kp = jnp.pad(k, pad).reshape(B, nb + 2, BLOCK, N_KV_HEADS, HEAD_DIM)
    vp = jnp.pad(v, pad).reshape(B, nb + 2, BLOCK, N_KV_HEADS, HEAD_DIM)
    kb = jnp.concatenate([kp[:, :-2], kp[:, 1:-1], kp[:, 2:]], axis=2)
    vb = jnp.concatenate([vp[:, :-2], vp[:, 1:-1], vp[:, 2:]], axis=2)
    s = jnp.einsum('bnqgrd,bnkgd->bngrqk', qb, kb).astype(jnp.float32) * (HEAD_DIM ** -0.5)
    qi = jnp.arange(BLOCK)
    kj = jnp.arange(3 * BLOCK)
    rel = kj[None, :] - BLOCK - qi[:, None]
    kpos = jnp.arange(nb)[:, None] * BLOCK + kj[None, :] - BLOCK
    valid = (jnp.abs(rel) <= WINDOW)[None] & ((kpos >= 0) & (kpos < L))[:, None, :]
    s = jnp.where(valid[None, :, None, None], s, -jnp.inf)
    sink_l = sink.astype(jnp.float32).reshape(N_KV_HEADS, GROUP)[None, None, :, :, None, None]
    m = jnp.maximum(jnp.max(s, axis=-1, keepdims=True), sink_l)
    p = jnp.exp(s - m)
    denom = jnp.sum(p, axis=-1, keepdims=True) + jnp.exp(sink_l - m)
    p = (p / denom).astype(vb.dtype)
    o = jnp.einsum('bngrqk,bnkgd->bnqgrd', p, vb).reshape(B, L, Q_DIM)
    return o @ w_o


def retention_direction(q, k, v, log_gamma, strict):
    B, L, H, dk = q.shape
    dv = v.shape[-1]
    nc = L // CHUNK
    i = jnp.arange(CHUNK, dtype=jnp.float32)
    diff = i[:, None] - i[None, :]
    mask = (diff > 0) if strict else (diff >= 0)
    decay = jnp.where(mask[None], jnp.exp(log_gamma[:, None, None] * jnp.maximum(diff, 0.0)[None]), 0.0)
    qc = q.reshape(B, nc, CHUNK, H, dk)
    kc = k.reshape(B, nc, CHUNK, H, dk)
    vc = v.reshape(B, nc, CHUNK, H, dv)
    s = jnp.einsum('bnihd,bnjhd->bnhij', qc, kc) * decay
    intra = jnp.einsum('bnhij,bnjhe->bnihe', s, vc)
    q_decay = jnp.exp(log_gamma[None, :] * (i[:, None] + 1.0))
    k_decay = jnp.exp(log_gamma[None, :] * (CHUNK - 1.0 - i)[:, None])
    chunk_decay = jnp.exp(log_gamma * CHUNK)

    def step(state, xs):
        qn, kn, vn = xs
        cross = jnp.einsum('bihd,bhde->bihe', qn, state) * q_decay[None, :, :, None]
        state = chunk_decay[None, :, None, None] * state + jnp.einsum(
            'bjhd,bjhe->bhde', kn * k_decay[None, :, :, None], vn)
        return state, cross

    state0 = jnp.zeros((B, H, dk, dv), jnp.float32)
    _, cross = lax.scan(step, state0, (jnp.moveaxis(qc, 1, 0), jnp.moveaxis(kc, 1, 0), jnp.moveaxis(vc, 1, 0)))
    out = intra + jnp.moveaxis(cross, 0, 1)
    return out.reshape(B, L, H, dv)


def retention(h, w_in, w_o, decay_fwd, decay_bwd):
    B, L, _ = h.shape
    proj = h @ w_in
    q = proj[..., :D_MODEL].reshape(B, L, RET_HEADS, RET_QK_DIM)
    k = proj[..., D_MODEL:2 * D_MODEL].reshape(B, L, RET_HEADS, RET_QK_DIM)
    v = proj[..., 2 * D_MODEL:2 * D_MODEL + RET_V_TOTAL].reshape(B, L, RET_HEADS, RET_V_DIM)
    g = proj[..., 2 * D_MODEL + RET_V_TOTAL:]
    q = rotary(q, RET_QK_DIM, RET_THETA).astype(jnp.float32)
    k = rotary(k, RET_QK_DIM, RET_THETA).astype(jnp.float32) * (RET_QK_DIM ** -0.5)
    v = v.astype(jnp.float32)
    lg_f = jax.nn.log_sigmoid(decay_fwd.astype(jnp.float32))
    lg_b = jax.nn.log_sigmoid(decay_bwd.astype(jnp.float32))
    y_f = retention_direction(q, k, v, lg_f, False)
    y_b = jnp.flip(retention_direction(jnp.flip(q, 1), jnp.flip(k, 1), jnp.flip(v, 1), lg_b, True), 1)
    y = y_f + y_b
    y = y * lax.rsqrt(jnp.mean(y * y, axis=-1, keepdims=True) + NORM_EPS)
    y = y.reshape(B, L, RET_V_TOTAL).astype(h.dtype)
    return (jax.nn.silu(g) * y) @ w_o


def trunk(x, norm_gains, ffn_w_in, ffn_w_out, attn_w_qkv, attn_w_o, attn_sink,
          ret_w_in, ret_w_o, ret_decay_fwd, ret_decay_bwd):
    for l in range(DEPTH):
        g = norm_gains[l]
        x = x + FFN_RESIDUAL * rmsnorm(swiglu(rmsnorm(x, g[0]), ffn_w_in[l, 0], ffn_w_out[l, 0]), g[1])
        h = rmsnorm(x, g[2])
        if l % N_MIXERS == 0:
            a = l // N_MIXERS
            mix = windowed_gqa(h, attn_w_qkv[a], attn_w_o[a], attn_sink[a])
        else:
            r = l // N_MIXERS
            mix = retention(h, ret_w_in[r], ret_w_o[r], ret_decay_fwd[r], ret_decay_bwd[r])
        x = x + rmsnorm(mix, g[3])
        x = x + FFN_RESIDUAL * rmsnorm(swiglu(rmsnorm(x, g[4]), ffn_w_in[l, 1], ffn_w_out[l, 1]), g[5])
    return x


def setup_inputs(seed: int = 0) -> dict:
    key = jax.random.key(seed)
    ks = jax.random.split(key, 14)
    f32 = jnp.float32
    base = 1.0 - 2.0 ** (-5.0 - np.arange(RET_HEADS))
    decay_logit = jnp.asarray(np.log(base / (1.0 - base)).astype(np.float32))[None, :]
    return {
        'x_prompt': jax.random.normal(ks[0], (BATCH, SEQ, D_MODEL), f32),
        'x_sample': jax.random.normal(ks[1], (DEC_BATCH, DEC_SEQ, D_MODEL), f32),
        'norm_gains': 1.0 + 0.05 * jax.random.normal(ks[2], (DEPTH, 6, D_MODEL), f32),
        'ffn_w_in': jax.random.normal(ks[3], (DEPTH, 2, D_MODEL, 2 * D_FF), f32) * D_MODEL ** -0.5,
        'ffn_w_out': jax.random.normal(ks[4], (DEPTH, 2, D_FF, D_MODEL), f32) * D_FF ** -0.5,
        'attn_w_qkv': jax.random.normal(ks[5], (N_ATTN_LAYERS, D_MODEL, QKV_DIM), f32) * D_MODEL ** -0.5,
        'attn_w_o': jax.random.normal(ks[6], (N_ATTN_LAYERS, Q_DIM, D_MODEL), f32) * Q_DIM ** -0.5,
        'attn_sink': 0.5 * jax.random.normal(ks[7], (N_ATTN_LAYERS, N_HEADS), f32),
        'ret_w_in': jax.random.normal(ks[8], (N_RET_LAYERS, D_MODEL, RET_IN_DIM), f32) * D_MODEL ** -0.5,
        'ret_w_o': jax.random.normal(ks[9], (N_RET_LAYERS, RET_V_TOTAL, D_MODEL), f32) * RET_V_TOTAL ** -0.5,
        'ret_decay_fwd': decay_logit + 0.01 * jax.random.normal(ks[10], (N_RET_LAYERS, RET_HEADS), f32),
        'ret_decay_bwd': decay_logit + 0.01 * jax.random.normal(ks[11], (N_RET_LAYERS, RET_HEADS), f32),
    }


def reference(x_prompt, x_sample, norm_gains, ffn_w_in, ffn_w_out, attn_w_qkv, attn_w_o, attn_sink,
              ret_w_in, ret_w_o, ret_decay_fwd, ret_decay_bwd):
    y_prompt = trunk(x_prompt, norm_gains, ffn_w_in, ffn_w_out, attn_w_qkv, attn_w_o, attn_sink,
                     ret_w_in, ret_w_o, ret_decay_fwd, ret_decay_bwd)
    y_sample = trunk(x_sample, norm_gains, ffn_w_in, ffn_w_out, attn_w_qkv, attn_w_o, attn_sink,
                     ret_w_in, ret_w_o, ret_decay_fwd, ret_decay_bwd)
    return (y_prompt, y_sample)
```

```python
import os
import numpy as np
import ml_dtypes
from contextlib import ExitStack
import concourse.bass as bass
import concourse.mybir as mybir
from concourse.bass_utils import run_bass_kernel_spmd

F32 = mybir.dt.float32
BF16 = mybir.dt.bfloat16
AF = mybir.ActivationFunctionType
ALU = mybir.AluOpType
AX = mybir.AxisListType

NTOK = 4096
NCH = 32
D = 1024
DFF = 2816
EPS = 1e-6
EPOCH = 1 << 30
NEG = -30000.0


class Dep:
    __slots__ = ("name", "w", "rs", "dsem", "dcnt", "ex", "retired")

    def __init__(self, name="", ex=False):
        self.name = name
        self.ex = ex
        self.w = None
        self.rs = {}
        self.dsem = None
        self.dcnt = 0
        self.retired = False


class Emitter:
    def __init__(self, nc):
        self.nc = nc
        self.eng = {"pe": nc.tensor, "act": nc.scalar, "dve": nc.vector,
                    "pool": nc.gpsimd, "sp": nc.sync}
        self.sem = {}
        self.cnt = {}
        self.nsem = 0
        for e in self.eng:
            self.sem[e] = self._newsem("e_" + e)
            self.cnt[e] = 0
        self.waited = {}
        self.dma_owners = []
        self.free_dsems = []
        self.no_recycle = set()

    def _newsem(self, name):
        self.nsem += 1
        return self.nc.alloc_semaphore("%s_%d" % (name, self.nsem))

    def _tick(self, e):
        if self.cnt[e] >= EPOCH:
            self.sem[e] = self._newsem("e_" + e)
            self.cnt[e] = 0
        self.cnt[e] += 1
        return self.sem[e], self.cnt[e]

    def _need(self, e, rec, needs):
        if rec is None:
            return
        if rec[0] == "e":
            _, pe, sem, val = rec
            if pe == e and e == "pe":
                return
            needs.append((sem, val))
        else:
            o = rec[1]
            needs.append((o.dsem, o.dcnt))

    def _collect(self, e, reads, writes):
        needs = []
        for d in reads:
            self._need(e, d.w, needs)
        for d in writes:
            if d.w is not None:
                self._need(e, d.w, needs)
            for k, r in d.rs.items():
                self._need(e, r, needs)
        return needs

    def _emit_waits(self, e, needs):
        eng = self.eng[e]
        best = {}
        for sem, val in needs:
            k = id(sem)
            if k not in best or best[k][1] < val:
                best[k] = (sem, val)
        for k, (sem, val) in best.items():
            wk = (e, k)
            if self.waited.get(wk, 0) >= val:
                continue
            self.waited[wk] = val
            eng.wait_ge(sem, val)

    def op(self, e, fn, reads=(), writes=()):
        xr = [d for d in reads if d.ex]
        needs = []
        if xr:
            for d in xr:
                self._need(e, d.w, needs)
            reads = [d for d in reads if not d.ex]
            writes = list(writes) + [d for d in xr if d not in writes]
        needs += self._collect(e, reads, writes)
        self._emit_waits(e, needs)
        inst = fn(self.eng[e])
        sem, val = self._tick(e)
        inst.then_inc(sem, 1)
        rec = ("e", e, sem, val)
        for d in reads:
            d.rs[e] = rec
        for d in writes:
            d.w = rec
            d.rs = {}
        return inst

    def dma(self, q, out, in_, reads=(), writes=(), owner=None):
        needs = self._collect(q, reads, writes)
        if owner is None:
            owner = writes[0] if writes else reads[0]
        if owner.dsem is None or owner.retired:
            if self.free_dsems and q != "pool":
                owner.dsem, owner.dcnt = self.free_dsems.pop()
            else:
                owner.dsem, owner.dcnt = self._newsem("d_" + owner.name), 0
                if q == "pool":
                    self.no_recycle.add(id(owner.dsem))
            owner.retired = False
            self.dma_owners.append(owner)
        self._emit_waits(q, needs)
        inst = self.eng[q].dma_start(out=out, in_=in_)
        owner.dcnt += 16
        inst.then_inc(owner.dsem, 16)
        rec = ("d", owner)
        for d in reads:
            d.rs["dma%d" % id(owner)] = rec
        for d in writes:
            d.w = rec
            d.rs = {}
        return inst

    def barrier(self):
        pts = [(self.sem[e], self.cnt[e]) for e in self.eng if self.cnt[e] > 0]
        pts += [(o.dsem, o.dcnt) for o in self.dma_owners]
        for e in self.eng:
            self._emit_waits(e, pts)
        for o in self.dma_owners:
            o.retired = True
            if id(o.dsem) not in self.no_recycle:
                self.free_dsems.append((o.dsem, o.dcnt))
        self.dma_owners = []

    def finish(self):
        sp = self.eng["sp"]
        pts = [(o.dsem, o.dcnt) for o in self.dma_owners]
        self._emit_waits("sp", pts)


def PDep(name):
    return Dep(name, ex=True)


class Ctx:
    pass


_uid = [0]


def _alloc(st, nc, name, shape, dt):
    _uid[0] += 1
    return st.enter_context(nc.sbuf_tensor("%s_u%d" % (name, _uid[0]), shape, dt))


def _palloc(st, nc, name, shape, dt):
    _uid[0] += 1
    return st.enter_context(nc.psum_tensor("%s_u%d" % (name, _uid[0]), shape, dt))


def emit_rstd(em, ss, dss, mhalf, dmh, n_feat):
    em.op("pool", lambda e: e.tensor_scalar(out=ss[:, 1:2], in0=ss[:, 0:1], scalar1=1.0 / n_feat,
                                            scalar2=EPS, op0=ALU.mult, op1=ALU.add),
          reads=[dss], writes=[dss])
    em.op("pool", lambda e: e.tensor_tensor(out=ss[:, 2:3], in0=ss[:, 1:2], in1=mhalf[:, 0:1], op=ALU.pow),
          reads=[dss, dmh], writes=[dss])


def emit_prenorm_T(C, em, xs, dxs, gpre, dgpre, hbs, dhbs, ss, dss, tp, dtp, hT_dst, dhT):
    em.op("act", lambda e: e.activation(out=hbs[:, :], in_=xs[:, :], func=AF.Square, accum_out=ss[:, 0:1]),
          reads=[dxs], writes=[dhbs, dss])
    emit_rstd(em, ss, dss, C.mhalf, C.dmh, D)
    em.op("dve", lambda e: e.scalar_tensor_tensor(out=hbs[:, :], in0=xs[:, :], scalar=ss[:, 2:3], in1=gpre[:, :],
                                                  op0=ALU.mult, op1=ALU.mult),
          reads=[dxs, dss, dgpre], writes=[dhbs])

    def tps(e):
        for k in range(8):
            i = e.transpose(tp[:, k, :], hbs[:, k * 128:(k + 1) * 128], C.ident[:, :])
        return i
    em.op("pe", tps, reads=[dhbs, C.dident], writes=[dtp])
    em.op("act", lambda e: e.activation(out=hT_dst, in_=tp[:, :, :], func=AF.Copy), reads=[dtp], writes=[dhT])


def load_gains(C, em, st, nc, l, ipre, ipost, post_scale):
    gpre = _alloc(st, nc, "gpre", [128, D], F32)
    gpost = _alloc(st, nc, "gpost", [128, D], F32)
    dgpre = Dep("gpre")
    dgpost = Dep("gpost")
    em.dma("sp", gpre[:, :], C.ng[l, ipre, :].partition_broadcast(128), writes=[dgpre])
    em.dma("sp", gpost[:, :], C.ng[l, ipost, :].partition_broadcast(128), writes=[dgpost])
    if post_scale != 1.0:
        em.op("pool", lambda e: e.tensor_scalar(out=gpost[:, :], in0=gpost[:, :], scalar1=post_scale, scalar2=0.0,
                                                op0=ALU.mult, op1=ALU.add), reads=[dgpost], writes=[dgpost])
    return gpre, dgpre, gpost, dgpost


def emit_post_residual(C, em, ps_out, dps, xs, dxs, gpost, dgpost, ss, dss, junk, djunk, dst_ap, ddst):
    em.op("act", lambda e: e.activation(out=junk[:, :], in_=ps_out, func=AF.Square, accum_out=ss[:, 0:1]),
          reads=[dps], writes=[djunk, dss])
    emit_rstd(em, ss, dss, C.mhalf, C.dmh, D)
    em.op("dve", lambda e: e.scalar_tensor_tensor(out=ps_out, in0=ps_out, scalar=ss[:, 2:3], in1=gpost[:, :],
                                                  op0=ALU.mult, op1=ALU.mult),
          reads=[dps, dss, dgpost], writes=[dps])
    em.op("dve", lambda e: e.tensor_tensor(out=xs[:, :], in0=ps_out, in1=xs[:, :], op=ALU.add),
          reads=[dps, dxs], writes=[dxs])
    em.dma("sp", dst_ap, xs[:, :], reads=[dxs], writes=[ddst], owner=dxs)


def emit_ffn(C, l, which, src, dsrc):
    nc, em = C.nc, C.em
    NJ = DFF // 128
    LAG = 3
    with ExitStack() as st:
        Win = _alloc(st, nc, "Win", [128, 8, 2 * DFF], BF16)
        Wout = _alloc(st, nc, "Wout", [128, NJ, D], BF16)
        dWin = [Dep("Win%d" % k) for k in range(4)]
        dWout = [Dep("Wout%d" % k) for k in range(2)]
        w_in = C.fwi[l, which]
        w_out = C.fwo[l, which]
        wi_v = w_in.rearrange("(k p) f -> p k f", p=128)
        for k in range(4):
            em.dma("pool", Win[:, 2 * k:2 * k + 2, :], wi_v[:, 2 * k:2 * k + 2, :], writes=[dWin[k]])
        wo_v = w_out.rearrange("(j p) d -> p j d", p=128)
        em.dma("pool", Wout[:, 0:11, :], wo_v[:, 0:11, :], writes=[dWout[0]])
        em.dma("pool", Wout[:, 11:22, :], wo_v[:, 11:22, :], writes=[dWout[1]])
        gpre, dgpre, gpost, dgpost = load_gains(C, em, st, nc, l, 0 if which == 0 else 4, 1 if which == 0 else 5, 0.5)

        xa = [_alloc(st, nc, "xa%d" % i, [128, D], F32) for i in range(3)]
        dxa = [Dep("xa%d" % i) for i in range(3)]
        xb = [_alloc(st, nc, "xb%d" % i, [128, D], F32) for i in range(3)]
        dxb = [Dep("xb%d" % i) for i in range(3)]
        hb = [_alloc(st, nc, "hb%d" % i, [128, D], BF16) for i in range(2)]
        dhb = [Dep("hb%d" % i) for i in range(2)]
        hT = [_alloc(st, nc, "hT%d" % i, [128, 8, 256], BF16) for i in range(2)]
        dhT = [[Dep("hT%d_%d" % (i, c)) for c in range(2)] for i in range(2)]
        NA = 6
        actT = [_alloc(st, nc, "actT%d" % i, [128, 256], BF16) for i in range(NA)]
        dact = [Dep("actT%d" % i) for i in range(NA)]
        sg = [_alloc(st, nc, "sg%d" % i, [128, 256], BF16) for i in range(2)]
        dsg = [Dep("sg%d" % i) for i in range(2)]
        junk = _alloc(st, nc, "junk", [128, D], BF16)
        djunk = Dep("junk")
        sst = [_alloc(st, nc, "ss%d" % i, [128, 4], F32) for i in range(4)]
        dsst = [Dep("ss%d" % i) for i in range(4)]
        tp = [_palloc(st, nc, "tp%d" % i, [128, 8, 128], BF16) for i in range(2)]
        dtp = [PDep("tp%d" % i) for i in range(2)]
        gu = [_palloc(st, nc, "gu%d" % i, [128, 2, 256], F32) for i in range(2)]
        dgu = [PDep("gu%d" % i) for i in range(2)]
        pout = _palloc(st, nc, "pout", [128, 2, D], F32)
        dpout = [PDep("pout%d" % i) for i in range(2)]

        NT = NTOK // 256
        sctr = [0]

        def prenorm(t):
            for c in range(2):
                ch = 2 * t + c
                xs, dxs = xa[ch % 3], dxa[ch % 3]
                em.dma("sp", xs[:, :], src[ch * 128:(ch + 1) * 128, :], reads=[dsrc[ch]], writes=[dxs])
                si = sctr[0] % 4
                sctr[0] += 1
                emit_prenorm_T(C, em, xs, dxs, gpre, dgpre, hb[ch % 2], dhb[ch % 2], sst[si], dsst[si],
                               tp[ch % 2], dtp[ch % 2], hT[t % 2][:, :, c * 128:(c + 1) * 128], dhT[t % 2][c])

        def p1(t, j):
            g = gu[j % 2]
            hTt = hT[t % 2]

            def f(e):
                for half in range(2):
                    for k in range(8):
                        i = e.matmul(g[:, half, :], lhsT=Win[:, k, half * DFF + j * 128: half * DFF + (j + 1) * 128],
                                     rhs=hTt[:, k, :], start=(k == 0), stop=(k == 7))
                return i
            em.op("pe", f, reads=dWin + dhT[t % 2], writes=[dgu[j % 2]])
            s = sg[j % 2]
            em.op("act", lambda e: e.activation(out=s[:, :], in_=g[:, 0, :], func=AF.Silu),
                  reads=[dgu[j % 2]], writes=[dsg[j % 2]])
            a = actT[j % NA]
            em.op("dve", lambda e: e.tensor_tensor(out=a[:, :], in0=g[:, 1, :], in1=s[:, :], op=ALU.mult),
                  reads=[dgu[j % 2], dsg[j % 2]], writes=[dact[j % NA]])

        def p2(t, j):
            a = actT[j % NA]

            def f(e):
                for tc in range(2):
                    for half in range(2):
                        i = e.matmul(pout[:, tc, half * 512:(half + 1) * 512], lhsT=a[:, tc * 128:(tc + 1) * 128],
                                     rhs=Wout[:, j, half * 512:(half + 1) * 512], start=(j == 0), stop=(j == NJ - 1))
                return i
            em.op("pe", f, reads=[dact[j % NA], dWout[0 if j < 11 else 1]], writes=dpout)

        def epilogue(t):
            for tc in range(2):
                ch = 2 * t + tc
                xs, dxs = xb[ch % 3], dxb[ch % 3]
                em.dma("sp", xs[:, :], src[ch * 128:(ch + 1) * 128, :], reads=[dsrc[ch]], writes=[dxs])
            for tc in range(2):
                ch = 2 * t + tc
                xs, dxs = xb[ch % 3], dxb[ch % 3]
                si = sctr[0] % 4
                sctr[0] += 1
                emit_post_residual(C, em, pout[:, tc, :], dpout[tc], xs, dxs, gpost, dgpost, sst[si], dsst[si],
                                   junk, djunk, C.y[ch * 128:(ch + 1) * 128, :], C.dy[ch])

        prenorm(0)
        for t in range(NT):
            for j in range(NJ + LAG):
                if j < NJ:
                    p1(t, j)
                if j >= LAG:
                    p2(t, j - LAG)
                if j == 10 and t + 1 < NT:
                    prenorm(t + 1)
            epilogue(t)
        em.barrier()


def emit_attn(C, l, src, dsrc):
    nc, em = C.nc, C.em
    SCALE = 0.125
    with ExitStack() as st:
        Wqkv = _alloc(st, nc, "Wqkv", [128, 8, 1536], BF16)
        Wo = _alloc(st, nc, "Wo", [128, 8, D], BF16)
        dWqkv, dWo = Dep("Wqkv"), Dep("Wo")
        em.dma("pool", Wqkv[:, :, :], C.wqkv[0].rearrange("(k p) f -> p k f", p=128), writes=[dWqkv])
        em.dma("pool", Wo[:, :, :], C.wo[0].rearrange("(k p) f -> p k f", p=128), writes=[dWo])
        gpre, dgpre, gpost, dgpost = load_gains(C, em, st, nc, l, 2, 3, 1.0)
        sinkt = _alloc(st, nc, "sinkt", [128, 16], F32)
        nsink = _alloc(st, nc, "nsink", [128, 16], F32)
        dsink = Dep("sink")
        em.dma("sp", sinkt[:, :], C.sink[0, :].partition_broadcast(128), writes=[dsink])
        em.op("pool", lambda e: e.tensor_scalar(out=nsink[:, :], in0=sinkt[:, :], scalar1=-1.0, scalar2=0.0,
                                                op0=ALU.mult, op1=ALU.add), reads=[dsink], writes=[dsink])

        kT = _alloc(st, nc, "kT_all", [128, 8, 34 * 128], BF16)
        vA = _alloc(st, nc, "v_all", [128, 34, 256], BF16)
        dkT = [Dep("kT%d" % i) for i in range(34)]
        dvA = [Dep("vA%d" % i) for i in range(34)]
        for i in (0, 33):
            em.op("dve", lambda e, i=i: e.memset(kT[:, :, i * 128:(i + 1) * 128], 0.0), writes=[dkT[i]])
            em.op("dve", lambda e, i=i: e.memset(vA[:, i, :], 0.0), writes=[dvA[i]])

        NX = 4
        xa = [_alloc(st, nc, "xa%d" % i, [128, D], F32) for i in range(NX)]
        dxa = [Dep("xa%d" % i) for i in range(NX)]
        hb = [_alloc(st, nc, "hb%d" % i, [128, D], BF16) for i in range(2)]
        dhb = [Dep("hb%d" % i) for i in range(2)]
        hT = [_alloc(st, nc, "hT%d" % i, [128, 8, 128], BF16) for i in range(2)]
        dhT = [Dep("hT%d" % i) for i in range(2)]
        cs = [_alloc(st, nc, "cs%d" % i, [128, 320], F32) for i in range(2)]
        dcs = [Dep("cs%d" % i) for i in range(2)]
        mk = [_alloc(st, nc, "mk%d" % i, [128, 384], BF16) for i in range(2)]
        dmk = [Dep("mk%d" % i) for i in range(2)]
        qtok = [_alloc(st, nc, "qtok%d" % i, [128, 16, 64], BF16) for i in range(2)]
        dqtok = [Dep("qtok%d" % i) for i in range(2)]
        kdtok = [_alloc(st, nc, "kdtok%d" % i, [128, 4, 2, 128], BF16) for i in range(2)]
        dkdtok = [Dep("kdtok%d" % i) for i in range(2)]
        for i in range(2):
            em.op("dve", lambda e, i=i: e.memset(kdtok[i][:, :, :, :], 0.0), writes=[dkdtok[i]])
        rt = [_alloc(st, nc, "rt%d" % i, [128, 20, 8], F32) for i in range(4)]
        drt = [Dep("rt%d" % i) for i in range(4)]
        NQ = 3
        qT = [_alloc(st, nc, "qT%d" % i, [128, 8, 128], BF16) for i in range(NQ)]
        dqT = [Dep("qT%d" % i) for i in range(NQ)]
        pb = [_alloc(st, nc, "pb%d" % i, [128, 384], BF16) for i in range(2)]
        dpb = [Dep("pb%d" % i) for i in range(2)]
        pT = [_alloc(st, nc, "pT%d" % i, [128, 3, 128], BF16) for i in range(2)]
        dpT = [Dep("pT%d" % i) for i in range(2)]
        stt = [_alloc(st, nc, "stt%d" % i, [128, 6, 16], F32) for i in range(2)]
        dsth = [[Dep("st%d_%d" % (i, h)) for h in range(16)] for i in range(2)]
        dfin = [[Dep("fin%d_%d" % (i, k)) for k in range(2)] for i in range(2)]
        otok = [_alloc(st, nc, "otok%d" % i, [128, 16, 64], BF16) for i in range(2)]
        dotok = [[Dep("otok%d_%d" % (i, k)) for k in range(2)] for i in range(2)]
        oT = _alloc(st, nc, "oT", [128, 8, 128], BF16)
        doT = Dep("oT")
        junk = _alloc(st, nc, "junk", [128, D], BF16)
        djunk = Dep("junk")
        sst = [_alloc(st, nc, "ss%d" % i, [128, 4], F32) for i in range(4)]
        dsst = [Dep("ss%d" % i) for i in range(4)]

        qkv = _palloc(st, nc, "qkv", [128, 1536], F32)
        dq01, dq2 = PDep("qkv01"), PDep("qkv2")
        tp = _palloc(st, nc, "tp", [128, 8, 128], BF16)
        dtp = PDep("tp")
        sps = _palloc(st, nc, "sps", [128, 2, 512], F32)
        dsps = [PDep("sps%d" % i) for i in range(2)]
        ptp = _palloc(st, nc, "ptp", [128, 2, 4, 128], BF16)
        dptp = PDep("ptp")
        ops = _palloc(st, nc, "ops", [128, 8, 64], F32)
        dops = PDep("ops")
        sctr = [0]

        def A_steps(b):
            xs, dxs = xa[b % NX], dxa[b % NX]
            c_, dc_ = cs[b % 2], dcs[b % 2]
            h_ = hT[b % 2]
            qt, dqt = qtok[b % 2], dqtok[b % 2]
            kd, dkd = kdtok[b % 2], dkdtok[b % 2]

            def a0():
                em.dma("sp", xs[:, :], src[b * 128:(b + 1) * 128, :], reads=[dsrc[b]], writes=[dxs])
                em.dma("sp", c_[:, :], C.acs[b * 128:(b + 1) * 128, :], writes=[dc_])
                si = sctr[0] % 4
                sctr[0] += 1
                emit_prenorm_T(C, em, xs, dxs, gpre, dgpre, hb[b % 2], dhb[b % 2], sst[si], dsst[si],
                               tp, dtp, h_[:, :, :], dhT[b % 2])

            def a1():
                for g in range(3):
                    def f(e, g=g):
                        for k in range(8):
                            i = e.matmul(qkv[:, g * 512:(g + 1) * 512], lhsT=h_[:, k, :], rhs=Wqkv[:, k, g * 512:(g + 1) * 512],
                                         start=(k == 0), stop=(k == 7))
                        return i
                    em.op("pe", f, reads=[dhT[b % 2], dWqkv], writes=[dq01 if g < 2 else dq2])
                qv = qkv[:, 0:1024].rearrange("p (h d) -> p h d", d=64)
                kv = qkv[:, 1024:1280].rearrange("p (h d) -> p h d", d=64)
                em.op("act", lambda e: e.activation(out=qt[:, :, 16:64], in_=qv[:, :, 16:64], func=AF.Copy),
                      reads=[dq01], writes=[dqt])
                for dup in range(2):
                    em.op("act", lambda e, dup=dup: e.activation(out=kd[:, :, dup, dup * 64 + 16:dup * 64 + 64], in_=kv[:, :, 16:64],
                                                                 func=AF.Copy), reads=[dq2], writes=[dkd])
                em.op("act", lambda e: e.activation(out=vA[:, b + 1, :], in_=qkv[:, 1280:1536], func=AF.Copy),
                      reads=[dq2], writes=[dvA[b + 1]])

            def a2():
                qkv20 = qkv[:, 0:1280].rearrange("p (h d) -> p h d", d=64)
                cosv = c_[:, 0:160].rearrange("p (h d) -> p h d", d=8)
                sinv = c_[:, 160:320].rearrange("p (h d) -> p h d", d=8)
                x1 = qkv20[:, :, 0:8]
                x2 = qkv20[:, :, 8:16]
                for i, (xx, tb) in enumerate([(x1, cosv), (x2, sinv), (x2, cosv), (x1, sinv)]):
                    em.op("dve", lambda e, i=i, xx=xx, tb=tb: e.tensor_tensor(out=rt[i][:, :, :], in0=xx, in1=tb, op=ALU.mult),
                          reads=[dq01, dq2, dc_], writes=[drt[i]])
                em.op("dve", lambda e: e.tensor_tensor(out=qt[:, :, 0:8], in0=rt[0][:, 0:16, :], in1=rt[1][:, 0:16, :], op=ALU.subtract),
                      reads=[drt[0], drt[1]], writes=[dqt])
                em.op("dve", lambda e: e.tensor_tensor(out=qt[:, :, 8:16], in0=rt[2][:, 0:16, :], in1=rt[3][:, 0:16, :], op=ALU.add),
                      reads=[drt[2], drt[3]], writes=[dqt])
                for dup in range(2):
                    em.op("dve", lambda e, dup=dup: e.tensor_tensor(out=kd[:, :, dup, dup * 64:dup * 64 + 8], in0=rt[0][:, 16:20, :],
                                                                    in1=rt[1][:, 16:20, :], op=ALU.subtract),
                          reads=[drt[0], drt[1]], writes=[dkd])
                    em.op("dve", lambda e, dup=dup: e.tensor_tensor(out=kd[:, :, dup, dup * 64 + 8:dup * 64 + 16], in0=rt[2][:, 16:20, :],
                                                                    in1=rt[3][:, 16:20, :], op=ALU.add),
                          reads=[drt[2], drt[3]], writes=[dkd])

            def a3():
                qflat = qt[:, :, :].rearrange("p h d -> p (h d)")

                def tq(e):
                    for k in range(8):
                        i = e.transpose(tp[:, k, :], qflat[:, k * 128:(k + 1) * 128], C.ident[:, :])
                    return i
                em.op("pe", tq, reads=[dqt, C.dident], writes=[dtp])
                em.op("act", lambda e: e.activation(out=qT[b % NQ][:, :, :], in_=tp[:, :, :], func=AF.Copy),
                      reads=[dtp], writes=[dqT[b % NQ]])

            def a4():
                kflat = kd[:, :, :, :].rearrange("p g u d -> p (g u d)")

                def tk(e):
                    for k in range(8):
                        i = e.transpose(tp[:, k, :], kflat[:, k * 128:(k + 1) * 128], C.ident[:, :])
                    return i
                em.op("pe", tk, reads=[dkd, C.dident], writes=[dtp])
                em.op("act", lambda e: e.activation(out=kT[:, :, (b + 1) * 128:(b + 2) * 128], in_=tp[:, :, :], func=AF.Copy),
                      reads=[dtp], writes=[dkT[b + 1]])
            return [a0, a1, a2, a3, a4]

        def C_steps(b):
            ot = otok[b % 2]

            def c0():
                oflat = ot[:, :, :].rearrange("p h d -> p (h d)")

                def to(e):
                    for k in range(8):
                        i = e.transpose(tp[:, k, :], oflat[:, k * 128:(k + 1) * 128], C.ident[:, :])
                    return i
                em.op("pe", to, reads=dotok[b % 2] + [C.dident], writes=[dtp])
                em.op("act", lambda e: e.activation(out=oT[:, :, :], in_=tp[:, :, :], func=AF.Copy), reads=[dtp], writes=[doT])

            def c1():
                def fw(e):
                    for half in range(2):
                        for k in range(8):
                            i = e.matmul(qkv[:, half * 512:(half + 1) * 512], lhsT=oT[:, k, :], rhs=Wo[:, k, half * 512:(half + 1) * 512],
                                         start=(k == 0), stop=(k == 7))
                    return i
                em.op("pe", fw, reads=[doT, dWo], writes=[dq01])

            def c2():
                si = sctr[0] % 4
                sctr[0] += 1
                emit_post_residual(C, em, qkv[:, 0:1024], dq01, xa[b % NX], dxa[b % NX], gpost, dgpost, sst[si], dsst[si],
                                   junk, djunk, C.y[b * 128:(b + 1) * 128, :], C.dy[b])
            return [c0, c1, c2]

        def finish_heads(b, h0):
            s_ = stt[b % 2]
            k = h0 // 8
            dsts = dsth[b % 2][h0:h0 + 8]
            df = dfin[b % 2][k]
            hs = slice(h0, h0 + 8)
            em.op("dve", lambda e: e.tensor_tensor(out=s_[:, 3, hs], in0=s_[:, 1, hs], in1=sinkt[:, hs], op=ALU.add),
                  reads=dsts + [dsink], writes=[df])
            em.op("act", lambda e: e.activation(out=s_[:, 4, hs], in_=s_[:, 3, hs], func=AF.Exp), reads=[df], writes=[df])
            em.op("dve", lambda e: e.tensor_tensor(out=s_[:, 4, hs], in0=s_[:, 4, hs], in1=s_[:, 2, hs], op=ALU.add),
                  reads=dsts + [df], writes=[df])
            em.op("dve", lambda e: e.reciprocal(out=s_[:, 5, hs], in_=s_[:, 4, hs]), reads=[df], writes=[df])
            em.op("dve", lambda e: e.tensor_tensor(out=otok[b % 2][:, hs, :], in0=ops[:, :, :],
                                                   in1=s_[:, 5, hs].unsqueeze(2).broadcast_to([128, 8, 64]), op=ALU.mult),
                  reads=[dops, df], writes=[dotok[b % 2][k]])

        def stageB(b, steps):
            m_, dm_ = mk[b % 2], dmk[b % 2]
            em.dma("sp", m_[:, :], C.amask[b], writes=[dm_])
            s_ = stt[b % 2]
            q_ = qT[b % NQ]

            def S(h):
                g, pr, hf = h // 4, h // 2, h % 2
                sp_, dsp_ = sps[:, h % 2, 0:384], dsps[h % 2]

                def f(e):
                    e.matmul(sp_, lhsT=q_[:, pr, :], rhs=kT[:, g * 2 + hf, b * 128: b * 128 + 384], start=True, stop=False)
                    return e.matmul(sp_, lhsT=C.ident[:, :], rhs=m_[:, :], start=False, stop=True)
                em.op("pe", f, reads=[dqT[b % NQ], dkT[b], dkT[b + 1], dkT[b + 2], dm_, C.dident], writes=[dsp_])
                ds_ = dsth[b % 2][h]
                em.op("dve", lambda e: e.tensor_reduce(out=s_[:, 0, h:h + 1], in_=sp_, op=ALU.max, axis=AX.X),
                      reads=[dsp_], writes=[ds_])
                em.op("dve", lambda e: e.tensor_scalar(out=s_[:, 1, h:h + 1], in0=s_[:, 0, h:h + 1], scalar1=-SCALE,
                                                       scalar2=nsink[:, h:h + 1], op0=ALU.mult, op1=ALU.min),
                      reads=[ds_, dsink], writes=[ds_])
                p_, dp_ = pb[h % 2], dpb[h % 2]
                em.op("act", lambda e: e.activation(out=p_[:, :], in_=sp_, func=AF.Exp, scale=SCALE,
                                                    bias=s_[:, 1, h:h + 1], accum_out=s_[:, 2, h:h + 1]),
                      reads=[dsp_, ds_], writes=[dp_, ds_])

            def T(h):
                p_, dp_ = pb[h % 2], dpb[h % 2]

                def ft(e):
                    for c in range(3):
                        i = e.transpose(ptp[:, h % 2, c, :], p_[:, c * 128:(c + 1) * 128], C.ident[:, :])
                    return i
                em.op("pe", ft, reads=[dp_, C.dident], writes=[dptp])
                em.op("act", lambda e: e.activation(out=pT[h % 2][:, :, :], in_=ptp[:, h % 2, 0:3, :], func=AF.Copy),
                      reads=[dptp], writes=[dpT[h % 2]])

            def PV(h):
                g = h // 4

                def fo(e):
                    for c in range(3):
                        i = e.matmul(ops[:, h % 8, :], lhsT=pT[h % 2][:, c, :], rhs=vA[:, b + c, g * 64:(g + 1) * 64],
                                     start=(c == 0), stop=(c == 2))
                    return i
                em.op("pe", fo, reads=[dpT[h % 2], dvA[b], dvA[b + 1], dvA[b + 2]], writes=[dops])

            S(0)
            S(1)
            for h in range(16):
                T(h)
                if h + 2 < 16:
                    S(h + 2)
                PV(h)
                if h % 8 == 7:
                    finish_heads(b, h - 7)
                if h % 2 == 1 and steps:
                    steps.pop(0)()
            while steps:
                steps.pop(0)()

        nblk = getattr(C, "nblk", NCH)
        for s_ in A_steps(0):
            s_()
        if NCH > 1:
            for s_ in A_steps(1):
                s_()
        for b in range(nblk):
            steps = []
            if b >= 1:
                steps += C_steps(b - 1)
            if b + 2 < NCH:
                steps += A_steps(b + 2)
            stageB(b, steps)
        for s_ in C_steps(nblk - 1):
            s_()
        em.barrier()


def emit_ret(C, l, src, dsrc):
    nc, em = C.nc, C.em
    nch = getattr(C, "nblk", NCH)
    half_b = NCH // 2
    with ExitStack() as st0:
        bnd = _alloc(st0, nc, "bnd", [128, 1], F32)
        kdec = _alloc(st0, nc, "kdec", [128, 8], F32)
        g128 = _alloc(st0, nc, "g128", [128, 8], F32)
        decrow = _alloc(st0, nc, "decrow", [128, 8, 128], F32)
        DT = _alloc(st0, nc, "DT", [128, 4, 128], F32)
        S32 = _alloc(st0, nc, "S32", [128, 4, 2, 512], F32)
        stT = ExitStack()
        rc = _alloc(stT, nc, "rc", [128, 6, 128], F32)
        cpos = _alloc(stT, nc, "cpos", [128, 4], F32)
        dl = _alloc(stT, nc, "dl", [128, 8], F32)
        lg = _alloc(stT, nc, "lg", [128, 8], F32)
        tmpD = _alloc(stT, nc, "tmpD", [128, 2, 128], F32)
        drc, dtab, dS32, dtmp = Dep("rc"), Dep("tab"), Dep("S32"), Dep("tmpD")
        em.dma("sp", rc[:, :, :], C.rconst[:, :, :], writes=[drc])
        em.dma("sp", cpos[:, :], C.rpos[:, :], writes=[drc], owner=drc)
        em.dma("sp", bnd[:, :], C.rbnd[:, :], writes=[drc], owner=drc)
        em.dma("sp", dl[:, 0:4], C.rdf[0, :].partition_broadcast(128), writes=[dtab])
        em.dma("sp", dl[:, 4:8], C.rdb[0, :].partition_broadcast(128), writes=[dtab], owner=dtab)
        em.op("act", lambda e: e.activation(out=lg[:, :], in_=dl[:, :], func=AF.Exp, scale=-1.0), reads=[dtab], writes=[dtab])
        em.op("dve", lambda e: e.tensor_scalar(out=lg[:, :], in0=lg[:, :], scalar1=1.0, scalar2=None, op0=ALU.add), reads=[dtab], writes=[dtab])
        em.op("act", lambda e: e.activation(out=lg[:, :], in_=lg[:, :], func=AF.Ln), reads=[dtab], writes=[dtab])
        em.op("dve", lambda e: e.tensor_scalar(out=lg[:, :], in0=lg[:, :], scalar1=-1.0, scalar2=None, op0=ALU.mult), reads=[dtab], writes=[dtab])
        em.op("act", lambda e: e.activation(out=g128[:, :], in_=lg[:, :], func=AF.Exp, scale=128.0), reads=[dtab], writes=[dtab])
        em.op("act", lambda e: e.activation(out=kdec[:, 0:4], in_=lg[:, 0:4], func=AF.Exp, scale=cpos[:, 2:3]), reads=[dtab, drc], writes=[dtab])
        em.op("act", lambda e: e.activation(out=kdec[:, 4:8], in_=lg[:, 4:8], func=AF.Exp, scale=cpos[:, 3:4]), reads=[dtab, drc], writes=[dtab])
        em.op("dve", lambda e: e.tensor_scalar(out=kdec[:, :], in0=kdec[:, :], scalar1=1.0 / 16, scalar2=None, op0=ALU.mult), reads=[dtab], writes=[dtab])
        for h in range(4):
            em.op("act", lambda e, h=h: e.activation(out=decrow[:, h, :], in_=rc[:, 4, :], func=AF.Exp, scale=lg[:, h:h + 1]), reads=[dtab, drc], writes=[dtab])
            em.op("act", lambda e, h=h: e.activation(out=decrow[:, 4 + h, :], in_=rc[:, 5, :], func=AF.Exp, scale=lg[:, 4 + h:5 + h]), reads=[dtab, drc], writes=[dtab])
            em.op("act", lambda e, h=h: e.activation(out=tmpD[:, 0, :], in_=rc[:, 0, :], func=AF.Exp, scale=lg[:, h:h + 1]), reads=[dtab, drc], writes=[dtmp])
            em.op("act", lambda e, h=h: e.activation(out=tmpD[:, 1, :], in_=rc[:, 1, :], func=AF.Exp, scale=lg[:, 4 + h:5 + h]), reads=[dtab, drc], writes=[dtmp])
            em.op("dve", lambda e, h=h: e.tensor_tensor(out=tmpD[:, :, :], in0=tmpD[:, :, :], in1=rc[:, 2:4, :], op=ALU.mult), reads=[dtmp, drc], writes=[dtmp])
            em.op("dve", lambda e, h=h: e.tensor_tensor(out=DT[:, h, :], in0=tmpD[:, 0, :], in1=tmpD[:, 1, :], op=ALU.add), reads=[dtmp], writes=[dtab])
        em.op("dve", lambda e: e.memset(S32[:, :, :, :], 0.0), writes=[dS32])
        em.barrier()
        stT.close()

        def rotary(ps, dps, c_, dc_, rt, drt, out_bf, dout):
            p4 = ps.rearrange("p (h t f) -> p h t f", h=4, t=2)
            o4 = out_bf[:, :].rearrange("p (h t f) -> p h t f", h=4, t=2)
            cosb = c_[:, 0:128].unsqueeze(1).broadcast_to([128, 4, 128])
            sinb = c_[:, 128:256].unsqueeze(1).broadcast_to([128, 4, 128])
            x1, x2 = p4[:, :, 0, :], p4[:, :, 1, :]
            prods = [(x1, cosb), (x2, sinb), (x2, cosb), (x1, sinb)]
            for i in (0, 1):
                xx, tb = prods[i]
                em.op("dve", lambda e, i=i, xx=xx, tb=tb: e.tensor_tensor(out=rt[i][:, :, :], in0=xx, in1=tb, op=ALU.mult),
                      reads=[dps, dc_], writes=[drt[i]])
            em.op("dve", lambda e: e.tensor_tensor(out=o4[:, :, 0, :], in0=rt[0][:, :, :], in1=rt[1][:, :, :], op=ALU.subtract),
                  reads=[drt[0], drt[1]], writes=[dout])
            for i in (2, 3):
                xx, tb = prods[i]
                em.op("dve", lambda e, i=i, xx=xx, tb=tb: e.tensor_tensor(out=rt[i][:, :, :], in0=xx, in1=tb, op=ALU.mult),
                      reads=[dps, dc_], writes=[drt[i]])
            em.op("dve", lambda e: e.tensor_tensor(out=o4[:, :, 1, :], in0=rt[2][:, :, :], in1=rt[3][:, :, :], op=ALU.add),
                  reads=[drt[2], drt[3]], writes=[dout])

        def state_update(S32, dS32, kd_tok, dkd, v_tok, dv, dsp, dsd, gcol0):
            for h in range(4):
                for c in range(2):
                    def f(e, h=h, c=c):
                        return e.matmul(dsp[:, :], lhsT=kd_tok[:, h * 256 + c * 128: h * 256 + (c + 1) * 128],
                                        rhs=v_tok[:, h * 512:(h + 1) * 512], start=True, stop=True)
                    em.op("pe", f, reads=[dkd, dv], writes=[dsd])
                    em.op("dve", lambda e, h=h, c=c: e.scalar_tensor_tensor(out=S32[:, h, c, :], in0=S32[:, h, c, :],
                                                                          scalar=g128[:, gcol0 + h:gcol0 + h + 1], in1=dsp[:, :],
                                                                          op0=ALU.mult, op1=ALU.add),
                          reads=[dS32, dsd, dtab], writes=[dS32])

        def boundary(S32, dS32):
            em.op("dve", lambda e: e.tensor_scalar(out=S32[:, :, :, :], in0=S32[:, :, :, :], scalar1=bnd[:, 0:1], scalar2=None,
                                                   op0=ALU.mult), reads=[dS32, drc], writes=[dS32])

        with ExitStack() as st:
            Wkv = _alloc(st, nc, "Wkv", [128, 8, 3072], BF16)
            dWkv = [Dep("Wkv%d" % k) for k in range(2)]
            wv = C.rwi[0].rearrange("(k p) f -> p k f", p=128)
            em.dma("pool", Wkv[:, 0:4, :], wv[:, 0:4, 1024:4096], writes=[dWkv[0]])
            em.dma("pool", Wkv[:, 4:8, :], wv[:, 4:8, 1024:4096], writes=[dWkv[1]])
            gpre = _alloc(st, nc, "gpre", [128, D], F32)
            dgpre = Dep("gpre")
            em.dma("sp", gpre[:, :], C.ng[l, 2, :].partition_broadcast(128), writes=[dgpre])
            xa = [_alloc(st, nc, "xa%d" % i, [128, D], F32) for i in range(3)]
            dxa = [Dep("xa%d" % i) for i in range(3)]
            hb2 = [_alloc(st, nc, "hb%d" % i, [128, D], BF16) for i in range(2)]
            dhb2 = [Dep("hb%d" % i) for i in range(2)]
            hT = [_alloc(st, nc, "hT%d" % i, [128, 8, 128], BF16) for i in range(2)]
            dhT = [Dep("hT%d" % i) for i in range(2)]
            NCS = 4
            cs = [_alloc(st, nc, "rcs%d" % i, [128, 256], F32) for i in range(NCS)]
            dcs = [Dep("rcs%d" % i) for i in range(NCS)]
            rt = [_alloc(st, nc, "rrt%d" % i, [128, 4, 128], F32) for i in range(4)]
            drt = [Dep("rrt%d" % i) for i in range(4)]
            krot = [_alloc(st, nc, "krot%d" % i, [128, 1024], BF16) for i in range(2)]
            dkrot = [Dep("krot%d" % i) for i in range(2)]
            kf = [_alloc(st, nc, "kf%d" % i, [128, 1024], BF16) for i in range(2)]
            dkf = [Dep("kf%d" % i) for i in range(2)]
            kb = [_alloc(st, nc, "kb%d" % i, [128, 1024], BF16) for i in range(2)]
            dkb = [Dep("kb%d" % i) for i in range(2)]
            kTb = [_alloc(st, nc, "kTb%d" % i, [128, 8, 128], BF16) for i in range(2)]
            dkTb = [Dep("kTb%d" % i) for i in range(2)]
            vtok = [_alloc(st, nc, "vtok%d" % i, [128, 2048], BF16) for i in range(2)]
            dvtok = [[Dep("vtok%d_%d" % (i, k)) for k in range(2)] for i in range(2)]
            Sbf = _alloc(st, nc, "Sbf", [128, 4096], BF16)
            dSbfh = [Dep("Sbf%d" % h) for h in range(4)]
            dS32h = [Dep("S32b_%d" % h) for h in range(4)]
            em.op("dve", lambda e: e.memset(S32[:, :, :, :], 0.0), reads=[dS32], writes=dS32h)
            sst = [_alloc(st, nc, "ss%d" % i, [128, 4], F32) for i in range(3)]
            dsst = [Dep("ss%d" % i) for i in range(3)]
            tp = _palloc(st, nc, "tp", [128, 8, 128], BF16)
            dtp = PDep("tp")
            pk = _palloc(st, nc, "pk", [128, 1024], F32)
            dpk = PDep("pk")
            pv1 = _palloc(st, nc, "pv", [128, 1024], F32)
            pv = [pv1, pv1]
            dpv1 = PDep("pv")
            dpv = [dpv1, dpv1]
            dspr = [_palloc(st, nc, "dsp%d" % i, [128, 512], F32) for i in range(2)]
            dsdr = [PDep("dsp%d" % i) for i in range(2)]

            def loads1(n):
                em.dma("sp", xa[n % 3][:, :], src[n * 128:(n + 1) * 128, :], reads=[dsrc[n]], writes=[dxa[n % 3]])
                em.dma("sp", cs[n % NCS][:, :], C.rcs[n * 128:(n + 1) * 128, :], writes=[dcs[n % NCS]])

            def front1(n):
                r = n % 2
                xs, dxs = xa[n % 3], dxa[n % 3]
                hbs, dhbs = hb2[r], dhb2[r]
                ss, dss = sst[n % 3], dsst[n % 3]
                em.op("act", lambda e: e.activation(out=hbs[:, :], in_=xs[:, :], func=AF.Square, accum_out=ss[:, 0:1]),
                      reads=[dxs], writes=[dhbs, dss])
                emit_rstd(em, ss, dss, C.mhalf, C.dmh, D)
                em.op("dve", lambda e: e.scalar_tensor_tensor(out=hbs[:, :], in0=xs[:, :], scalar=ss[:, 2:3], in1=gpre[:, :],
                                                              op0=ALU.mult, op1=ALU.mult),
                      reads=[dxs, dss, dgpre], writes=[dhbs])

            def P1_steps(n):
                r = n % 2
                c_, dc_ = cs[n % NCS], dcs[n % NCS]

                def p0():
                    hbs = hb2[r]

                    def tps(e):
                        for k in range(8):
                            i = e.transpose(tp[:, k, :], hbs[:, k * 128:(k + 1) * 128], C.ident[:, :])
                        return i
                    em.op("pe", tps, reads=[dhb2[r], C.dident], writes=[dtp])
                    em.op("act", lambda e: e.activation(out=hT[r][:, :, :], in_=tp[:, :, :], func=AF.Copy), reads=[dtp], writes=[dhT[r]])

                def p1():
                    for g in range(2):
                        def f(e, g=g):
                            for k in range(8):
                                i = e.matmul(pk[:, g * 512:(g + 1) * 512], lhsT=hT[r][:, k, :], rhs=Wkv[:, k, g * 512:(g + 1) * 512],
                                             start=(k == 0), stop=(k == 7))
                            return i
                        em.op("pe", f, reads=[dhT[r]] + dWkv, writes=[dpk])
                    rotary(pk[:, :], dpk, c_, dc_, rt, drt, krot[r], dkrot[r])

                def p2():
                    for (dst, ddst, col) in ((kf[r], dkf[r], 0), (kb[r], dkb[r], 4)):
                        em.op("dve", lambda e, dst=dst, col=col: e.tensor_tensor(
                            out=dst[:, :].rearrange("p (h f) -> p h f", h=4), in0=krot[r][:, :].rearrange("p (h f) -> p h f", h=4),
                            in1=kdec[:, col:col + 4].unsqueeze(2).broadcast_to([128, 4, 256]), op=ALU.mult),
                            reads=[dkrot[r], dtab], writes=[ddst])

                    def tk(e):
                        for k in range(8):
                            i = e.transpose(tp[:, k, :], krot[r][:, k * 128:(k + 1) * 128], C.ident[:, :])
                        return i
                    em.op("pe", tk, reads=[dkrot[r], C.dident], writes=[dtp])
                    em.op("act", lambda e: e.activation(out=kTb[r][:, :, :], in_=tp[:, :, :], func=AF.Copy), reads=[dtp], writes=[dkTb[r]])
                    em.dma("sp", C.s_kT[n], kTb[r][:, :, :].rearrange("p k t -> p (k t)"), reads=[dkTb[r]], writes=[C.dscr[n]], owner=dkTb[r])
                    em.dma("sp", C.s_kf[n], kf[r][:, :], reads=[dkf[r]], writes=[C.dscr[n]], owner=dkf[r])

                def mkv(hv):
                    def pv_():
                        for g in range(2):
                            def f(e, g=g):
                                col = 1024 + hv * 1024 + g * 512
                                for k in range(8):
                                    i = e.matmul(pv[hv][:, g * 512:(g + 1) * 512], lhsT=hT[r][:, k, :], rhs=Wkv[:, k, col:col + 512],
                                                 start=(k == 0), stop=(k == 7))
                                return i
                            em.op("pe", f, reads=[dhT[r]] + dWkv, writes=[dpv[hv]])
                        em.op("act", lambda e: e.activation(out=vtok[r][:, hv * 1024:(hv + 1) * 1024], in_=pv[hv][:, :], func=AF.Copy),
                              reads=[dpv[hv]], writes=[dvtok[r][hv]])
                    return pv_

                def p5():
                    em.dma("sp", C.s_v[n], vtok[r][:, :], reads=dvtok[r], writes=[C.dscr[n]], owner=dvtok[r][0])
                return [p0, p1, mkv(0), mkv(1), p2, p5]

            def U(n, steps):
                r = n % 2
                if n - 3 >= 0:
                    loads1(n - 3)
                if n - 2 >= 0:
                    front1(n - 2)
                for h in range(4):
                    em.op("act", lambda e, h=h: e.activation(out=Sbf[:, h * 1024:(h + 1) * 1024],
                                                             in_=S32[:, h, :, :].rearrange("p c f -> p (c f)"), func=AF.Copy),
                          reads=[dS32h[h]], writes=[dSbfh[h]])
                    if h == 3:
                        em.dma("sp", C.s_sb[n], Sbf[:, :], reads=dSbfh, writes=[C.dscr[n]], owner=dSbfh[0])
                    for c in range(2):
                        if n > 0:
                            dsp, dsd = dspr[c], dsdr[c]

                            def f(e, h=h, c=c, dsp=dsp):
                                return e.matmul(dsp[:, :], lhsT=kb[r][:, h * 256 + c * 128: h * 256 + (c + 1) * 128],
                                                rhs=vtok[r][:, h * 512:(h + 1) * 512], start=True, stop=True)
                            em.op("pe", f, reads=[dkb[r], dvtok[r][h // 2]], writes=[dsd])
                            em.op("dve", lambda e, h=h, c=c, dsp=dsp: e.scalar_tensor_tensor(out=S32[:, h, c, :], in0=S32[:, h, c, :],
                                                                                           scalar=g128[:, 4 + h:5 + h], in1=dsp[:, :],
                                                                                           op0=ALU.mult, op1=ALU.add),
                                  reads=[dS32h[h], dsd, dtab], writes=[dS32h[h]])
                        if steps:
                            steps.pop(0)()
                while steps:
                    steps.pop(0)()
                if n > 0 and n == half_b:
                    em.op("dve", lambda e: e.tensor_scalar(out=S32[:, :, :, :], in0=S32[:, :, :, :], scalar1=bnd[:, 0:1], scalar2=None,
                                                           op0=ALU.mult), reads=dS32h + [drc], writes=dS32h)

            for i in range(1, 4):
                if nch - i >= 0:
                    loads1(nch - i)
            front1(nch - 1)
            if nch > 1:
                front1(nch - 2)
            for s_ in P1_steps(nch - 1):
                s_()
            for n in range(nch - 1, -1, -1):
                U(n, P1_steps(n - 1) if n > 0 else [])
            em.barrier()

        em.op("dve", lambda e: e.memset(S32[:, :, :, :], 0.0), writes=[dS32])
        with ExitStack() as st:
            Wqg = _alloc(st, nc, "Wqg", [128, 8, 3072], BF16)
            dWqg = [Dep("Wqg%d" % k) for k in range(2)]
            wv = C.rwi[0].rearrange("(k p) f -> p k f", p=128)
            em.dma("pool", Wqg[:, :, 0:1024], wv[:, :, 0:1024], writes=[dWqg[0]])
            em.dma("pool", Wqg[:, :, 1024:3072], wv[:, :, 4096:6144], writes=[dWqg[1]])
            Wro = _alloc(st, nc, "Wro", [128, 16, D], BF16)
            dWro = Dep("Wro")
            em.dma("pool", Wro[:, :, :], C.rwo[0].rearrange("(k p) f -> p k f", p=128), writes=[dWro])
            gpre, dgpre, gpost, dgpost = load_gains(C, em, st, nc, l, 2, 3, 1.0)
            NX = 4
            xa = [_alloc(st, nc, "xa%d" % i, [128, D], F32) for i in range(NX)]
            dxa = [Dep("xa%d" % i) for i in range(NX)]
            hb = _alloc(st, nc, "hb", [128, D], BF16)
            dhb = Dep("hb")
            NH = 3
            hT = [_alloc(st, nc, "hT%d" % i, [128, 8, 128], BF16) for i in range(NH)]
            dhT = [Dep("hT%d" % i) for i in range(NH)]
            cs = [_alloc(st, nc, "rcs%d" % i, [128, 256], F32) for i in range(2)]
            dcs = [Dep("rcs%d" % i) for i in range(2)]
            rt2 = [_alloc(st, nc, "rrt%d" % i, [128, 4, 128], F32) for i in range(2)]
            drt2 = [Dep("rrt%d" % i) for i in range(2)]
            rt = [rt2[0], rt2[1], rt2[0], rt2[1]]
            drt = [drt2[0], drt2[1], drt2[0], drt2[1]]
            qrot = _alloc(st, nc, "qrot", [128, 1024], BF16)
            dqrot = Dep("qrot")
            qT = [[_alloc(st, nc, "qT%d_%d" % (r, i), [128, 8, 128], BF16) for i in range(3)] for r in range(2)]
            dqT = [[Dep("qT%d_%d" % (r, i)) for i in range(3)] for r in range(2)]
            sg = [_alloc(st, nc, "sg%d" % i, [128, 512], BF16) for i in range(2)]
            dsg = [Dep("sg%d" % i) for i in range(2)]
            NR = 2
            kTl = [_alloc(st, nc, "kTl%d" % i, [128, 8, 128], BF16) for i in range(NR)]
            kfl = [_alloc(st, nc, "kfl%d" % i, [128, 1024], BF16) for i in range(NR)]
            vl = [_alloc(st, nc, "vl%d" % i, [128, 2048], BF16) for i in range(NR)]
            sbl1 = _alloc(st, nc, "sbl", [128, 4, 2, 512], BF16)
            sbl = [sbl1, sbl1]
            dkTl = [Dep("kTl%d" % i) for i in range(NR)]
            dkfl = [Dep("kfl%d" % i) for i in range(NR)]
            dvl = [Dep("vl%d" % i) for i in range(NR)]
            dsblh = [Dep("sbl_%d" % h) for h in range(4)]
            Sfb = _alloc(st, nc, "Sfb", [128, 4, 2, 512], BF16)
            dSfb = [Dep("Sfb%d" % h) for h in range(4)]
            dS32h = [Dep("S32_%d" % h) for h in range(4)]
            STb = [_alloc(st, nc, "STb%d" % i, [128, 128], BF16) for i in range(2)]
            dSTb = [Dep("STb%d" % i) for i in range(2)]
            otok = [_alloc(st, nc, "otok%d" % i, [128, 2048], BF16) for i in range(2)]
            dotok = [[Dep("otok%d_%d" % (i, h)) for h in range(4)] for i in range(2)]
            oT = _alloc(st, nc, "oT", [128, 16, 128], BF16)
            doT = Dep("oT")
            junk = _alloc(st, nc, "junk", [128, D], BF16)
            djunk = Dep("junk")
            junk2 = junk
            djunk2 = djunk
            sst = [_alloc(st, nc, "ss%d" % i, [128, 4], F32) for i in range(6)]
            dsst = [Dep("ss%d" % i) for i in range(6)]
            tp = _palloc(st, nc, "tp", [128, 8, 128], BF16)
            dtp = PDep("tp")
            pq = _palloc(st, nc, "pq", [128, 1024], F32)
            dpq = PDep("pq")
            pG = _palloc(st, nc, "pG", [128, 512], F32)
            dpG = PDep("pG")
            pS = _palloc(st, nc, "pS", [128, 512], F32)
            dpS = PDep("pS")
            pY = [_palloc(st, nc, "pY%d" % i, [128, 512], F32) for i in range(2)]
            dpY = [PDep("pY%d" % i) for i in range(2)]
            pD = _palloc(st, nc, "pD", [128, 512], F32)
            dpD = PDep("pD")
            sctr = [0]
            em.op("dve", lambda e: e.memset(Sfb[:, :, :, :], 0.0), writes=dSfb)
            em.op("dve", lambda e: e.memset(S32[:, :, :, :], 0.0), reads=[dS32], writes=dS32h)

            def nss():
                si = sctr[0] % 6
                sctr[0] += 1
                return sst[si], dsst[si]

            def load_sb(n, h):
                em.dma("sp", sbl1[:, h, :, :].rearrange("p c f -> p (c f)"), C.s_sb[n][:, h * 1024:(h + 1) * 1024],
                       reads=[C.dscr[n]], writes=[dsblh[h]])

            hb2 = [hb, _alloc(st, nc, "hb_b", [128, D], BF16)]
            dhb2 = [dhb, Dep("hb_b")]

            def front(n):
                xs, dxs = xa[n % NX], dxa[n % NX]
                hbs, dhbs = hb2[n % 2], dhb2[n % 2]
                em.dma("sp", xs[:, :], src[n * 128:(n + 1) * 128, :], reads=[dsrc[n]], writes=[dxs])
                em.dma("sp", cs[n % 2][:, :], C.rcs[n * 128:(n + 1) * 128, :], writes=[dcs[n % 2]])
                ss, dss = nss()
                em.op("act", lambda e: e.activation(out=hbs[:, :], in_=xs[:, :], func=AF.Square, accum_out=ss[:, 0:1]),
                      reads=[dxs], writes=[dhbs, dss])
                emit_rstd(em, ss, dss, C.mhalf, C.dmh, D)
                em.op("dve", lambda e: e.scalar_tensor_tensor(out=hbs[:, :], in0=xs[:, :], scalar=ss[:, 2:3], in1=gpre[:, :],
                                                              op0=ALU.mult, op1=ALU.mult),
                      reads=[dxs, dss, dgpre], writes=[dhbs])

            def P_steps(n):
                r = n % 2
                xs, dxs = xa[n % NX], dxa[n % NX]
                c_, dc_ = cs[r], dcs[r]

                def p0():
                    if n == 0:
                        for h in range(4):
                            load_sb(0, h)
                    hbs = hb2[n % 2]

                    def tps(e):
                        for k in range(8):
                            i = e.transpose(tp[:, k, :], hbs[:, k * 128:(k + 1) * 128], C.ident[:, :])
                        return i
                    em.op("pe", tps, reads=[dhb2[n % 2], C.dident], writes=[dtp])
                    em.op("act", lambda e: e.activation(out=hT[n % NH][:, :, :], in_=tp[:, :, :], func=AF.Copy), reads=[dtp], writes=[dhT[n % NH]])

                def p1():
                    em.dma("sp", kTl[r][:, :, :].rearrange("p k t -> p (k t)"), C.s_kT[n], reads=[C.dscr[n]], writes=[dkTl[r]])
                    em.dma("sp", kfl[r][:, :], C.s_kf[n], reads=[C.dscr[n]], writes=[dkfl[r]])
                    em.dma("sp", vl[r][:, :], C.s_v[n], reads=[C.dscr[n]], writes=[dvl[r]])
                    for g in range(2):
                        def f(e, g=g):
                            for k in range(8):
                                i = e.matmul(pq[:, g * 512:(g + 1) * 512], lhsT=hT[n % NH][:, k, :], rhs=Wqg[:, k, g * 512:(g + 1) * 512],
                                             start=(k == 0), stop=(k == 7))
                            return i
                        em.op("pe", f, reads=[dhT[n % NH], dWqg[0]], writes=[dpq])
                    rotary(pq[:, :], dpq, c_, dc_, rt, drt, qrot, dqrot)

                def p2():
                    def tq(e):
                        for k in range(8):
                            i = e.transpose(tp[:, k, :], qrot[:, k * 128:(k + 1) * 128], C.ident[:, :])
                        return i
                    em.op("pe", tq, reads=[dqrot, C.dident], writes=[dtp])
                    em.op("act", lambda e: e.activation(out=qT[r][0][:, :, :], in_=tp[:, :, :], func=AF.Copy), reads=[dtp], writes=[dqT[r][0]])
                    for i in range(2):
                        em.op("dve", lambda e, i=i: e.tensor_tensor(
                            out=qT[r][1 + i][:, :, :].rearrange("p (h c) t -> p h c t", c=2),
                            in0=qT[r][0][:, :, :].rearrange("p (h c) t -> p h c t", c=2),
                            in1=decrow[:, 4 * i:4 * i + 4, :].unsqueeze(2).broadcast_to([128, 4, 2, 128]), op=ALU.mult),
                            reads=[dqT[r][0], dtab], writes=[dqT[r][1 + i]])
                return [p0, p1, p2]

            def O_steps(n):
                r = n % 2
                steps = []
                for half in range(2):
                    def o_t(half=half):
                        def to(e):
                            for k in range(8):
                                i = e.transpose(tp[:, k, :], otok[r][:, (half * 8 + k) * 128:(half * 8 + k + 1) * 128], C.ident[:, :])
                            return i
                        em.op("pe", to, reads=dotok[r] + [C.dident], writes=[dtp])
                        em.op("act", lambda e: e.activation(out=oT[:, half * 8:(half + 1) * 8, :], in_=tp[:, :, :], func=AF.Copy),
                              reads=[dtp], writes=[doT])
                    steps.append(o_t)

                def o_w():
                    def fw(e):
                        for half in range(2):
                            for k in range(16):
                                i = e.matmul(pq[:, half * 512:(half + 1) * 512], lhsT=oT[:, k, :], rhs=Wro[:, k, half * 512:(half + 1) * 512],
                                             start=(k == 0), stop=(k == 15))
                        return i
                    em.op("pe", fw, reads=[doT, dWro], writes=[dpq])

                def o_e():
                    ss, dss = nss()
                    emit_post_residual(C, em, pq[:, :], dpq, xa[n % NX], dxa[n % NX], gpost, dgpost, ss, dss,
                                       junk, djunk, C.y[n * 128:(n + 1) * 128, :], C.dy[n])
                steps += [o_w, o_e]
                return steps

            def H(n, steps):
                r = n % 2
                last = (n + 1 >= nch)

                def GS(h):
                    def fg(e):
                        for k in range(8):
                            i = e.matmul(pG[:, :], lhsT=hT[n % NH][:, k, :], rhs=Wqg[:, k, 1024 + h * 512:1024 + (h + 1) * 512],
                                         start=(k == 0), stop=(k == 7))
                        return i
                    em.op("pe", fg, reads=[dhT[n % NH], dWqg[1]], writes=[dpG])
                    em.op("act", lambda e: e.activation(out=sg[h % 2][:, :], in_=pG[:, :], func=AF.Silu), reads=[dpG], writes=[dsg[h % 2]])

                    def fs(e):
                        for c in range(2):
                            i = e.matmul(pS[:, 0:128], lhsT=kTl[r][:, 2 * h + c, :], rhs=qT[r][0][:, 2 * h + c, :], start=(c == 0), stop=(c == 1))
                        return i
                    em.op("pe", fs, reads=[dkTl[r], dqT[r][0]], writes=[dpS])
                    em.op("dve", lambda e: e.tensor_tensor(out=STb[h % 2][:, :], in0=pS[:, 0:128], in1=DT[:, h, :], op=ALU.mult),
                          reads=[dpS, dtab], writes=[dSTb[h % 2]])

                def Y(h):
                    yh, dyh = pY[h % 2], dpY[h % 2]

                    def fy(e):
                        e.matmul(yh[:, :], lhsT=STb[h % 2][:, :], rhs=vl[r][:, h * 512:(h + 1) * 512], start=True, stop=False)
                        for c in range(2):
                            e.matmul(yh[:, :], lhsT=qT[r][1][:, 2 * h + c, :], rhs=Sfb[:, h, c, :], start=False, stop=False)
                        for c in range(2):
                            i = e.matmul(yh[:, :], lhsT=qT[r][2][:, 2 * h + c, :], rhs=sbl[r][:, h, c, :], start=False, stop=(c == 1))
                        return i
                    em.op("pe", fy, reads=[dSTb[h % 2], dvl[r], dqT[r][1], dqT[r][2], dSfb[h], dsblh[h]], writes=[dyh])
                    ss, dss = nss()
                    em.op("act", lambda e: e.activation(out=junk2[:, 0:512], in_=yh[:, :], func=AF.Square, accum_out=ss[:, 0:1]),
                          reads=[dyh], writes=[djunk2, dss])
                    emit_rstd(em, ss, dss, C.mhalf, C.dmh, 512)
                    em.op("dve", lambda e: e.scalar_tensor_tensor(out=otok[r][:, h * 512:(h + 1) * 512], in0=yh[:, :], scalar=ss[:, 2:3],
                                                                  in1=sg[h % 2][:, :], op0=ALU.mult, op1=ALU.mult),
                          reads=[dyh, dss, dsg[h % 2]], writes=[dotok[r][h]])

                def UPD(h, cs_):
                    for c in cs_:
                        def f(e, c=c):
                            return e.matmul(pD[:, :], lhsT=kfl[r][:, h * 256 + c * 128: h * 256 + (c + 1) * 128],
                                            rhs=vl[r][:, h * 512:(h + 1) * 512], start=True, stop=True)
                        em.op("pe", f, reads=[dkfl[r], dvl[r]], writes=[dpD])
                        em.op("dve", lambda e, c=c: e.scalar_tensor_tensor(out=S32[:, h, c, :], in0=S32[:, h, c, :],
                                                                         scalar=g128[:, h:h + 1], in1=pD[:, :],
                                                                         op0=ALU.mult, op1=ALU.add),
                              reads=[dS32h[h], dpD, dtab], writes=[dS32h[h]])
                    if 1 not in cs_:
                        return
                    if n + 1 == half_b:
                        em.op("dve", lambda e: e.tensor_scalar(out=S32[:, h, :, :], in0=S32[:, h, :, :], scalar1=bnd[:, 0:1], scalar2=None,
                                                               op0=ALU.mult), reads=[dS32h[h], drc], writes=[dS32h[h]])
                    em.op("act", lambda e: e.activation(out=Sfb[:, h, :, :], in_=S32[:, h, :, :], func=AF.Copy),
                          reads=[dS32h[h]], writes=[dSfb[h]])

                GS(0)
                for h in range(4):
                    if h + 1 < 4:
                        GS(h + 1)
                    if h >= 1 and not last:
                        UPD(h - 1, [1])
                    Y(h)
                    if not last:
                        load_sb(n + 1, h)
                        UPD(h, [0])
                    for _ in range(2):
                        if steps:
                            steps.pop(0)()
                if not last:
                    UPD(3, [1])
                while steps:
                    steps.pop(0)()

            nop = lambda: None
            front(0)
            if nch > 1:
                front(1)
            for s_ in P_steps(0):
                s_()
            if nch > 1:
                P_steps(1)[0]()
            for n in range(nch):
                O = O_steps(n - 1) if n >= 1 else [nop] * 4
                P = P_steps(n + 1) if n + 1 < nch else [nop] * 3
                P0n = P_steps(n + 2)[0] if n + 2 < nch else nop
                fr = (lambda n=n: front(n + 2)) if n + 2 < nch else nop
                steps = [O[0], P[1], O[1], fr, P[2], O[2], P0n, O[3]]
                H(n, steps)
            for s_ in O_steps(nch - 1):
                s_()
            em.barrier()


def build(nsub=6, subs=None, nblk=NCH, dbg=3):
    nc = bass.Bass("TRN2", target_bir_lowering=False)
    em = Emitter(nc)
    C = Ctx()
    C.nc, C.em = nc, em

    def din(name, shape, dt=F32):
        return nc.dram_tensor(name, shape, dt, kind="ExternalInput").ap()
    C.xin = din("xin", [NTOK, D])
    C.ng = din("norm_gains", [2, 6, D])
    C.fwi = din("ffn_w_in", [2, 2, D, 2 * DFF])
    C.fwo = din("ffn_w_out", [2, 2, DFF, D])
    C.wqkv = din("attn_w_qkv", [1, D, 1536])
    C.wo = din("attn_w_o", [1, D, D])
    C.sink = din("attn_sink", [1, 16])
    C.rwi = din("ret_w_in", [1, D, 6144])
    C.rwo = din("ret_w_o", [1, 2048, D])
    C.rdf = din("ret_decay_fwd", [1, 4])
    C.rdb = din("ret_decay_bwd", [1, 4])
    C.identd = din("c_ident", [128, 128], BF16)
    C.acs = din("c_acs", [NTOK, 320])
    C.amask = din("c_amask", [NCH, 128, 384], BF16)
    C.rconst = din("c_rconst", [128, 6, 128])
    C.rpos = din("c_rpos", [128, 4])
    C.rbnd = din("c_rbnd", [128, 1])
    C.rcs = din("c_rcs", [NTOK, 256])
    C.s_kT = nc.dram_tensor("s_kT", [NCH, 128, 1024], BF16, kind="Internal").ap()
    C.s_kf = nc.dram_tensor("s_kf", [NCH, 128, 1024], BF16, kind="Internal").ap()
    C.s_v = nc.dram_tensor("s_v", [NCH, 128, 2048], BF16, kind="Internal").ap()
    C.s_sb = nc.dram_tensor("s_sb", [NCH, 128, 4096], BF16, kind="Internal").ap()
    C.dscr = [Dep("scr%d" % i) for i in range(NCH)]
    C.y = nc.dram_tensor("y", [NTOK, D], F32, kind="ExternalOutput").ap()
    C.dy = [Dep("y%d" % i) for i in range(NCH)]
    C.dxin = [Dep("xin%d" % i) for i in range(NCH)]

    C.ident = nc.alloc_sbuf_tensor("ident", [128, 128], BF16)
    C.dident = Dep("ident")
    C.mhalf = nc.alloc_sbuf_tensor("mhalf", [128, 1], F32)
    C.dmh = Dep("mhalf")
    em.dma("sp", C.ident[:, :], C.identd[:, :], writes=[C.dident])
    em.op("pool", lambda e: e.memset(C.mhalf[:, :], -0.5), writes=[C.dmh])

    C.nblk = nblk
    C.dbg = dbg
    if subs is None:
        subs = [("ffn", 0, 0), ("attn", 0, 0), ("ffn", 0, 1), ("ffn", 1, 0), ("ret", 1, 0), ("ffn", 1, 1)]
    src, dsrc = C.xin, C.dxin
    for i, (kind, l, which) in enumerate(subs[:nsub]):
        if kind == "ffn":
            emit_ffn(C, l, which, src, dsrc)
        elif kind == "attn":
            emit_attn(C, l, src, dsrc)
        elif kind == "ret":
            emit_ret(C, l, src, dsrc)
        src, dsrc = C.y, C.dy
    em.finish()
    return nc


def make_consts():
    c = {}
    c["c_ident"] = np.eye(128, dtype=np.float32).astype(ml_dtypes.bfloat16)
    j = np.arange(128, dtype=np.float32)[:, None]
    i = np.arange(128, dtype=np.float32)[None, :]
    rc = np.zeros((128, 6, 128), np.float32)
    rc[:, 0] = np.maximum(i - j, 0.0)
    rc[:, 1] = np.maximum(j - i, 0.0)
    rc[:, 2] = (i >= j) / 16.0
    rc[:, 3] = (j > i) / 16.0
    rc[:, 4] = np.broadcast_to(i + 1.0, (128, 128))
    rc[:, 5] = np.broadcast_to(128.0 - i, (128, 128))
    c["c_rconst"] = rc
    p = np.arange(128, dtype=np.float32)
    c["c_rpos"] = np.stack([p + 1, 128 - p, 127 - p, p], axis=1).astype(np.float32)
    return c


def make_core_consts(seqlen):
    c = {}
    tok = np.arange(NTOK)
    pos = (tok % seqlen).astype(np.float32)
    inv = (500000.0 ** (-np.arange(8, dtype=np.float32) / 8)).astype(np.float32)
    ang = pos[:, None] * inv[None, :]
    cos = np.cos(ang).astype(np.float32)
    sin = np.sin(ang).astype(np.float32)
    acs = np.concatenate([np.tile(cos[:, None, :], (1, 20, 1)).reshape(NTOK, 160),
                          np.tile(sin[:, None, :], (1, 20, 1)).reshape(NTOK, 160)], axis=1)
    c["c_acs"] = np.ascontiguousarray(acs, dtype=np.float32)
    b = np.arange(NCH)[:, None, None]
    qi = np.arange(128)[None, :, None]
    kc = np.arange(384)[None, None, :]
    tq = 128 * b + qi
    tk = 128 * (b - 1) + kc
    valid = (tk >= 0) & (tk < NTOK) & ((tk // seqlen) == (tq // seqlen)) & (np.abs(tk - tq) <= 128)
    c["c_amask"] = np.where(valid, 0.0, NEG).astype(np.float32).astype(ml_dtypes.bfloat16)
    invr = (10000.0 ** (-np.arange(128, dtype=np.float32) / 128)).astype(np.float32)
    angr = pos[:, None] * invr[None, :]
    c["c_rcs"] = np.ascontiguousarray(np.concatenate([np.cos(angr), np.sin(angr)], axis=1), dtype=np.float32)
    c["c_rbnd"] = np.full((128, 1), 1.0 if seqlen == NTOK else 0.0, np.float32)
    return c


def kernel(x_prompt, x_sample, norm_gains, ffn_w_in, ffn_w_out, attn_w_qkv, attn_w_o, attn_sink,
           ret_w_in, ret_w_o, ret_decay_fwd, ret_decay_bwd, _nsub=6, _subs=None, _nblk=NCH, _cores=None, _trace=False, _dbg=3):
    f = lambda a: np.ascontiguousarray(np.asarray(a, dtype=np.float32))
    xp = f(x_prompt).reshape(4, NTOK, D)
    xs = f(x_sample).reshape(4, NTOK, D)
    shared = {
        "norm_gains": f(norm_gains), "ffn_w_in": f(ffn_w_in), "ffn_w_out": f(ffn_w_out),
        "attn_w_qkv": f(attn_w_qkv), "attn_w_o": f(attn_w_o), "attn_sink": f(attn_sink),
        "ret_w_in": f(ret_w_in), "ret_w_o": f(ret_w_o), "ret_decay_fwd": f(ret_decay_fwd),
        "ret_decay_bwd": f(ret_decay_bwd),
    }
    shared.update(make_consts())
    in_maps = []
    cc = [make_core_consts(2048), make_core_consts(4096)]
    for c in range(8):
        m = dict(shared)
        m.update(cc[0] if c < 4 else cc[1])
        m["xin"] = xp[c] if c < 4 else xs[c - 4]
        in_maps.append(m)
    nc = build(_nsub, _subs, _nblk, _dbg)
    if _cores is not None:
        res = run_bass_kernel_spmd(nc, [in_maps[c] for c in _cores], core_ids=list(range(len(_cores))), trace=_trace)
        if _trace:
            print("exec_time_ns", res.exec_time_ns)
        return [np.asarray(r["y"], dtype=np.float32) for r in res.results]
    res = run_bass_kernel_spmd(nc, in_maps, core_ids=list(range(8)))
    outs = [np.asarray(r["y"], dtype=np.float32) for r in res.results]
    y_prompt = np.stack(outs[:4]).reshape(8, 2048, D)
    y_sample = np.stack(outs[4:]).reshape(4, 4096, D)
    return (y_prompt, y_sample)
```

```python
import os
import numpy as np
import ml_dtypes
from contextlib import ExitStack
import concourse.bass as bass
import concourse.mybir as mybir
from concourse.bass_utils import run_bass_kernel_spmd

F32 = mybir.dt.float32
BF16 = mybir.dt.bfloat16
AF = mybir.ActivationFunctionType
ALU = mybir.AluOpType
AX = mybir.AxisListType

NTOK = 4096
NCH = 32
D = 1024
DFF = 2816
EPS = 1e-6
EPOCH = 1 << 30
NEG = -30000.0


class Dep:
    __slots__ = ("name", "w", "rs", "dsem", "dcnt", "ex", "retired")

    def __init__(self, name="", ex=False):
        self.name = name
        self.ex = ex
        self.w = None
        self.rs = {}
        self.dsem = None
        self.dcnt = 0
        self.retired = False


class Emitter:
    def __init__(self, nc):
        self.nc = nc
        self.eng = {"pe": nc.tensor, "act": nc.scalar, "dve": nc.vector,
                    "pool": nc.gpsimd, "sp": nc.sync}
        self.sem = {}
        self.cnt = {}
        self.nsem = 0
        for e in self.eng:
            self.sem[e] = self._newsem("e_" + e)
            self.cnt[e] = 0
        self.waited = {}
        self.dma_owners = []
        self.free_dsems = []
        self.no_recycle = set()

    def _newsem(self, name):
        self.nsem += 1
        return self.nc.alloc_semaphore("%s_%d" % (name, self.nsem))

    def _tick(self, e):
        if self.cnt[e] >= EPOCH:
            self.sem[e] = self._newsem("e_" + e)
            self.cnt[e] = 0
        self.cnt[e] += 1
        return self.sem[e], self.cnt[e]

    def _need(self, e, rec, needs):
        if rec is None:
            return
        if rec[0] == "e":
            _, pe, sem, val = rec
            if pe == e and e == "pe":
                return
            needs.append((sem, val))
        else:
            o = rec[1]
            needs.append((o.dsem, o.dcnt))

    def _collect(self, e, reads, writes):
        needs = []
        for d in reads:
            self._need(e, d.w, needs)
        for d in writes:
            if d.w is not None:
                self._need(e, d.w, needs)
            for k, r in d.rs.items():
                self._need(e, r, needs)
        return needs

    def _emit_waits(self, e, needs):
        eng = self.eng[e]
        best = {}
        for sem, val in needs:
            k = id(sem)
            if k not in best or best[k][1] < val:
                best[k] = (sem, val)
        for k, (sem, val) in best.items():
            wk = (e, k)
            if self.waited.get(wk, 0) >= val:
                continue
            self.waited[wk] = val
            eng.wait_ge(sem, val)

    def op(self, e, fn, reads=(), writes=()):
        xr = [d for d in reads if d.ex]
        needs = []
        if xr:
            for d in xr:
                self._need(e, d.w, needs)
            reads = [d for d in reads if not d.ex]
            writes = list(writes) + [d for d in xr if d not in writes]
        needs += self._collect(e, reads, writes)
        self._emit_waits(e, needs)
        inst = fn(self.eng[e])
        sem, val = self._tick(e)
        inst.then_inc(sem, 1)
        rec = ("e", e, sem, val)
        for d in reads:
            d.rs[e] = rec
        for d in writes:
            d.w = rec
            d.rs = {}
        return inst

    def dma(self, q, out, in_, reads=(), writes=(), owner=None):
        needs = self._collect(q, reads, writes)
        if owner is None:
            owner = writes[0] if writes else reads[0]
        if owner.dsem is None or owner.retired:
            if self.free_dsems and q != "pool":
                owner.dsem, owner.dcnt = self.free_dsems.pop()
            else:
                owner.dsem, owner.dcnt = self._newsem("d_" + owner.name), 0
                if q == "pool":
                    self.no_recycle.add(id(owner.dsem))
            owner.retired = False
            self.dma_owners.append(owner)
        self._emit_waits(q, needs)
        inst = self.eng[q].dma_start(out=out, in_=in_)
        owner.dcnt += 16
        inst.then_inc(owner.dsem, 16)
        rec = ("d", owner)
        for d in reads:
            d.rs["dma%d" % id(owner)] = rec
        for d in writes:
            d.w = rec
            d.rs = {}
        return inst

    def barrier(self):
        pts = [(self.sem[e], self.cnt[e]) for e in self.eng if self.cnt[e] > 0]
        pts += [(o.dsem, o.dcnt) for o in self.dma_owners]
        for e in self.eng:
            self._emit_waits(e, pts)
        for o in self.dma_owners:
            o.retired = True
            if id(o.dsem) not in self.no_recycle:
                self.free_dsems.append((o.dsem, o.dcnt))
        self.dma_owners = []

    def finish(self):
        sp = self.eng["sp"]
        pts = [(o.dsem, o.dcnt) for o in self.dma_owners]
        self._emit_waits("sp", pts)


def PDep(name):
    return Dep(name, ex=True)


class Ctx:
    pass


_uid = [0]


def _alloc(st, nc, name, shape, dt):
    _uid[0] += 1
    return st.enter_context(nc.sbuf_tensor("%s_u%d" % (name, _uid[0]), shape, dt))


def _palloc(st, nc, name, shape, dt):
    _uid[0] += 1
    return st.enter_context(nc.psum_tensor("%s_u%d" % (name, _uid[0]), shape, dt))


def emit_rstd(em, ss, dss, mhalf, dmh, n_feat):
    em.op("pool", lambda e: e.tensor_scalar(out=ss[:, 1:2], in0=ss[:, 0:1], scalar1=1.0 / n_feat,
                                            scalar2=EPS, op0=ALU.mult, op1=ALU.add),
          reads=[dss], writes=[dss])
    em.op("pool", lambda e: e.tensor_tensor(out=ss[:, 2:3], in0=ss[:, 1:2], in1=mhalf[:, 0:1], op=ALU.pow),
          reads=[dss, dmh], writes=[dss])


def emit_prenorm_T(C, em, xs, dxs, gpre, dgpre, hbs, dhbs, ss, dss, tp, dtp, hT_dst, dhT):
    em.op("act", lambda e: e.activation(out=hbs[:, :], in_=xs[:, :], func=AF.Square, accum_out=ss[:, 0:1]),
          reads=[dxs], writes=[dhbs, dss])
    emit_rstd(em, ss, dss, C.mhalf, C.dmh, D)
    em.op("dve", lambda e: e.scalar_tensor_tensor(out=hbs[:, :], in0=xs[:, :], scalar=ss[:, 2:3], in1=gpre[:, :],
                                                  op0=ALU.mult, op1=ALU.mult),
          reads=[dxs, dss, dgpre], writes=[dhbs])

    def tps(e):
        for k in range(8):
            i = e.transpose(tp[:, k, :], hbs[:, k * 128:(k + 1) * 128], C.ident[:, :])
        return i
    em.op("pe", tps, reads=[dhbs, C.dident], writes=[dtp])
    em.op("act", lambda e: e.activation(out=hT_dst, in_=tp[:, :, :], func=AF.Copy), reads=[dtp], writes=[dhT])


def load_gains(C, em, st, nc, l, ipre, ipost, post_scale):
    gpre = _alloc(st, nc, "gpre", [128, D], F32)
    gpost = _alloc(st, nc, "gpost", [128, D], F32)
    dgpre = Dep("gpre")
    dgpost = Dep("gpost")
    em.dma("sp", gpre[:, :], C.ng[l, ipre, :].partition_broadcast(128), writes=[dgpre])
    em.dma("sp", gpost[:, :], C.ng[l, ipost, :].partition_broadcast(128), writes=[dgpost])
    if post_scale != 1.0:
        em.op("pool", lambda e: e.tensor_scalar(out=gpost[:, :], in0=gpost[:, :], scalar1=post_scale, scalar2=0.0,
                                                op0=ALU.mult, op1=ALU.add), reads=[dgpost], writes=[dgpost])
    return gpre, dgpre, gpost, dgpost


def emit_post_residual(C, em, ps_out, dps, xs, dxs, gpost, dgpost, ss, dss, junk, djunk, dst_ap, ddst):
    em.op("act", lambda e: e.activation(out=junk[:, :], in_=ps_out, func=AF.Square, accum_out=ss[:, 0:1]),
          reads=[dps], writes=[djunk, dss])
    emit_rstd(em, ss, dss, C.mhalf, C.dmh, D)
    em.op("dve", lambda e: e.scalar_tensor_tensor(out=ps_out, in0=ps_out, scalar=ss[:, 2:3], in1=gpost[:, :],
                                                  op0=ALU.mult, op1=ALU.mult),
          reads=[dps, dss, dgpost], writes=[dps])
    em.op("dve", lambda e: e.tensor_tensor(out=xs[:, :], in0=ps_out, in1=xs[:, :], op=ALU.add),
          reads=[dps, dxs], writes=[dxs])
    em.dma("sp", dst_ap, xs[:, :], reads=[dxs], writes=[ddst], owner=dxs)


def emit_ffn(C, l, which, src, dsrc):
    nc, em = C.nc, C.em
    NJ = DFF // 128
    LAG = 3
    with ExitStack() as st:
        Win = _alloc(st, nc, "Win", [128, 8, 2 * DFF], BF16)
        Wout = _alloc(st, nc, "Wout", [128, NJ, D], BF16)
        dWin = [Dep("Win%d" % k) for k in range(4)]
        dWout = [Dep("Wout%d" % k) for k in range(2)]
        w_in = C.fwi[l, which]
        w_out = C.fwo[l, which]
        wi_v = w_in.rearrange("(k p) f -> p k f", p=128)
        for k in range(4):
            em.dma("pool", Win[:, 2 * k:2 * k + 2, :], wi_v[:, 2 * k:2 * k + 2, :], writes=[dWin[k]])
        wo_v = w_out.rearrange("(j p) d -> p j d", p=128)
        em.dma("pool", Wout[:, 0:11, :], wo_v[:, 0:11, :], writes=[dWout[0]])
        em.dma("pool", Wout[:, 11:22, :], wo_v[:, 11:22, :], writes=[dWout[1]])
        gpre, dgpre, gpost, dgpost = load_gains(C, em, st, nc, l, 0 if which == 0 else 4, 1 if which == 0 else 5, 0.5)

        xa = [_alloc(st, nc, "xa%d" % i, [128, D], F32) for i in range(3)]
        dxa = [Dep("xa%d" % i) for i in range(3)]
        xb = [_alloc(st, nc, "xb%d" % i, [128, D], F32) for i in range(3)]
        dxb = [Dep("xb%d" % i) for i in range(3)]
        hb = [_alloc(st, nc, "hb%d" % i, [128, D], BF16) for i in range(2)]
        dhb = [Dep("hb%d" % i) for i in range(2)]
        hT = [_alloc(st, nc, "hT%d" % i, [128, 8, 256], BF16) for i in range(2)]
        dhT = [[Dep("hT%d_%d" % (i, c)) for c in range(2)] for i in range(2)]
        NA = 6
        actT = [_alloc(st, nc, "actT%d" % i, [128, 256], BF16) for i in range(NA)]
        dact = [Dep("actT%d" % i) for i in range(NA)]
        sg = [_alloc(st, nc, "sg%d" % i, [128, 256], BF16) for i in range(2)]
        dsg = [Dep("sg%d" % i) for i in range(2)]
        junk = _alloc(st, nc, "junk", [128, D], BF16)
        djunk = Dep("junk")
        sst = [_alloc(st, nc, "ss%d" % i, [128, 4], F32) for i in range(4)]
        dsst = [Dep("ss%d" % i) for i in range(4)]
        tp = [_palloc(st, nc, "tp%d" % i, [128, 8, 128], BF16) for i in range(2)]
        dtp = [PDep("tp%d" % i) for i in range(2)]
        gu = [_palloc(st, nc, "gu%d" % i, [128, 2, 256], F32) for i in range(2)]
        dgu = [PDep("gu%d" % i) for i in range(2)]
        pout = _palloc(st, nc, "pout", [128, 2, D], F32)
        dpout = [PDep("pout%d" % i) for i in range(2)]

        NT = NTOK // 256
        sctr = [0]

        def loads(t):
            for c in range(2):
                ch = 2 * t + c
                em.dma("sp", xa[ch % 3][:, :], src[ch * 128:(ch + 1) * 128, :], reads=[dsrc[ch]], writes=[dxa[ch % 3]])

        def front(t):
            for c in range(2):
                ch = 2 * t + c
                xs, dxs = xa[ch % 3], dxa[ch % 3]
                hbs, dhbs = hb[ch % 2], dhb[ch % 2]
                si = sctr[0] % 4
                sctr[0] += 1
                ss, dss = sst[si], dsst[si]
                em.op("act", lambda e, hbs=hbs, xs=xs, ss=ss: e.activation(out=hbs[:, :], in_=xs[:, :], func=AF.Square, accum_out=ss[:, 0:1]),
                      reads=[dxs], writes=[dhbs, dss])
                emit_rstd(em, ss, dss, C.mhalf, C.dmh, D)
                em.op("dve", lambda e, hbs=hbs, xs=xs, ss=ss: e.scalar_tensor_tensor(out=hbs[:, :], in0=xs[:, :], scalar=ss[:, 2:3], in1=gpre[:, :],
                                                                                  op0=ALU.mult, op1=ALU.mult),
                      reads=[dxs, dss, dgpre], writes=[dhbs])

        def transp(t, c):
            ch = 2 * t + c
            hbs = hb[ch % 2]

            def tps(e):
                for k in range(8):
                    i = e.transpose(tp[ch % 2][:, k, :], hbs[:, k * 128:(k + 1) * 128], C.ident[:, :])
                return i
            em.op("pe", tps, reads=[dhb[ch % 2], C.dident], writes=[dtp[ch % 2]])
            em.op("act", lambda e: e.activation(out=hT[t % 2][:, :, c * 128:(c + 1) * 128], in_=tp[ch % 2][:, :, :], func=AF.Copy),
                  reads=[dtp[ch % 2]], writes=[dhT[t % 2][c]])

        def xb_loads(t):
            for tc in range(2):
                ch = 2 * t + tc
                em.dma("sp", xb[ch % 3][:, :], src[ch * 128:(ch + 1) * 128, :], reads=[dsrc[ch]], writes=[dxb[ch % 3]])

        def p1(t, j):
            g = gu[j % 2]
            hTt = hT[t % 2]

            def f(e):
                for half in range(2):
                    for k in range(8):
                        i = e.matmul(g[:, half, :], lhsT=Win[:, k, half * DFF + j * 128: half * DFF + (j + 1) * 128],
                                     rhs=hTt[:, k, :], start=(k == 0), stop=(k == 7))
                return i
            em.op("pe", f, reads=dWin + dhT[t % 2], writes=[dgu[j % 2]])
            s = sg[j % 2]
            em.op("act", lambda e: e.activation(out=s[:, :], in_=g[:, 0, :], func=AF.Silu),
                  reads=[dgu[j % 2]], writes=[dsg[j % 2]])
            a = actT[j % NA]
            em.op("dve", lambda e: e.tensor_tensor(out=a[:, :], in0=g[:, 1, :], in1=s[:, :], op=ALU.mult),
                  reads=[dgu[j % 2], dsg[j % 2]], writes=[dact[j % NA]])

        def p2(t, j):
            a = actT[j % NA]

            def f(e):
                for tc in range(2):
                    for half in range(2):
                        i = e.matmul(pout[:, tc, half * 512:(half + 1) * 512], lhsT=a[:, tc * 128:(tc + 1) * 128],
                                     rhs=Wout[:, j, half * 512:(half + 1) * 512], start=(j == 0), stop=(j == NJ - 1))
                return i
            em.op("pe", f, reads=[dact[j % NA], dWout[0 if j < 11 else 1]], writes=dpout)

        def epilogue(t):
            for tc in range(2):
                ch = 2 * t + tc
                xs, dxs = xb[ch % 3], dxb[ch % 3]
                si = sctr[0] % 4
                sctr[0] += 1
                emit_post_residual(C, em, pout[:, tc, :], dpout[tc], xs, dxs, gpost, dgpost, sst[si], dsst[si],
                                   junk, djunk, C.y[ch * 128:(ch + 1) * 128, :], C.dy[ch])

        loads(0)
        front(0)
        transp(0, 0)
        transp(0, 1)
        if NT > 1:
            loads(1)
        for t in range(NT):
            for j in range(NJ + LAG):
                if j < NJ:
                    p1(t, j)
                if j >= LAG:
                    p2(t, j - LAG)
                if t + 1 < NT:
                    if j == 5:
                        front(t + 1)
                    elif j == 11:
                        transp(t + 1, 0)
                    elif j == 15:
                        transp(t + 1, 1)
                    elif j == 18 and t + 2 < NT:
                        loads(t + 2)
                if j == 13:
                    xb_loads(t)
            epilogue(t)
        em.barrier()


def emit_attn(C, l, src, dsrc):
    nc, em = C.nc, C.em
    SCALE = 0.125
    with ExitStack() as st:
        Wqkv = _alloc(st, nc, "Wqkv", [128, 8, 1536], BF16)
        Wo = _alloc(st, nc, "Wo", [128, 8, D], BF16)
        dWqkv, dWo = Dep("Wqkv"), Dep("Wo")
        em.dma("pool", Wqkv[:, :, :], C.wqkv[0].rearrange("(k p) f -> p k f", p=128), writes=[dWqkv])
        em.dma("pool", Wo[:, :, :], C.wo[0].rearrange("(k p) f -> p k f", p=128), writes=[dWo])
        gpre, dgpre, gpost, dgpost = load_gains(C, em, st, nc, l, 2, 3, 1.0)
        sinkt = _alloc(st, nc, "sinkt", [128, 16], F32)
        nsink = _alloc(st, nc, "nsink", [128, 16], F32)
        dsink = Dep("sink")
        em.dma("sp", sinkt[:, :], C.sink[0, :].partition_broadcast(128), writes=[dsink])
        em.op("pool", lambda e: e.tensor_scalar(out=nsink[:, :], in0=sinkt[:, :], scalar1=-1.0, scalar2=0.0,
                                                op0=ALU.mult, op1=ALU.add), reads=[dsink], writes=[dsink])

        kT = _alloc(st, nc, "kT_all", [128, 8, 34 * 128], BF16)
        vA = _alloc(st, nc, "v_all", [128, 34, 256], BF16)
        dkT = [Dep("kT%d" % i) for i in range(34)]
        dvA = [Dep("vA%d" % i) for i in range(34)]
        for i in (0, 33):
            em.op("dve", lambda e, i=i: e.memset(kT[:, :, i * 128:(i + 1) * 128], 0.0), writes=[dkT[i]])
            em.op("dve", lambda e, i=i: e.memset(vA[:, i, :], 0.0), writes=[dvA[i]])

        NX = 4
        xa = [_alloc(st, nc, "xa%d" % i, [128, D], F32) for i in range(NX)]
        dxa = [Dep("xa%d" % i) for i in range(NX)]
        hb = [_alloc(st, nc, "hb%d" % i, [128, D], BF16) for i in range(2)]
        dhb = [Dep("hb%d" % i) for i in range(2)]
        hT = [_alloc(st, nc, "hT%d" % i, [128, 8, 128], BF16) for i in range(2)]
        dhT = [Dep("hT%d" % i) for i in range(2)]
        cs = [_alloc(st, nc, "cs%d" % i, [128, 320], F32) for i in range(2)]
        dcs = [Dep("cs%d" % i) for i in range(2)]
        mk = [_alloc(st, nc, "mk%d" % i, [128, 384], BF16) for i in range(2)]
        dmk = [Dep("mk%d" % i) for i in range(2)]
        qtok = [_alloc(st, nc, "qtok%d" % i, [128, 16, 64], BF16) for i in range(2)]
        dqtok = [Dep("qtok%d" % i) for i in range(2)]
        kdtok = [_alloc(st, nc, "kdtok%d" % i, [128, 4, 2, 128], BF16) for i in range(2)]
        dkdtok = [Dep("kdtok%d" % i) for i in range(2)]
        for i in range(2):
            em.op("dve", lambda e, i=i: e.memset(kdtok[i][:, :, :, :], 0.0), writes=[dkdtok[i]])
        rt = [_alloc(st, nc, "rt%d" % i, [128, 20, 8], F32) for i in range(4)]
        drt = [Dep("rt%d" % i) for i in range(4)]
        NQ = 3
        qT = [_alloc(st, nc, "qT%d" % i, [128, 8, 128], BF16) for i in range(NQ)]
        dqT = [Dep("qT%d" % i) for i in range(NQ)]
        pb = [_alloc(st, nc, "pb%d" % i, [128, 384], BF16) for i in range(2)]
        dpb = [Dep("pb%d" % i) for i in range(2)]
        pT = [_alloc(st, nc, "pT%d" % i, [128, 3, 128], BF16) for i in range(2)]
        dpT = [Dep("pT%d" % i) for i in range(2)]
        stt = [_alloc(st, nc, "stt%d" % i, [128, 6, 16], F32) for i in range(2)]
        dsth = [[Dep("st%d_%d" % (i, h)) for h in range(16)] for i in range(2)]
        dfin = [[Dep("fin%d_%d" % (i, k)) for k in range(2)] for i in range(2)]
        otok = [_alloc(st, nc, "otok%d" % i, [128, 16, 64], BF16) for i in range(2)]
        dotok = [[Dep("otok%d_%d" % (i, k)) for k in range(2)] for i in range(2)]
        oT = _alloc(st, nc, "oT", [128, 8, 128], BF16)
        doT = Dep("oT")
        junk = _alloc(st, nc, "junk", [128, D], BF16)
        djunk = Dep("junk")
        sst = [_alloc(st, nc, "ss%d" % i, [128, 4], F32) for i in range(4)]
        dsst = [Dep("ss%d" % i) for i in range(4)]

        qkv = _palloc(st, nc, "qkv", [128, 1536], F32)
        dq01, dq2 = PDep("qkv01"), PDep("qkv2")
        tp = _palloc(st, nc, "tp", [128, 8, 128], BF16)
        dtp = PDep("tp")
        sps = _palloc(st, nc, "sps", [128, 2, 512], F32)
        dsps = [PDep("sps%d" % i) for i in range(2)]
        ptp = _palloc(st, nc, "ptp", [128, 2, 4, 128], BF16)
        dptp = PDep("ptp")
        ops = _palloc(st, nc, "ops", [128, 8, 64], F32)
        dops = PDep("ops")
        sctr = [0]

        def A_steps(b):
            xs, dxs = xa[b % NX], dxa[b % NX]
            c_, dc_ = cs[b % 2], dcs[b % 2]
            h_ = hT[b % 2]
            qt, dqt = qtok[b % 2], dqtok[b % 2]
            kd, dkd = kdtok[b % 2], dkdtok[b % 2]

            def a0():
                em.dma("sp", xs[:, :], src[b * 128:(b + 1) * 128, :], reads=[dsrc[b]], writes=[dxs])
                em.dma("sp", c_[:, :], C.acs[b * 128:(b + 1) * 128, :], writes=[dc_])
                si = sctr[0] % 4
                sctr[0] += 1
                emit_prenorm_T(C, em, xs, dxs, gpre, dgpre, hb[b % 2], dhb[b % 2], sst[si], dsst[si],
                               tp, dtp, h_[:, :, :], dhT[b % 2])

            def a1():
                for g in range(3):
                    def f(e, g=g):
                        for k in range(8):
                            i = e.matmul(qkv[:, g * 512:(g + 1) * 512], lhsT=h_[:, k, :], rhs=Wqkv[:, k, g * 512:(g + 1) * 512],
                                         start=(k == 0), stop=(k == 7))
                        return i
                    em.op("pe", f, reads=[dhT[b % 2], dWqkv], writes=[dq01 if g < 2 else dq2])
                qv = qkv[:, 0:1024].rearrange("p (h d) -> p h d", d=64)
                kv = qkv[:, 1024:1280].rearrange("p (h d) -> p h d", d=64)
                em.op("act", lambda e: e.activation(out=qt[:, :, 16:64], in_=qv[:, :, 16:64], func=AF.Copy),
                      reads=[dq01], writes=[dqt])
                for dup in range(2):
                    em.op("act", lambda e, dup=dup: e.activation(out=kd[:, :, dup, dup * 64 + 16:dup * 64 + 64], in_=kv[:, :, 16:64],
                                                                 func=AF.Copy), reads=[dq2], writes=[dkd])
                em.op("act", lambda e: e.activation(out=vA[:, b + 1, :], in_=qkv[:, 1280:1536], func=AF.Copy),
                      reads=[dq2], writes=[dvA[b + 1]])

            def a2():
                qkv20 = qkv[:, 0:1280].rearrange("p (h d) -> p h d", d=64)
                cosv = c_[:, 0:160].rearrange("p (h d) -> p h d", d=8)
                sinv = c_[:, 160:320].rearrange("p (h d) -> p h d", d=8)
                x1 = qkv20[:, :, 0:8]
                x2 = qkv20[:, :, 8:16]
                for i, (xx, tb) in enumerate([(x1, cosv), (x2, sinv), (x2, cosv), (x1, sinv)]):
                    em.op("dve", lambda e, i=i, xx=xx, tb=tb: e.tensor_tensor(out=rt[i][:, :, :], in0=xx, in1=tb, op=ALU.mult),
                          reads=[dq01, dq2, dc_], writes=[drt[i]])
                em.op("dve", lambda e: e.tensor_tensor(out=qt[:, :, 0:8], in0=rt[0][:, 0:16, :], in1=rt[1][:, 0:16, :], op=ALU.subtract),
                      reads=[drt[0], drt[1]], writes=[dqt])
                em.op("dve", lambda e: e.tensor_tensor(out=qt[:, :, 8:16], in0=rt[2][:, 0:16, :], in1=rt[3][:, 0:16, :], op=ALU.add),
                      reads=[drt[2], drt[3]], writes=[dqt])
                for dup in range(2):
                    em.op("dve", lambda e, dup=dup: e.tensor_tensor(out=kd[:, :, dup, dup * 64:dup * 64 + 8], in0=rt[0][:, 16:20, :],
                                                                    in1=rt[1][:, 16:20, :], op=ALU.subtract),
                          reads=[drt[0], drt[1]], writes=[dkd])
                    em.op("dve", lambda e, dup=dup: e.tensor_tensor(out=kd[:, :, dup, dup * 64 + 8:dup * 64 + 16], in0=rt[2][:, 16:20, :],
                                                                    in1=rt[3][:, 16:20, :], op=ALU.add),
                          reads=[drt[2], drt[3]], writes=[dkd])

            def a3():
                qflat = qt[:, :, :].rearrange("p h d -> p (h d)")

                def tq(e):
                    for k in range(8):
                        i = e.transpose(tp[:, k, :], qflat[:, k * 128:(k + 1) * 128], C.ident[:, :])
                    return i
                em.op("pe", tq, reads=[dqt, C.dident], writes=[dtp])
                em.op("act", lambda e: e.activation(out=qT[b % NQ][:, :, :], in_=tp[:, :, :], func=AF.Copy),
                      reads=[dtp], writes=[dqT[b % NQ]])

            def a4():
                kflat = kd[:, :, :, :].rearrange("p g u d -> p (g u d)")

                def tk(e):
                    for k in range(8):
                        i = e.transpose(tp[:, k, :], kflat[:, k * 128:(k + 1) * 128], C.ident[:, :])
                    return i
                em.op("pe", tk, reads=[dkd, C.dident], writes=[dtp])
                em.op("act", lambda e: e.activation(out=kT[:, :, (b + 1) * 128:(b + 2) * 128], in_=tp[:, :, :], func=AF.Copy),
                      reads=[dtp], writes=[dkT[b + 1]])
            return [a0, a1, a2, a3, a4]

        def C_steps(b):
            ot = otok[b % 2]

            def c0():
                oflat = ot[:, :, :].rearrange("p h d -> p (h d)")

                def to(e):
                    for k in range(8):
                        i = e.transpose(tp[:, k, :], oflat[:, k * 128:(k + 1) * 128], C.ident[:, :])
                    return i
                em.op("pe", to, reads=dotok[b % 2] + [C.dident], writes=[dtp])
                em.op("act", lambda e: e.activation(out=oT[:, :, :], in_=tp[:, :, :], func=AF.Copy), reads=[dtp], writes=[doT])

            def c1():
                def fw(e):
                    for half in range(2):
                        for k in range(8):
                            i = e.matmul(qkv[:, half * 512:(half + 1) * 512], lhsT=oT[:, k, :], rhs=Wo[:, k, half * 512:(half + 1) * 512],
                                         start=(k == 0), stop=(k == 7))
                    return i
                em.op("pe", fw, reads=[doT, dWo], writes=[dq01])

            def c2():
                si = sctr[0] % 4
                sctr[0] += 1
                emit_post_residual(C, em, qkv[:, 0:1024], dq01, xa[b % NX], dxa[b % NX], gpost, dgpost, sst[si], dsst[si],
                                   junk, djunk, C.y[b * 128:(b + 1) * 128, :], C.dy[b])
            return [c0, c1, c2]

        def finish_heads(b, h0):
            s_ = stt[b % 2]
            k = h0 // 8
            dsts = dsth[b % 2][h0:h0 + 8]
            df = dfin[b % 2][k]
            hs = slice(h0, h0 + 8)
            em.op("dve", lambda e: e.tensor_tensor(out=s_[:, 3, hs], in0=s_[:, 1, hs], in1=sinkt[:, hs], op=ALU.add),
                  reads=dsts + [dsink], writes=[df])
            em.op("act", lambda e: e.activation(out=s_[:, 4, hs], in_=s_[:, 3, hs], func=AF.Exp), reads=[df], writes=[df])
            em.op("dve", lambda e: e.tensor_tensor(out=s_[:, 4, hs], in0=s_[:, 4, hs], in1=s_[:, 2, hs], op=ALU.add),
                  reads=dsts + [df], writes=[df])
            em.op("dve", lambda e: e.reciprocal(out=s_[:, 5, hs], in_=s_[:, 4, hs]), reads=[df], writes=[df])
            em.op("dve", lambda e: e.tensor_tensor(out=otok[b % 2][:, hs, :], in0=ops[:, :, :],
                                                   in1=s_[:, 5, hs].unsqueeze(2).broadcast_to([128, 8, 64]), op=ALU.mult),
                  reads=[dops, df], writes=[dotok[b % 2][k]])

        def stageB(b, steps):
            m_, dm_ = mk[b % 2], dmk[b % 2]
            em.dma("sp", m_[:, :], C.amask[b], writes=[dm_])
            s_ = stt[b % 2]
            q_ = qT[b % NQ]

            def S(h):
                g, pr, hf = h // 4, h // 2, h % 2
                sp_, dsp_ = sps[:, h % 2, 0:384], dsps[h % 2]

                def f(e):
                    e.matmul(sp_, lhsT=q_[:, pr, :], rhs=kT[:, g * 2 + hf, b * 128: b * 128 + 384], start=True, stop=False)
                    return e.matmul(sp_, lhsT=C.ident[:, :], rhs=m_[:, :], start=False, stop=True)
                em.op("pe", f, reads=[dqT[b % NQ], dkT[b], dkT[b + 1], dkT[b + 2], dm_, C.dident], writes=[dsp_])
                ds_ = dsth[b % 2][h]
                em.op("dve", lambda e: e.tensor_reduce(out=s_[:, 0, h:h + 1], in_=sp_, op=ALU.max, axis=AX.X),
                      reads=[dsp_], writes=[ds_])
                em.op("dve", lambda e: e.tensor_scalar(out=s_[:, 1, h:h + 1], in0=s_[:, 0, h:h + 1], scalar1=-SCALE,
                                                       scalar2=nsink[:, h:h + 1], op0=ALU.mult, op1=ALU.min),
                      reads=[ds_, dsink], writes=[ds_])
                p_, dp_ = pb[h % 2], dpb[h % 2]
                em.op("act", lambda e: e.activation(out=p_[:, :], in_=sp_, func=AF.Exp, scale=SCALE,
                                                    bias=s_[:, 1, h:h + 1], accum_out=s_[:, 2, h:h + 1]),
                      reads=[dsp_, ds_], writes=[dp_, ds_])

            def T(h):
                p_, dp_ = pb[h % 2], dpb[h % 2]

                def ft(e):
                    for c in range(3):
                        i = e.transpose(ptp[:, h % 2, c, :], p_[:, c * 128:(c + 1) * 128], C.ident[:, :])
                    return i
                em.op("pe", ft, reads=[dp_, C.dident], writes=[dptp])
                em.op("act", lambda e: e.activation(out=pT[h % 2][:, :, :], in_=ptp[:, h % 2, 0:3, :], func=AF.Copy),
                      reads=[dptp], writes=[dpT[h % 2]])

            def PV(h):
                g = h // 4

                def fo(e):
                    for c in range(3):
                        i = e.matmul(ops[:, h % 8, :], lhsT=pT[h % 2][:, c, :], rhs=vA[:, b + c, g * 64:(g + 1) * 64],
                                     start=(c == 0), stop=(c == 2))
                    return i
                em.op("pe", fo, reads=[dpT[h % 2], dvA[b], dvA[b + 1], dvA[b + 2]], writes=[dops])

            S(0)
            S(1)
            for h in range(16):
                T(h)
                if h + 2 < 16:
                    S(h + 2)
                PV(h)
                if h % 8 == 7:
                    finish_heads(b, h - 7)
                if h % 2 == 1 and steps:
                    steps.pop(0)()
            while steps:
                steps.pop(0)()

        nblk = getattr(C, "nblk", NCH)
        for s_ in A_steps(0):
            s_()
        if NCH > 1:
            for s_ in A_steps(1):
                s_()
        for b in range(nblk):
            steps = []
            if b >= 1:
                steps += C_steps(b - 1)
            if b + 2 < NCH:
                steps += A_steps(b + 2)
            stageB(b, steps)
        for s_ in C_steps(nblk - 1):
            s_()
        em.barrier()


def emit_ret(C, l, src, dsrc):
    nc, em = C.nc, C.em
    nch = getattr(C, "nblk", NCH)
    half_b = NCH // 2
    with ExitStack() as st0:
        bnd = _alloc(st0, nc, "bnd", [128, 1], F32)
        kdec = _alloc(st0, nc, "kdec", [128, 8], F32)
        g128 = _alloc(st0, nc, "g128", [128, 8], F32)
        decrow = _alloc(st0, nc, "decrow", [128, 8, 128], F32)
        DT = _alloc(st0, nc, "DT", [128, 4, 128], F32)
        S32 = _alloc(st0, nc, "S32", [128, 4, 2, 512], F32)
        stT = ExitStack()
        rc = _alloc(stT, nc, "rc", [128, 6, 128], F32)
        cpos = _alloc(stT, nc, "cpos", [128, 4], F32)
        dl = _alloc(stT, nc, "dl", [128, 8], F32)
        lg = _alloc(stT, nc, "lg", [128, 8], F32)
        tmpD = _alloc(stT, nc, "tmpD", [128, 2, 128], F32)
        drc, dtab, dS32, dtmp = Dep("rc"), Dep("tab"), Dep("S32"), Dep("tmpD")
        em.dma("sp", rc[:, :, :], C.rconst[:, :, :], writes=[drc])
        em.dma("sp", cpos[:, :], C.rpos[:, :], writes=[drc], owner=drc)
        em.dma("sp", bnd[:, :], C.rbnd[:, :], writes=[drc], owner=drc)
        em.dma("sp", dl[:, 0:4], C.rdf[0, :].partition_broadcast(128), writes=[dtab])
        em.dma("sp", dl[:, 4:8], C.rdb[0, :].partition_broadcast(128), writes=[dtab], owner=dtab)
        em.op("act", lambda e: e.activation(out=lg[:, :], in_=dl[:, :], func=AF.Exp, scale=-1.0), reads=[dtab], writes=[dtab])
        em.op("dve", lambda e: e.tensor_scalar(out=lg[:, :], in0=lg[:, :], scalar1=1.0, scalar2=None, op0=ALU.add), reads=[dtab], writes=[dtab])
        em.op("act", lambda e: e.activation(out=lg[:, :], in_=lg[:, :], func=AF.Ln), reads=[dtab], writes=[dtab])
        em.op("dve", lambda e: e.tensor_scalar(out=lg[:, :], in0=lg[:, :], scalar1=-1.0, scalar2=None, op0=ALU.mult), reads=[dtab], writes=[dtab])
        em.op("act", lambda e: e.activation(out=g128[:, :], in_=lg[:, :], func=AF.Exp, scale=128.0), reads=[dtab], writes=[dtab])
        em.op("act", lambda e: e.activation(out=kdec[:, 0:4], in_=lg[:, 0:4], func=AF.Exp, scale=cpos[:, 2:3]), reads=[dtab, drc], writes=[dtab])
        em.op("act", lambda e: e.activation(out=kdec[:, 4:8], in_=lg[:, 4:8], func=AF.Exp, scale=cpos[:, 3:4]), reads=[dtab, drc], writes=[dtab])
        em.op("dve", lambda e: e.tensor_scalar(out=kdec[:, :], in0=kdec[:, :], scalar1=1.0 / 16, scalar2=None, op0=ALU.mult), reads=[dtab], writes=[dtab])
        for h in range(4):
            em.op("act", lambda e, h=h: e.activation(out=decrow[:, h, :], in_=rc[:, 4, :], func=AF.Exp, scale=lg[:, h:h + 1]), reads=[dtab, drc], writes=[dtab])
            em.op("act", lambda e, h=h: e.activation(out=decrow[:, 4 + h, :], in_=rc[:, 5, :], func=AF.Exp, scale=lg[:, 4 + h:5 + h]), reads=[dtab, drc], writes=[dtab])
            em.op("act", lambda e, h=h: e.activation(out=tmpD[:, 0, :], in_=rc[:, 0, :], func=AF.Exp, scale=lg[:, h:h + 1]), reads=[dtab, drc], writes=[dtmp])
            em.op("act", lambda e, h=h: e.activation(out=tmpD[:, 1, :], in_=rc[:, 1, :], func=AF.Exp, scale=lg[:, 4 + h:5 + h]), reads=[dtab, drc], writes=[dtmp])
            em.op("dve", lambda e, h=h: e.tensor_tensor(out=tmpD[:, :, :], in0=tmpD[:, :, :], in1=rc[:, 2:4, :], op=ALU.mult), reads=[dtmp, drc], writes=[dtmp])
            em.op("dve", lambda e, h=h: e.tensor_tensor(out=DT[:, h, :], in0=tmpD[:, 0, :], in1=tmpD[:, 1, :], op=ALU.add), reads=[dtmp], writes=[dtab])
        em.op("dve", lambda e: e.memset(S32[:, :, :, :], 0.0), writes=[dS32])
        em.barrier()
        stT.close()

        def rotary(ps, dps, c_, dc_, rt, drt, out_bf, dout):
            p4 = ps.rearrange("p (h t f) -> p h t f", h=4, t=2)
            o4 = out_bf[:, :].rearrange("p (h t f) -> p h t f", h=4, t=2)
            cosb = c_[:, 0:128].unsqueeze(1).broadcast_to([128, 4, 128])
            sinb = c_[:, 128:256].unsqueeze(1).broadcast_to([128, 4, 128])
            x1, x2 = p4[:, :, 0, :], p4[:, :, 1, :]
            prods = [(x1, cosb), (x2, sinb), (x2, cosb), (x1, sinb)]
            for i in (0, 1):
                xx, tb = prods[i]
                em.op("dve", lambda e, i=i, xx=xx, tb=tb: e.tensor_tensor(out=rt[i][:, :, :], in0=xx, in1=tb, op=ALU.mult),
                      reads=[dps, dc_], writes=[drt[i]])
            em.op("dve", lambda e: e.tensor_tensor(out=o4[:, :, 0, :], in0=rt[0][:, :, :], in1=rt[1][:, :, :], op=ALU.subtract),
                  reads=[drt[0], drt[1]], writes=[dout])
            for i in (2, 3):
                xx, tb = prods[i]
                em.op("dve", lambda e, i=i, xx=xx, tb=tb: e.tensor_tensor(out=rt[i][:, :, :], in0=xx, in1=tb, op=ALU.mult),
                      reads=[dps, dc_], writes=[drt[i]])
            em.op("dve", lambda e: e.tensor_tensor(out=o4[:, :, 1, :], in0=rt[2][:, :, :], in1=rt[3][:, :, :], op=ALU.add),
                  reads=[drt[2], drt[3]], writes=[dout])

        def state_update(S32, dS32, kd_tok, dkd, v_tok, dv, dsp, dsd, gcol0):
            for h in range(4):
                for c in range(2):
                    def f(e, h=h, c=c):
                        return e.matmul(dsp[:, :], lhsT=kd_tok[:, h * 256 + c * 128: h * 256 + (c + 1) * 128],
                                        rhs=v_tok[:, h * 512:(h + 1) * 512], start=True, stop=True)
                    em.op("pe", f, reads=[dkd, dv], writes=[dsd])
                    em.op("dve", lambda e, h=h, c=c: e.scalar_tensor_tensor(out=S32[:, h, c, :], in0=S32[:, h, c, :],
                                                                          scalar=g128[:, gcol0 + h:gcol0 + h + 1], in1=dsp[:, :],
                                                                          op0=ALU.mult, op1=ALU.add),
                          reads=[dS32, dsd, dtab], writes=[dS32])

        def boundary(S32, dS32):
            em.op("dve", lambda e: e.tensor_scalar(out=S32[:, :, :, :], in0=S32[:, :, :, :], scalar1=bnd[:, 0:1], scalar2=None,
                                                   op0=ALU.mult), reads=[dS32, drc], writes=[dS32])

        with ExitStack() as st:
            Wkv = _alloc(st, nc, "Wkv", [128, 8, 3072], BF16)
            dWkv = [Dep("Wkv%d" % k) for k in range(2)]
            wv = C.rwi[0].rearrange("(k p) f -> p k f", p=128)
            em.dma("pool", Wkv[:, 0:4, :], wv[:, 0:4, 1024:4096], writes=[dWkv[0]])
            em.dma("pool", Wkv[:, 4:8, :], wv[:, 4:8, 1024:4096], writes=[dWkv[1]])
            gpre = _alloc(st, nc, "gpre", [128, D], F32)
            dgpre = Dep("gpre")
            em.dma("sp", gpre[:, :], C.ng[l, 2, :].partition_broadcast(128), writes=[dgpre])
            xa = [_alloc(st, nc, "xa%d" % i, [128, D], F32) for i in range(3)]
            dxa = [Dep("xa%d" % i) for i in range(3)]
            hb2 = [_alloc(st, nc, "hb%d" % i, [128, D], BF16) for i in range(2)]
            dhb2 = [Dep("hb%d" % i) for i in range(2)]
            hT = [_alloc(st, nc, "hT%d" % i, [128, 8, 128], BF16) for i in range(2)]
            dhT = [Dep("hT%d" % i) for i in range(2)]
            NCS = 4
            cs = [_alloc(st, nc, "rcs%d" % i, [128, 256], F32) for i in range(NCS)]
            dcs = [Dep("rcs%d" % i) for i in range(NCS)]
            rt = [_alloc(st, nc, "rrt%d" % i, [128, 4, 128], F32) for i in range(4)]
            drt = [Dep("rrt%d" % i) for i in range(4)]
            krot = [_alloc(st, nc, "krot%d" % i, [128, 1024], BF16) for i in range(2)]
            dkrot = [Dep("krot%d" % i) for i in range(2)]
            kf = [_alloc(st, nc, "kf%d" % i, [128, 1024], BF16) for i in range(2)]
            dkf = [Dep("kf%d" % i) for i in range(2)]
            kb = [_alloc(st, nc, "kb%d" % i, [128, 1024], BF16) for i in range(2)]
            dkb = [Dep("kb%d" % i) for i in range(2)]
            kTb = [_alloc(st, nc, "kTb%d" % i, [128, 8, 128], BF16) for i in range(2)]
            dkTb = [Dep("kTb%d" % i) for i in range(2)]
            vtok = [_alloc(st, nc, "vtok%d" % i, [128, 2048], BF16) for i in range(2)]
            dvtok = [[Dep("vtok%d_%d" % (i, k)) for k in range(2)] for i in range(2)]
            Sbf = _alloc(st, nc, "Sbf", [128, 4096], BF16)
            dSbfh = [Dep("Sbf%d" % h) for h in range(4)]
            dS32h = [Dep("S32b_%d" % h) for h in range(4)]
            em.op("dve", lambda e: e.memset(S32[:, :, :, :], 0.0), reads=[dS32], writes=dS32h)
            sst = [_alloc(st, nc, "ss%d" % i, [128, 4], F32) for i in range(3)]
            dsst = [Dep("ss%d" % i) for i in range(3)]
            tp = _palloc(st, nc, "tp", [128, 8, 128], BF16)
            dtp = PDep("tp")
            pk = _palloc(st, nc, "pk", [128, 1024], F32)
            dpk = PDep("pk")
            pv1 = _palloc(st, nc, "pv", [128, 1024], F32)
            pv = [pv1, pv1]
            dpv1 = PDep("pv")
            dpv = [dpv1, dpv1]
            dspr = [_palloc(st, nc, "dsp%d" % i, [128, 512], F32) for i in range(2)]
            dsdr = [PDep("dsp%d" % i) for i in range(2)]

            def loads1(n):
                em.dma("sp", xa[n % 3][:, :], src[n * 128:(n + 1) * 128, :], reads=[dsrc[n]], writes=[dxa[n % 3]])
                em.dma("sp", cs[n % NCS][:, :], C.rcs[n * 128:(n + 1) * 128, :], writes=[dcs[n % NCS]])

            def front1(n):
                r = n % 2
                xs, dxs = xa[n % 3], dxa[n % 3]
                hbs, dhbs = hb2[r], dhb2[r]
                ss, dss = sst[n % 3], dsst[n % 3]
                em.op("act", lambda e: e.activation(out=hbs[:, :], in_=xs[:, :], func=AF.Square, accum_out=ss[:, 0:1]),
                      reads=[dxs], writes=[dhbs, dss])
                emit_rstd(em, ss, dss, C.mhalf, C.dmh, D)
                em.op("dve", lambda e: e.scalar_tensor_tensor(out=hbs[:, :], in0=xs[:, :], scalar=ss[:, 2:3], in1=gpre[:, :],
                                                              op0=ALU.mult, op1=ALU.mult),
                      reads=[dxs, dss, dgpre], writes=[dhbs])

            def P1_steps(n):
                r = n % 2
                c_, dc_ = cs[n % NCS], dcs[n % NCS]

                def p0():
                    hbs = hb2[r]

                    def tps(e):
                        for k in range(8):
                            i = e.transpose(tp[:, k, :], hbs[:, k * 128:(k + 1) * 128], C.ident[:, :])
                        return i
                    em.op("pe", tps, reads=[dhb2[r], C.dident], writes=[dtp])
                    em.op("act", lambda e: e.activation(out=hT[r][:, :, :], in_=tp[:, :, :], func=AF.Copy), reads=[dtp], writes=[dhT[r]])

                def p1():
                    for g in range(2):
                        def f(e, g=g):
                            for k in range(8):
                                i = e.matmul(pk[:, g * 512:(g + 1) * 512], lhsT=hT[r][:, k, :], rhs=Wkv[:, k, g * 512:(g + 1) * 512],
                                             start=(k == 0), stop=(k == 7))
                            return i
                        em.op("pe", f, reads=[dhT[r]] + dWkv, writes=[dpk])
                    rotary(pk[:, :], dpk, c_, dc_, rt, drt, krot[r], dkrot[r])

                def p2():
                    for (dst, ddst, col) in ((kf[r], dkf[r], 0), (kb[r], dkb[r], 4)):
                        em.op("dve", lambda e, dst=dst, col=col: e.tensor_tensor(
                            out=dst[:, :].rearrange("p (h f) -> p h f", h=4), in0=krot[r][:, :].rearrange("p (h f) -> p h f", h=4),
                            in1=kdec[:, col:col + 4].unsqueeze(2).broadcast_to([128, 4, 256]), op=ALU.mult),
                            reads=[dkrot[r], dtab], writes=[ddst])

                    def tk(e):
                        for k in range(8):
                            i = e.transpose(tp[:, k, :], krot[r][:, k * 128:(k + 1) * 128], C.ident[:, :])
                        return i
                    em.op("pe", tk, reads=[dkrot[r], C.dident], writes=[dtp])
                    em.op("act", lambda e: e.activation(out=kTb[r][:, :, :], in_=tp[:, :, :], func=AF.Copy), reads=[dtp], writes=[dkTb[r]])
                    em.dma("sp", C.s_kT[n], kTb[r][:, :, :].rearrange("p k t -> p (k t)"), reads=[dkTb[r]], writes=[C.dscr[n]], owner=dkTb[r])
                    em.dma("sp", C.s_kf[n], kf[r][:, :], reads=[dkf[r]], writes=[C.dscr[n]], owner=dkf[r])

                def mkv(hv):
                    def pv_():
                        for g in range(2):
                            def f(e, g=g):
                                col = 1024 + hv * 1024 + g * 512
                                for k in range(8):
                                    i = e.matmul(pv[hv][:, g * 512:(g + 1) * 512], lhsT=hT[r][:, k, :], rhs=Wkv[:, k, col:col + 512],
                                                 start=(k == 0), stop=(k == 7))
                                return i
                            em.op("pe", f, reads=[dhT[r]] + dWkv, writes=[dpv[hv]])
                        em.op("act", lambda e: e.activation(out=vtok[r][:, hv * 1024:(hv + 1) * 1024], in_=pv[hv][:, :], func=AF.Copy),
                              reads=[dpv[hv]], writes=[dvtok[r][hv]])
                    return pv_

                def p5():
                    em.dma("sp", C.s_v[n], vtok[r][:, :], reads=dvtok[r], writes=[C.dscr[n]], owner=dvtok[r][0])
                return [p0, p1, mkv(0), mkv(1), p2, p5]

            def U(n, steps):
                r = n % 2
                if n - 3 >= 0:
                    loads1(n - 3)
                if n - 2 >= 0:
                    front1(n - 2)
                for h in range(4):
                    em.op("act", lambda e, h=h: e.activation(out=Sbf[:, h * 1024:(h + 1) * 1024],
                                                             in_=S32[:, h, :, :].rearrange("p c f -> p (c f)"), func=AF.Copy),
                          reads=[dS32h[h]], writes=[dSbfh[h]])
                    if h == 3:
                        em.dma("sp", C.s_sb[n], Sbf[:, :], reads=dSbfh, writes=[C.dscr[n]], owner=dSbfh[0])
                    for c in range(2):
                        if n > 0:
                            dsp, dsd = dspr[c], dsdr[c]

                            def f(e, h=h, c=c, dsp=dsp):
                                return e.matmul(dsp[:, :], lhsT=kb[r][:, h * 256 + c * 128: h * 256 + (c + 1) * 128],
                                                rhs=vtok[r][:, h * 512:(h + 1) * 512], start=True, stop=True)
                            em.op("pe", f, reads=[dkb[r], dvtok[r][h // 2]], writes=[dsd])
                            em.op("dve", lambda e, h=h, c=c, dsp=dsp: e.scalar_tensor_tensor(out=S32[:, h, c, :], in0=S32[:, h, c, :],
                                                                                           scalar=g128[:, 4 + h:5 + h], in1=dsp[:, :],
                                                                                           op0=ALU.mult, op1=ALU.add),
                                  reads=[dS32h[h], dsd, dtab], writes=[dS32h[h]])
                        if steps:
                            steps.pop(0)()
                while steps:
                    steps.pop(0)()
                if n > 0 and n == half_b:
                    em.op("dve", lambda e: e.tensor_scalar(out=S32[:, :, :, :], in0=S32[:, :, :, :], scalar1=bnd[:, 0:1], scalar2=None,
                                                           op0=ALU.mult), reads=dS32h + [drc], writes=dS32h)

            for i in range(1, 4):
                if nch - i >= 0:
                    loads1(nch - i)
            front1(nch - 1)
            if nch > 1:
                front1(nch - 2)
            for s_ in P1_steps(nch - 1):
                s_()
            for n in range(nch - 1, -1, -1):
                U(n, P1_steps(n - 1) if n > 0 else [])
            em.barrier()

        em.op("dve", lambda e: e.memset(S32[:, :, :, :], 0.0), writes=[dS32])
        with ExitStack() as st:
            Wqg = _alloc(st, nc, "Wqg", [128, 8, 3072], BF16)
            dWqg = [Dep("Wqg%d" % k) for k in range(2)]
            wv = C.rwi[0].rearrange("(k p) f -> p k f", p=128)
            em.dma("pool", Wqg[:, :, 0:1024], wv[:, :, 0:1024], writes=[dWqg[0]])
            em.dma("pool", Wqg[:, :, 1024:3072], wv[:, :, 4096:6144], writes=[dWqg[1]])
            Wro = _alloc(st, nc, "Wro", [128, 16, D], BF16)
            dWro = Dep("Wro")
            em.dma("pool", Wro[:, :, :], C.rwo[0].rearrange("(k p) f -> p k f", p=128), writes=[dWro])
            gpre, dgpre, gpost, dgpost = load_gains(C, em, st, nc, l, 2, 3, 1.0)
            NX = 4
            xa = [_alloc(st, nc, "xa%d" % i, [128, D], F32) for i in range(NX)]
            dxa = [Dep("xa%d" % i) for i in range(NX)]
            hb = _alloc(st, nc, "hb", [128, D], BF16)
            dhb = Dep("hb")
            NH = 3
            hT = [_alloc(st, nc, "hT%d" % i, [128, 8, 128], BF16) for i in range(NH)]
            dhT = [Dep("hT%d" % i) for i in range(NH)]
            cs = [_alloc(st, nc, "rcs%d" % i, [128, 256], F32) for i in range(2)]
            dcs = [Dep("rcs%d" % i) for i in range(2)]
            rt2 = [_alloc(st, nc, "rrt%d" % i, [128, 4, 128], F32) for i in range(2)]
            drt2 = [Dep("rrt%d" % i) for i in range(2)]
            rt = [rt2[0], rt2[1], rt2[0], rt2[1]]
            drt = [drt2[0], drt2[1], drt2[0], drt2[1]]
            qrot = _alloc(st, nc, "qrot", [128, 1024], BF16)
            dqrot = Dep("qrot")
            qT = [[_alloc(st, nc, "qT%d_%d" % (r, i), [128, 8, 128], BF16) for i in range(3)] for r in range(2)]
            dqT = [[Dep("qT%d_%d" % (r, i)) for i in range(3)] for r in range(2)]
            sg = [_alloc(st, nc, "sg%d" % i, [128, 512], BF16) for i in range(2)]
            dsg = [Dep("sg%d" % i) for i in range(2)]
            NR = 2
            kTl = [_alloc(st, nc, "kTl%d" % i, [128, 8, 128], BF16) for i in range(NR)]
            kfl = [_alloc(st, nc, "kfl%d" % i, [128, 1024], BF16) for i in range(NR)]
            vl = [_alloc(st, nc, "vl%d" % i, [128, 2048], BF16) for i in range(NR)]
            sbl1 = _alloc(st, nc, "sbl", [128, 4, 2, 512], BF16)
            sbl = [sbl1, sbl1]
            dkTl = [Dep("kTl%d" % i) for i in range(NR)]
            dkfl = [Dep("kfl%d" % i) for i in range(NR)]
            dvl = [Dep("vl%d" % i) for i in range(NR)]
            dsblh = [Dep("sbl_%d" % h) for h in range(4)]
            Sfb = _alloc(st, nc, "Sfb", [128, 4, 2, 512], BF16)
            dSfb = [Dep("Sfb%d" % h) for h in range(4)]
            dS32h = [Dep("S32_%d" % h) for h in range(4)]
            STb = [_alloc(st, nc, "STb%d" % i, [128, 128], BF16) for i in range(2)]
            dSTb = [Dep("STb%d" % i) for i in range(2)]
            otok = [_alloc(st, nc, "otok%d" % i, [128, 2048], BF16) for i in range(2)]
            dotok = [[Dep("otok%d_%d" % (i, h)) for h in range(4)] for i in range(2)]
            oT = _alloc(st, nc, "oT", [128, 16, 128], BF16)
            doT = Dep("oT")
            junk = _alloc(st, nc, "junk", [128, D], BF16)
            djunk = Dep("junk")
            junk2 = junk
            djunk2 = djunk
            sst = [_alloc(st, nc, "ss%d" % i, [128, 4], F32) for i in range(6)]
            dsst = [Dep("ss%d" % i) for i in range(6)]
            tp = _palloc(st, nc, "tp", [128, 8, 128], BF16)
            dtp = PDep("tp")
            pq = _palloc(st, nc, "pq", [128, 1024], F32)
            dpq = PDep("pq")
            pG = _palloc(st, nc, "pG", [128, 512], F32)
            dpG = PDep("pG")
            pS = _palloc(st, nc, "pS", [128, 512], F32)
            dpS = PDep("pS")
            pY = [_palloc(st, nc, "pY%d" % i, [128, 512], F32) for i in range(2)]
            dpY = [PDep("pY%d" % i) for i in range(2)]
            pD = _palloc(st, nc, "pD", [128, 512], F32)
            dpD = PDep("pD")
            sctr = [0]
            em.op("dve", lambda e: e.memset(Sfb[:, :, :, :], 0.0), writes=dSfb)
            em.op("dve", lambda e: e.memset(S32[:, :, :, :], 0.0), reads=[dS32], writes=dS32h)

            def nss():
                si = sctr[0] % 6
                sctr[0] += 1
                return sst[si], dsst[si]

            def load_sb(n, h):
                em.dma("sp", sbl1[:, h, :, :].rearrange("p c f -> p (c f)"), C.s_sb[n][:, h * 1024:(h + 1) * 1024],
                       reads=[C.dscr[n]], writes=[dsblh[h]])

            hb2 = [hb, _alloc(st, nc, "hb_b", [128, D], BF16)]
            dhb2 = [dhb, Dep("hb_b")]

            def front(n):
                xs, dxs = xa[n % NX], dxa[n % NX]
                hbs, dhbs = hb2[n % 2], dhb2[n % 2]
                em.dma("sp", xs[:, :], src[n * 128:(n + 1) * 128, :], reads=[dsrc[n]], writes=[dxs])
                em.dma("sp", cs[n % 2][:, :], C.rcs[n * 128:(n + 1) * 128, :], writes=[dcs[n % 2]])
                ss, dss = nss()
                em.op("act", lambda e: e.activation(out=hbs[:, :], in_=xs[:, :], func=AF.Square, accum_out=ss[:, 0:1]),
                      reads=[dxs], writes=[dhbs, dss])
                emit_rstd(em, ss, dss, C.mhalf, C.dmh, D)
                em.op("dve", lambda e: e.scalar_tensor_tensor(out=hbs[:, :], in0=xs[:, :], scalar=ss[:, 2:3], in1=gpre[:, :],
                                                              op0=ALU.mult, op1=ALU.mult),
                      reads=[dxs, dss, dgpre], writes=[dhbs])

            def P_steps(n):
                r = n % 2
                xs, dxs = xa[n % NX], dxa[n % NX]
                c_, dc_ = cs[r], dcs[r]

                def p0():
                    if n == 0:
                        for h in range(4):
                            load_sb(0, h)
                    hbs = hb2[n % 2]

                    def tps(e):
                        for k in range(8):
                            i = e.transpose(tp[:, k, :], hbs[:, k * 128:(k + 1) * 128], C.ident[:, :])
                        return i
                    em.op("pe", tps, reads=[dhb2[n % 2], C.dident], writes=[dtp])
                    em.op("act", lambda e: e.activation(out=hT[n % NH][:, :, :], in_=tp[:, :, :], func=AF.Copy), reads=[dtp], writes=[dhT[n % NH]])

                def p1():
                    em.dma("sp", kTl[r][:, :, :].rearrange("p k t -> p (k t)"), C.s_kT[n], reads=[C.dscr[n]], writes=[dkTl[r]])
                    em.dma("sp", kfl[r][:, :], C.s_kf[n], reads=[C.dscr[n]], writes=[dkfl[r]])
                    em.dma("sp", vl[r][:, :], C.s_v[n], reads=[C.dscr[n]], writes=[dvl[r]])
                    for g in range(2):
                        def f(e, g=g):
                            for k in range(8):
                                i = e.matmul(pq[:, g * 512:(g + 1) * 512], lhsT=hT[n % NH][:, k, :], rhs=Wqg[:, k, g * 512:(g + 1) * 512],
                                             start=(k == 0), stop=(k == 7))
                            return i
                        em.op("pe", f, reads=[dhT[n % NH], dWqg[0]], writes=[dpq])
                    rotary(pq[:, :], dpq, c_, dc_, rt, drt, qrot, dqrot)

                def p2():
                    def tq(e):
                        for k in range(8):
                            i = e.transpose(tp[:, k, :], qrot[:, k * 128:(k + 1) * 128], C.ident[:, :])
                        return i
                    em.op("pe", tq, reads=[dqrot, C.dident], writes=[dtp])
                    em.op("act", lambda e: e.activation(out=qT[r][0][:, :, :], in_=tp[:, :, :], func=AF.Copy), reads=[dtp], writes=[dqT[r][0]])
                    for i in range(2):
                        em.op("dve", lambda e, i=i: e.tensor_tensor(
                            out=qT[r][1 + i][:, :, :].rearrange("p (h c) t -> p h c t", c=2),
                            in0=qT[r][0][:, :, :].rearrange("p (h c) t -> p h c t", c=2),
                            in1=decrow[:, 4 * i:4 * i + 4, :].unsqueeze(2).broadcast_to([128, 4, 2, 128]), op=ALU.mult),
                            reads=[dqT[r][0], dtab], writes=[dqT[r][1 + i]])
                return [p0, p1, p2]

            def O_steps(n):
                r = n % 2
                steps = []
                for half in range(2):
                    def o_t(half=half):
                        def to(e):
                            for k in range(8):
                                i = e.transpose(tp[:, k, :], otok[r][:, (half * 8 + k) * 128:(half * 8 + k + 1) * 128], C.ident[:, :])
                            return i
                        em.op("pe", to, reads=dotok[r] + [C.dident], writes=[dtp])
                        em.op("act", lambda e: e.activation(out=oT[:, half * 8:(half + 1) * 8, :], in_=tp[:, :, :], func=AF.Copy),
                              reads=[dtp], writes=[doT])
                    steps.append(o_t)

                def o_w():
                    def fw(e):
                        for half in range(2):
                            for k in range(16):
                                i = e.matmul(pq[:, half * 512:(half + 1) * 512], lhsT=oT[:, k, :], rhs=Wro[:, k, half * 512:(half + 1) * 512],
                                             start=(k == 0), stop=(k == 15))
                        return i
                    em.op("pe", fw, reads=[doT, dWro], writes=[dpq])

                def o_e():
                    ss, dss = nss()
                    emit_post_residual(C, em, pq[:, :], dpq, xa[n % NX], dxa[n % NX], gpost, dgpost, ss, dss,
                                       junk, djunk, C.y[n * 128:(n + 1) * 128, :], C.dy[n])
                steps += [o_w, o_e]
                return steps

            def H(n, steps):
                r = n % 2
                last = (n + 1 >= nch)

                def GS(h):
                    def fg(e):
                        for k in range(8):
                            i = e.matmul(pG[:, :], lhsT=hT[n % NH][:, k, :], rhs=Wqg[:, k, 1024 + h * 512:1024 + (h + 1) * 512],
                                         start=(k == 0), stop=(k == 7))
                        return i
                    em.op("pe", fg, reads=[dhT[n % NH], dWqg[1]], writes=[dpG])
                    em.op("act", lambda e: e.activation(out=sg[h % 2][:, :], in_=pG[:, :], func=AF.Silu), reads=[dpG], writes=[dsg[h % 2]])

                    def fs(e):
                        for c in range(2):
                            i = e.matmul(pS[:, 0:128], lhsT=kTl[r][:, 2 * h + c, :], rhs=qT[r][0][:, 2 * h + c, :], start=(c == 0), stop=(c == 1))
                        return i
                    em.op("pe", fs, reads=[dkTl[r], dqT[r][0]], writes=[dpS])
                    em.op("dve", lambda e: e.tensor_tensor(out=STb[h % 2][:, :], in0=pS[:, 0:128], in1=DT[:, h, :], op=ALU.mult),
                          reads=[dpS, dtab], writes=[dSTb[h % 2]])

                def Y(h):
                    yh, dyh = pY[h % 2], dpY[h % 2]

                    def fy(e):
                        e.matmul(yh[:, :], lhsT=STb[h % 2][:, :], rhs=vl[r][:, h * 512:(h + 1) * 512], start=True, stop=False)
                        for c in range(2):
                            e.matmul(yh[:, :], lhsT=qT[r][1][:, 2 * h + c, :], rhs=Sfb[:, h, c, :], start=False, stop=False)
                        for c in range(2):
                            i = e.matmul(yh[:, :], lhsT=qT[r][2][:, 2 * h + c, :], rhs=sbl[r][:, h, c, :], start=False, stop=(c == 1))
                        return i
                    em.op("pe", fy, reads=[dSTb[h % 2], dvl[r], dqT[r][1], dqT[r][2], dSfb[h], dsblh[h]], writes=[dyh])
                    ss, dss = nss()
                    em.op("act", lambda e: e.activation(out=junk2[:, 0:512], in_=yh[:, :], func=AF.Square, accum_out=ss[:, 0:1]),
                          reads=[dyh], writes=[djunk2, dss])
                    emit_rstd(em, ss, dss, C.mhalf, C.dmh, 512)
                    em.op("dve", lambda e: e.scalar_tensor_tensor(out=otok[r][:, h * 512:(h + 1) * 512], in0=yh[:, :], scalar=ss[:, 2:3],
                                                                  in1=sg[h % 2][:, :], op0=ALU.mult, op1=ALU.mult),
                          reads=[dyh, dss, dsg[h % 2]], writes=[dotok[r][h]])

                def UPD(h, cs_):
                    for c in cs_:
                        def f(e, c=c):
                            return e.matmul(pD[:, :], lhsT=kfl[r][:, h * 256 + c * 128: h * 256 + (c + 1) * 128],
                                            rhs=vl[r][:, h * 512:(h + 1) * 512], start=True, stop=True)
                        em.op("pe", f, reads=[dkfl[r], dvl[r]], writes=[dpD])
                        em.op("dve", lambda e, c=c: e.scalar_tensor_tensor(out=S32[:, h, c, :], in0=S32[:, h, c, :],
                                                                         scalar=g128[:, h:h + 1], in1=pD[:, :],
                                                                         op0=ALU.mult, op1=ALU.add),
                              reads=[dS32h[h], dpD, dtab], writes=[dS32h[h]])
                    if 1 not in cs_:
                        return
                    if n + 1 == half_b:
                        em.op("dve", lambda e: e.tensor_scalar(out=S32[:, h, :, :], in0=S32[:, h, :, :], scalar1=bnd[:, 0:1], scalar2=None,
                                                               op0=ALU.mult), reads=[dS32h[h], drc], writes=[dS32h[h]])
                    em.op("act", lambda e: e.activation(out=Sfb[:, h, :, :], in_=S32[:, h, :, :], func=AF.Copy),
                          reads=[dS32h[h]], writes=[dSfb[h]])

                GS(0)
                for h in range(4):
                    if h + 1 < 4:
                        GS(h + 1)
                    if h >= 1 and not last:
                        UPD(h - 1, [1])
                    Y(h)
                    if not last:
                        load_sb(n + 1, h)
                        UPD(h, [0])
                    for _ in range(2):
                        if steps:
                            steps.pop(0)()
                if not last:
                    UPD(3, [1])
                while steps:
                    steps.pop(0)()

            nop = lambda: None
            front(0)
            if nch > 1:
                front(1)
            for s_ in P_steps(0):
                s_()
            if nch > 1:
                P_steps(1)[0]()
            for n in range(nch):
                O = O_steps(n - 1) if n >= 1 else [nop] * 4
                P = P_steps(n + 1) if n + 1 < nch else [nop] * 3
                P0n = P_steps(n + 2)[0] if n + 2 < nch else nop
                fr = (lambda n=n: front(n + 2)) if n + 2 < nch else nop
                steps = [O[0], P[1], O[1], fr, P[2], O[2], P0n, O[3]]
                H(n, steps)
            for s_ in O_steps(nch - 1):
                s_()
            em.barrier()


def build(nsub=6, subs=None, nblk=NCH, dbg=3):
    nc = bass.Bass("TRN2", target_bir_lowering=False)
    em = Emitter(nc)
    C = Ctx()
    C.nc, C.em = nc, em

    def din(name, shape, dt=F32):
        return nc.dram_tensor(name, shape, dt, kind="ExternalInput").ap()
    C.xin = din("xin", [NTOK, D])
    C.ng = din("norm_gains", [2, 6, D])
    C.fwi = din("ffn_w_in", [2, 2, D, 2 * DFF])
    C.fwo = din("ffn_w_out", [2, 2, DFF, D])
    C.wqkv = din("attn_w_qkv", [1, D, 1536])
    C.wo = din("attn_w_o", [1, D, D])
    C.sink = din("attn_sink", [1, 16])
    C.rwi = din("ret_w_in", [1, D, 6144])
    C.rwo = din("ret_w_o", [1, 2048, D])
    C.rdf = din("ret_decay_fwd", [1, 4])
    C.rdb = din("ret_decay_bwd", [1, 4])
    C.identd = din("c_ident", [128, 128], BF16)
    C.acs = din("c_acs", [NTOK, 320])
    C.amask = din("c_amask", [NCH, 128, 384], BF16)
    C.rconst = din("c_rconst", [128, 6, 128])
    C.rpos = din("c_rpos", [128, 4])
    C.rbnd = din("c_rbnd", [128, 1])
    C.rcs = din("c_rcs", [NTOK, 256])
    C.s_kT = nc.dram_tensor("s_kT", [NCH, 128, 1024], BF16, kind="Internal").ap()
    C.s_kf = nc.dram_tensor("s_kf", [NCH, 128, 1024], BF16, kind="Internal").ap()
    C.s_v = nc.dram_tensor("s_v", [NCH, 128, 2048], BF16, kind="Internal").ap()
    C.s_sb = nc.dram_tensor("s_sb", [NCH, 128, 4096], BF16, kind="Internal").ap()
    C.dscr = [Dep("scr%d" % i) for i in range(NCH)]
    C.y = nc.dram_tensor("y", [NTOK, D], F32, kind="ExternalOutput").ap()
    C.dy = [Dep("y%d" % i) for i in range(NCH)]
    C.dxin = [Dep("xin%d" % i) for i in range(NCH)]

    C.ident = nc.alloc_sbuf_tensor("ident", [128, 128], BF16)
    C.dident = Dep("ident")
    C.mhalf = nc.alloc_sbuf_tensor("mhalf", [128, 1], F32)
    C.dmh = Dep("mhalf")
    em.dma("sp", C.ident[:, :], C.identd[:, :], writes=[C.dident])
    em.op("pool", lambda e: e.memset(C.mhalf[:, :], -0.5), writes=[C.dmh])

    C.nblk = nblk
    C.dbg = dbg
    if subs is None:
        subs = [("ffn", 0, 0), ("attn", 0, 0), ("ffn", 0, 1), ("ffn", 1, 0), ("ret", 1, 0), ("ffn", 1, 1)]
    src, dsrc = C.xin, C.dxin
    for i, (kind, l, which) in enumerate(subs[:nsub]):
        if kind == "ffn":
            emit_ffn(C, l, which, src, dsrc)
        elif kind == "attn":
            emit_attn(C, l, src, dsrc)
        elif kind == "ret":
            emit_ret(C, l, src, dsrc)
        src, dsrc = C.y, C.dy
    em.finish()
    return nc


def make_consts():
    c = {}
    c["c_ident"] = np.eye(128, dtype=np.float32).astype(ml_dtypes.bfloat16)
    j = np.arange(128, dtype=np.float32)[:, None]
    i = np.arange(128, dtype=np.float32)[None, :]
    rc = np.zeros((128, 6, 128), np.float32)
    rc[:, 0] = np.maximum(i - j, 0.0)
    rc[:, 1] = np.maximum(j - i, 0.0)
    rc[:, 2] = (i >= j) / 16.0
    rc[:, 3] = (j > i) / 16.0
    rc[:, 4] = np.broadcast_to(i + 1.0, (128, 128))
    rc[:, 5] = np.broadcast_to(128.0 - i, (128, 128))
    c["c_rconst"] = rc
    p = np.arange(128, dtype=np.float32)
    c["c_rpos"] = np.stack([p + 1, 128 - p, 127 - p, p], axis=1).astype(np.float32)
    return c


def make_core_consts(seqlen):
    c = {}
    tok = np.arange(NTOK)
    pos = (tok % seqlen).astype(np.float32)
    inv = (500000.0 ** (-np.arange(8, dtype=np.float32) / 8)).astype(np.float32)
    ang = pos[:, None] * inv[None, :]
    cos = np.cos(ang).astype(np.float32)
    sin = np.sin(ang).astype(np.float32)
    acs = np.concatenate([np.tile(cos[:, None, :], (1, 20, 1)).reshape(NTOK, 160),
                          np.tile(sin[:, None, :], (1, 20, 1)).reshape(NTOK, 160)], axis=1)
    c["c_acs"] = np.ascontiguousarray(acs, dtype=np.float32)
    b = np.arange(NCH)[:, None, None]
    qi = np.arange(128)[None, :, None]
    kc = np.arange(384)[None, None, :]
    tq = 128 * b + qi
    tk = 128 * (b - 1) + kc
    valid = (tk >= 0) & (tk < NTOK) & ((tk // seqlen) == (tq // seqlen)) & (np.abs(tk - tq) <= 128)
    c["c_amask"] = np.where(valid, 0.0, NEG).astype(np.float32).astype(ml_dtypes.bfloat16)
    invr = (10000.0 ** (-np.arange(128, dtype=np.float32) / 128)).astype(np.float32)
    angr = pos[:, None] * invr[None, :]
    c["c_rcs"] = np.ascontiguousarray(np.concatenate([np.cos(angr), np.sin(angr)], axis=1), dtype=np.float32)
    c["c_rbnd"] = np.full((128, 1), 1.0 if seqlen == NTOK else 0.0, np.float32)
    return c


def kernel(x_prompt, x_sample, norm_gains, ffn_w_in, ffn_w_out, attn_w_qkv, attn_w_o, attn_sink,
           ret_w_in, ret_w_o, ret_decay_fwd, ret_decay_bwd, _nsub=6, _subs=None, _nblk=NCH, _cores=None, _trace=False, _dbg=3):
    f = lambda a: np.ascontiguousarray(np.asarray(a, dtype=np.float32))
    xp = f(x_prompt).reshape(4, NTOK, D)
    xs = f(x_sample).reshape(4, NTOK, D)
    shared = {
        "norm_gains": f(norm_gains), "ffn_w_in": f(ffn_w_in), "ffn_w_out": f(ffn_w_out),
        "attn_w_qkv": f(attn_w_qkv), "attn_w_o": f(attn_w_o), "attn_sink": f(attn_sink),
        "ret_w_in": f(ret_w_in), "ret_w_o": f(ret_w_o), "ret_decay_fwd": f(ret_decay_fwd),
        "ret_decay_bwd": f(ret_decay_bwd),
    }
    shared.update(make_consts())
    in_maps = []
    cc = [make_core_consts(2048), make_core_consts(4096)]
    for c in range(8):
        m = dict(shared)
        m.update(cc[0] if c < 4 else cc[1])
        m["xin"] = xp[c] if c < 4 else xs[c - 4]
        in_maps.append(m)
    nc = build(_nsub, _subs, _nblk, _dbg)
    if _cores is not None:
        res = run_bass_kernel_spmd(nc, [in_maps[c] for c in _cores], core_ids=list(range(len(_cores))), trace=_trace)
        if _trace:
            print("exec_time_ns", res.exec_time_ns)
        return [np.asarray(r["y"], dtype=np.float32) for r in res.results]
    res = run_bass_kernel_spmd(nc, in_maps, core_ids=list(range(8)))
    outs = [np.asarray(r["y"], dtype=np.float32) for r in res.results]
    y_prompt = np.stack(outs[:4]).reshape(8, 2048, D)
    y_sample = np.stack(outs[4:]).reshape(4, 4096, D)
    return (y_prompt, y_sample)
```

```python
import os
import numpy as np
import ml_dtypes
from contextlib import ExitStack
import concourse.bass as bass
import concourse.mybir as mybir
from concourse.bass_utils import run_bass_kernel_spmd

F32 = mybir.dt.float32
BF16 = mybir.dt.bfloat16
AF = mybir.ActivationFunctionType
ALU = mybir.AluOpType
AX = mybir.AxisListType

NTOK = 4096
NCH = 32
D = 1024
DFF = 2816
EPS = 1e-6
EPOCH = 1 << 30
NEG = -30000.0


class Dep:
    __slots__ = ("name", "w", "rs", "dsem", "dcnt", "ex", "retired")

    def __init__(self, name="", ex=False):
        self.name = name
        self.ex = ex
        self.w = None
        self.rs = {}
        self.dsem = None
        self.dcnt = 0
        self.retired = False


class Emitter:
    def __init__(self, nc):
        self.nc = nc
        self.eng = {"pe": nc.tensor, "act": nc.scalar, "dve": nc.vector,
                    "pool": nc.gpsimd, "sp": nc.sync}
        self.sem = {}
        self.cnt = {}
        self.nsem = 0
        for e in self.eng:
            self.sem[e] = self._newsem("e_" + e)
            self.cnt[e] = 0
        self.waited = {}
        self.dma_owners = []
        self.free_dsems = []
        self.no_recycle = set()

    def _newsem(self, name):
        self.nsem += 1
        return self.nc.alloc_semaphore("%s_%d" % (name, self.nsem))

    def _tick(self, e):
        if self.cnt[e] >= EPOCH:
            self.sem[e] = self._newsem("e_" + e)
            self.cnt[e] = 0
        self.cnt[e] += 1
        return self.sem[e], self.cnt[e]

    def _need(self, e, rec, needs):
        if rec is None:
            return
        if rec[0] == "e":
            _, pe, sem, val = rec
            if pe == e and e == "pe":
                return
            needs.append((sem, val))
        else:
            o = rec[1]
            needs.append((o.dsem, o.dcnt))

    def _collect(self, e, reads, writes):
        needs = []
        for d in reads:
            self._need(e, d.w, needs)
        for d in writes:
            if d.w is not None:
                self._need(e, d.w, needs)
            for k, r in d.rs.items():
                self._need(e, r, needs)
        return needs

    def _emit_waits(self, e, needs):
        eng = self.eng[e]
        best = {}
        for sem, val in needs:
            k = id(sem)
            if k not in best or best[k][1] < val:
                best[k] = (sem, val)
        for k, (sem, val) in best.items():
            wk = (e, k)
            if self.waited.get(wk, 0) >= val:
                continue
            self.waited[wk] = val
            eng.wait_ge(sem, val)

    def op(self, e, fn, reads=(), writes=()):
        xr = [d for d in reads if d.ex]
        needs = []
        if xr:
            for d in xr:
                self._need(e, d.w, needs)
            reads = [d for d in reads if not d.ex]
            writes = list(writes) + [d for d in xr if d not in writes]
        needs += self._collect(e, reads, writes)
        self._emit_waits(e, needs)
        inst = fn(self.eng[e])
        sem, val = self._tick(e)
        inst.then_inc(sem, 1)
        rec = ("e", e, sem, val)
        for d in reads:
            d.rs[e] = rec
        for d in writes:
            d.w = rec
            d.rs = {}
        return inst

    def dma(self, q, out, in_, reads=(), writes=(), owner=None):
        needs = self._collect(q, reads, writes)
        if owner is None:
            owner = writes[0] if writes else reads[0]
        if owner.dsem is None or owner.retired:
            if self.free_dsems and q != "pool":
                owner.dsem, owner.dcnt = self.free_dsems.pop()
            else:
                owner.dsem, owner.dcnt = self._newsem("d_" + owner.name), 0
                if q == "pool":
                    self.no_recycle.add(id(owner.dsem))
            owner.retired = False
            self.dma_owners.append(owner)
        self._emit_waits(q, needs)
        inst = self.eng[q].dma_start(out=out, in_=in_)
        owner.dcnt += 16
        inst.then_inc(owner.dsem, 16)
        rec = ("d", owner)
        for d in reads:
            d.rs["dma%d" % id(owner)] = rec
        for d in writes:
            d.w = rec
            d.rs = {}
        return inst

    def barrier(self):
        pts = [(self.sem[e], self.cnt[e]) for e in self.eng if self.cnt[e] > 0]
        pts += [(o.dsem, o.dcnt) for o in self.dma_owners]
        for e in self.eng:
            self._emit_waits(e, pts)
        for o in self.dma_owners:
            o.retired = True
            if id(o.dsem) not in self.no_recycle:
                self.free_dsems.append((o.dsem, o.dcnt))
        self.dma_owners = []

    def finish(self):
        sp = self.eng["sp"]
        pts = [(o.dsem, o.dcnt) for o in self.dma_owners]
        self._emit_waits("sp", pts)


def PDep(name):
    return Dep(name, ex=True)


class Ctx:
    pass


_uid = [0]


def _alloc(st, nc, name, shape, dt):
    _uid[0] += 1
    return st.enter_context(nc.sbuf_tensor("%s_u%d" % (name, _uid[0]), shape, dt))


def _palloc(st, nc, name, shape, dt):
    _uid[0] += 1
    return st.enter_context(nc.psum_tensor("%s_u%d" % (name, _uid[0]), shape, dt))


def emit_rstd(em, ss, dss, mhalf, dmh, n_feat):
    em.op("pool", lambda e: e.tensor_scalar(out=ss[:, 1:2], in0=ss[:, 0:1], scalar1=1.0 / n_feat,
                                            scalar2=EPS, op0=ALU.mult, op1=ALU.add),
          reads=[dss], writes=[dss])
    em.op("pool", lambda e: e.tensor_tensor(out=ss[:, 2:3], in0=ss[:, 1:2], in1=mhalf[:, 0:1], op=ALU.pow),
          reads=[dss, dmh], writes=[dss])


def emit_prenorm_T(C, em, xs, dxs, gpre, dgpre, hbs, dhbs, ss, dss, tp, dtp, hT_dst, dhT):
    em.op("act", lambda e: e.activation(out=hbs[:, :], in_=xs[:, :], func=AF.Square, accum_out=ss[:, 0:1]),
          reads=[dxs], writes=[dhbs, dss])
    emit_rstd(em, ss, dss, C.mhalf, C.dmh, D)
    em.op("dve", lambda e: e.scalar_tensor_tensor(out=hbs[:, :], in0=xs[:, :], scalar=ss[:, 2:3], in1=gpre[:, :],
                                                  op0=ALU.mult, op1=ALU.mult),
          reads=[dxs, dss, dgpre], writes=[dhbs])

    def tps(e):
        for k in range(8):
            i = e.transpose(tp[:, k, :], hbs[:, k * 128:(k + 1) * 128], C.ident[:, :])
        return i
    em.op("pe", tps, reads=[dhbs, C.dident], writes=[dtp])
    em.op("act", lambda e: e.activation(out=hT_dst, in_=tp[:, :, :], func=AF.Copy), reads=[dtp], writes=[dhT])


def load_gains(C, em, st, nc, l, ipre, ipost, post_scale):
    gpre = _alloc(st, nc, "gpre", [128, D], F32)
    gpost = _alloc(st, nc, "gpost", [128, D], F32)
    dgpre = Dep("gpre")
    dgpost = Dep("gpost")
    em.dma("sp", gpre[:, :], C.ng[l, ipre, :].partition_broadcast(128), writes=[dgpre])
    em.dma("sp", gpost[:, :], C.ng[l, ipost, :].partition_broadcast(128), writes=[dgpost])
    if post_scale != 1.0:
        em.op("pool", lambda e: e.tensor_scalar(out=gpost[:, :], in0=gpost[:, :], scalar1=post_scale, scalar2=0.0,
                                                op0=ALU.mult, op1=ALU.add), reads=[dgpost], writes=[dgpost])
    return gpre, dgpre, gpost, dgpost


def emit_post_residual(C, em, ps_out, dps, xs, dxs, gpost, dgpost, ss, dss, junk, djunk, dst_ap, ddst):
    em.op("act", lambda e: e.activation(out=junk[:, :], in_=ps_out, func=AF.Square, accum_out=ss[:, 0:1]),
          reads=[dps], writes=[djunk, dss])
    emit_rstd(em, ss, dss, C.mhalf, C.dmh, D)
    em.op("dve", lambda e: e.scalar_tensor_tensor(out=ps_out, in0=ps_out, scalar=ss[:, 2:3], in1=gpost[:, :],
                                                  op0=ALU.mult, op1=ALU.mult),
          reads=[dps, dss, dgpost], writes=[dps])
    em.op("dve", lambda e: e.tensor_tensor(out=xs[:, :], in0=ps_out, in1=xs[:, :], op=ALU.add),
          reads=[dps, dxs], writes=[dxs])
    em.dma("sp", dst_ap, xs[:, :], reads=[dxs], writes=[ddst], owner=dxs)


def emit_ffn(C, l, which, src, dsrc):
    nc, em = C.nc, C.em
    NJ = DFF // 128
    LAG = 3
    with ExitStack() as st:
        Win = _alloc(st, nc, "Win", [128, 8, 2 * DFF], BF16)
        Wout = _alloc(st, nc, "Wout", [128, NJ, D], BF16)
        dWin = [Dep("Win%d" % k) for k in range(4)]
        dWout = [Dep("Wout%d" % k) for k in range(2)]
        w_in = C.fwi[l, which]
        w_out = C.fwo[l, which]
        wi_v = w_in.rearrange("(k p) f -> p k f", p=128)
        for k in range(4):
            em.dma("pool", Win[:, 2 * k:2 * k + 2, :], wi_v[:, 2 * k:2 * k + 2, :], writes=[dWin[k]])
        wo_v = w_out.rearrange("(j p) d -> p j d", p=128)
        em.dma("pool", Wout[:, 0:11, :], wo_v[:, 0:11, :], writes=[dWout[0]])
        em.dma("pool", Wout[:, 11:22, :], wo_v[:, 11:22, :], writes=[dWout[1]])
        gpre, dgpre, gpost, dgpost = load_gains(C, em, st, nc, l, 0 if which == 0 else 4, 1 if which == 0 else 5, 0.5)

        xa = [_alloc(st, nc, "xa%d" % i, [128, D], F32) for i in range(3)]
        dxa = [Dep("xa%d" % i) for i in range(3)]
        xb = [_alloc(st, nc, "xb%d" % i, [128, D], F32) for i in range(3)]
        dxb = [Dep("xb%d" % i) for i in range(3)]
        hb = [_alloc(st, nc, "hb%d" % i, [128, D], BF16) for i in range(2)]
        dhb = [Dep("hb%d" % i) for i in range(2)]
        hT = [_alloc(st, nc, "hT%d" % i, [128, 8, 256], BF16) for i in range(2)]
        dhT = [[Dep("hT%d_%d" % (i, c)) for c in range(2)] for i in range(2)]
        NA = 6
        actT = [_alloc(st, nc, "actT%d" % i, [128, 256], BF16) for i in range(NA)]
        dact = [Dep("actT%d" % i) for i in range(NA)]
        sg = [_alloc(st, nc, "sg%d" % i, [128, 256], BF16) for i in range(2)]
        dsg = [Dep("sg%d" % i) for i in range(2)]
        junk = _alloc(st, nc, "junk", [128, D], BF16)
        djunk = Dep("junk")
        sst = [_alloc(st, nc, "ss%d" % i, [128, 4], F32) for i in range(4)]
        dsst = [Dep("ss%d" % i) for i in range(4)]
        tp = [_palloc(st, nc, "tp%d" % i, [128, 8, 128], BF16) for i in range(2)]
        dtp = [PDep("tp%d" % i) for i in range(2)]
        gu = [_palloc(st, nc, "gu%d" % i, [128, 2, 256], F32) for i in range(2)]
        dgu = [PDep("gu%d" % i) for i in range(2)]
        pout = _palloc(st, nc, "pout", [128, 2, D], F32)
        dpout = [PDep("pout%d" % i) for i in range(2)]

        NT = NTOK // 256
        sctr = [0]

        def loads(t):
            for c in range(2):
                ch = 2 * t + c
                em.dma("sp", xa[ch % 3][:, :], src[ch * 128:(ch + 1) * 128, :], reads=[dsrc[ch]], writes=[dxa[ch % 3]])

        def front(t):
            for c in range(2):
                ch = 2 * t + c
                xs, dxs = xa[ch % 3], dxa[ch % 3]
                hbs, dhbs = hb[ch % 2], dhb[ch % 2]
                si = sctr[0] % 4
                sctr[0] += 1
                ss, dss = sst[si], dsst[si]
                em.op("act", lambda e, hbs=hbs, xs=xs, ss=ss: e.activation(out=hbs[:, :], in_=xs[:, :], func=AF.Square, accum_out=ss[:, 0:1]),
                      reads=[dxs], writes=[dhbs, dss])
                emit_rstd(em, ss, dss, C.mhalf, C.dmh, D)
                em.op("dve", lambda e, hbs=hbs, xs=xs, ss=ss: e.scalar_tensor_tensor(out=hbs[:, :], in0=xs[:, :], scalar=ss[:, 2:3], in1=gpre[:, :],
                                                                                  op0=ALU.mult, op1=ALU.mult),
                      reads=[dxs, dss, dgpre], writes=[dhbs])

        def transp(t, c):
            ch = 2 * t + c
            hbs = hb[ch % 2]

            def tps(e):
                for k in range(8):
                    i = e.transpose(tp[ch % 2][:, k, :], hbs[:, k * 128:(k + 1) * 128], C.ident[:, :])
                return i
            em.op("pe", tps, reads=[dhb[ch % 2], C.dident], writes=[dtp[ch % 2]])
            em.op("act", lambda e: e.activation(out=hT[t % 2][:, :, c * 128:(c + 1) * 128], in_=tp[ch % 2][:, :, :], func=AF.Copy),
                  reads=[dtp[ch % 2]], writes=[dhT[t % 2][c]])

        def xb_loads(t):
            for tc in range(2):
                ch = 2 * t + tc
                em.dma("sp", xb[ch % 3][:, :], src[ch * 128:(ch + 1) * 128, :], reads=[dsrc[ch]], writes=[dxb[ch % 3]])

        def p1(t, j):
            g = gu[j % 2]
            hTt = hT[t % 2]

            def f(e):
                for half in range(2):
                    for k in range(8):
                        i = e.matmul(g[:, half, :], lhsT=Win[:, k, half * DFF + j * 128: half * DFF + (j + 1) * 128],
                                     rhs=hTt[:, k, :], start=(k == 0), stop=(k == 7))
                return i
            em.op("pe", f, reads=dWin + dhT[t % 2], writes=[dgu[j % 2]])
            s = sg[j % 2]
            em.op("act", lambda e: e.activation(out=s[:, :], in_=g[:, 0, :], func=AF.Silu),
                  reads=[dgu[j % 2]], writes=[dsg[j % 2]])
            a = actT[j % NA]
            em.op("dve", lambda e: e.tensor_tensor(out=a[:, :], in0=g[:, 1, :], in1=s[:, :], op=ALU.mult),
                  reads=[dgu[j % 2], dsg[j % 2]], writes=[dact[j % NA]])

        def p2(t, j):
            a = actT[j % NA]

            def f(e):
                for tc in range(2):
                    for half in range(2):
                        i = e.matmul(pout[:, tc, half * 512:(half + 1) * 512], lhsT=a[:, tc * 128:(tc + 1) * 128],
                                     rhs=Wout[:, j, half * 512:(half + 1) * 512], start=(j == 0), stop=(j == NJ - 1))
                return i
            em.op("pe", f, reads=[dact[j % NA], dWout[0 if j < 11 else 1]], writes=dpout)

        def epilogue(t):
            for tc in range(2):
                ch = 2 * t + tc
                xs, dxs = xb[ch % 3], dxb[ch % 3]
                si = sctr[0] % 4
                sctr[0] += 1
                emit_post_residual(C, em, pout[:, tc, :], dpout[tc], xs, dxs, gpost, dgpost, sst[si], dsst[si],
                                   junk, djunk, C.y[ch * 128:(ch + 1) * 128, :], C.dy[ch])

        loads(0)
        front(0)
        transp(0, 0)
        transp(0, 1)
        if NT > 1:
            loads(1)
        for t in range(NT):
            for j in range(NJ + LAG):
                if j < NJ:
                    p1(t, j)
                if j >= LAG:
                    p2(t, j - LAG)
                if t + 1 < NT:
                    if j == 5:
                        front(t + 1)
                    elif j == 11:
                        transp(t + 1, 0)
                    elif j == 15:
                        transp(t + 1, 1)
                    elif j == 18 and t + 2 < NT:
                        loads(t + 2)
                if j == 13:
                    xb_loads(t)
            epilogue(t)
        em.barrier()


def emit_attn(C, l, src, dsrc):
    nc, em = C.nc, C.em
    SCALE = 0.125
    with ExitStack() as st:
        Wqkv = _alloc(st, nc, "Wqkv", [128, 8, 1536], BF16)
        Wo = _alloc(st, nc, "Wo", [128, 8, D], BF16)
        dWqkv, dWo = Dep("Wqkv"), Dep("Wo")
        em.dma("pool", Wqkv[:, :, :], C.wqkv[0].rearrange("(k p) f -> p k f", p=128), writes=[dWqkv])
        em.dma("pool", Wo[:, :, :], C.wo[0].rearrange("(k p) f -> p k f", p=128), writes=[dWo])
        gpre, dgpre, gpost, dgpost = load_gains(C, em, st, nc, l, 2, 3, 1.0)
        sinkt = _alloc(st, nc, "sinkt", [128, 16], F32)
        nsink = _alloc(st, nc, "nsink", [128, 16], F32)
        dsink = Dep("sink")
        em.dma("sp", sinkt[:, :], C.sink[0, :].partition_broadcast(128), writes=[dsink])
        em.op("pool", lambda e: e.tensor_scalar(out=nsink[:, :], in0=sinkt[:, :], scalar1=-1.0, scalar2=0.0,
                                                op0=ALU.mult, op1=ALU.add), reads=[dsink], writes=[dsink])

        kT = _alloc(st, nc, "kT_all", [128, 8, 34 * 128], BF16)
        vA = _alloc(st, nc, "v_all", [128, 34, 256], BF16)
        dkT = [Dep("kT%d" % i) for i in range(34)]
        dvA = [Dep("vA%d" % i) for i in range(34)]
        for i in (0, 33):
            em.op("dve", lambda e, i=i: e.memset(kT[:, :, i * 128:(i + 1) * 128], 0.0), writes=[dkT[i]])
            em.op("dve", lambda e, i=i: e.memset(vA[:, i, :], 0.0), writes=[dvA[i]])

        NX = 5
        xa = [_alloc(st, nc, "xa%d" % i, [128, D], F32) for i in range(NX)]
        dxa = [Dep("xa%d" % i) for i in range(NX)]
        hb = [_alloc(st, nc, "hb%d" % i, [128, D], BF16) for i in range(2)]
        dhb = [Dep("hb%d" % i) for i in range(2)]
        hT = [_alloc(st, nc, "hT%d" % i, [128, 8, 128], BF16) for i in range(2)]
        dhT = [Dep("hT%d" % i) for i in range(2)]
        cs = [_alloc(st, nc, "cs%d" % i, [128, 320], F32) for i in range(2)]
        dcs = [Dep("cs%d" % i) for i in range(2)]
        mk = [_alloc(st, nc, "mk%d" % i, [128, 384], BF16) for i in range(2)]
        dmk = [Dep("mk%d" % i) for i in range(2)]
        qtok = [_alloc(st, nc, "qtok%d" % i, [128, 16, 64], BF16) for i in range(2)]
        dqtok = [Dep("qtok%d" % i) for i in range(2)]
        kdtok = [_alloc(st, nc, "kdtok%d" % i, [128, 4, 2, 128], BF16) for i in range(2)]
        dkdtok = [Dep("kdtok%d" % i) for i in range(2)]
        for i in range(2):
            em.op("dve", lambda e, i=i: e.memset(kdtok[i][:, :, :, :], 0.0), writes=[dkdtok[i]])
        rt = [_alloc(st, nc, "rt%d" % i, [128, 20, 8], F32) for i in range(4)]
        drt = [Dep("rt%d" % i) for i in range(4)]
        NQ = 3
        qT = [_alloc(st, nc, "qT%d" % i, [128, 8, 128], BF16) for i in range(NQ)]
        dqT = [Dep("qT%d" % i) for i in range(NQ)]
        pb = [_alloc(st, nc, "pb%d" % i, [128, 384], BF16) for i in range(2)]
        dpb = [Dep("pb%d" % i) for i in range(2)]
        pT = [_alloc(st, nc, "pT%d" % i, [128, 3, 128], BF16) for i in range(2)]
        dpT = [Dep("pT%d" % i) for i in range(2)]
        stt = [_alloc(st, nc, "stt%d" % i, [128, 6, 16], F32) for i in range(2)]
        dsth = [[Dep("st%d_%d" % (i, h)) for h in range(16)] for i in range(2)]
        dfin = [[Dep("fin%d_%d" % (i, k)) for k in range(2)] for i in range(2)]
        otok = [_alloc(st, nc, "otok%d" % i, [128, 16, 64], BF16) for i in range(2)]
        dotok = [[Dep("otok%d_%d" % (i, k)) for k in range(2)] for i in range(2)]
        oT = _alloc(st, nc, "oT", [128, 8, 128], BF16)
        doT = Dep("oT")
        junk = _alloc(st, nc, "junk", [128, D], BF16)
        djunk = Dep("junk")
        sst = [_alloc(st, nc, "ss%d" % i, [128, 4], F32) for i in range(4)]
        dsst = [Dep("ss%d" % i) for i in range(4)]

        qkv = _palloc(st, nc, "qkv", [128, 1536], F32)
        dq01, dq2 = PDep("qkv01"), PDep("qkv2")
        tp = _palloc(st, nc, "tp", [128, 8, 128], BF16)
        dtp = PDep("tp")
        sps = _palloc(st, nc, "sps", [128, 2, 512], F32)
        dsps = [PDep("sps%d" % i) for i in range(2)]
        ptp = _palloc(st, nc, "ptp", [128, 2, 4, 128], BF16)
        dptp = PDep("ptp")
        ops = _palloc(st, nc, "ops", [128, 8, 64], F32)
        dops = PDep("ops")
        sctr = [0]

        def A_steps(b):
            xs, dxs = xa[b % NX], dxa[b % NX]
            c_, dc_ = cs[b % 2], dcs[b % 2]
            h_ = hT[b % 2]
            qt, dqt = qtok[b % 2], dqtok[b % 2]
            kd, dkd = kdtok[b % 2], dkdtok[b % 2]

            def a_front():
                si = sctr[0] % 4
                sctr[0] += 1
                ss, dss = sst[si], dsst[si]
                hbs, dhbs = hb[b % 2], dhb[b % 2]
                em.op("act", lambda e: e.activation(out=hbs[:, :], in_=xs[:, :], func=AF.Square, accum_out=ss[:, 0:1]),
                      reads=[dxs], writes=[dhbs, dss])
                emit_rstd(em, ss, dss, C.mhalf, C.dmh, D)
                em.op("dve", lambda e: e.scalar_tensor_tensor(out=hbs[:, :], in0=xs[:, :], scalar=ss[:, 2:3], in1=gpre[:, :],
                                                              op0=ALU.mult, op1=ALU.mult),
                      reads=[dxs, dss, dgpre], writes=[dhbs])

            def a0():
                hbs = hb[b % 2]

                def tps(e):
                    for k in range(8):
                        i = e.transpose(tp[:, k, :], hbs[:, k * 128:(k + 1) * 128], C.ident[:, :])
                    return i
                em.op("pe", tps, reads=[dhb[b % 2], C.dident], writes=[dtp])
                em.op("act", lambda e: e.activation(out=h_[:, :, :], in_=tp[:, :, :], func=AF.Copy), reads=[dtp], writes=[dhT[b % 2]])

            def a1():
                for g in range(3):
                    def f(e, g=g):
                        for k in range(8):
                            i = e.matmul(qkv[:, g * 512:(g + 1) * 512], lhsT=h_[:, k, :], rhs=Wqkv[:, k, g * 512:(g + 1) * 512],
                                         start=(k == 0), stop=(k == 7))
                        return i
                    em.op("pe", f, reads=[dhT[b % 2], dWqkv], writes=[dq01 if g < 2 else dq2])
                qv = qkv[:, 0:1024].rearrange("p (h d) -> p h d", d=64)
                kv = qkv[:, 1024:1280].rearrange("p (h d) -> p h d", d=64)
                em.op("act", lambda e: e.activation(out=qt[:, :, 16:64], in_=qv[:, :, 16:64], func=AF.Copy),
                      reads=[dq01], writes=[dqt])
                for dup in range(2):
                    em.op("act", lambda e, dup=dup: e.activation(out=kd[:, :, dup, dup * 64 + 16:dup * 64 + 64], in_=kv[:, :, 16:64],
                                                                 func=AF.Copy), reads=[dq2], writes=[dkd])
                em.op("act", lambda e: e.activation(out=vA[:, b + 1, :], in_=qkv[:, 1280:1536], func=AF.Copy),
                      reads=[dq2], writes=[dvA[b + 1]])

            def a2():
                qkv20 = qkv[:, 0:1280].rearrange("p (h d) -> p h d", d=64)
                cosv = c_[:, 0:160].rearrange("p (h d) -> p h d", d=8)
                sinv = c_[:, 160:320].rearrange("p (h d) -> p h d", d=8)
                x1 = qkv20[:, :, 0:8]
                x2 = qkv20[:, :, 8:16]
                for i, (xx, tb) in enumerate([(x1, cosv), (x2, sinv), (x2, cosv), (x1, sinv)]):
                    em.op("dve", lambda e, i=i, xx=xx, tb=tb: e.tensor_tensor(out=rt[i][:, :, :], in0=xx, in1=tb, op=ALU.mult),
                          reads=[dq01, dq2, dc_], writes=[drt[i]])
                em.op("dve", lambda e: e.tensor_tensor(out=qt[:, :, 0:8], in0=rt[0][:, 0:16, :], in1=rt[1][:, 0:16, :], op=ALU.subtract),
                      reads=[drt[0], drt[1]], writes=[dqt])
                em.op("dve", lambda e: e.tensor_tensor(out=qt[:, :, 8:16], in0=rt[2][:, 0:16, :], in1=rt[3][:, 0:16, :], op=ALU.add),
                      reads=[drt[2], drt[3]], writes=[dqt])
                for dup in range(2):
                    em.op("dve", lambda e, dup=dup: e.tensor_tensor(out=kd[:, :, dup, dup * 64:dup * 64 + 8], in0=rt[0][:, 16:20, :],
                                                                    in1=rt[1][:, 16:20, :], op=ALU.subtract),
                          reads=[drt[0], drt[1]], writes=[dkd])
                    em.op("dve", lambda e, dup=dup: e.tensor_tensor(out=kd[:, :, dup, dup * 64 + 8:dup * 64 + 16], in0=rt[2][:, 16:20, :],
                                                                    in1=rt[3][:, 16:20, :], op=ALU.add),
                          reads=[drt[2], drt[3]], writes=[dkd])

            def a3():
                qflat = qt[:, :, :].rearrange("p h d -> p (h d)")

                def tq(e):
                    for k in range(8):
                        i = e.transpose(tp[:, k, :], qflat[:, k * 128:(k + 1) * 128], C.ident[:, :])
                    return i
                em.op("pe", tq, reads=[dqt, C.dident], writes=[dtp])
                em.op("act", lambda e: e.activation(out=qT[b % NQ][:, :, :], in_=tp[:, :, :], func=AF.Copy),
                      reads=[dtp], writes=[dqT[b % NQ]])

            def a4():
                kflat = kd[:, :, :, :].rearrange("p g u d -> p (g u d)")

                def tk(e):
                    for k in range(8):
                        i = e.transpose(tp[:, k, :], kflat[:, k * 128:(k + 1) * 128], C.ident[:, :])
                    return i
                em.op("pe", tk, reads=[dkd, C.dident], writes=[dtp])
                em.op("act", lambda e: e.activation(out=kT[:, :, (b + 1) * 128:(b + 2) * 128], in_=tp[:, :, :], func=AF.Copy),
                      reads=[dtp], writes=[dkT[b + 1]])
            return [a_front, a0, a1, a2, a3, a4]

        def A_loads(b):
            em.dma("sp", xa[b % NX][:, :], src[b * 128:(b + 1) * 128, :], reads=[dsrc[b]], writes=[dxa[b % NX]])
            em.dma("sp", cs[b % 2][:, :], C.acs[b * 128:(b + 1) * 128, :], writes=[dcs[b % 2]])

        def C_steps(b):
            ot = otok[b % 2]

            def c0():
                oflat = ot[:, :, :].rearrange("p h d -> p (h d)")

                def to(e):
                    for k in range(8):
                        i = e.transpose(tp[:, k, :], oflat[:, k * 128:(k + 1) * 128], C.ident[:, :])
                    return i
                em.op("pe", to, reads=dotok[b % 2] + [C.dident], writes=[dtp])
                em.op("act", lambda e: e.activation(out=oT[:, :, :], in_=tp[:, :, :], func=AF.Copy), reads=[dtp], writes=[doT])

            def c1():
                def fw(e):
                    for half in range(2):
                        for k in range(8):
                            i = e.matmul(qkv[:, half * 512:(half + 1) * 512], lhsT=oT[:, k, :], rhs=Wo[:, k, half * 512:(half + 1) * 512],
                                         start=(k == 0), stop=(k == 7))
                    return i
                em.op("pe", fw, reads=[doT, dWo], writes=[dq01])

            def c2():
                si = sctr[0] % 4
                sctr[0] += 1
                emit_post_residual(C, em, qkv[:, 0:1024], dq01, xa[b % NX], dxa[b % NX], gpost, dgpost, sst[si], dsst[si],
                                   junk, djunk, C.y[b * 128:(b + 1) * 128, :], C.dy[b])
            return [c0, c1, c2]

        def finish_heads(b, h0):
            s_ = stt[b % 2]
            k = h0 // 8
            dsts = dsth[b % 2][h0:h0 + 8]
            df = dfin[b % 2][k]
            hs = slice(h0, h0 + 8)
            em.op("dve", lambda e: e.tensor_tensor(out=s_[:, 3, hs], in0=s_[:, 1, hs], in1=sinkt[:, hs], op=ALU.add),
                  reads=dsts + [dsink], writes=[df])
            em.op("act", lambda e: e.activation(out=s_[:, 4, hs], in_=s_[:, 3, hs], func=AF.Exp), reads=[df], writes=[df])
            em.op("dve", lambda e: e.tensor_tensor(out=s_[:, 4, hs], in0=s_[:, 4, hs], in1=s_[:, 2, hs], op=ALU.add),
                  reads=dsts + [df], writes=[df])
            em.op("dve", lambda e: e.reciprocal(out=s_[:, 5, hs], in_=s_[:, 4, hs]), reads=[df], writes=[df])
            em.op("dve", lambda e: e.tensor_tensor(out=otok[b % 2][:, hs, :], in0=ops[:, :, :],
                                                   in1=s_[:, 5, hs].unsqueeze(2).broadcast_to([128, 8, 64]), op=ALU.mult),
                  reads=[dops, df], writes=[dotok[b % 2][k]])

        def stageB(b, steps):
            m_, dm_ = mk[b % 2], dmk[b % 2]
            em.dma("sp", m_[:, :], C.amask[b], writes=[dm_])
            s_ = stt[b % 2]
            q_ = qT[b % NQ]

            def S(h):
                g, pr, hf = h // 4, h // 2, h % 2
                sp_, dsp_ = sps[:, h % 2, 0:384], dsps[h % 2]

                def f(e):
                    e.matmul(sp_, lhsT=q_[:, pr, :], rhs=kT[:, g * 2 + hf, b * 128: b * 128 + 384], start=True, stop=False)
                    return e.matmul(sp_, lhsT=C.ident[:, :], rhs=m_[:, :], start=False, stop=True)
                em.op("pe", f, reads=[dqT[b % NQ], dkT[b], dkT[b + 1], dkT[b + 2], dm_, C.dident], writes=[dsp_])
                ds_ = dsth[b % 2][h]
                em.op("dve", lambda e: e.tensor_reduce(out=s_[:, 0, h:h + 1], in_=sp_, op=ALU.max, axis=AX.X),
                      reads=[dsp_], writes=[ds_])
                em.op("dve", lambda e: e.tensor_scalar(out=s_[:, 1, h:h + 1], in0=s_[:, 0, h:h + 1], scalar1=-SCALE,
                                                       scalar2=nsink[:, h:h + 1], op0=ALU.mult, op1=ALU.min),
                      reads=[ds_, dsink], writes=[ds_])
                p_, dp_ = pb[h % 2], dpb[h % 2]
                em.op("act", lambda e: e.activation(out=p_[:, :], in_=sp_, func=AF.Exp, scale=SCALE,
                                                    bias=s_[:, 1, h:h + 1], accum_out=s_[:, 2, h:h + 1]),
                      reads=[dsp_, ds_], writes=[dp_, ds_])

            def T(h):
                p_, dp_ = pb[h % 2], dpb[h % 2]

                def ft(e):
                    for c in range(3):
                        i = e.transpose(ptp[:, h % 2, c, :], p_[:, c * 128:(c + 1) * 128], C.ident[:, :])
                    return i
                em.op("pe", ft, reads=[dp_, C.dident], writes=[dptp])
                em.op("act", lambda e: e.activation(out=pT[h % 2][:, :, :], in_=ptp[:, h % 2, 0:3, :], func=AF.Copy),
                      reads=[dptp], writes=[dpT[h % 2]])

            def PV(h):
                g = h // 4

                def fo(e):
                    for c in range(3):
                        i = e.matmul(ops[:, h % 8, :], lhsT=pT[h % 2][:, c, :], rhs=vA[:, b + c, g * 64:(g + 1) * 64],
                                     start=(c == 0), stop=(c == 2))
                    return i
                em.op("pe", fo, reads=[dpT[h % 2], dvA[b], dvA[b + 1], dvA[b + 2]], writes=[dops])

            S(0)
            S(1)
            for h in range(16):
                T(h)
                if h + 2 < 16:
                    S(h + 2)
                PV(h)
                if h % 8 == 7:
                    finish_heads(b, h - 7)
                if h in (1, 3, 5, 7, 9, 11, 13, 14, 15) and steps:
                    steps.pop(0)()
            while steps:
                steps.pop(0)()

        nblk = getattr(C, "nblk", NCH)
        nop = lambda: None
        for b0 in range(min(2, NCH)):
            A_loads(b0)
        for s_ in A_steps(0):
            s_()
        if NCH > 2:
            A_loads(2)
        if NCH > 1:
            for s_ in A_steps(1):
                s_()
        for b in range(nblk):
            Cs = C_steps(b - 1) if b >= 1 else [nop] * 3
            As = A_steps(b + 2) if b + 2 < NCH else [nop] * 6
            ld = (lambda b=b: A_loads(b + 3)) if b + 3 < NCH else nop
            steps = [Cs[0], As[0], Cs[1], As[1], Cs[2], As[2], lambda As=As, ld=ld: (As[3](), ld()), As[4], As[5]]
            stageB(b, steps)
        for s_ in C_steps(nblk - 1):
            s_()
        em.barrier()


def emit_ret(C, l, src, dsrc):
    nc, em = C.nc, C.em
    nch = getattr(C, "nblk", NCH)
    half_b = NCH // 2
    with ExitStack() as st0:
        bnd = _alloc(st0, nc, "bnd", [128, 1], F32)
        kdec = _alloc(st0, nc, "kdec", [128, 8], F32)
        g128 = _alloc(st0, nc, "g128", [128, 8], F32)
        decrow = _alloc(st0, nc, "decrow", [128, 8, 128], F32)
        DT = _alloc(st0, nc, "DT", [128, 4, 128], F32)
        S32 = _alloc(st0, nc, "S32", [128, 4, 2, 512], F32)
        stT = ExitStack()
        rc = _alloc(stT, nc, "rc", [128, 6, 128], F32)
        cpos = _alloc(stT, nc, "cpos", [128, 4], F32)
        dl = _alloc(stT, nc, "dl", [128, 8], F32)
        lg = _alloc(stT, nc, "lg", [128, 8], F32)
        tmpD = _alloc(stT, nc, "tmpD", [128, 2, 128], F32)
        drc, dtab, dS32, dtmp = Dep("rc"), Dep("tab"), Dep("S32"), Dep("tmpD")
        em.dma("sp", rc[:, :, :], C.rconst[:, :, :], writes=[drc])
        em.dma("sp", cpos[:, :], C.rpos[:, :], writes=[drc], owner=drc)
        em.dma("sp", bnd[:, :], C.rbnd[:, :], writes=[drc], owner=drc)
        em.dma("sp", dl[:, 0:4], C.rdf[0, :].partition_broadcast(128), writes=[dtab])
        em.dma("sp", dl[:, 4:8], C.rdb[0, :].partition_broadcast(128), writes=[dtab], owner=dtab)
        em.op("act", lambda e: e.activation(out=lg[:, :], in_=dl[:, :], func=AF.Exp, scale=-1.0), reads=[dtab], writes=[dtab])
        em.op("dve", lambda e: e.tensor_scalar(out=lg[:, :], in0=lg[:, :], scalar1=1.0, scalar2=None, op0=ALU.add), reads=[dtab], writes=[dtab])
        em.op("act", lambda e: e.activation(out=lg[:, :], in_=lg[:, :], func=AF.Ln), reads=[dtab], writes=[dtab])
        em.op("dve", lambda e: e.tensor_scalar(out=lg[:, :], in0=lg[:, :], scalar1=-1.0, scalar2=None, op0=ALU.mult), reads=[dtab], writes=[dtab])
        em.op("act", lambda e: e.activation(out=g128[:, :], in_=lg[:, :], func=AF.Exp, scale=128.0), reads=[dtab], writes=[dtab])
        em.op("act", lambda e: e.activation(out=kdec[:, 0:4], in_=lg[:, 0:4], func=AF.Exp, scale=cpos[:, 2:3]), reads=[dtab, drc], writes=[dtab])
        em.op("act", lambda e: e.activation(out=kdec[:, 4:8], in_=lg[:, 4:8], func=AF.Exp, scale=cpos[:, 3:4]), reads=[dtab, drc], writes=[dtab])
        em.op("dve", lambda e: e.tensor_scalar(out=kdec[:, :], in0=kdec[:, :], scalar1=1.0 / 16, scalar2=None, op0=ALU.mult), reads=[dtab], writes=[dtab])
        for h in range(4):
            em.op("act", lambda e, h=h: e.activation(out=decrow[:, h, :], in_=rc[:, 4, :], func=AF.Exp, scale=lg[:, h:h + 1]), reads=[dtab, drc], writes=[dtab])
            em.op("act", lambda e, h=h: e.activation(out=decrow[:, 4 + h, :], in_=rc[:, 5, :], func=AF.Exp, scale=lg[:, 4 + h:5 + h]), reads=[dtab, drc], writes=[dtab])
            em.op("act", lambda e, h=h: e.activation(out=tmpD[:, 0, :], in_=rc[:, 0, :], func=AF.Exp, scale=lg[:, h:h + 1]), reads=[dtab, drc], writes=[dtmp])
            em.op("act", lambda e, h=h: e.activation(out=tmpD[:, 1, :], in_=rc[:, 1, :], func=AF.Exp, scale=lg[:, 4 + h:5 + h]), reads=[dtab, drc], writes=[dtmp])
            em.op("dve", lambda e, h=h: e.tensor_tensor(out=tmpD[:, :, :], in0=tmpD[:, :, :], in1=rc[:, 2:4, :], op=ALU.mult), reads=[dtmp, drc], writes=[dtmp])
            em.op("dve", lambda e, h=h: e.tensor_tensor(out=DT[:, h, :], in0=tmpD[:, 0, :], in1=tmpD[:, 1, :], op=ALU.add), reads=[dtmp], writes=[dtab])
        em.op("dve", lambda e: e.memset(S32[:, :, :, :], 0.0), writes=[dS32])
        em.barrier()
        stT.close()

        def rotary(ps, dps, c_, dc_, rt, drt, out_bf, dout):
            p4 = ps.rearrange("p (h t f) -> p h t f", h=4, t=2)
            o4 = out_bf[:, :].rearrange("p (h t f) -> p h t f", h=4, t=2)
            cosb = c_[:, 0:128].unsqueeze(1).broadcast_to([128, 4, 128])
            sinb = c_[:, 128:256].unsqueeze(1).broadcast_to([128, 4, 128])
            x1, x2 = p4[:, :, 0, :], p4[:, :, 1, :]
            prods = [(x1, cosb), (x2, sinb), (x2, cosb), (x1, sinb)]
            for i in (0, 1):
                xx, tb = prods[i]
                em.op("dve", lambda e, i=i, xx=xx, tb=tb: e.tensor_tensor(out=rt[i][:, :, :], in0=xx, in1=tb, op=ALU.mult),
                      reads=[dps, dc_], writes=[drt[i]])
            em.op("dve", lambda e: e.tensor_tensor(out=o4[:, :, 0, :], in0=rt[0][:, :, :], in1=rt[1][:, :, :], op=ALU.subtract),
                  reads=[drt[0], drt[1]], writes=[dout])
            for i in (2, 3):
                xx, tb = prods[i]
                em.op("dve", lambda e, i=i, xx=xx, tb=tb: e.tensor_tensor(out=rt[i][:, :, :], in0=xx, in1=tb, op=ALU.mult),
                      reads=[dps, dc_], writes=[drt[i]])
            em.op("dve", lambda e: e.tensor_tensor(out=o4[:, :, 1, :], in0=rt[2][:, :, :], in1=rt[3][:, :, :], op=ALU.add),
                  reads=[drt[2], drt[3]], writes=[dout])

        def state_update(S32, dS32, kd_tok, dkd, v_tok, dv, dsp, dsd, gcol0):
            for h in range(4):
                for c in range(2):
                    def f(e, h=h, c=c):
                        return e.matmul(dsp[:, :], lhsT=kd_tok[:, h * 256 + c * 128: h * 256 + (c + 1) * 128],
                                        rhs=v_tok[:, h * 512:(h + 1) * 512], start=True, stop=True)
                    em.op("pe", f, reads=[dkd, dv], writes=[dsd])
                    em.op("dve", lambda e, h=h, c=c: e.scalar_tensor_tensor(out=S32[:, h, c, :], in0=S32[:, h, c, :],
                                                                          scalar=g128[:, gcol0 + h:gcol0 + h + 1], in1=dsp[:, :],
                                                                          op0=ALU.mult, op1=ALU.add),
                          reads=[dS32, dsd, dtab], writes=[dS32])

        def boundary(S32, dS32):
            em.op("dve", lambda e: e.tensor_scalar(out=S32[:, :, :, :], in0=S32[:, :, :, :], scalar1=bnd[:, 0:1], scalar2=None,
                                                   op0=ALU.mult), reads=[dS32, drc], writes=[dS32])

        with ExitStack() as st:
            Wkv = _alloc(st, nc, "Wkv", [128, 8, 3072], BF16)
            dWkv = [Dep("Wkv%d" % k) for k in range(2)]
            wv = C.rwi[0].rearrange("(k p) f -> p k f", p=128)
            em.dma("pool", Wkv[:, 0:4, :], wv[:, 0:4, 1024:4096], writes=[dWkv[0]])
            em.dma("pool", Wkv[:, 4:8, :], wv[:, 4:8, 1024:4096], writes=[dWkv[1]])
            gpre = _alloc(st, nc, "gpre", [128, D], F32)
            dgpre = Dep("gpre")
            em.dma("sp", gpre[:, :], C.ng[l, 2, :].partition_broadcast(128), writes=[dgpre])
            xa = [_alloc(st, nc, "xa%d" % i, [128, D], F32) for i in range(3)]
            dxa = [Dep("xa%d" % i) for i in range(3)]
            hb2 = [_alloc(st, nc, "hb%d" % i, [128, D], BF16) for i in range(2)]
            dhb2 = [Dep("hb%d" % i) for i in range(2)]
            hT = [_alloc(st, nc, "hT%d" % i, [128, 8, 128], BF16) for i in range(2)]
            dhT = [Dep("hT%d" % i) for i in range(2)]
            NCS = 4
            cs = [_alloc(st, nc, "rcs%d" % i, [128, 256], F32) for i in range(NCS)]
            dcs = [Dep("rcs%d" % i) for i in range(NCS)]
            rt = [_alloc(st, nc, "rrt%d" % i, [128, 4, 128], F32) for i in range(4)]
            drt = [Dep("rrt%d" % i) for i in range(4)]
            krot = [_alloc(st, nc, "krot%d" % i, [128, 1024], BF16) for i in range(2)]
            dkrot = [Dep("krot%d" % i) for i in range(2)]
            kf = [_alloc(st, nc, "kf%d" % i, [128, 1024], BF16) for i in range(2)]
            dkf = [Dep("kf%d" % i) for i in range(2)]
            kb = [_alloc(st, nc, "kb%d" % i, [128, 1024], BF16) for i in range(2)]
            dkb = [Dep("kb%d" % i) for i in range(2)]
            kTb = [_alloc(st, nc, "kTb%d" % i, [128, 8, 128], BF16) for i in range(2)]
            dkTb = [Dep("kTb%d" % i) for i in range(2)]
            vtok = [_alloc(st, nc, "vtok%d" % i, [128, 2048], BF16) for i in range(2)]
            dvtok = [[Dep("vtok%d_%d" % (i, k)) for k in range(2)] for i in range(2)]
            Sbf2 = [_alloc(st, nc, "Sbf%d" % i, [128, 4096], BF16) for i in range(2)]
            dSbfh2 = [[Dep("Sbf%d_%d" % (i, h)) for h in range(4)] for i in range(2)]
            dS32h = [Dep("S32b_%d" % h) for h in range(4)]
            em.op("dve", lambda e: e.memset(S32[:, :, :, :], 0.0), reads=[dS32], writes=dS32h)
            sst = [_alloc(st, nc, "ss%d" % i, [128, 4], F32) for i in range(3)]
            dsst = [Dep("ss%d" % i) for i in range(3)]
            tp = _palloc(st, nc, "tp", [128, 8, 128], BF16)
            dtp = PDep("tp")
            pk = _palloc(st, nc, "pk", [128, 1024], F32)
            dpk = PDep("pk")
            pv1 = _palloc(st, nc, "pv", [128, 1024], F32)
            pv = [pv1, pv1]
            dpv1 = PDep("pv")
            dpv = [dpv1, dpv1]
            dspr = [_palloc(st, nc, "dsp%d" % i, [128, 512], F32) for i in range(2)]
            dsdr = [PDep("dsp%d" % i) for i in range(2)]

            def loads1(n):
                em.dma("sp", xa[n % 3][:, :], src[n * 128:(n + 1) * 128, :], reads=[dsrc[n]], writes=[dxa[n % 3]])
                em.dma("sp", cs[n % NCS][:, :], C.rcs[n * 128:(n + 1) * 128, :], writes=[dcs[n % NCS]])

            def front1(n):
                r = n % 2
                xs, dxs = xa[n % 3], dxa[n % 3]
                hbs, dhbs = hb2[r], dhb2[r]
                ss, dss = sst[n % 3], dsst[n % 3]
                em.op("act", lambda e: e.activation(out=hbs[:, :], in_=xs[:, :], func=AF.Square, accum_out=ss[:, 0:1]),
                      reads=[dxs], writes=[dhbs, dss])
                emit_rstd(em, ss, dss, C.mhalf, C.dmh, D)
                em.op("dve", lambda e: e.scalar_tensor_tensor(out=hbs[:, :], in0=xs[:, :], scalar=ss[:, 2:3], in1=gpre[:, :],
                                                              op0=ALU.mult, op1=ALU.mult),
                      reads=[dxs, dss, dgpre], writes=[dhbs])

            def P1_steps(n):
                r = n % 2
                c_, dc_ = cs[n % NCS], dcs[n % NCS]

                def p0():
                    hbs = hb2[r]

                    def tps(e):
                        for k in range(8):
                            i = e.transpose(tp[:, k, :], hbs[:, k * 128:(k + 1) * 128], C.ident[:, :])
                        return i
                    em.op("pe", tps, reads=[dhb2[r], C.dident], writes=[dtp])
                    em.op("act", lambda e: e.activation(out=hT[r][:, :, :], in_=tp[:, :, :], func=AF.Copy), reads=[dtp], writes=[dhT[r]])

                def p1():
                    for g in range(2):
                        def f(e, g=g):
                            for k in range(8):
                                i = e.matmul(pk[:, g * 512:(g + 1) * 512], lhsT=hT[r][:, k, :], rhs=Wkv[:, k, g * 512:(g + 1) * 512],
                                             start=(k == 0), stop=(k == 7))
                            return i
                        em.op("pe", f, reads=[dhT[r]] + dWkv, writes=[dpk])
                    rotary(pk[:, :], dpk, c_, dc_, rt, drt, krot[r], dkrot[r])

                def p2():
                    for (dst, ddst, col) in ((kf[r], dkf[r], 0), (kb[r], dkb[r], 4)):
                        em.op("dve", lambda e, dst=dst, col=col: e.tensor_tensor(
                            out=dst[:, :].rearrange("p (h f) -> p h f", h=4), in0=krot[r][:, :].rearrange("p (h f) -> p h f", h=4),
                            in1=kdec[:, col:col + 4].unsqueeze(2).broadcast_to([128, 4, 256]), op=ALU.mult),
                            reads=[dkrot[r], dtab], writes=[ddst])

                    def tk(e):
                        for k in range(8):
                            i = e.transpose(tp[:, k, :], krot[r][:, k * 128:(k + 1) * 128], C.ident[:, :])
                        return i
                    em.op("pe", tk, reads=[dkrot[r], C.dident], writes=[dtp])
                    em.op("act", lambda e: e.activation(out=kTb[r][:, :, :], in_=tp[:, :, :], func=AF.Copy), reads=[dtp], writes=[dkTb[r]])
                    em.dma("sp", C.s_kT[n], kTb[r][:, :, :].rearrange("p k t -> p (k t)"), reads=[dkTb[r]], writes=[C.dscr[n]], owner=dkTb[r])
                    em.dma("sp", C.s_kf[n], kf[r][:, :], reads=[dkf[r]], writes=[C.dscr[n]], owner=dkf[r])

                def mkv(hv):
                    def pv_():
                        for g in range(2):
                            def f(e, g=g):
                                col = 1024 + hv * 1024 + g * 512
                                for k in range(8):
                                    i = e.matmul(pv[hv][:, g * 512:(g + 1) * 512], lhsT=hT[r][:, k, :], rhs=Wkv[:, k, col:col + 512],
                                                 start=(k == 0), stop=(k == 7))
                                return i
                            em.op("pe", f, reads=[dhT[r]] + dWkv, writes=[dpv[hv]])
                        em.op("act", lambda e: e.activation(out=vtok[r][:, hv * 1024:(hv + 1) * 1024], in_=pv[hv][:, :], func=AF.Copy),
                              reads=[dpv[hv]], writes=[dvtok[r][hv]])
                    return pv_

                def p5():
                    em.dma("sp", C.s_v[n], vtok[r][:, :], reads=dvtok[r], writes=[C.dscr[n]], owner=dvtok[r][0])
                return [p0, p1, mkv(0), mkv(1), p2, p5]

            def U(n, steps):
                r = n % 2
                if n - 3 >= 0:
                    loads1(n - 3)
                if n - 2 >= 0:
                    front1(n - 2)
                Sbf, dSbfh = Sbf2[n % 2], dSbfh2[n % 2]
                for h in range(4):
                    em.op("act", lambda e, h=h: e.activation(out=Sbf[:, h * 1024:(h + 1) * 1024],
                                                             in_=S32[:, h, :, :].rearrange("p c f -> p (c f)"), func=AF.Copy),
                          reads=[dS32h[h]], writes=[dSbfh[h]])
                    if h == 3:
                        em.dma("sp", C.s_sb[n], Sbf[:, :], reads=dSbfh, writes=[C.dscr[n]], owner=dSbfh[0])
                    for c in range(2):
                        if n > 0:
                            dsp, dsd = dspr[c], dsdr[c]

                            def f(e, h=h, c=c, dsp=dsp):
                                return e.matmul(dsp[:, :], lhsT=kb[r][:, h * 256 + c * 128: h * 256 + (c + 1) * 128],
                                                rhs=vtok[r][:, h * 512:(h + 1) * 512], start=True, stop=True)
                            em.op("pe", f, reads=[dkb[r], dvtok[r][h // 2]], writes=[dsd])
                            em.op("dve", lambda e, h=h, c=c, dsp=dsp: e.scalar_tensor_tensor(out=S32[:, h, c, :], in0=S32[:, h, c, :],
                                                                                           scalar=g128[:, 4 + h:5 + h], in1=dsp[:, :],
                                                                                           op0=ALU.mult, op1=ALU.add),
                                  reads=[dS32h[h], dsd, dtab], writes=[dS32h[h]])
                        if steps:
                            steps.pop(0)()
                while steps:
                    steps.pop(0)()
                if n > 0 and n == half_b:
                    em.op("dve", lambda e: e.tensor_scalar(out=S32[:, :, :, :], in0=S32[:, :, :, :], scalar1=bnd[:, 0:1], scalar2=None,
                                                           op0=ALU.mult), reads=dS32h + [drc], writes=dS32h)

            for i in range(1, 4):
                if nch - i >= 0:
                    loads1(nch - i)
            front1(nch - 1)
            if nch > 1:
                front1(nch - 2)
            for s_ in P1_steps(nch - 1):
                s_()
            for n in range(nch - 1, -1, -1):
                U(n, P1_steps(n - 1) if n > 0 else [])
            em.barrier()

        em.op("dve", lambda e: e.memset(S32[:, :, :, :], 0.0), writes=[dS32])
        with ExitStack() as st:
            Wqg = _alloc(st, nc, "Wqg", [128, 8, 3072], BF16)
            dWqg = [Dep("Wqg%d" % k) for k in range(2)]
            wv = C.rwi[0].rearrange("(k p) f -> p k f", p=128)
            em.dma("pool", Wqg[:, :, 0:1024], wv[:, :, 0:1024], writes=[dWqg[0]])
            em.dma("pool", Wqg[:, :, 1024:3072], wv[:, :, 4096:6144], writes=[dWqg[1]])
            Wro = _alloc(st, nc, "Wro", [128, 16, D], BF16)
            dWro = Dep("Wro")
            em.dma("pool", Wro[:, :, :], C.rwo[0].rearrange("(k p) f -> p k f", p=128), writes=[dWro])
            gpre, dgpre, gpost, dgpost = load_gains(C, em, st, nc, l, 2, 3, 1.0)
            NX = 4
            xa = [_alloc(st, nc, "xa%d" % i, [128, D], F32) for i in range(NX)]
            dxa = [Dep("xa%d" % i) for i in range(NX)]
            hb = _alloc(st, nc, "hb", [128, D], BF16)
            dhb = Dep("hb")
            NH = 3
            hT = [_alloc(st, nc, "hT%d" % i, [128, 8, 128], BF16) for i in range(NH)]
            dhT = [Dep("hT%d" % i) for i in range(NH)]
            cs = [_alloc(st, nc, "rcs%d" % i, [128, 256], F32) for i in range(2)]
            dcs = [Dep("rcs%d" % i) for i in range(2)]
            rt2 = [_alloc(st, nc, "rrt%d" % i, [128, 4, 128], F32) for i in range(2)]
            drt2 = [Dep("rrt%d" % i) for i in range(2)]
            rt = [rt2[0], rt2[1], rt2[0], rt2[1]]
            drt = [drt2[0], drt2[1], drt2[0], drt2[1]]
            qrot = _alloc(st, nc, "qrot", [128, 1024], BF16)
            dqrot = Dep("qrot")
            qT = [[_alloc(st, nc, "qT%d_%d" % (r, i), [128, 8, 128], BF16) for i in range(3)] for r in range(2)]
            dqT = [[Dep("qT%d_%d" % (r, i)) for i in range(3)] for r in range(2)]
            sg = [_alloc(st, nc, "sg%d" % i, [128, 512], BF16) for i in range(2)]
            dsg = [Dep("sg%d" % i) for i in range(2)]
            NR = 2
            kTl = [_alloc(st, nc, "kTl%d" % i, [128, 8, 128], BF16) for i in range(NR)]
            kfl = [_alloc(st, nc, "kfl%d" % i, [128, 1024], BF16) for i in range(NR)]
            vl = [_alloc(st, nc, "vl%d" % i, [128, 2048], BF16) for i in range(NR)]
            sbl1 = _alloc(st, nc, "sbl", [128, 4, 2, 512], BF16)
            sbl = [sbl1, sbl1]
            dkTl = [Dep("kTl%d" % i) for i in range(NR)]
            dkfl = [Dep("kfl%d" % i) for i in range(NR)]
            dvl = [Dep("vl%d" % i) for i in range(NR)]
            dsblh = [Dep("sbl_%d" % h) for h in range(4)]
            Sfb = _alloc(st, nc, "Sfb", [128, 4, 2, 512], BF16)
            dSfb = [Dep("Sfb%d" % h) for h in range(4)]
            dS32h = [Dep("S32_%d" % h) for h in range(4)]
            STb = [_alloc(st, nc, "STb%d" % i, [128, 128], BF16) for i in range(2)]
            dSTb = [Dep("STb%d" % i) for i in range(2)]
            otok = [_alloc(st, nc, "otok%d" % i, [128, 2048], BF16) for i in range(2)]
            dotok = [[Dep("otok%d_%d" % (i, h)) for h in range(4)] for i in range(2)]
            oT = _alloc(st, nc, "oT", [128, 16, 128], BF16)
            doT = Dep("oT")
            junk = _alloc(st, nc, "junk", [128, D], BF16)
            djunk = Dep("junk")
            junk2 = junk
            djunk2 = djunk
            sst = [_alloc(st, nc, "ss%d" % i, [128, 4], F32) for i in range(6)]
            dsst = [Dep("ss%d" % i) for i in range(6)]
            tp = _palloc(st, nc, "tp", [128, 8, 128], BF16)
            dtp = PDep("tp")
            pq = _palloc(st, nc, "pq", [128, 1024], F32)
            dpq = PDep("pq")
            pG = _palloc(st, nc, "pG", [128, 512], F32)
            dpG = PDep("pG")
            pS = _palloc(st, nc, "pS", [128, 512], F32)
            dpS = PDep("pS")
            pY = [_palloc(st, nc, "pY%d" % i, [128, 512], F32) for i in range(2)]
            dpY = [PDep("pY%d" % i) for i in range(2)]
            pD = _palloc(st, nc, "pD", [128, 512], F32)
            dpD = PDep("pD")
            sctr = [0]
            em.op("dve", lambda e: e.memset(Sfb[:, :, :, :], 0.0), writes=dSfb)
            em.op("dve", lambda e: e.memset(S32[:, :, :, :], 0.0), reads=[dS32], writes=dS32h)

            def nss():
                si = sctr[0] % 6
                sctr[0] += 1
                return sst[si], dsst[si]

            def load_sb(n, h):
                em.dma("sp", sbl1[:, h, :, :].rearrange("p c f -> p (c f)"), C.s_sb[n][:, h * 1024:(h + 1) * 1024],
                       reads=[C.dscr[n]], writes=[dsblh[h]])

            hb2 = [hb, _alloc(st, nc, "hb_b", [128, D], BF16)]
            dhb2 = [dhb, Dep("hb_b")]

            def front(n):
                xs, dxs = xa[n % NX], dxa[n % NX]
                hbs, dhbs = hb2[n % 2], dhb2[n % 2]
                em.dma("sp", xs[:, :], src[n * 128:(n + 1) * 128, :], reads=[dsrc[n]], writes=[dxs])
                em.dma("sp", cs[n % 2][:, :], C.rcs[n * 128:(n + 1) * 128, :], writes=[dcs[n % 2]])
                ss, dss = nss()
                em.op("act", lambda e: e.activation(out=hbs[:, :], in_=xs[:, :], func=AF.Square, accum_out=ss[:, 0:1]),
                      reads=[dxs], writes=[dhbs, dss])
                emit_rstd(em, ss, dss, C.mhalf, C.dmh, D)
                em.op("dve", lambda e: e.scalar_tensor_tensor(out=hbs[:, :], in0=xs[:, :], scalar=ss[:, 2:3], in1=gpre[:, :],
                                                              op0=ALU.mult, op1=ALU.mult),
                      reads=[dxs, dss, dgpre], writes=[dhbs])

            def P_steps(n):
                r = n % 2
                xs, dxs = xa[n % NX], dxa[n % NX]
                c_, dc_ = cs[r], dcs[r]

                def p0():
                    if n == 0:
                        for h in range(4):
                            load_sb(0, h)
                    hbs = hb2[n % 2]

                    def tps(e):
                        for k in range(8):
                            i = e.transpose(tp[:, k, :], hbs[:, k * 128:(k + 1) * 128], C.ident[:, :])
                        return i
                    em.op("pe", tps, reads=[dhb2[n % 2], C.dident], writes=[dtp])
                    em.op("act", lambda e: e.activation(out=hT[n % NH][:, :, :], in_=tp[:, :, :], func=AF.Copy), reads=[dtp], writes=[dhT[n % NH]])

                def p1():
                    em.dma("sp", kTl[r][:, :, :].rearrange("p k t -> p (k t)"), C.s_kT[n], reads=[C.dscr[n]], writes=[dkTl[r]])
                    em.dma("sp", kfl[r][:, :], C.s_kf[n], reads=[C.dscr[n]], writes=[dkfl[r]])
                    em.dma("sp", vl[r][:, :], C.s_v[n], reads=[C.dscr[n]], writes=[dvl[r]])
                    for g in range(2):
                        def f(e, g=g):
                            for k in range(8):
                                i = e.matmul(pq[:, g * 512:(g + 1) * 512], lhsT=hT[n % NH][:, k, :], rhs=Wqg[:, k, g * 512:(g + 1) * 512],
                                             start=(k == 0), stop=(k == 7))
                            return i
                        em.op("pe", f, reads=[dhT[n % NH], dWqg[0]], writes=[dpq])
                    rotary(pq[:, :], dpq, c_, dc_, rt, drt, qrot, dqrot)

                def p2():
                    def tq(e):
                        for k in range(8):
                            i = e.transpose(tp[:, k, :], qrot[:, k * 128:(k + 1) * 128], C.ident[:, :])
                        return i
                    em.op("pe", tq, reads=[dqrot, C.dident], writes=[dtp])
                    em.op("act", lambda e: e.activation(out=qT[r][0][:, :, :], in_=tp[:, :, :], func=AF.Copy), reads=[dtp], writes=[dqT[r][0]])
                    for i in range(2):
                        em.op("dve", lambda e, i=i: e.tensor_tensor(
                            out=qT[r][1 + i][:, :, :].rearrange("p (h c) t -> p h c t", c=2),
                            in0=qT[r][0][:, :, :].rearrange("p (h c) t -> p h c t", c=2),
                            in1=decrow[:, 4 * i:4 * i + 4, :].unsqueeze(2).broadcast_to([128, 4, 2, 128]), op=ALU.mult),
                            reads=[dqT[r][0], dtab], writes=[dqT[r][1 + i]])
                return [p0, p1, p2]

            def O_steps(n):
                r = n % 2
                steps = []
                for half in range(2):
                    def o_t(half=half):
                        def to(e):
                            for k in range(8):
                                i = e.transpose(tp[:, k, :], otok[r][:, (half * 8 + k) * 128:(half * 8 + k + 1) * 128], C.ident[:, :])
                            return i
                        em.op("pe", to, reads=dotok[r] + [C.dident], writes=[dtp])
                        em.op("act", lambda e: e.activation(out=oT[:, half * 8:(half + 1) * 8, :], in_=tp[:, :, :], func=AF.Copy),
                              reads=[dtp], writes=[doT])
                    steps.append(o_t)

                def o_w():
                    def fw(e):
                        for half in range(2):
                            for k in range(16):
                                i = e.matmul(pq[:, half * 512:(half + 1) * 512], lhsT=oT[:, k, :], rhs=Wro[:, k, half * 512:(half + 1) * 512],
                                             start=(k == 0), stop=(k == 15))
                        return i
                    em.op("pe", fw, reads=[doT, dWro], writes=[dpq])

                def o_e():
                    ss, dss = nss()
                    emit_post_residual(C, em, pq[:, :], dpq, xa[n % NX], dxa[n % NX], gpost, dgpost, ss, dss,
                                       junk, djunk, C.y[n * 128:(n + 1) * 128, :], C.dy[n])
                steps += [o_w, o_e]
                return steps

            def H(n, steps):
                r = n % 2
                last = (n + 1 >= nch)

                def GS(h):
                    def fg(e):
                        for k in range(8):
                            i = e.matmul(pG[:, :], lhsT=hT[n % NH][:, k, :], rhs=Wqg[:, k, 1024 + h * 512:1024 + (h + 1) * 512],
                                         start=(k == 0), stop=(k == 7))
                        return i
                    em.op("pe", fg, reads=[dhT[n % NH], dWqg[1]], writes=[dpG])
                    em.op("act", lambda e: e.activation(out=sg[h % 2][:, :], in_=pG[:, :], func=AF.Silu), reads=[dpG], writes=[dsg[h % 2]])

                    def fs(e):
                        for c in range(2):
                            i = e.matmul(pS[:, 0:128], lhsT=kTl[r][:, 2 * h + c, :], rhs=qT[r][0][:, 2 * h + c, :], start=(c == 0), stop=(c == 1))
                        return i
                    em.op("pe", fs, reads=[dkTl[r], dqT[r][0]], writes=[dpS])
                    em.op("dve", lambda e: e.tensor_tensor(out=STb[h % 2][:, :], in0=pS[:, 0:128], in1=DT[:, h, :], op=ALU.mult),
                          reads=[dpS, dtab], writes=[dSTb[h % 2]])

                def Y(h):
                    yh, dyh = pY[h % 2], dpY[h % 2]

                    def fy(e):
                        e.matmul(yh[:, :], lhsT=STb[h % 2][:, :], rhs=vl[r][:, h * 512:(h + 1) * 512], start=True, stop=False)
                        for c in range(2):
                            e.matmul(yh[:, :], lhsT=qT[r][1][:, 2 * h + c, :], rhs=Sfb[:, h, c, :], start=False, stop=False)
                        for c in range(2):
                            i = e.matmul(yh[:, :], lhsT=qT[r][2][:, 2 * h + c, :], rhs=sbl[r][:, h, c, :], start=False, stop=(c == 1))
                        return i
                    em.op("pe", fy, reads=[dSTb[h % 2], dvl[r], dqT[r][1], dqT[r][2], dSfb[h], dsblh[h]], writes=[dyh])
                    ss, dss = nss()
                    em.op("act", lambda e: e.activation(out=junk2[:, 0:512], in_=yh[:, :], func=AF.Square, accum_out=ss[:, 0:1]),
                          reads=[dyh], writes=[djunk2, dss])
                    emit_rstd(em, ss, dss, C.mhalf, C.dmh, 512)
                    em.op("dve", lambda e: e.scalar_tensor_tensor(out=otok[r][:, h * 512:(h + 1) * 512], in0=yh[:, :], scalar=ss[:, 2:3],
                                                                  in1=sg[h % 2][:, :], op0=ALU.mult, op1=ALU.mult),
                          reads=[dyh, dss, dsg[h % 2]], writes=[dotok[r][h]])

                def UPD(h, cs_):
                    for c in cs_:
                        def f(e, c=c):
                            return e.matmul(pD[:, :], lhsT=kfl[r][:, h * 256 + c * 128: h * 256 + (c + 1) * 128],
                                            rhs=vl[r][:, h * 512:(h + 1) * 512], start=True, stop=True)
                        em.op("pe", f, reads=[dkfl[r], dvl[r]], writes=[dpD])
                        em.op("dve", lambda e, c=c: e.scalar_tensor_tensor(out=S32[:, h, c, :], in0=S32[:, h, c, :],
                                                                         scalar=g128[:, h:h + 1], in1=pD[:, :],
                                                                         op0=ALU.mult, op1=ALU.add),
                              reads=[dS32h[h], dpD, dtab], writes=[dS32h[h]])
                    if 1 not in cs_:
                        return
                    if n + 1 == half_b:
                        em.op("dve", lambda e: e.tensor_scalar(out=S32[:, h, :, :], in0=S32[:, h, :, :], scalar1=bnd[:, 0:1], scalar2=None,
                                                               op0=ALU.mult), reads=[dS32h[h], drc], writes=[dS32h[h]])
                    em.op("act", lambda e: e.activation(out=Sfb[:, h, :, :], in_=S32[:, h, :, :], func=AF.Copy),
                          reads=[dS32h[h]], writes=[dSfb[h]])

                GS(0)
                for h in range(4):
                    if h + 1 < 4:
                        GS(h + 1)
                    if h >= 1 and not last:
                        UPD(h - 1, [1])
                    Y(h)
                    if not last:
                        load_sb(n + 1, h)
                        UPD(h, [0])
                    for _ in range(2):
                        if steps:
                            steps.pop(0)()
                if not last:
                    UPD(3, [1])
                while steps:
                    steps.pop(0)()

            nop = lambda: None
            front(0)
            if nch > 1:
                front(1)
            for s_ in P_steps(0):
                s_()
            if nch > 1:
                P_steps(1)[0]()
            for n in range(nch):
                O = O_steps(n - 1) if n >= 1 else [nop] * 4
                P = P_steps(n + 1) if n + 1 < nch else [nop] * 3
                P0n = P_steps(n + 2)[0] if n + 2 < nch else nop
                fr = (lambda n=n: front(n + 2)) if n + 2 < nch else nop
                steps = [O[0], P[1], O[1], fr, P[2], O[2], P0n, O[3]]
                H(n, steps)
            for s_ in O_steps(nch - 1):
                s_()
            em.barrier()


def build(nsub=6, subs=None, nblk=NCH, dbg=3):
    nc = bass.Bass("TRN2", target_bir_lowering=False)
    em = Emitter(nc)
    C = Ctx()
    C.nc, C.em = nc, em

    def din(name, shape, dt=F32):
        return nc.dram_tensor(name, shape, dt, kind="ExternalInput").ap()
    C.xin = din("xin", [NTOK, D])
    C.ng = din("norm_gains", [2, 6, D])
    C.fwi = din("ffn_w_in", [2, 2, D, 2 * DFF])
    C.fwo = din("ffn_w_out", [2, 2, DFF, D])
    C.wqkv = din("attn_w_qkv", [1, D, 1536])
    C.wo = din("attn_w_o", [1, D, D])
    C.sink = din("attn_sink", [1, 16])
    C.rwi = din("ret_w_in", [1, D, 6144])
    C.rwo = din("ret_w_o", [1, 2048, D])
    C.rdf = din("ret_decay_fwd", [1, 4])
    C.rdb = din("ret_decay_bwd", [1, 4])
    C.identd = din("c_ident", [128, 128], BF16)
    C.acs = din("c_acs", [NTOK, 320])
    C.amask = din("c_amask", [NCH, 128, 384], BF16)
    C.rconst = din("c_rconst", [128, 6, 128])
    C.rpos = din("c_rpos", [128, 4])
    C.rbnd = din("c_rbnd", [128, 1])
    C.rcs = din("c_rcs", [NTOK, 256])
    C.s_kT = nc.dram_tensor("s_kT", [NCH, 128, 1024], BF16, kind="Internal").ap()
    C.s_kf = nc.dram_tensor("s_kf", [NCH, 128, 1024], BF16, kind="Internal").ap()
    C.s_v = nc.dram_tensor("s_v", [NCH, 128, 2048], BF16, kind="Internal").ap()
    C.s_sb = nc.dram_tensor("s_sb", [NCH, 128, 4096], BF16, kind="Internal").ap()
    C.dscr = [Dep("scr%d" % i) for i in range(NCH)]
    C.y = nc.dram_tensor("y", [NTOK, D], F32, kind="ExternalOutput").ap()
    C.dy = [Dep("y%d" % i) for i in range(NCH)]
    C.dxin = [Dep("xin%d" % i) for i in range(NCH)]

    C.ident = nc.alloc_sbuf_tensor("ident", [128, 128], BF16)
    C.dident = Dep("ident")
    C.mhalf = nc.alloc_sbuf_tensor("mhalf", [128, 1], F32)
    C.dmh = Dep("mhalf")
    em.dma("sp", C.ident[:, :], C.identd[:, :], writes=[C.dident])
    em.op("pool", lambda e: e.memset(C.mhalf[:, :], -0.5), writes=[C.dmh])

    C.nblk = nblk
    C.dbg = dbg
    if subs is None:
        subs = [("ffn", 0, 0), ("attn", 0, 0), ("ffn", 0, 1), ("ffn", 1, 0), ("ret", 1, 0), ("ffn", 1, 1)]
    src, dsrc = C.xin, C.dxin
    for i, (kind, l, which) in enumerate(subs[:nsub]):
        if kind == "ffn":
            emit_ffn(C, l, which, src, dsrc)
        elif kind == "attn":
            emit_attn(C, l, src, dsrc)
        elif kind == "ret":
            emit_ret(C, l, src, dsrc)
        src, dsrc = C.y, C.dy
    em.finish()
    return nc


def make_consts():
    c = {}
    c["c_ident"] = np.eye(128, dtype=np.float32).astype(ml_dtypes.bfloat16)
    j = np.arange(128, dtype=np.float32)[:, None]
    i = np.arange(128, dtype=np.float32)[None, :]
    rc = np.zeros((128, 6, 128), np.float32)
    rc[:, 0] = np.maximum(i - j, 0.0)
    rc[:, 1] = np.maximum(j - i, 0.0)
    rc[:, 2] = (i >= j) / 16.0
    rc[:, 3] = (j > i) / 16.0
    rc[:, 4] = np.broadcast_to(i + 1.0, (128, 128))
    rc[:, 5] = np.broadcast_to(128.0 - i, (128, 128))
    c["c_rconst"] = rc
    p = np.arange(128, dtype=np.float32)
    c["c_rpos"] = np.stack([p + 1, 128 - p, 127 - p, p], axis=1).astype(np.float32)
    return c


def make_core_consts(seqlen):
    c = {}
    tok = np.arange(NTOK)
    pos = (tok % seqlen).astype(np.float32)
    inv = (500000.0 ** (-np.arange(8, dtype=np.float32) / 8)).astype(np.float32)
    ang = pos[:, None] * inv[None, :]
    cos = np.cos(ang).astype(np.float32)
    sin = np.sin(ang).astype(np.float32)
    acs = np.concatenate([np.tile(cos[:, None, :], (1, 20, 1)).reshape(NTOK, 160),
                          np.tile(sin[:, None, :], (1, 20, 1)).reshape(NTOK, 160)], axis=1)
    c["c_acs"] = np.ascontiguousarray(acs, dtype=np.float32)
    b = np.arange(NCH)[:, None, None]
    qi = np.arange(128)[None, :, None]
    kc = np.arange(384)[None, None, :]
    tq = 128 * b + qi
    tk = 128 * (b - 1) + kc
    valid = (tk >= 0) & (tk < NTOK) & ((tk // seqlen) == (tq // seqlen)) & (np.abs(tk - tq) <= 128)
    c["c_amask"] = np.where(valid, 0.0, NEG).astype(np.float32).astype(ml_dtypes.bfloat16)
    invr = (10000.0 ** (-np.arange(128, dtype=np.float32) / 128)).astype(np.float32)
    angr = pos[:, None] * invr[None, :]
    c["c_rcs"] = np.ascontiguousarray(np.concatenate([np.cos(angr), np.sin(angr)], axis=1), dtype=np.float32)
    c["c_rbnd"] = np.full((128, 1), 1.0 if seqlen == NTOK else 0.0, np.float32)
    return c


def kernel(x_prompt, x_sample, norm_gains, ffn_w_in, ffn_w_out, attn_w_qkv, attn_w_o, attn_sink,
           ret_w_in, ret_w_o, ret_decay_fwd, ret_decay_bwd, _nsub=6, _subs=None, _nblk=NCH, _cores=None, _trace=False, _dbg=3):
    f = lambda a: np.ascontiguousarray(np.asarray(a, dtype=np.float32))
    xp = f(x_prompt).reshape(4, NTOK, D)
    xs = f(x_sample).reshape(4, NTOK, D)
    shared = {
        "norm_gains": f(norm_gains), "ffn_w_in": f(ffn_w_in), "ffn_w_out": f(ffn_w_out),
        "attn_w_qkv": f(attn_w_qkv), "attn_w_o": f(attn_w_o), "attn_sink": f(attn_sink),
        "ret_w_in": f(ret_w_in), "ret_w_o": f(ret_w_o), "ret_decay_fwd": f(ret_decay_fwd),
        "ret_decay_bwd": f(ret_decay_bwd),
    }
    shared.update(make_consts())
    in_maps = []
    cc = [make_core_consts(2048), make_core_consts(4096)]
    for c in range(8):
        m = dict(shared)
        m.update(cc[0] if c < 4 else cc[1])
        m["xin"] = xp[c] if c < 4 else xs[c - 4]
        in_maps.append(m)
    nc = build(_nsub, _subs, _nblk, _dbg)
    if _cores is not None:
        res = run_bass_kernel_spmd(nc, [in_maps[c] for c in _cores], core_ids=list(range(len(_cores))), trace=_trace)
        if _trace:
            print("exec_time_ns", res.exec_time_ns)
        return [np.asarray(r["y"], dtype=np.float32) for r in res.results]
    res = run_bass_kernel_spmd(nc, in_maps, core_ids=list(range(8)))
    outs = [np.asarray(r["y"], dtype=np.float32) for r in res.results]
    y_prompt = np.stack(outs[:4]).reshape(8, 2048, D)
    y_sample = np.stack(outs[4:]).reshape(4, 4096, D)
    return (y_prompt, y_sample)
```

```python
import os
import numpy as np
import ml_dtypes
from contextlib import ExitStack
import concourse.bass as bass
import concourse.mybir as mybir
from concourse.bass_utils import run_bass_kernel_spmd

F32 = mybir.dt.float32
BF16 = mybir.dt.bfloat16
AF = mybir.ActivationFunctionType
ALU = mybir.AluOpType
AX = mybir.AxisListType

NTOK = 4096
NCH = 32
D = 1024
DFF = 2816
EPS = 1e-6
EPOCH = 1 << 30
NEG = -30000.0


class Dep:
    __slots__ = ("name", "w", "rs", "dsem", "dcnt", "ex", "retired")

    def __init__(self, name="", ex=False):
        self.name = name
        self.ex = ex
        self.w = None
        self.rs = {}
        self.dsem = None
        self.dcnt = 0
        self.retired = False


class Emitter:
    def __init__(self, nc):
        self.nc = nc
        self.eng = {"pe": nc.tensor, "act": nc.scalar, "dve": nc.vector,
                    "pool": nc.gpsimd, "sp": nc.sync}
        self.sem = {}
        self.cnt = {}
        self.nsem = 0
        for e in self.eng:
            self.sem[e] = self._newsem("e_" + e)
            self.cnt[e] = 0
        self.waited = {}
        self.dma_owners = []
        self.free_dsems = []
        self.no_recycle = set()

    def _newsem(self, name):
        self.nsem += 1
        return self.nc.alloc_semaphore("%s_%d" % (name, self.nsem))

    def _tick(self, e):
        if self.cnt[e] >= EPOCH:
            self.sem[e] = self._newsem("e_" + e)
            self.cnt[e] = 0
        self.cnt[e] += 1
        return self.sem[e], self.cnt[e]

    def _need(self, e, rec, needs):
        if rec is None:
            return
        if rec[0] == "e":
            _, pe, sem, val = rec
            if pe == e and e == "pe":
                return
            needs.append((sem, val))
        else:
            o = rec[1]
            needs.append((o.dsem, o.dcnt))

    def _collect(self, e, reads, writes):
        needs = []
        for d in reads:
            self._need(e, d.w, needs)
        for d in writes:
            if d.w is not None:
                self._need(e, d.w, needs)
            for k, r in d.rs.items():
                self._need(e, r, needs)
        return needs

    def _emit_waits(self, e, needs):
        eng = self.eng[e]
        best = {}
        for sem, val in needs:
            k = id(sem)
            if k not in best or best[k][1] < val:
                best[k] = (sem, val)
        for k, (sem, val) in best.items():
            wk = (e, k)
            if self.waited.get(wk, 0) >= val:
                continue
            self.waited[wk] = val
            eng.wait_ge(sem, val)

    def op(self, e, fn, reads=(), writes=()):
        xr = [d for d in reads if d.ex]
        needs = []
        if xr:
            for d in xr:
                self._need(e, d.w, needs)
            reads = [d for d in reads if not d.ex]
            writes = list(writes) + [d for d in xr if d not in writes]
        needs += self._collect(e, reads, writes)
        self._emit_waits(e, needs)
        inst = fn(self.eng[e])
        sem, val = self._tick(e)
        inst.then_inc(sem, 1)
        rec = ("e", e, sem, val)
        for d in reads:
            d.rs[e] = rec
        for d in writes:
            d.w = rec
            d.rs = {}
        return inst

    def dma(self, q, out, in_, reads=(), writes=(), owner=None):
        needs = self._collect(q, reads, writes)
        if owner is None:
            owner = writes[0] if writes else reads[0]
        if owner.dsem is None or owner.retired:
            if self.free_dsems and q != "pool":
                owner.dsem, owner.dcnt = self.free_dsems.pop()
            else:
                owner.dsem, owner.dcnt = self._newsem("d_" + owner.name), 0
                if q == "pool":
                    self.no_recycle.add(id(owner.dsem))
            owner.retired = False
            self.dma_owners.append(owner)
        self._emit_waits(q, needs)
        inst = self.eng[q].dma_start(out=out, in_=in_)
        owner.dcnt += 16
        inst.then_inc(owner.dsem, 16)
        rec = ("d", owner)
        for d in reads:
            d.rs["dma%d" % id(owner)] = rec
        for d in writes:
            d.w = rec
            d.rs = {}
        return inst

    def barrier(self):
        pts = [(self.sem[e], self.cnt[e]) for e in self.eng if self.cnt[e] > 0]
        pts += [(o.dsem, o.dcnt) for o in self.dma_owners]
        for e in self.eng:
            self._emit_waits(e, pts)
        for o in self.dma_owners:
            o.retired = True
            if id(o.dsem) not in self.no_recycle:
                self.free_dsems.append((o.dsem, o.dcnt))
        self.dma_owners = []

    def finish(self):
        sp = self.eng["sp"]
        pts = [(o.dsem, o.dcnt) for o in self.dma_owners]
        self._emit_waits("sp", pts)


def PDep(name):
    return Dep(name, ex=True)


class Ctx:
    pass


_uid = [0]


def _alloc(st, nc, name, shape, dt):
    _uid[0] += 1
    return st.enter_context(nc.sbuf_tensor("%s_u%d" % (name, _uid[0]), shape, dt))


def _palloc(st, nc, name, shape, dt):
    _uid[0] += 1
    return st.enter_context(nc.psum_tensor("%s_u%d" % (name, _uid[0]), shape, dt))


def emit_rstd(em, ss, dss, mhalf, dmh, n_feat):
    em.op("pool", lambda e: e.tensor_scalar(out=ss[:, 1:2], in0=ss[:, 0:1], scalar1=1.0 / n_feat,
                                            scalar2=EPS, op0=ALU.mult, op1=ALU.add),
          reads=[dss], writes=[dss])
    em.op("pool", lambda e: e.tensor_tensor(out=ss[:, 2:3], in0=ss[:, 1:2], in1=mhalf[:, 0:1], op=ALU.pow),
          reads=[dss, dmh], writes=[dss])


def emit_prenorm_T(C, em, xs, dxs, gpre, dgpre, hbs, dhbs, ss, dss, tp, dtp, hT_dst, dhT):
    em.op("act", lambda e: e.activation(out=hbs[:, :], in_=xs[:, :], func=AF.Square, accum_out=ss[:, 0:1]),
          reads=[dxs], writes=[dhbs, dss])
    emit_rstd(em, ss, dss, C.mhalf, C.dmh, D)
    em.op("dve", lambda e: e.scalar_tensor_tensor(out=hbs[:, :], in0=xs[:, :], scalar=ss[:, 2:3], in1=gpre[:, :],
                                                  op0=ALU.mult, op1=ALU.mult),
          reads=[dxs, dss, dgpre], writes=[dhbs])

    def tps(e):
        for k in range(8):
            i = e.transpose(tp[:, k, :], hbs[:, k * 128:(k + 1) * 128], C.ident[:, :])
        return i
    em.op("pe", tps, reads=[dhbs, C.dident], writes=[dtp])
    em.op("act", lambda e: e.activation(out=hT_dst, in_=tp[:, :, :], func=AF.Copy), reads=[dtp], writes=[dhT])


def load_gains(C, em, st, nc, l, ipre, ipost, post_scale):
    gpre = _alloc(st, nc, "gpre", [128, D], F32)
    gpost = _alloc(st, nc, "gpost", [128, D], F32)
    dgpre = Dep("gpre")
    dgpost = Dep("gpost")
    em.dma("sp", gpre[:, :], C.ng[l, ipre, :].partition_broadcast(128), writes=[dgpre])
    em.dma("sp", gpost[:, :], C.ng[l, ipost, :].partition_broadcast(128), writes=[dgpost])
    if post_scale != 1.0:
        em.op("pool", lambda e: e.tensor_scalar(out=gpost[:, :], in0=gpost[:, :], scalar1=post_scale, scalar2=0.0,
                                                op0=ALU.mult, op1=ALU.add), reads=[dgpost], writes=[dgpost])
    return gpre, dgpre, gpost, dgpost


def emit_post_residual(C, em, ps_out, dps, xs, dxs, gpost, dgpost, ss, dss, junk, djunk, dst_ap, ddst):
    em.op("act", lambda e: e.activation(out=junk[:, :], in_=ps_out, func=AF.Square, accum_out=ss[:, 0:1]),
          reads=[dps], writes=[djunk, dss])
    emit_rstd(em, ss, dss, C.mhalf, C.dmh, D)
    em.op("dve", lambda e: e.scalar_tensor_tensor(out=ps_out, in0=ps_out, scalar=ss[:, 2:3], in1=gpost[:, :],
                                                  op0=ALU.mult, op1=ALU.mult),
          reads=[dps, dss, dgpost], writes=[dps])
    em.op("dve", lambda e: e.tensor_tensor(out=xs[:, :], in0=ps_out, in1=xs[:, :], op=ALU.add),
          reads=[dps, dxs], writes=[dxs])
    em.dma("sp", dst_ap, xs[:, :], reads=[dxs], writes=[ddst], owner=dxs)


def emit_ffn(C, l, which, src, dsrc):
    nc, em = C.nc, C.em
    NJ = DFF // 128
    LAG = 3
    with ExitStack() as st:
        Win = _alloc(st, nc, "Win", [128, 8, 2 * DFF], BF16)
        Wout = _alloc(st, nc, "Wout", [128, NJ, D], BF16)
        dWin = [Dep("Win%d" % k) for k in range(4)]
        dWout = [Dep("Wout%d" % k) for k in range(2)]
        w_in = C.fwi[l, which]
        w_out = C.fwo[l, which]
        wi_v = w_in.rearrange("(k p) f -> p k f", p=128)
        for k in range(4):
            em.dma("pool", Win[:, 2 * k:2 * k + 2, :], wi_v[:, 2 * k:2 * k + 2, :], writes=[dWin[k]])
        wo_v = w_out.rearrange("(j p) d -> p j d", p=128)
        em.dma("pool", Wout[:, 0:11, :], wo_v[:, 0:11, :], writes=[dWout[0]])
        em.dma("pool", Wout[:, 11:22, :], wo_v[:, 11:22, :], writes=[dWout[1]])
        gpre, dgpre, gpost, dgpost = load_gains(C, em, st, nc, l, 0 if which == 0 else 4, 1 if which == 0 else 5, 0.5)

        xa = [_alloc(st, nc, "xa%d" % i, [128, D], F32) for i in range(3)]
        dxa = [Dep("xa%d" % i) for i in range(3)]
        xb = [_alloc(st, nc, "xb%d" % i, [128, D], F32) for i in range(3)]
        dxb = [Dep("xb%d" % i) for i in range(3)]
        hb = [_alloc(st, nc, "hb%d" % i, [128, D], BF16) for i in range(2)]
        dhb = [Dep("hb%d" % i) for i in range(2)]
        hT = [_alloc(st, nc, "hT%d" % i, [128, 8, 256], BF16) for i in range(2)]
        dhT = [[Dep("hT%d_%d" % (i, c)) for c in range(2)] for i in range(2)]
        NA = 6
        actT = [_alloc(st, nc, "actT%d" % i, [128, 256], BF16) for i in range(NA)]
        dact = [Dep("actT%d" % i) for i in range(NA)]
        sg = [_alloc(st, nc, "sg%d" % i, [128, 256], BF16) for i in range(2)]
        dsg = [Dep("sg%d" % i) for i in range(2)]
        junk = _alloc(st, nc, "junk", [128, D], BF16)
        djunk = Dep("junk")
        sst = [_alloc(st, nc, "ss%d" % i, [128, 4], F32) for i in range(4)]
        dsst = [Dep("ss%d" % i) for i in range(4)]
        tp = [_palloc(st, nc, "tp%d" % i, [128, 8, 128], BF16) for i in range(2)]
        dtp = [PDep("tp%d" % i) for i in range(2)]
        gu = [_palloc(st, nc, "gu%d" % i, [128, 2, 256], F32) for i in range(2)]
        dgu = [PDep("gu%d" % i) for i in range(2)]
        pout = _palloc(st, nc, "pout", [128, 2, D], F32)
        dpout = [PDep("pout%d" % i) for i in range(2)]

        NT = NTOK // 256
        sctr = [0]

        def loads(t):
            for c in range(2):
                ch = 2 * t + c
                em.dma("sp", xa[ch % 3][:, :], src[ch * 128:(ch + 1) * 128, :], reads=[dsrc[ch]], writes=[dxa[ch % 3]])

        def front(t):
            for c in range(2):
                ch = 2 * t + c
                xs, dxs = xa[ch % 3], dxa[ch % 3]
                hbs, dhbs = hb[ch % 2], dhb[ch % 2]
                si = sctr[0] % 4
                sctr[0] += 1
                ss, dss = sst[si], dsst[si]
                em.op("act", lambda e, hbs=hbs, xs=xs, ss=ss: e.activation(out=hbs[:, :], in_=xs[:, :], func=AF.Square, accum_out=ss[:, 0:1]),
                      reads=[dxs], writes=[dhbs, dss])
                emit_rstd(em, ss, dss, C.mhalf, C.dmh, D)
                em.op("dve", lambda e, hbs=hbs, xs=xs, ss=ss: e.scalar_tensor_tensor(out=hbs[:, :], in0=xs[:, :], scalar=ss[:, 2:3], in1=gpre[:, :],
                                                                                  op0=ALU.mult, op1=ALU.mult),
                      reads=[dxs, dss, dgpre], writes=[dhbs])

        def transp(t, c):
            ch = 2 * t + c
            hbs = hb[ch % 2]

            def tps(e):
                for k in range(8):
                    i = e.transpose(tp[ch % 2][:, k, :], hbs[:, k * 128:(k + 1) * 128], C.ident[:, :])
                return i
            em.op("pe", tps, reads=[dhb[ch % 2], C.dident], writes=[dtp[ch % 2]])
            em.op("act", lambda e: e.activation(out=hT[t % 2][:, :, c * 128:(c + 1) * 128], in_=tp[ch % 2][:, :, :], func=AF.Copy),
                  reads=[dtp[ch % 2]], writes=[dhT[t % 2][c]])

        def xb_loads(t):
            for tc in range(2):
                ch = 2 * t + tc
                em.dma("sp", xb[ch % 3][:, :], src[ch * 128:(ch + 1) * 128, :], reads=[dsrc[ch]], writes=[dxb[ch % 3]])

        def p1(t, j):
            g = gu[j % 2]
            hTt = hT[t % 2]

            def f(e):
                for half in range(2):
                    for k in range(8):
                        i = e.matmul(g[:, half, :], lhsT=Win[:, k, half * DFF + j * 128: half * DFF + (j + 1) * 128],
                                     rhs=hTt[:, k, :], start=(k == 0), stop=(k == 7))
                return i
            em.op("pe", f, reads=dWin + dhT[t % 2], writes=[dgu[j % 2]])
            s = sg[j % 2]
            em.op("act", lambda e: e.activation(out=s[:, :], in_=g[:, 0, :], func=AF.Silu),
                  reads=[dgu[j % 2]], writes=[dsg[j % 2]])
            a = actT[j % NA]
            em.op("dve", lambda e: e.tensor_tensor(out=a[:, :], in0=g[:, 1, :], in1=s[:, :], op=ALU.mult),
                  reads=[dgu[j % 2], dsg[j % 2]], writes=[dact[j % NA]])

        def p2(t, j):
            a = actT[j % NA]

            def f(e):
                for tc in range(2):
                    for half in range(2):
                        i = e.matmul(pout[:, tc, half * 512:(half + 1) * 512], lhsT=a[:, tc * 128:(tc + 1) * 128],
                                     rhs=Wout[:, j, half * 512:(half + 1) * 512], start=(j == 0), stop=(j == NJ - 1))
                return i
            em.op("pe", f, reads=[dact[j % NA], dWout[0 if j < 11 else 1]], writes=dpout)

        def epilogue(t):
            for tc in range(2):
                ch = 2 * t + tc
                xs, dxs = xb[ch % 3], dxb[ch % 3]
                si = sctr[0] % 4
                sctr[0] += 1
                emit_post_residual(C, em, pout[:, tc, :], dpout[tc], xs, dxs, gpost, dgpost, sst[si], dsst[si],
                                   junk, djunk, C.y[ch * 128:(ch + 1) * 128, :], C.dy[ch])

        loads(0)
        front(0)
        transp(0, 0)
        transp(0, 1)
        if NT > 1:
            loads(1)
        for t in range(NT):
            for j in range(NJ + LAG):
                if j < NJ:
                    p1(t, j)
                if j >= LAG:
                    p2(t, j - LAG)
                if t + 1 < NT:
                    if j == 5:
                        front(t + 1)
                    elif j == 11:
                        transp(t + 1, 0)
                    elif j == 15:
                        transp(t + 1, 1)
                    elif j == 18 and t + 2 < NT:
                        loads(t + 2)
                if j == 13:
                    xb_loads(t)
            epilogue(t)
        em.barrier()


def emit_attn(C, l, src, dsrc):
    nc, em = C.nc, C.em
    SCALE = 0.125
    with ExitStack() as st:
        Wqkv = _alloc(st, nc, "Wqkv", [128, 8, 1536], BF16)
        Wo = _alloc(st, nc, "Wo", [128, 8, D], BF16)
        dWqkv, dWo = Dep("Wqkv"), Dep("Wo")
        em.dma("pool", Wqkv[:, :, :], C.wqkv[0].rearrange("(k p) f -> p k f", p=128), writes=[dWqkv])
        em.dma("pool", Wo[:, :, :], C.wo[0].rearrange("(k p) f -> p k f", p=128), writes=[dWo])
        gpre, dgpre, gpost, dgpost = load_gains(C, em, st, nc, l, 2, 3, 1.0)
        sinkt = _alloc(st, nc, "sinkt", [128, 16], F32)
        nsink = _alloc(st, nc, "nsink", [128, 16], F32)
        dsink = Dep("sink")
        em.dma("sp", sinkt[:, :], C.sink[0, :].partition_broadcast(128), writes=[dsink])
        em.op("pool", lambda e: e.tensor_scalar(out=nsink[:, :], in0=sinkt[:, :], scalar1=-1.0, scalar2=0.0,
                                                op0=ALU.mult, op1=ALU.add), reads=[dsink], writes=[dsink])

        kT = _alloc(st, nc, "kT_all", [128, 8, 34 * 128], BF16)
        vA = _alloc(st, nc, "v_all", [128, 34, 256], BF16)
        dkT = [Dep("kT%d" % i) for i in range(34)]
        dvA = [Dep("vA%d" % i) for i in range(34)]
        for i in (0, 33):
            em.op("dve", lambda e, i=i: e.memset(kT[:, :, i * 128:(i + 1) * 128], 0.0), writes=[dkT[i]])
            em.op("dve", lambda e, i=i: e.memset(vA[:, i, :], 0.0), writes=[dvA[i]])

        NX = 5
        xa = [_alloc(st, nc, "xa%d" % i, [128, D], F32) for i in range(NX)]
        dxa = [Dep("xa%d" % i) for i in range(NX)]
        hb = [_alloc(st, nc, "hb%d" % i, [128, D], BF16) for i in range(2)]
        dhb = [Dep("hb%d" % i) for i in range(2)]
        hT = [_alloc(st, nc, "hT%d" % i, [128, 8, 128], BF16) for i in range(2)]
        dhT = [Dep("hT%d" % i) for i in range(2)]
        cs = [_alloc(st, nc, "cs%d" % i, [128, 320], F32) for i in range(2)]
        dcs = [Dep("cs%d" % i) for i in range(2)]
        mk = [_alloc(st, nc, "mk%d" % i, [128, 384], BF16) for i in range(2)]
        dmk = [Dep("mk%d" % i) for i in range(2)]
        qtok = [_alloc(st, nc, "qtok%d" % i, [128, 16, 64], BF16) for i in range(2)]
        dqtok = [Dep("qtok%d" % i) for i in range(2)]
        kdtok = [_alloc(st, nc, "kdtok%d" % i, [128, 4, 2, 128], BF16) for i in range(2)]
        dkdtok = [Dep("kdtok%d" % i) for i in range(2)]
        for i in range(2):
            em.op("dve", lambda e, i=i: e.memset(kdtok[i][:, :, :, :], 0.0), writes=[dkdtok[i]])
        rt = [_alloc(st, nc, "rt%d" % i, [128, 20, 8], F32) for i in range(4)]
        drt = [Dep("rt%d" % i) for i in range(4)]
        NQ = 3
        qT = [_alloc(st, nc, "qT%d" % i, [128, 8, 128], BF16) for i in range(NQ)]
        dqT = [Dep("qT%d" % i) for i in range(NQ)]
        pb = [_alloc(st, nc, "pb%d" % i, [128, 2, 384], BF16) for i in range(2)]
        dpb = [Dep("pb%d" % i) for i in range(2)]
        pT = [_alloc(st, nc, "pT%d" % i, [128, 6, 128], BF16) for i in range(2)]
        dpT = [Dep("pT%d" % i) for i in range(2)]
        stt = [_alloc(st, nc, "stt%d" % i, [128, 6, 16], F32) for i in range(2)]
        dsth = [[Dep("st%d_%d" % (i, h)) for h in range(16)] for i in range(2)]
        dfin = [[Dep("fin%d_%d" % (i, k)) for k in range(2)] for i in range(2)]
        otok = [_alloc(st, nc, "otok%d" % i, [128, 16, 64], BF16) for i in range(2)]
        dotok = [[Dep("otok%d_%d" % (i, k)) for k in range(2)] for i in range(2)]
        oT = _alloc(st, nc, "oT", [128, 8, 128], BF16)
        doT = Dep("oT")
        junk = _alloc(st, nc, "junk", [128, D], BF16)
        djunk = Dep("junk")
        sst = [_alloc(st, nc, "ss%d" % i, [128, 4], F32) for i in range(4)]
        dsst = [Dep("ss%d" % i) for i in range(4)]

        qkv = _palloc(st, nc, "qkv", [128, 1024], F32)
        dq01 = PDep("qkv01")
        tp = _palloc(st, nc, "tp", [128, 8, 128], BF16)
        dtp = PDep("tp")
        sps = _palloc(st, nc, "sps", [128, 4, 512], F32)
        dsps = [PDep("sps%d" % i) for i in range(2)]
        ops = _palloc(st, nc, "ops", [128, 8, 64], F32)
        dops = PDep("ops")
        sctr = [0]

        def A_steps(b):
            xs, dxs = xa[b % NX], dxa[b % NX]
            c_, dc_ = cs[b % 2], dcs[b % 2]
            h_ = hT[b % 2]
            qt, dqt = qtok[b % 2], dqtok[b % 2]
            kd, dkd = kdtok[b % 2], dkdtok[b % 2]

            def a_front():
                si = sctr[0] % 4
                sctr[0] += 1
                ss, dss = sst[si], dsst[si]
                hbs, dhbs = hb[b % 2], dhb[b % 2]
                em.op("act", lambda e: e.activation(out=hbs[:, :], in_=xs[:, :], func=AF.Square, accum_out=ss[:, 0:1]),
                      reads=[dxs], writes=[dhbs, dss])
                emit_rstd(em, ss, dss, C.mhalf, C.dmh, D)
                em.op("dve", lambda e: e.scalar_tensor_tensor(out=hbs[:, :], in0=xs[:, :], scalar=ss[:, 2:3], in1=gpre[:, :],
                                                              op0=ALU.mult, op1=ALU.mult),
                      reads=[dxs, dss, dgpre], writes=[dhbs])

            def a0():
                hbs = hb[b % 2]

                def tps(e):
                    for k in range(8):
                        i = e.transpose(tp[:, k, :], hbs[:, k * 128:(k + 1) * 128], C.ident[:, :])
                    return i
                em.op("pe", tps, reads=[dhb[b % 2], C.dident], writes=[dtp])
                em.op("act", lambda e: e.activation(out=h_[:, :, :], in_=tp[:, :, :], func=AF.Copy), reads=[dtp], writes=[dhT[b % 2]])

            def a1q():
                for g in range(2):
                    def f(e, g=g):
                        for k in range(8):
                            i = e.matmul(qkv[:, g * 512:(g + 1) * 512], lhsT=h_[:, k, :], rhs=Wqkv[:, k, g * 512:(g + 1) * 512],
                                         start=(k == 0), stop=(k == 7))
                        return i
                    em.op("pe", f, reads=[dhT[b % 2], dWqkv], writes=[dq01])
                qv = qkv[:, 0:1024].rearrange("p (h d) -> p h d", d=64)
                em.op("act", lambda e: e.activation(out=qt[:, :, 16:64], in_=qv[:, :, 16:64], func=AF.Copy, scale=SCALE),
                      reads=[dq01], writes=[dqt])

            def a2q():
                qv = qkv[:, 0:1024].rearrange("p (h d) -> p h d", d=64)
                cosv = c_[:, 0:160].rearrange("p (h d) -> p h d", d=8)[:, 0:16, :]
                sinv = c_[:, 160:320].rearrange("p (h d) -> p h d", d=8)[:, 0:16, :]
                x1, x2 = qv[:, :, 0:8], qv[:, :, 8:16]
                for i, (xx, tb) in enumerate([(x1, cosv), (x2, sinv), (x2, cosv), (x1, sinv)]):
                    em.op("dve", lambda e, i=i, xx=xx, tb=tb: e.tensor_tensor(out=rt[i][:, 0:16, :], in0=xx, in1=tb, op=ALU.mult),
                          reads=[dq01, dc_], writes=[drt[i]])
                em.op("dve", lambda e: e.tensor_tensor(out=qt[:, :, 0:8], in0=rt[0][:, 0:16, :], in1=rt[1][:, 0:16, :], op=ALU.subtract),
                      reads=[drt[0], drt[1]], writes=[dqt])
                em.op("dve", lambda e: e.tensor_tensor(out=qt[:, :, 8:16], in0=rt[2][:, 0:16, :], in1=rt[3][:, 0:16, :], op=ALU.add),
                      reads=[drt[2], drt[3]], writes=[dqt])

            def a1kv():
                def f(e):
                    for k in range(8):
                        i = e.matmul(qkv[:, 0:512], lhsT=h_[:, k, :], rhs=Wqkv[:, k, 1024:1536], start=(k == 0), stop=(k == 7))
                    return i
                em.op("pe", f, reads=[dhT[b % 2], dWqkv], writes=[dq01])
                kv = qkv[:, 0:256].rearrange("p (h d) -> p h d", d=64)
                for dup in range(2):
                    em.op("act", lambda e, dup=dup: e.activation(out=kd[:, :, dup, dup * 64 + 16:dup * 64 + 64], in_=kv[:, :, 16:64],
                                                                 func=AF.Copy), reads=[dq01], writes=[dkd])
                em.op("act", lambda e: e.activation(out=vA[:, b + 1, :], in_=qkv[:, 256:512], func=AF.Copy),
                      reads=[dq01], writes=[dvA[b + 1]])

            def a2kv():
                kv = qkv[:, 0:256].rearrange("p (h d) -> p h d", d=64)
                cosv = c_[:, 0:160].rearrange("p (h d) -> p h d", d=8)[:, 16:20, :]
                sinv = c_[:, 160:320].rearrange("p (h d) -> p h d", d=8)[:, 16:20, :]
                x1, x2 = kv[:, :, 0:8], kv[:, :, 8:16]
                for i, (xx, tb) in enumerate([(x1, cosv), (x2, sinv), (x2, cosv), (x1, sinv)]):
                    em.op("dve", lambda e, i=i, xx=xx, tb=tb: e.tensor_tensor(out=rt[i][:, 16:20, :], in0=xx, in1=tb, op=ALU.mult),
                          reads=[dq01, dc_], writes=[drt[i]])
                for dup in range(2):
                    em.op("dve", lambda e, dup=dup: e.tensor_tensor(out=kd[:, :, dup, dup * 64:dup * 64 + 8], in0=rt[0][:, 16:20, :],
                                                                    in1=rt[1][:, 16:20, :], op=ALU.subtract),
                          reads=[drt[0], drt[1]], writes=[dkd])
                    em.op("dve", lambda e, dup=dup: e.tensor_tensor(out=kd[:, :, dup, dup * 64 + 8:dup * 64 + 16], in0=rt[2][:, 16:20, :],
                                                                    in1=rt[3][:, 16:20, :], op=ALU.add),
                          reads=[drt[2], drt[3]], writes=[dkd])

            def a3():
                qflat = qt[:, :, :].rearrange("p h d -> p (h d)")

                def tq(e):
                    for k in range(8):
                        i = e.transpose(tp[:, k, :], qflat[:, k * 128:(k + 1) * 128], C.ident[:, :])
                    return i
                em.op("pe", tq, reads=[dqt, C.dident], writes=[dtp])
                em.op("act", lambda e: e.activation(out=qT[b % NQ][:, :, :], in_=tp[:, :, :], func=AF.Copy),
                      reads=[dtp], writes=[dqT[b % NQ]])

            def a4():
                kflat = kd[:, :, :, :].rearrange("p g u d -> p (g u d)")

                def tk(e):
                    for k in range(8):
                        i = e.transpose(tp[:, k, :], kflat[:, k * 128:(k + 1) * 128], C.ident[:, :])
                    return i
                em.op("pe", tk, reads=[dkd, C.dident], writes=[dtp])
                em.op("act", lambda e: e.activation(out=kT[:, :, (b + 1) * 128:(b + 2) * 128], in_=tp[:, :, :], func=AF.Copy),
                      reads=[dtp], writes=[dkT[b + 1]])
            return [a_front, a0, a1q, a2q, a1kv, a2kv, a3, a4]

        def A_loads(b):
            em.dma("sp", xa[b % NX][:, :], src[b * 128:(b + 1) * 128, :], reads=[dsrc[b]], writes=[dxa[b % NX]])
            em.dma("sp", cs[b % 2][:, :], C.acs[b * 128:(b + 1) * 128, :], writes=[dcs[b % 2]])

        def C_steps(b):
            ot = otok[b % 2]

            def c0():
                oflat = ot[:, :, :].rearrange("p h d -> p (h d)")

                def to(e):
                    for k in range(8):
                        i = e.transpose(tp[:, k, :], oflat[:, k * 128:(k + 1) * 128], C.ident[:, :])
                    return i
                em.op("pe", to, reads=dotok[b % 2] + [C.dident], writes=[dtp])
                em.op("act", lambda e: e.activation(out=oT[:, :, :], in_=tp[:, :, :], func=AF.Copy), reads=[dtp], writes=[doT])

            def c1():
                def fw(e):
                    for half in range(2):
                        for k in range(8):
                            i = e.matmul(qkv[:, half * 512:(half + 1) * 512], lhsT=oT[:, k, :], rhs=Wo[:, k, half * 512:(half + 1) * 512],
                                         start=(k == 0), stop=(k == 7))
                    return i
                em.op("pe", fw, reads=[doT, dWo], writes=[dq01])

            def c2():
                si = sctr[0] % 4
                sctr[0] += 1
                emit_post_residual(C, em, qkv[:, 0:1024], dq01, xa[b % NX], dxa[b % NX], gpost, dgpost, sst[si], dsst[si],
                                   junk, djunk, C.y[b * 128:(b + 1) * 128, :], C.dy[b])
            return [c0, c1, c2]

        def finish_heads(b, h0):
            s_ = stt[b % 2]
            k = h0 // 8
            dsts = dsth[b % 2][h0:h0 + 8]
            df = dfin[b % 2][k]
            hs = slice(h0, h0 + 8)
            em.op("dve", lambda e: e.tensor_tensor(out=s_[:, 3, hs], in0=s_[:, 1, hs], in1=sinkt[:, hs], op=ALU.add),
                  reads=dsts + [dsink], writes=[df])
            em.op("act", lambda e: e.activation(out=s_[:, 4, hs], in_=s_[:, 3, hs], func=AF.Exp), reads=[df], writes=[df])
            em.op("dve", lambda e: e.tensor_tensor(out=s_[:, 4, hs], in0=s_[:, 4, hs], in1=s_[:, 2, hs], op=ALU.add),
                  reads=dsts + [df], writes=[df])
            em.op("dve", lambda e: e.reciprocal(out=s_[:, 5, hs], in_=s_[:, 4, hs]), reads=[df], writes=[df])
            em.op("dve", lambda e: e.tensor_tensor(out=otok[b % 2][:, hs, :], in0=ops[:, :, :],
                                                   in1=s_[:, 5, hs].unsqueeze(2).broadcast_to([128, 8, 64]), op=ALU.mult),
                  reads=[dops, df], writes=[dotok[b % 2][k]])

        def stageB(b, steps):
            m_, dm_ = mk[b % 2], dmk[b % 2]
            em.dma("sp", m_[:, :], C.amask[b], writes=[dm_])
            s_ = stt[b % 2]
            q_ = qT[b % NQ]

            def S(p):
                sl = p % 2
                g = p // 2

                def f(e):
                    for i in range(2):
                        bank = sps[:, 2 * sl + i, 0:384]
                        e.matmul(bank, lhsT=q_[:, p, :], rhs=kT[:, g * 2 + i, b * 128: b * 128 + 384], start=True, stop=False)
                        r = e.matmul(bank, lhsT=C.ident[:, :], rhs=m_[:, :], start=False, stop=True)
                    return r
                em.op("pe", f, reads=[dqT[b % NQ], dkT[b], dkT[b + 1], dkT[b + 2], dm_, C.dident], writes=[dsps[sl]])
                hs = slice(2 * p, 2 * p + 2)
                dst = dsth[b % 2][2 * p]
                em.op("dve", lambda e: e.tensor_reduce(out=s_[:, 0, hs], in_=sps[:, 2 * sl:2 * sl + 2, 0:384], op=ALU.max, axis=AX.X,
                                                       negate=True), reads=[dsps[sl]], writes=[dst])
                em.op("dve", lambda e: e.tensor_tensor(out=s_[:, 1, hs], in0=s_[:, 0, hs], in1=nsink[:, hs], op=ALU.min),
                      reads=[dst, dsink], writes=[dst])
                for i in range(2):
                    h = 2 * p + i
                    em.op("act", lambda e, i=i, h=h: e.activation(out=pb[sl][:, i, :], in_=sps[:, 2 * sl + i, 0:384], func=AF.Exp,
                                                                 bias=s_[:, 1, h:h + 1], accum_out=s_[:, 2, h:h + 1]),
                          reads=[dsps[sl], dst], writes=[dpb[sl], dsth[b % 2][h]] if i == 1 else [dpb[sl], dst])

            def T(p):
                sl = p % 2

                def ft(e):
                    for i in range(2):
                        for c in range(3):
                            r = e.transpose(tp[:, i * 3 + c, :], pb[sl][:, i, c * 128:(c + 1) * 128], C.ident[:, :])
                    return r
                em.op("pe", ft, reads=[dpb[sl], C.dident], writes=[dtp])
                if p % 2 == 0:
                    em.op("dve", lambda e: e.tensor_copy(out=pT[sl][:, :, :], in_=tp[:, 0:6, :]), reads=[dtp], writes=[dpT[sl]])
                else:
                    em.op("act", lambda e: e.activation(out=pT[sl][:, :, :], in_=tp[:, 0:6, :], func=AF.Copy), reads=[dtp], writes=[dpT[sl]])

            def PV(p):
                sl = p % 2
                g = p // 2

                def fo(e):
                    for i in range(2):
                        h = 2 * p + i
                        for c in range(3):
                            r = e.matmul(ops[:, h % 8, :], lhsT=pT[sl][:, i * 3 + c, :], rhs=vA[:, b + c, g * 64:(g + 1) * 64],
                                         start=(c == 0), stop=(c == 2))
                    return r
                em.op("pe", fo, reads=[dpT[sl], dvA[b], dvA[b + 1], dvA[b + 2]], writes=[dops])

            S(0)
            S(1)
            for p in range(8):
                T(p)
                if p + 2 < 8:
                    S(p + 2)
                PV(p)
                if p % 4 == 3:
                    finish_heads(b, 2 * p - 6)
                if steps:
                    steps.pop(0)()
            while steps:
                steps.pop(0)()

        nblk = getattr(C, "nblk", NCH)
        nop = lambda: None
        for b0 in range(min(2, NCH)):
            A_loads(b0)
        for s_ in A_steps(0):
            s_()
        if NCH > 2:
            A_loads(2)
        if NCH > 1:
            for s_ in A_steps(1):
                s_()
        for b in range(nblk):
            Cs = C_steps(b - 1) if b >= 1 else [nop] * 3
            As = A_steps(b + 2) if b + 2 < NCH else [nop] * 8
            ld = (lambda b=b: A_loads(b + 3)) if b + 3 < NCH else nop
            steps = [lambda Cs=Cs, As=As: (Cs[0](), As[0]()), Cs[1], As[1], lambda Cs=Cs, As=As: (Cs[2](), As[2]()),
                     lambda As=As, ld=ld: (As[3](), ld()), lambda As=As: (As[4](), As[6]()), As[5], As[7]]
            stageB(b, steps)
        for s_ in C_steps(nblk - 1):
            s_()
        em.barrier()


def emit_ret(C, l, src, dsrc):
    nc, em = C.nc, C.em
    nch = getattr(C, "nblk", NCH)
    half_b = NCH // 2
    with ExitStack() as st0:
        bnd = _alloc(st0, nc, "bnd", [128, 1], F32)
        kdec = _alloc(st0, nc, "kdec", [128, 8], F32)
        g128 = _alloc(st0, nc, "g128", [128, 8], F32)
        decrow = _alloc(st0, nc, "decrow", [128, 8, 128], F32)
        DT = _alloc(st0, nc, "DT", [128, 4, 128], F32)
        S32 = _alloc(st0, nc, "S32", [128, 4, 2, 512], F32)
        stT = ExitStack()
        rc = _alloc(stT, nc, "rc", [128, 6, 128], F32)
        cpos = _alloc(stT, nc, "cpos", [128, 4], F32)
        dl = _alloc(stT, nc, "dl", [128, 8], F32)
        lg = _alloc(stT, nc, "lg", [128, 8], F32)
        tmpD = _alloc(stT, nc, "tmpD", [128, 2, 128], F32)
        drc, dtab, dS32, dtmp = Dep("rc"), Dep("tab"), Dep("S32"), Dep("tmpD")
        em.dma("sp", rc[:, :, :], C.rconst[:, :, :], writes=[drc])
        em.dma("sp", cpos[:, :], C.rpos[:, :], writes=[drc], owner=drc)
        em.dma("sp", bnd[:, :], C.rbnd[:, :], writes=[drc], owner=drc)
        em.dma("sp", dl[:, 0:4], C.rdf[0, :].partition_broadcast(128), writes=[dtab])
        em.dma("sp", dl[:, 4:8], C.rdb[0, :].partition_broadcast(128), writes=[dtab], owner=dtab)
        em.op("act", lambda e: e.activation(out=lg[:, :], in_=dl[:, :], func=AF.Exp, scale=-1.0), reads=[dtab], writes=[dtab])
        em.op("dve", lambda e: e.tensor_scalar(out=lg[:, :], in0=lg[:, :], scalar1=1.0, scalar2=None, op0=ALU.add), reads=[dtab], writes=[dtab])
        em.op("act", lambda e: e.activation(out=lg[:, :], in_=lg[:, :], func=AF.Ln), reads=[dtab], writes=[dtab])
        em.op("dve", lambda e: e.tensor_scalar(out=lg[:, :], in0=lg[:, :], scalar1=-1.0, scalar2=None, op0=ALU.mult), reads=[dtab], writes=[dtab])
        em.op("act", lambda e: e.activation(out=g128[:, :], in_=lg[:, :], func=AF.Exp, scale=128.0), reads=[dtab], writes=[dtab])
        em.op("act", lambda e: e.activation(out=kdec[:, 0:4], in_=lg[:, 0:4], func=AF.Exp, scale=cpos[:, 2:3]), reads=[dtab, drc], writes=[dtab])
        em.op("act", lambda e: e.activation(out=kdec[:, 4:8], in_=lg[:, 4:8], func=AF.Exp, scale=cpos[:, 3:4]), reads=[dtab, drc], writes=[dtab])
        em.op("dve", lambda e: e.tensor_scalar(out=kdec[:, :], in0=kdec[:, :], scalar1=1.0 / 16, scalar2=None, op0=ALU.mult), reads=[dtab], writes=[dtab])
        for h in range(4):
            em.op("act", lambda e, h=h: e.activation(out=decrow[:, h, :], in_=rc[:, 4, :], func=AF.Exp, scale=lg[:, h:h + 1]), reads=[dtab, drc], writes=[dtab])
            em.op("act", lambda e, h=h: e.activation(out=decrow[:, 4 + h, :], in_=rc[:, 5, :], func=AF.Exp, scale=lg[:, 4 + h:5 + h]), reads=[dtab, drc], writes=[dtab])
            em.op("act", lambda e, h=h: e.activation(out=tmpD[:, 0, :], in_=rc[:, 0, :], func=AF.Exp, scale=lg[:, h:h + 1]), reads=[dtab, drc], writes=[dtmp])
            em.op("act", lambda e, h=h: e.activation(out=tmpD[:, 1, :], in_=rc[:, 1, :], func=AF.Exp, scale=lg[:, 4 + h:5 + h]), reads=[dtab, drc], writes=[dtmp])
            em.op("dve", lambda e, h=h: e.tensor_tensor(out=tmpD[:, :, :], in0=tmpD[:, :, :], in1=rc[:, 2:4, :], op=ALU.mult), reads=[dtmp, drc], writes=[dtmp])
            em.op("dve", lambda e, h=h: e.tensor_tensor(out=DT[:, h, :], in0=tmpD[:, 0, :], in1=tmpD[:, 1, :], op=ALU.add), reads=[dtmp], writes=[dtab])
        em.op("dve", lambda e: e.memset(S32[:, :, :, :], 0.0), writes=[dS32])
        em.barrier()
        stT.close()

        def rotary(ps, dps, c_, dc_, rt, drt, out_bf, dout):
            p4 = ps.rearrange("p (h t f) -> p h t f", h=4, t=2)
            o4 = out_bf[:, :].rearrange("p (h t f) -> p h t f", h=4, t=2)
            cosb = c_[:, 0:128].unsqueeze(1).broadcast_to([128, 4, 128])
            sinb = c_[:, 128:256].unsqueeze(1).broadcast_to([128, 4, 128])
            x1, x2 = p4[:, :, 0, :], p4[:, :, 1, :]
            prods = [(x1, cosb), (x2, sinb), (x2, cosb), (x1, sinb)]
            for i in (0, 1):
                xx, tb = prods[i]
                em.op("dve", lambda e, i=i, xx=xx, tb=tb: e.tensor_tensor(out=rt[i][:, :, :], in0=xx, in1=tb, op=ALU.mult),
                      reads=[dps, dc_], writes=[drt[i]])
            em.op("dve", lambda e: e.tensor_tensor(out=o4[:, :, 0, :], in0=rt[0][:, :, :], in1=rt[1][:, :, :], op=ALU.subtract),
                  reads=[drt[0], drt[1]], writes=[dout])
            for i in (2, 3):
                xx, tb = prods[i]
                em.op("dve", lambda e, i=i, xx=xx, tb=tb: e.tensor_tensor(out=rt[i][:, :, :], in0=xx, in1=tb, op=ALU.mult),
                      reads=[dps, dc_], writes=[drt[i]])
            em.op("dve", lambda e: e.tensor_tensor(out=o4[:, :, 1, :], in0=rt[2][:, :, :], in1=rt[3][:, :, :], op=ALU.add),
                  reads=[drt[2], drt[3]], writes=[dout])

        def state_update(S32, dS32, kd_tok, dkd, v_tok, dv, dsp, dsd, gcol0):
            for h in range(4):
                for c in range(2):
                    def f(e, h=h, c=c):
                        return e.matmul(dsp[:, :], lhsT=kd_tok[:, h * 256 + c * 128: h * 256 + (c + 1) * 128],
                                        rhs=v_tok[:, h * 512:(h + 1) * 512], start=True, stop=True)
                    em.op("pe", f, reads=[dkd, dv], writes=[dsd])
                    em.op("dve", lambda e, h=h, c=c: e.scalar_tensor_tensor(out=S32[:, h, c, :], in0=S32[:, h, c, :],
                                                                          scalar=g128[:, gcol0 + h:gcol0 + h + 1], in1=dsp[:, :],
                                                                          op0=ALU.mult, op1=ALU.add),
                          reads=[dS32, dsd, dtab], writes=[dS32])

        def boundary(S32, dS32):
            em.op("dve", lambda e: e.tensor_scalar(out=S32[:, :, :, :], in0=S32[:, :, :, :], scalar1=bnd[:, 0:1], scalar2=None,
                                                   op0=ALU.mult), reads=[dS32, drc], writes=[dS32])

        with ExitStack() as st:
            Wkv = _alloc(st, nc, "Wkv", [128, 8, 3072], BF16)
            dWkv = [Dep("Wkv%d" % k) for k in range(2)]
            wv = C.rwi[0].rearrange("(k p) f -> p k f", p=128)
            em.dma("pool", Wkv[:, 0:4, :], wv[:, 0:4, 1024:4096], writes=[dWkv[0]])
            em.dma("pool", Wkv[:, 4:8, :], wv[:, 4:8, 1024:4096], writes=[dWkv[1]])
            gpre = _alloc(st, nc, "gpre", [128, D], F32)
            dgpre = Dep("gpre")
            em.dma("sp", gpre[:, :], C.ng[l, 2, :].partition_broadcast(128), writes=[dgpre])
            xa = [_alloc(st, nc, "xa%d" % i, [128, D], F32) for i in range(3)]
            dxa = [Dep("xa%d" % i) for i in range(3)]
            hb2 = [_alloc(st, nc, "hb%d" % i, [128, D], BF16) for i in range(2)]
            dhb2 = [Dep("hb%d" % i) for i in range(2)]
            hT = [_alloc(st, nc, "hT%d" % i, [128, 8, 128], BF16) for i in range(2)]
            dhT = [Dep("hT%d" % i) for i in range(2)]
            NCS = 4
            cs = [_alloc(st, nc, "rcs%d" % i, [128, 256], F32) for i in range(NCS)]
            dcs = [Dep("rcs%d" % i) for i in range(NCS)]
            rt = [_alloc(st, nc, "rrt%d" % i, [128, 4, 128], F32) for i in range(4)]
            drt = [Dep("rrt%d" % i) for i in range(4)]
            krot = [_alloc(st, nc, "krot%d" % i, [128, 1024], BF16) for i in range(2)]
            dkrot = [Dep("krot%d" % i) for i in range(2)]
            kf = [_alloc(st, nc, "kf%d" % i, [128, 1024], BF16) for i in range(2)]
            dkf = [Dep("kf%d" % i) for i in range(2)]
            kb = [_alloc(st, nc, "kb%d" % i, [128, 1024], BF16) for i in range(2)]
            dkb = [Dep("kb%d" % i) for i in range(2)]
            kTb = [_alloc(st, nc, "kTb%d" % i, [128, 8, 128], BF16) for i in range(2)]
            dkTb = [Dep("kTb%d" % i) for i in range(2)]
            vtok = [_alloc(st, nc, "vtok%d" % i, [128, 2048], BF16) for i in range(2)]
            dvtok = [[Dep("vtok%d_%d" % (i, k)) for k in range(2)] for i in range(2)]
            Sbf2 = [_alloc(st, nc, "Sbf%d" % i, [128, 4096], BF16) for i in range(2)]
            dSbfh2 = [[Dep("Sbf%d_%d" % (i, h)) for h in range(4)] for i in range(2)]
            dS32h = [Dep("S32b_%d" % h) for h in range(4)]
            em.op("dve", lambda e: e.memset(S32[:, :, :, :], 0.0), reads=[dS32], writes=dS32h)
            sst = [_alloc(st, nc, "ss%d" % i, [128, 4], F32) for i in range(3)]
            dsst = [Dep("ss%d" % i) for i in range(3)]
            tp = _palloc(st, nc, "tp", [128, 8, 128], BF16)
            dtp = PDep("tp")
            pk = _palloc(st, nc, "pk", [128, 1024], F32)
            dpk = PDep("pk")
            pv1 = _palloc(st, nc, "pv", [128, 1024], F32)
            pv = [pv1, pv1]
            dpv1 = PDep("pv")
            dpv = [dpv1, dpv1]
            dspr = [_palloc(st, nc, "dsp%d" % i, [128, 512], F32) for i in range(2)]
            dsdr = [PDep("dsp%d" % i) for i in range(2)]

            def loads1(n):
                em.dma("sp", xa[n % 3][:, :], src[n * 128:(n + 1) * 128, :], reads=[dsrc[n]], writes=[dxa[n % 3]])
                em.dma("sp", cs[n % NCS][:, :], C.rcs[n * 128:(n + 1) * 128, :], writes=[dcs[n % NCS]])

            def front1(n):
                r = n % 2
                xs, dxs = xa[n % 3], dxa[n % 3]
                hbs, dhbs = hb2[r], dhb2[r]
                ss, dss = sst[n % 3], dsst[n % 3]
                em.op("act", lambda e: e.activation(out=hbs[:, :], in_=xs[:, :], func=AF.Square, accum_out=ss[:, 0:1]),
                      reads=[dxs], writes=[dhbs, dss])
                emit_rstd(em, ss, dss, C.mhalf, C.dmh, D)
                em.op("dve", lambda e: e.scalar_tensor_tensor(out=hbs[:, :], in0=xs[:, :], scalar=ss[:, 2:3], in1=gpre[:, :],
                                                              op0=ALU.mult, op1=ALU.mult),
                      reads=[dxs, dss, dgpre], writes=[dhbs])

            def P1_steps(n):
                r = n % 2
                c_, dc_ = cs[n % NCS], dcs[n % NCS]

                def p0():
                    hbs = hb2[r]

                    def tps(e):
                        for k in range(8):
                            i = e.transpose(tp[:, k, :], hbs[:, k * 128:(k + 1) * 128], C.ident[:, :])
                        return i
                    em.op("pe", tps, reads=[dhb2[r], C.dident], writes=[dtp])
                    em.op("act", lambda e: e.activation(out=hT[r][:, :, :], in_=tp[:, :, :], func=AF.Copy), reads=[dtp], writes=[dhT[r]])

                def p1():
                    for g in range(2):
                        def f(e, g=g):
                            for k in range(8):
                                i = e.matmul(pk[:, g * 512:(g + 1) * 512], lhsT=hT[r][:, k, :], rhs=Wkv[:, k, g * 512:(g + 1) * 512],
                                             start=(k == 0), stop=(k == 7))
                            return i
                        em.op("pe", f, reads=[dhT[r]] + dWkv, writes=[dpk])
                    rotary(pk[:, :], dpk, c_, dc_, rt, drt, krot[r], dkrot[r])

                def p2():
                    for (dst, ddst, col) in ((kf[r], dkf[r], 0), (kb[r], dkb[r], 4)):
                        em.op("dve", lambda e, dst=dst, col=col: e.tensor_tensor(
                            out=dst[:, :].rearrange("p (h f) -> p h f", h=4), in0=krot[r][:, :].rearrange("p (h f) -> p h f", h=4),
                            in1=kdec[:, col:col + 4].unsqueeze(2).broadcast_to([128, 4, 256]), op=ALU.mult),
                            reads=[dkrot[r], dtab], writes=[ddst])

                    def tk(e):
                        for k in range(8):
                            i = e.transpose(tp[:, k, :], krot[r][:, k * 128:(k + 1) * 128], C.ident[:, :])
                        return i
                    em.op("pe", tk, reads=[dkrot[r], C.dident], writes=[dtp])
                    em.op("act", lambda e: e.activation(out=kTb[r][:, :, :], in_=tp[:, :, :], func=AF.Copy), reads=[dtp], writes=[dkTb[r]])
                    em.dma("sp", C.s_kT[n], kTb[r][:, :, :].rearrange("p k t -> p (k t)"), reads=[dkTb[r]], writes=[C.dscr[n]], owner=dkTb[r])
                    em.dma("sp", C.s_kf[n], kf[r][:, :], reads=[dkf[r]], writes=[C.dscr[n]], owner=dkf[r])

                def mkv(hv):
                    def pv_():
                        for g in range(2):
                            def f(e, g=g):
                                col = 1024 + hv * 1024 + g * 512
                                for k in range(8):
                                    i = e.matmul(pv[hv][:, g * 512:(g + 1) * 512], lhsT=hT[r][:, k, :], rhs=Wkv[:, k, col:col + 512],
                                                 start=(k == 0), stop=(k == 7))
                                return i
                            em.op("pe", f, reads=[dhT[r]] + dWkv, writes=[dpv[hv]])
                        em.op("act", lambda e: e.activation(out=vtok[r][:, hv * 1024:(hv + 1) * 1024], in_=pv[hv][:, :], func=AF.Copy),
                              reads=[dpv[hv]], writes=[dvtok[r][hv]])
                    return pv_

                def p5():
                    em.dma("sp", C.s_v[n], vtok[r][:, :], reads=dvtok[r], writes=[C.dscr[n]], owner=dvtok[r][0])
                return [p0, p1, mkv(0), mkv(1), p2, p5]

            def U(n, steps):
                r = n % 2
                if n - 3 >= 0:
                    loads1(n - 3)
                if n - 2 >= 0:
                    front1(n - 2)
                Sbf, dSbfh = Sbf2[n % 2], dSbfh2[n % 2]
                for h in range(4):
                    em.op("act", lambda e, h=h: e.activation(out=Sbf[:, h * 1024:(h + 1) * 1024],
                                                             in_=S32[:, h, :, :].rearrange("p c f -> p (c f)"), func=AF.Copy),
                          reads=[dS32h[h]], writes=[dSbfh[h]])
                    if h == 3:
                        em.dma("sp", C.s_sb[n], Sbf[:, :], reads=dSbfh, writes=[C.dscr[n]], owner=dSbfh[0])
                    for c in range(2):
                        if n > 0:
                            dsp, dsd = dspr[c], dsdr[c]

                            def f(e, h=h, c=c, dsp=dsp):
                                return e.matmul(dsp[:, :], lhsT=kb[r][:, h * 256 + c * 128: h * 256 + (c + 1) * 128],
                                                rhs=vtok[r][:, h * 512:(h + 1) * 512], start=True, stop=True)
                            em.op("pe", f, reads=[dkb[r], dvtok[r][h // 2]], writes=[dsd])
                            em.op("dve", lambda e, h=h, c=c, dsp=dsp: e.scalar_tensor_tensor(out=S32[:, h, c, :], in0=S32[:, h, c, :],
                                                                                           scalar=g128[:, 4 + h:5 + h], in1=dsp[:, :],
                                                                                           op0=ALU.mult, op1=ALU.add),
                                  reads=[dS32h[h], dsd, dtab], writes=[dS32h[h]])
                        if steps:
                            steps.pop(0)()
                while steps:
                    steps.pop(0)()
                if n > 0 and n == half_b:
                    em.op("dve", lambda e: e.tensor_scalar(out=S32[:, :, :, :], in0=S32[:, :, :, :], scalar1=bnd[:, 0:1], scalar2=None,
                                                           op0=ALU.mult), reads=dS32h + [drc], writes=dS32h)

            for i in range(1, 4):
                if nch - i >= 0:
                    loads1(nch - i)
            front1(nch - 1)
            if nch > 1:
                front1(nch - 2)
            for s_ in P1_steps(nch - 1):
                s_()
            for n in range(nch - 1, -1, -1):
                U(n, P1_steps(n - 1) if n > 0 else [])
            em.barrier()

        em.op("dve", lambda e: e.memset(S32[:, :, :, :], 0.0), writes=[dS32])
        with ExitStack() as st:
            Wqg = _alloc(st, nc, "Wqg", [128, 8, 3072], BF16)
            dWqg = [Dep("Wqg%d" % k) for k in range(2)]
            wv = C.rwi[0].rearrange("(k p) f -> p k f", p=128)
            em.dma("pool", Wqg[:, :, 0:1024], wv[:, :, 0:1024], writes=[dWqg[0]])
            em.dma("pool", Wqg[:, :, 1024:3072], wv[:, :, 4096:6144], writes=[dWqg[1]])
            Wro = _alloc(st, nc, "Wro", [128, 16, D], BF16)
            dWro = Dep("Wro")
            em.dma("pool", Wro[:, :, :], C.rwo[0].rearrange("(k p) f -> p k f", p=128), writes=[dWro])
            gpre, dgpre, gpost, dgpost = load_gains(C, em, st, nc, l, 2, 3, 1.0)
            NX = 4
            xa = [_alloc(st, nc, "xa%d" % i, [128, D], F32) for i in range(NX)]
            dxa = [Dep("xa%d" % i) for i in range(NX)]
            hb = _alloc(st, nc, "hb", [128, D], BF16)
            dhb = Dep("hb")
            NH = 3
            hT = [_alloc(st, nc, "hT%d" % i, [128, 8, 128], BF16) for i in range(NH)]
            dhT = [Dep("hT%d" % i) for i in range(NH)]
            cs = [_alloc(st, nc, "rcs%d" % i, [128, 256], F32) for i in range(2)]
            dcs = [Dep("rcs%d" % i) for i in range(2)]
            rt2 = [_alloc(st, nc, "rrt%d" % i, [128, 4, 128], F32) for i in range(2)]
            drt2 = [Dep("rrt%d" % i) for i in range(2)]
            rt = [rt2[0], rt2[1], rt2[0], rt2[1]]
            drt = [drt2[0], drt2[1], drt2[0], drt2[1]]
            qrot = _alloc(st, nc, "qrot", [128, 1024], BF16)
            dqrot = Dep("qrot")
            qT = [[_alloc(st, nc, "qT%d_%d" % (r, i), [128, 8, 128], BF16) for i in range(3)] for r in range(2)]
            dqT = [[Dep("qT%d_%d" % (r, i)) for i in range(3)] for r in range(2)]
            sg = [_alloc(st, nc, "sg%d" % i, [128, 512], BF16) for i in range(2)]
            dsg = [Dep("sg%d" % i) for i in range(2)]
            NR = 2
            kTl = [_alloc(st, nc, "kTl%d" % i, [128, 8, 128], BF16) for i in range(NR)]
            kfl = [_alloc(st, nc, "kfl%d" % i, [128, 1024], BF16) for i in range(NR)]
            vl = [_alloc(st, nc, "vl%d" % i, [128, 2048], BF16) for i in range(NR)]
            sbl1 = _alloc(st, nc, "sbl", [128, 4, 2, 512], BF16)
            sbl = [sbl1, sbl1]
            dkTl = [Dep("kTl%d" % i) for i in range(NR)]
            dkfl = [Dep("kfl%d" % i) for i in range(NR)]
            dvl = [Dep("vl%d" % i) for i in range(NR)]
            dsblh = [Dep("sbl_%d" % h) for h in range(4)]
            Sfb = _alloc(st, nc, "Sfb", [128, 4, 2, 512], BF16)
            dSfb = [Dep("Sfb%d" % h) for h in range(4)]
            dS32h = [Dep("S32_%d" % h) for h in range(4)]
            STb = [_alloc(st, nc, "STb%d" % i, [128, 128], BF16) for i in range(2)]
            dSTb = [Dep("STb%d" % i) for i in range(2)]
            otok = [_alloc(st, nc, "otok%d" % i, [128, 2048], BF16) for i in range(2)]
            dotok = [[Dep("otok%d_%d" % (i, h)) for h in range(4)] for i in range(2)]
            oT = _alloc(st, nc, "oT", [128, 16, 128], BF16)
            doT = Dep("oT")
            junk = _alloc(st, nc, "junk", [128, D], BF16)
            djunk = Dep("junk")
            junk2 = junk
            djunk2 = djunk
            sst = [_alloc(st, nc, "ss%d" % i, [128, 4], F32) for i in range(6)]
            dsst = [Dep("ss%d" % i) for i in range(6)]
            tp = _palloc(st, nc, "tp", [128, 8, 128], BF16)
            dtp = PDep("tp")
            pq = _palloc(st, nc, "pq", [128, 1024], F32)
            dpq = PDep("pq")
            pG = _palloc(st, nc, "pG", [128, 512], F32)
            dpG = PDep("pG")
            pS = _palloc(st, nc, "pS", [128, 512], F32)
            dpS = PDep("pS")
            pY = [_palloc(st, nc, "pY%d" % i, [128, 512], F32) for i in range(2)]
            dpY = [PDep("pY%d" % i) for i in range(2)]
            pD = _palloc(st, nc, "pD", [128, 512], F32)
            dpD = PDep("pD")
            sctr = [0]
            em.op("dve", lambda e: e.memset(Sfb[:, :, :, :], 0.0), writes=dSfb)
            em.op("dve", lambda e: e.memset(S32[:, :, :, :], 0.0), reads=[dS32], writes=dS32h)

            def nss():
                si = sctr[0] % 6
                sctr[0] += 1
                return sst[si], dsst[si]

            def load_sb(n, h):
                em.dma("sp", sbl1[:, h, :, :].rearrange("p c f -> p (c f)"), C.s_sb[n][:, h * 1024:(h + 1) * 1024],
                       reads=[C.dscr[n]], writes=[dsblh[h]])

            hb2 = [hb, _alloc(st, nc, "hb_b", [128, D], BF16)]
            dhb2 = [dhb, Dep("hb_b")]

            def front(n):
                xs, dxs = xa[n % NX], dxa[n % NX]
                hbs, dhbs = hb2[n % 2], dhb2[n % 2]
                em.dma("sp", xs[:, :], src[n * 128:(n + 1) * 128, :], reads=[dsrc[n]], writes=[dxs])
                em.dma("sp", cs[n % 2][:, :], C.rcs[n * 128:(n + 1) * 128, :], writes=[dcs[n % 2]])
                ss, dss = nss()
                em.op("act", lambda e: e.activation(out=hbs[:, :], in_=xs[:, :], func=AF.Square, accum_out=ss[:, 0:1]),
                      reads=[dxs], writes=[dhbs, dss])
                emit_rstd(em, ss, dss, C.mhalf, C.dmh, D)
                em.op("dve", lambda e: e.scalar_tensor_tensor(out=hbs[:, :], in0=xs[:, :], scalar=ss[:, 2:3], in1=gpre[:, :],
                                                              op0=ALU.mult, op1=ALU.mult),
                      reads=[dxs, dss, dgpre], writes=[dhbs])

            def P_steps(n):
                r = n % 2
                xs, dxs = xa[n % NX], dxa[n % NX]
                c_, dc_ = cs[r], dcs[r]

                def p0():
                    if n == 0:
                        for h in range(4):
                            load_sb(0, h)
                    hbs = hb2[n % 2]

                    def tps(e):
                        for k in range(8):
                            i = e.transpose(tp[:, k, :], hbs[:, k * 128:(k + 1) * 128], C.ident[:, :])
                        return i
                    em.op("pe", tps, reads=[dhb2[n % 2], C.dident], writes=[dtp])
                    em.op("act", lambda e: e.activation(out=hT[n % NH][:, :, :], in_=tp[:, :, :], func=AF.Copy), reads=[dtp], writes=[dhT[n % NH]])

                def p1():
                    em.dma("sp", kTl[r][:, :, :].rearrange("p k t -> p (k t)"), C.s_kT[n], reads=[C.dscr[n]], writes=[dkTl[r]])
                    em.dma("sp", kfl[r][:, :], C.s_kf[n], reads=[C.dscr[n]], writes=[dkfl[r]])
                    em.dma("sp", vl[r][:, :], C.s_v[n], reads=[C.dscr[n]], writes=[dvl[r]])
                    for g in range(2):
                        def f(e, g=g):
                            for k in range(8):
                                i = e.matmul(pq[:, g * 512:(g + 1) * 512], lhsT=hT[n % NH][:, k, :], rhs=Wqg[:, k, g * 512:(g + 1) * 512],
                                             start=(k == 0), stop=(k == 7))
                            return i
                        em.op("pe", f, reads=[dhT[n % NH], dWqg[0]], writes=[dpq])
                    rotary(pq[:, :], dpq, c_, dc_, rt, drt, qrot, dqrot)

                def p2():
                    def tq(e):
                        for k in range(8):
                            i = e.transpose(tp[:, k, :], qrot[:, k * 128:(k + 1) * 128], C.ident[:, :])
                        return i
                    em.op("pe", tq, reads=[dqrot, C.dident], writes=[dtp])
                    em.op("act", lambda e: e.activation(out=qT[r][0][:, :, :], in_=tp[:, :, :], func=AF.Copy), reads=[dtp], writes=[dqT[r][0]])
                    for i in range(2):
                        em.op("dve", lambda e, i=i: e.tensor_tensor(
                            out=qT[r][1 + i][:, :, :].rearrange("p (h c) t -> p h c t", c=2),
                            in0=qT[r][0][:, :, :].rearrange("p (h c) t -> p h c t", c=2),
                            in1=decrow[:, 4 * i:4 * i + 4, :].unsqueeze(2).broadcast_to([128, 4, 2, 128]), op=ALU.mult),
                            reads=[dqT[r][0], dtab], writes=[dqT[r][1 + i]])
                return [p0, p1, p2]

            def O_steps(n):
                r = n % 2
                steps = []
                for half in range(2):
                    def o_t(half=half):
                        def to(e):
                            for k in range(8):
                                i = e.transpose(tp[:, k, :], otok[r][:, (half * 8 + k) * 128:(half * 8 + k + 1) * 128], C.ident[:, :])
                            return i
                        em.op("pe", to, reads=dotok[r] + [C.dident], writes=[dtp])
                        em.op("act", lambda e: e.activation(out=oT[:, half * 8:(half + 1) * 8, :], in_=tp[:, :, :], func=AF.Copy),
                              reads=[dtp], writes=[doT])
                    steps.append(o_t)

                def o_w():
                    def fw(e):
                        for half in range(2):
                            for k in range(16):
                                i = e.matmul(pq[:, half * 512:(half + 1) * 512], lhsT=oT[:, k, :], rhs=Wro[:, k, half * 512:(half + 1) * 512],
                                             start=(k == 0), stop=(k == 15))
                        return i
                    em.op("pe", fw, reads=[doT, dWro], writes=[dpq])

                def o_e():
                    ss, dss = nss()
                    emit_post_residual(C, em, pq[:, :], dpq, xa[n % NX], dxa[n % NX], gpost, dgpost, ss, dss,
                                       junk, djunk, C.y[n * 128:(n + 1) * 128, :], C.dy[n])
                steps += [o_w, o_e]
                return steps

            def H(n, steps):
                r = n % 2
                last = (n + 1 >= nch)

                def GS(h):
                    def fg(e):
                        for k in range(8):
                            i = e.matmul(pG[:, :], lhsT=hT[n % NH][:, k, :], rhs=Wqg[:, k, 1024 + h * 512:1024 + (h + 1) * 512],
                                         start=(k == 0), stop=(k == 7))
                        return i
                    em.op("pe", fg, reads=[dhT[n % NH], dWqg[1]], writes=[dpG])
                    em.op("act", lambda e: e.activation(out=sg[h % 2][:, :], in_=pG[:, :], func=AF.Silu), reads=[dpG], writes=[dsg[h % 2]])

                    def fs(e):
                        for c in range(2):
                            i = e.matmul(pS[:, 0:128], lhsT=kTl[r][:, 2 * h + c, :], rhs=qT[r][0][:, 2 * h + c, :], start=(c == 0), stop=(c == 1))
                        return i
                    em.op("pe", fs, reads=[dkTl[r], dqT[r][0]], writes=[dpS])
                    em.op("dve", lambda e: e.tensor_tensor(out=STb[h % 2][:, :], in0=pS[:, 0:128], in1=DT[:, h, :], op=ALU.mult),
                          reads=[dpS, dtab], writes=[dSTb[h % 2]])

                def Y(h):
                    yh, dyh = pY[h % 2], dpY[h % 2]

                    def fy(e):
                        e.matmul(yh[:, :], lhsT=STb[h % 2][:, :], rhs=vl[r][:, h * 512:(h + 1) * 512], start=True, stop=False)
                        for c in range(2):
                            e.matmul(yh[:, :], lhsT=qT[r][1][:, 2 * h + c, :], rhs=Sfb[:, h, c, :], start=False, stop=False)
                        for c in range(2):
                            i = e.matmul(yh[:, :], lhsT=qT[r][2][:, 2 * h + c, :], rhs=sbl[r][:, h, c, :], start=False, stop=(c == 1))
                        return i
                    em.op("pe", fy, reads=[dSTb[h % 2], dvl[r], dqT[r][1], dqT[r][2], dSfb[h], dsblh[h]], writes=[dyh])
                    ss, dss = nss()
                    em.op("act", lambda e: e.activation(out=junk2[:, 0:512], in_=yh[:, :], func=AF.Square, accum_out=ss[:, 0:1]),
                          reads=[dyh], writes=[djunk2, dss])
                    emit_rstd(em, ss, dss, C.mhalf, C.dmh, 512)
                    em.op("dve", lambda e: e.scalar_tensor_tensor(out=otok[r][:, h * 512:(h + 1) * 512], in0=yh[:, :], scalar=ss[:, 2:3],
                                                                  in1=sg[h % 2][:, :], op0=ALU.mult, op1=ALU.mult),
                          reads=[dyh, dss, dsg[h % 2]], writes=[dotok[r][h]])

                def UPD(h, cs_):
                    for c in cs_:
                        def f(e, c=c):
                            return e.matmul(pD[:, :], lhsT=kfl[r][:, h * 256 + c * 128: h * 256 + (c + 1) * 128],
                                            rhs=vl[r][:, h * 512:(h + 1) * 512], start=True, stop=True)
                        em.op("pe", f, reads=[dkfl[r], dvl[r]], writes=[dpD])
                        em.op("dve", lambda e, c=c: e.scalar_tensor_tensor(out=S32[:, h, c, :], in0=S32[:, h, c, :],
                                                                         scalar=g128[:, h:h + 1], in1=pD[:, :],
                                                                         op0=ALU.mult, op1=ALU.add),
                              reads=[dS32h[h], dpD, dtab], writes=[dS32h[h]])
                    if 1 not in cs_:
                        return
                    if n + 1 == half_b:
                        em.op("dve", lambda e: e.tensor_scalar(out=S32[:, h, :, :], in0=S32[:, h, :, :], scalar1=bnd[:, 0:1], scalar2=None,
                                                               op0=ALU.mult), reads=[dS32h[h], drc], writes=[dS32h[h]])
                    em.op("act", lambda e: e.activation(out=Sfb[:, h, :, :], in_=S32[:, h, :, :], func=AF.Copy),
                          reads=[dS32h[h]], writes=[dSfb[h]])

                GS(0)
                for h in range(4):
                    if h + 1 < 4:
                        GS(h + 1)
                    if h >= 1 and not last:
                        UPD(h - 1, [1])
                    Y(h)
                    if not last:
                        load_sb(n + 1, h)
                        UPD(h, [0])
                    for _ in range(2):
                        if steps:
                            steps.pop(0)()
                if not last:
                    UPD(3, [1])
                while steps:
                    steps.pop(0)()

            nop = lambda: None
            front(0)
            if nch > 1:
                front(1)
            for s_ in P_steps(0):
                s_()
            if nch > 1:
                P_steps(1)[0]()
            for n in range(nch):
                O = O_steps(n - 1) if n >= 1 else [nop] * 4
                P = P_steps(n + 1) if n + 1 < nch else [nop] * 3
                P0n = P_steps(n + 2)[0] if n + 2 < nch else nop
                fr = (lambda n=n: front(n + 2)) if n + 2 < nch else nop
                steps = [O[0], P[1], O[1], fr, P[2], O[2], P0n, O[3]]
                H(n, steps)
            for s_ in O_steps(nch - 1):
                s_()
            em.barrier()


def build(nsub=6, subs=None, nblk=NCH, dbg=3):
    nc = bass.Bass("TRN2", target_bir_lowering=False)
    em = Emitter(nc)
    C = Ctx()
    C.nc, C.em = nc, em

    def din(name, shape, dt=F32):
        return nc.dram_tensor(name, shape, dt, kind="ExternalInput").ap()
    C.xin = din("xin", [NTOK, D])
    C.ng = din("norm_gains", [2, 6, D])
    C.fwi = din("ffn_w_in", [2, 2, D, 2 * DFF])
    C.fwo = din("ffn_w_out", [2, 2, DFF, D])
    C.wqkv = din("attn_w_qkv", [1, D, 1536])
    C.wo = din("attn_w_o", [1, D, D])
    C.sink = din("attn_sink", [1, 16])
    C.rwi = din("ret_w_in", [1, D, 6144])
    C.rwo = din("ret_w_o", [1, 2048, D])
    C.rdf = din("ret_decay_fwd", [1, 4])
    C.rdb = din("ret_decay_bwd", [1, 4])
    C.identd = din("c_ident", [128, 128], BF16)
    C.acs = din("c_acs", [NTOK, 320])
    C.amask = din("c_amask", [NCH, 128, 384], BF16)
    C.rconst = din("c_rconst", [128, 6, 128])
    C.rpos = din("c_rpos", [128, 4])
    C.rbnd = din("c_rbnd", [128, 1])
    C.rcs = din("c_rcs", [NTOK, 256])
    C.s_kT = nc.dram_tensor("s_kT", [NCH, 128, 1024], BF16, kind="Internal").ap()
    C.s_kf = nc.dram_tensor("s_kf", [NCH, 128, 1024], BF16, kind="Internal").ap()
    C.s_v = nc.dram_tensor("s_v", [NCH, 128, 2048], BF16, kind="Internal").ap()
    C.s_sb = nc.dram_tensor("s_sb", [NCH, 128, 4096], BF16, kind="Internal").ap()
    C.dscr = [Dep("scr%d" % i) for i in range(NCH)]
    C.y = nc.dram_tensor("y", [NTOK, D], F32, kind="ExternalOutput").ap()
    C.dy = [Dep("y%d" % i) for i in range(NCH)]
    C.dxin = [Dep("xin%d" % i) for i in range(NCH)]

    C.ident = nc.alloc_sbuf_tensor("ident", [128, 128], BF16)
    C.dident = Dep("ident")
    C.mhalf = nc.alloc_sbuf_tensor("mhalf", [128, 1], F32)
    C.dmh = Dep("mhalf")
    em.dma("sp", C.ident[:, :], C.identd[:, :], writes=[C.dident])
    em.op("pool", lambda e: e.memset(C.mhalf[:, :], -0.5), writes=[C.dmh])

    C.nblk = nblk
    C.dbg = dbg
    if subs is None:
        subs = [("ffn", 0, 0), ("attn", 0, 0), ("ffn", 0, 1), ("ffn", 1, 0), ("ret", 1, 0), ("ffn", 1, 1)]
    src, dsrc = C.xin, C.dxin
    for i, (kind, l, which) in enumerate(subs[:nsub]):
        if kind == "ffn":
            emit_ffn(C, l, which, src, dsrc)
        elif kind == "attn":
            emit_attn(C, l, src, dsrc)
        elif kind == "ret":
            emit_ret(C, l, src, dsrc)
        src, dsrc = C.y, C.dy
    em.finish()
    return nc


def make_consts():
    c = {}
    c["c_ident"] = np.eye(128, dtype=np.float32).astype(ml_dtypes.bfloat16)
    j = np.arange(128, dtype=np.float32)[:, None]
    i = np.arange(128, dtype=np.float32)[None, :]
    rc = np.zeros((128, 6, 128), np.float32)
    rc[:, 0] = np.maximum(i - j, 0.0)
    rc[:, 1] = np.maximum(j - i, 0.0)
    rc[:, 2] = (i >= j) / 16.0
    rc[:, 3] = (j > i) / 16.0
    rc[:, 4] = np.broadcast_to(i + 1.0, (128, 128))
    rc[:, 5] = np.broadcast_to(128.0 - i, (128, 128))
    c["c_rconst"] = rc
    p = np.arange(128, dtype=np.float32)
    c["c_rpos"] = np.stack([p + 1, 128 - p, 127 - p, p], axis=1).astype(np.float32)
    return c


def make_core_consts(seqlen):
    c = {}
    tok = np.arange(NTOK)
    pos = (tok % seqlen).astype(np.float32)
    inv = (500000.0 ** (-np.arange(8, dtype=np.float32) / 8)).astype(np.float32)
    ang = pos[:, None] * inv[None, :]
    cos = np.cos(ang).astype(np.float32)
    sin = np.sin(ang).astype(np.float32)
    hs = np.ones((1, 20, 1), np.float32)
    hs[:, :16] = 0.125
    acs = np.concatenate([(np.tile(cos[:, None, :], (1, 20, 1)) * hs).reshape(NTOK, 160),
                          (np.tile(sin[:, None, :], (1, 20, 1)) * hs).reshape(NTOK, 160)], axis=1)
    c["c_acs"] = np.ascontiguousarray(acs, dtype=np.float32)
    b = np.arange(NCH)[:, None, None]
    qi = np.arange(128)[None, :, None]
    kc = np.arange(384)[None, None, :]
    tq = 128 * b + qi
    tk = 128 * (b - 1) + kc
    valid = (tk >= 0) & (tk < NTOK) & ((tk // seqlen) == (tq // seqlen)) & (np.abs(tk - tq) <= 128)
    c["c_amask"] = np.where(valid, 0.0, NEG).astype(np.float32).astype(ml_dtypes.bfloat16)
    invr = (10000.0 ** (-np.arange(128, dtype=np.float32) / 128)).astype(np.float32)
    angr = pos[:, None] * invr[None, :]
    c["c_rcs"] = np.ascontiguousarray(np.concatenate([np.cos(angr), np.sin(angr)], axis=1), dtype=np.float32)
    c["c_rbnd"] = np.full((128, 1), 1.0 if seqlen == NTOK else 0.0, np.float32)
    return c


def kernel(x_prompt, x_sample, norm_gains, ffn_w_in, ffn_w_out, attn_w_qkv, attn_w_o, attn_sink,
           ret_w_in, ret_w_o, ret_decay_fwd, ret_decay_bwd, _nsub=6, _subs=None, _nblk=NCH, _cores=None, _trace=False, _dbg=3):
    f = lambda a: np.ascontiguousarray(np.asarray(a, dtype=np.float32))
    xp = f(x_prompt).reshape(4, NTOK, D)
    xs = f(x_sample).reshape(4, NTOK, D)
    shared = {
        "norm_gains": f(norm_gains), "ffn_w_in": f(ffn_w_in), "ffn_w_out": f(ffn_w_out),
        "attn_w_qkv": f(attn_w_qkv), "attn_w_o": f(attn_w_o), "attn_sink": f(attn_sink),
        "ret_w_in": f(ret_w_in), "ret_w_o": f(ret_w_o), "ret_decay_fwd": f(ret_decay_fwd),
        "ret_decay_bwd": f(ret_decay_bwd),
    }
    shared.update(make_consts())
    in_maps = []
    cc = [make_core_consts(2048), make_core_consts(4096)]
    for c in range(8):
        m = dict(shared)
        m.update(cc[0] if c < 4 else cc[1])
        m["xin"] = xp[c] if c < 4 else xs[c - 4]
        in_maps.append(m)
    nc = build(_nsub, _subs, _nblk, _dbg)
    if _cores is not None:
        res = run_bass_kernel_spmd(nc, [in_maps[c] for c in _cores], core_ids=list(range(len(_cores))), trace=_trace)
        if _trace:
            print("exec_time_ns", res.exec_time_ns)
        return [np.asarray(r["y"], dtype=np.float32) for r in res.results]
    res = run_bass_kernel_spmd(nc, in_maps, core_ids=list(range(8)))
    outs = [np.asarray(r["y"], dtype=np.float32) for r in res.results]
    y_prompt = np.stack(outs[:4]).reshape(8, 2048, D)
    y_sample = np.stack(outs[4:]).reshape(4, 4096, D)
    return (y_prompt, y_sample)
```

```python
import os
import numpy as np
import ml_dtypes
from contextlib import ExitStack
import concourse.bass as bass
import concourse.mybir as mybir
from concourse.bass_utils import run_bass_kernel_spmd

F32 = mybir.dt.float32
BF16 = mybir.dt.bfloat16
AF = mybir.ActivationFunctionType
ALU = mybir.AluOpType
AX = mybir.AxisListType

NTOK = 4096
NCH = 32
D = 1024
DFF = 2816
EPS = 1e-6
EPOCH = 1 << 30
NEG = -30000.0


class Dep:
    __slots__ = ("name", "w", "rs", "dsem", "dcnt", "ex", "retired")

    def __init__(self, name="", ex=False):
        self.name = name
        self.ex = ex
        self.w = None
        self.rs = {}
        self.dsem = None
        self.dcnt = 0
        self.retired = False


class Emitter:
    def __init__(self, nc):
        self.nc = nc
        self.eng = {"pe": nc.tensor, "act": nc.scalar, "dve": nc.vector,
                    "pool": nc.gpsimd, "sp": nc.sync}
        self.sem = {}
        self.cnt = {}
        self.nsem = 0
        for e in self.eng:
            self.sem[e] = self._newsem("e_" + e)
            self.cnt[e] = 0
        self.waited = {}
        self.dma_owners = []
        self.free_dsems = []
        self.no_recycle = set()

    def _newsem(self, name):
        self.nsem += 1
        return self.nc.alloc_semaphore("%s_%d" % (name, self.nsem))

    def _tick(self, e):
        if self.cnt[e] >= EPOCH:
            self.sem[e] = self._newsem("e_" + e)
            self.cnt[e] = 0
        self.cnt[e] += 1
        return self.sem[e], self.cnt[e]

    def _need(self, e, rec, needs):
        if rec is None:
            return
        if rec[0] == "e":
            _, pe, sem, val = rec
            if pe == e and e == "pe":
                return
            needs.append((sem, val))
        else:
            o = rec[1]
            needs.append((o.dsem, o.dcnt))

    def _collect(self, e, reads, writes):
        needs = []
        for d in reads:
            self._need(e, d.w, needs)
        for d in writes:
            if d.w is not None:
                self._need(e, d.w, needs)
            for k, r in d.rs.items():
                self._need(e, r, needs)
        return needs

    def _emit_waits(self, e, needs):
        eng = self.eng[e]
        best = {}
        for sem, val in needs:
            k = id(sem)
            if k not in best or best[k][1] < val:
                best[k] = (sem, val)
        for k, (sem, val) in best.items():
            wk = (e, k)
            if self.waited.get(wk, 0) >= val:
                continue
            self.waited[wk] = val
            eng.wait_ge(sem, val)

    def op(self, e, fn, reads=(), writes=()):
        xr = [d for d in reads if d.ex]
        needs = []
        if xr:
            for d in xr:
                self._need(e, d.w, needs)
            reads = [d for d in reads if not d.ex]
            writes = list(writes) + [d for d in xr if d not in writes]
        needs += self._collect(e, reads, writes)
        self._emit_waits(e, needs)
        inst = fn(self.eng[e])
        sem, val = self._tick(e)
        inst.then_inc(sem, 1)
        rec = ("e", e, sem, val)
        for d in reads:
            d.rs[e] = rec
        for d in writes:
            d.w = rec
            d.rs = {}
        return inst

    def dma(self, q, out, in_, reads=(), writes=(), owner=None):
        needs = self._collect(q, reads, writes)
        if owner is None:
            owner = writes[0] if writes else reads[0]
        if owner.dsem is None or owner.retired:
            if self.free_dsems and q != "pool":
                owner.dsem, owner.dcnt = self.free_dsems.pop()
            else:
                owner.dsem, owner.dcnt = self._newsem("d_" + owner.name), 0
                if q == "pool":
                    self.no_recycle.add(id(owner.dsem))
            owner.retired = False
            self.dma_owners.append(owner)
        self._emit_waits(q, needs)
        inst = self.eng[q].dma_start(out=out, in_=in_)
        owner.dcnt += 16
        inst.then_inc(owner.dsem, 16)
        rec = ("d", owner)
        for d in reads:
            d.rs["dma%d" % id(owner)] = rec
        for d in writes:
            d.w = rec
            d.rs = {}
        return inst

    def barrier(self):
        pts = [(self.sem[e], self.cnt[e]) for e in self.eng if self.cnt[e] > 0]
        pts += [(o.dsem, o.dcnt) for o in self.dma_owners]
        for e in self.eng:
            self._emit_waits(e, pts)
        for o in self.dma_owners:
            o.retired = True
            if id(o.dsem) not in self.no_recycle:
                self.free_dsems.append((o.dsem, o.dcnt))
        self.dma_owners = []

    def finish(self):
        sp = self.eng["sp"]
        pts = [(o.dsem, o.dcnt) for o in self.dma_owners]
        self._emit_waits("sp", pts)


def PDep(name):
    return Dep(name, ex=True)


class Ctx:
    pass


_uid = [0]


def _alloc(st, nc, name, shape, dt):
    _uid[0] += 1
    return st.enter_context(nc.sbuf_tensor("%s_u%d" % (name, _uid[0]), shape, dt))


def _palloc(st, nc, name, shape, dt):
    _uid[0] += 1
    return st.enter_context(nc.psum_tensor("%s_u%d" % (name, _uid[0]), shape, dt))


def emit_rstd(em, ss, dss, mhalf, dmh, n_feat):
    em.op("pool", lambda e: e.tensor_scalar(out=ss[:, 1:2], in0=ss[:, 0:1], scalar1=1.0 / n_feat,
                                            scalar2=EPS, op0=ALU.mult, op1=ALU.add),
          reads=[dss], writes=[dss])
    em.op("pool", lambda e: e.tensor_tensor(out=ss[:, 2:3], in0=ss[:, 1:2], in1=mhalf[:, 0:1], op=ALU.pow),
          reads=[dss, dmh], writes=[dss])


def emit_prenorm_T(C, em, xs, dxs, gpre, dgpre, hbs, dhbs, ss, dss, tp, dtp, hT_dst, dhT):
    em.op("act", lambda e: e.activation(out=hbs[:, :], in_=xs[:, :], func=AF.Square, accum_out=ss[:, 0:1]),
          reads=[dxs], writes=[dhbs, dss])
    emit_rstd(em, ss, dss, C.mhalf, C.dmh, D)
    em.op("dve", lambda e: e.scalar_tensor_tensor(out=hbs[:, :], in0=xs[:, :], scalar=ss[:, 2:3], in1=gpre[:, :],
                                                  op0=ALU.mult, op1=ALU.mult),
          reads=[dxs, dss, dgpre], writes=[dhbs])

    def tps(e):
        for k in range(8):
            i = e.transpose(tp[:, k, :], hbs[:, k * 128:(k + 1) * 128], C.ident[:, :])
        return i
    em.op("pe", tps, reads=[dhbs, C.dident], writes=[dtp])
    em.op("act", lambda e: e.activation(out=hT_dst, in_=tp[:, :, :], func=AF.Copy), reads=[dtp], writes=[dhT])


def load_gains(C, em, st, nc, l, ipre, ipost, post_scale):
    gpre = _alloc(st, nc, "gpre", [128, D], F32)
    gpost = _alloc(st, nc, "gpost", [128, D], F32)
    dgpre = Dep("gpre")
    dgpost = Dep("gpost")
    em.dma("sp", gpre[:, :], C.ng[l, ipre, :].partition_broadcast(128), writes=[dgpre])
    em.dma("sp", gpost[:, :], C.ng[l, ipost, :].partition_broadcast(128), writes=[dgpost])
    if post_scale != 1.0:
        em.op("pool", lambda e: e.tensor_scalar(out=gpost[:, :], in0=gpost[:, :], scalar1=post_scale, scalar2=0.0,
                                                op0=ALU.mult, op1=ALU.add), reads=[dgpost], writes=[dgpost])
    return gpre, dgpre, gpost, dgpost


def emit_post_A(C, em, ps_out, dps, ss, dss, junk, djunk):
    em.op("act", lambda e: e.activation(out=junk[:, :], in_=ps_out, func=AF.Square, accum_out=ss[:, 0:1]),
          reads=[dps], writes=[djunk, dss])
    emit_rstd(em, ss, dss, C.mhalf, C.dmh, D)


def emit_post_B(C, em, ps_out, dps, xs, dxs, gpost, dgpost, ss, dss, dst_ap, ddst):
    em.op("dve", lambda e: e.scalar_tensor_tensor(out=ps_out, in0=ps_out, scalar=ss[:, 2:3], in1=gpost[:, :],
                                                  op0=ALU.mult, op1=ALU.mult),
          reads=[dps, dss, dgpost], writes=[dps])
    em.op("dve", lambda e: e.tensor_tensor(out=xs[:, :], in0=ps_out, in1=xs[:, :], op=ALU.add),
          reads=[dps, dxs], writes=[dxs])
    em.dma("sp", dst_ap, xs[:, :], reads=[dxs], writes=[ddst], owner=dxs)


def emit_post_residual(C, em, ps_out, dps, xs, dxs, gpost, dgpost, ss, dss, junk, djunk, dst_ap, ddst):
    emit_post_A(C, em, ps_out, dps, ss, dss, junk, djunk)
    emit_post_B(C, em, ps_out, dps, xs, dxs, gpost, dgpost, ss, dss, dst_ap, ddst)


def emit_front_A(C, em, xs, dxs, hbs, dhbs, ss, dss):
    em.op("act", lambda e: e.activation(out=hbs[:, :], in_=xs[:, :], func=AF.Square, accum_out=ss[:, 0:1]),
          reads=[dxs], writes=[dhbs, dss])
    emit_rstd(em, ss, dss, C.mhalf, C.dmh, D)


def emit_front_B(C, em, xs, dxs, hbs, dhbs, ss, dss, gpre, dgpre):
    em.op("dve", lambda e: e.scalar_tensor_tensor(out=hbs[:, :], in0=xs[:, :], scalar=ss[:, 2:3], in1=gpre[:, :],
                                                  op0=ALU.mult, op1=ALU.mult),
          reads=[dxs, dss, dgpre], writes=[dhbs])


def emit_ffn(C, l, which, src, dsrc):
    nc, em = C.nc, C.em
    NJ = DFF // 128
    LAG = 3
    with ExitStack() as st:
        Win = _alloc(st, nc, "Win", [128, 8, 2 * DFF], BF16)
        Wout = _alloc(st, nc, "Wout", [128, NJ, D], BF16)
        dWin = [Dep("Win%d" % k) for k in range(4)]
        dWout = [Dep("Wout%d" % k) for k in range(2)]
        w_in = C.fwi[l, which]
        w_out = C.fwo[l, which]
        wi_v = w_in.rearrange("(k p) f -> p k f", p=128)
        for k in range(4):
            em.dma("pool", Win[:, 2 * k:2 * k + 2, :], wi_v[:, 2 * k:2 * k + 2, :], writes=[dWin[k]])
        wo_v = w_out.rearrange("(j p) d -> p j d", p=128)
        em.dma("pool", Wout[:, 0:11, :], wo_v[:, 0:11, :], writes=[dWout[0]])
        em.dma("pool", Wout[:, 11:22, :], wo_v[:, 11:22, :], writes=[dWout[1]])
        gpre, dgpre, gpost, dgpost = load_gains(C, em, st, nc, l, 0 if which == 0 else 4, 1 if which == 0 else 5, 0.5)

        xa = [_alloc(st, nc, "xa%d" % i, [128, D], F32) for i in range(3)]
        dxa = [Dep("xa%d" % i) for i in range(3)]
        xb = [_alloc(st, nc, "xb%d" % i, [128, D], F32) for i in range(3)]
        dxb = [Dep("xb%d" % i) for i in range(3)]
        hb = [_alloc(st, nc, "hb%d" % i, [128, D], BF16) for i in range(2)]
        dhb = [Dep("hb%d" % i) for i in range(2)]
        hT = [_alloc(st, nc, "hT%d" % i, [128, 8, 256], BF16) for i in range(2)]
        dhT = [[Dep("hT%d_%d" % (i, c)) for c in range(2)] for i in range(2)]
        NA = 6
        actT = [_alloc(st, nc, "actT%d" % i, [128, 256], BF16) for i in range(NA)]
        dact = [Dep("actT%d" % i) for i in range(NA)]
        sg = [_alloc(st, nc, "sg%d" % i, [128, 256], BF16) for i in range(2)]
        dsg = [Dep("sg%d" % i) for i in range(2)]
        junk = _alloc(st, nc, "junk", [128, D], BF16)
        djunk = Dep("junk")
        sst = [_alloc(st, nc, "ss%d" % i, [128, 4], F32) for i in range(4)]
        dsst = [Dep("ss%d" % i) for i in range(4)]
        tp = [_palloc(st, nc, "tp%d" % i, [128, 8, 128], BF16) for i in range(2)]
        dtp = [PDep("tp%d" % i) for i in range(2)]
        gu = [_palloc(st, nc, "gu%d" % i, [128, 2, 256], F32) for i in range(2)]
        dgu = [PDep("gu%d" % i) for i in range(2)]
        pout = _palloc(st, nc, "pout", [128, 2, D], F32)
        dpout = [PDep("pout%d" % i) for i in range(2)]

        NT = NTOK // 256
        sctr = [0]

        def loads(t):
            for c in range(2):
                ch = 2 * t + c
                em.dma("sp", xa[ch % 3][:, :], src[ch * 128:(ch + 1) * 128, :], reads=[dsrc[ch]], writes=[dxa[ch % 3]])

        fst = {}

        def frontA(t):
            for c in range(2):
                ch = 2 * t + c
                si = sctr[0] % 4
                sctr[0] += 1
                fst[ch] = si
                emit_front_A(C, em, xa[ch % 3], dxa[ch % 3], hb[ch % 2], dhb[ch % 2], sst[si], dsst[si])

        def frontB(t):
            for c in range(2):
                ch = 2 * t + c
                si = fst.pop(ch)
                emit_front_B(C, em, xa[ch % 3], dxa[ch % 3], hb[ch % 2], dhb[ch % 2], sst[si], dsst[si], gpre, dgpre)

        def transp(t, c):
            ch = 2 * t + c
            hbs = hb[ch % 2]

            def tps(e):
                for k in range(8):
                    i = e.transpose(tp[ch % 2][:, k, :], hbs[:, k * 128:(k + 1) * 128], C.ident[:, :])
                return i
            em.op("pe", tps, reads=[dhb[ch % 2], C.dident], writes=[dtp[ch % 2]])
            em.op("act", lambda e: e.activation(out=hT[t % 2][:, :, c * 128:(c + 1) * 128], in_=tp[ch % 2][:, :, :], func=AF.Copy),
                  reads=[dtp[ch % 2]], writes=[dhT[t % 2][c]])

        def xb_loads(t):
            for tc in range(2):
                ch = 2 * t + tc
                em.dma("sp", xb[ch % 3][:, :], src[ch * 128:(ch + 1) * 128, :], reads=[dsrc[ch]], writes=[dxb[ch % 3]])

        def p1(t, j):
            g = gu[j % 2]
            hTt = hT[t % 2]

            def f(e):
                for half in range(2):
                    for k in range(8):
                        i = e.matmul(g[:, half, :], lhsT=Win[:, k, half * DFF + j * 128: half * DFF + (j + 1) * 128],
                                     rhs=hTt[:, k, :], start=(k == 0), stop=(k == 7))
                return i
            em.op("pe", f, reads=dWin + dhT[t % 2], writes=[dgu[j % 2]])
            s = sg[j % 2]
            em.op("act", lambda e: e.activation(out=s[:, :], in_=g[:, 0, :], func=AF.Silu),
                  reads=[dgu[j % 2]], writes=[dsg[j % 2]])
            a = actT[j % NA]
            em.op("dve", lambda e: e.tensor_tensor(out=a[:, :], in0=g[:, 1, :], in1=s[:, :], op=ALU.mult),
                  reads=[dgu[j % 2], dsg[j % 2]], writes=[dact[j % NA]])

        def p2(t, j):
            a = actT[j % NA]

            def f(e):
                for tc in range(2):
                    for half in range(2):
                        i = e.matmul(pout[:, tc, half * 512:(half + 1) * 512], lhsT=a[:, tc * 128:(tc + 1) * 128],
                                     rhs=Wout[:, j, half * 512:(half + 1) * 512], start=(j == 0), stop=(j == NJ - 1))
                return i
            em.op("pe", f, reads=[dact[j % NA], dWout[0 if j < 11 else 1]], writes=dpout)

        est = {}

        def epilogueA(t):
            for tc in range(2):
                si = sctr[0] % 4
                sctr[0] += 1
                est[(t, tc)] = si
                emit_post_A(C, em, pout[:, tc, :], dpout[tc], sst[si], dsst[si], junk, djunk)

        def epilogueB(t):
            for tc in range(2):
                ch = 2 * t + tc
                si = est.pop((t, tc))
                emit_post_B(C, em, pout[:, tc, :], dpout[tc], xb[ch % 3], dxb[ch % 3], gpost, dgpost, sst[si], dsst[si],
                            C.y[ch * 128:(ch + 1) * 128, :], C.dy[ch])

        loads(0)
        frontA(0)
        frontB(0)
        transp(0, 0)
        transp(0, 1)
        if NT > 1:
            loads(1)
        for t in range(NT):
            for j in range(NJ + LAG):
                if j < NJ:
                    p1(t, j)
                if j >= LAG:
                    p2(t, j - LAG)
                if j == 1 and t >= 1:
                    epilogueB(t - 1)
                if t + 1 < NT:
                    if j == 4:
                        frontA(t + 1)
                    elif j == 7:
                        frontB(t + 1)
                    elif j == 11:
                        transp(t + 1, 0)
                    elif j == 15:
                        transp(t + 1, 1)
                    elif j == 18 and t + 2 < NT:
                        loads(t + 2)
                if j == 13:
                    xb_loads(t)
            epilogueA(t)
        epilogueB(NT - 1)
        em.barrier()


def emit_attn(C, l, src, dsrc):
    nc, em = C.nc, C.em
    SCALE = 0.125
    with ExitStack() as st:
        Wqkv = _alloc(st, nc, "Wqkv", [128, 8, 1536], BF16)
        Wo = _alloc(st, nc, "Wo", [128, 8, D], BF16)
        dWqkv, dWo = Dep("Wqkv"), Dep("Wo")
        em.dma("pool", Wqkv[:, :, :], C.wqkv[0].rearrange("(k p) f -> p k f", p=128), writes=[dWqkv])
        em.dma("pool", Wo[:, :, :], C.wo[0].rearrange("(k p) f -> p k f", p=128), writes=[dWo])
        gpre, dgpre, gpost, dgpost = load_gains(C, em, st, nc, l, 2, 3, 1.0)
        sinkt = _alloc(st, nc, "sinkt", [128, 16], F32)
        nsink = _alloc(st, nc, "nsink", [128, 16], F32)
        dsink = Dep("sink")
        em.dma("sp", sinkt[:, :], C.sink[0, :].partition_broadcast(128), writes=[dsink])
        em.op("pool", lambda e: e.tensor_scalar(out=nsink[:, :], in0=sinkt[:, :], scalar1=-1.0, scalar2=0.0,
                                                op0=ALU.mult, op1=ALU.add), reads=[dsink], writes=[dsink])

        kT = _alloc(st, nc, "kT_all", [128, 8, 34 * 128], BF16)
        vA = _alloc(st, nc, "v_all", [128, 34, 256], BF16)
        dkT = [Dep("kT%d" % i) for i in range(34)]
        dvA = [Dep("vA%d" % i) for i in range(34)]
        for i in (0, 33):
            em.op("dve", lambda e, i=i: e.memset(kT[:, :, i * 128:(i + 1) * 128], 0.0), writes=[dkT[i]])
            em.op("dve", lambda e, i=i: e.memset(vA[:, i, :], 0.0), writes=[dvA[i]])

        NX = 5
        xa = [_alloc(st, nc, "xa%d" % i, [128, D], F32) for i in range(NX)]
        dxa = [Dep("xa%d" % i) for i in range(NX)]
        hb = [_alloc(st, nc, "hb%d" % i, [128, D], BF16) for i in range(2)]
        dhb = [Dep("hb%d" % i) for i in range(2)]
        hT = [_alloc(st, nc, "hT%d" % i, [128, 8, 128], BF16) for i in range(2)]
        dhT = [Dep("hT%d" % i) for i in range(2)]
        cs = [_alloc(st, nc, "cs%d" % i, [128, 320], F32) for i in range(2)]
        dcs = [Dep("cs%d" % i) for i in range(2)]
        mk = [_alloc(st, nc, "mk%d" % i, [128, 384], BF16) for i in range(2)]
        dmk = [Dep("mk%d" % i) for i in range(2)]
        qtok = [_alloc(st, nc, "qtok%d" % i, [128, 16, 64], BF16) for i in range(2)]
        dqtok = [Dep("qtok%d" % i) for i in range(2)]
        kdtok = [_alloc(st, nc, "kdtok%d" % i, [128, 4, 2, 128], BF16) for i in range(2)]
        dkdtok = [Dep("kdtok%d" % i) for i in range(2)]
        for i in range(2):
            em.op("dve", lambda e, i=i: e.memset(kdtok[i][:, :, :, :], 0.0), writes=[dkdtok[i]])
        rt = [_alloc(st, nc, "rt%d" % i, [128, 20, 8], F32) for i in range(4)]
        drt = [Dep("rt%d" % i) for i in range(4)]
        NQ = 3
        qT = [_alloc(st, nc, "qT%d" % i, [128, 8, 128], BF16) for i in range(NQ)]
        dqT = [Dep("qT%d" % i) for i in range(NQ)]
        pb = [_alloc(st, nc, "pb%d" % i, [128, 384], BF16) for i in range(4)]
        dpb = [Dep("pb%d" % i) for i in range(4)]
        pT = [_alloc(st, nc, "pT%d" % i, [128, 3, 128], BF16) for i in range(3)]
        dpT = [Dep("pT%d" % i) for i in range(3)]
        stt = [_alloc(st, nc, "stt%d" % i, [128, 6, 16], F32) for i in range(2)]
        dsth = [[Dep("st%d_%d" % (i, h)) for h in range(16)] for i in range(2)]
        dfin = [[Dep("fin%d_%d" % (i, k)) for k in range(2)] for i in range(2)]
        otok = [_alloc(st, nc, "otok%d" % i, [128, 16, 64], BF16) for i in range(2)]
        dotok = [[Dep("otok%d_%d" % (i, k)) for k in range(2)] for i in range(2)]
        oT = _alloc(st, nc, "oT", [128, 8, 128], BF16)
        doT = Dep("oT")
        junk = _alloc(st, nc, "junk", [128, D], BF16)
        djunk = Dep("junk")
        sst = [_alloc(st, nc, "ss%d" % i, [128, 4], F32) for i in range(4)]
        dsst = [Dep("ss%d" % i) for i in range(4)]

        qkv = _palloc(st, nc, "qkv", [128, 1024], F32)
        dq01 = PDep("qkv01")
        tp = _palloc(st, nc, "tp", [128, 8, 128], BF16)
        dtp = PDep("tp")
        sps = _palloc(st, nc, "sps", [128, 4, 512], F32)
        dsps = [PDep("sps%d" % i) for i in range(4)]
        ops = _palloc(st, nc, "ops", [128, 8, 64], F32)
        dops = PDep("ops")
        sctr = [0]

        fsl = {}
        csl = {}
        def A_steps(b):
            xs, dxs = xa[b % NX], dxa[b % NX]
            c_, dc_ = cs[b % 2], dcs[b % 2]
            h_ = hT[b % 2]
            qt, dqt = qtok[b % 2], dqtok[b % 2]
            kd, dkd = kdtok[b % 2], dkdtok[b % 2]

            def a_front():
                si = sctr[0] % 4
                sctr[0] += 1
                fsl[b] = si
                emit_front_A(C, em, xs, dxs, hb[b % 2], dhb[b % 2], sst[si], dsst[si])

            def a_frontB():
                si = fsl.pop(b)
                emit_front_B(C, em, xs, dxs, hb[b % 2], dhb[b % 2], sst[si], dsst[si], gpre, dgpre)

            def a0():
                hbs = hb[b % 2]

                def tps(e):
                    for k in range(8):
                        i = e.transpose(tp[:, k, :], hbs[:, k * 128:(k + 1) * 128], C.ident[:, :])
                    return i
                em.op("pe", tps, reads=[dhb[b % 2], C.dident], writes=[dtp])
                em.op("act", lambda e: e.activation(out=h_[:, :, :], in_=tp[:, :, :], func=AF.Copy), reads=[dtp], writes=[dhT[b % 2]])

            def a1q():
                for g in range(2):
                    def f(e, g=g):
                        for k in range(8):
                            i = e.matmul(qkv[:, g * 512:(g + 1) * 512], lhsT=h_[:, k, :], rhs=Wqkv[:, k, g * 512:(g + 1) * 512],
                                         start=(k == 0), stop=(k == 7))
                        return i
                    em.op("pe", f, reads=[dhT[b % 2], dWqkv], writes=[dq01])
                qv = qkv[:, 0:1024].rearrange("p (h d) -> p h d", d=64)
                em.op("act", lambda e: e.activation(out=qt[:, :, 16:64], in_=qv[:, :, 16:64], func=AF.Copy, scale=SCALE),
                      reads=[dq01], writes=[dqt])

            def a2q():
                qv = qkv[:, 0:1024].rearrange("p (h d) -> p h d", d=64)
                cosv = c_[:, 0:160].rearrange("p (h d) -> p h d", d=8)[:, 0:16, :]
                sinv = c_[:, 160:320].rearrange("p (h d) -> p h d", d=8)[:, 0:16, :]
                x1, x2 = qv[:, :, 0:8], qv[:, :, 8:16]
                for i, (xx, tb) in enumerate([(x1, cosv), (x2, sinv), (x2, cosv), (x1, sinv)]):
                    em.op("dve", lambda e, i=i, xx=xx, tb=tb: e.tensor_tensor(out=rt[i][:, 0:16, :], in0=xx, in1=tb, op=ALU.mult),
                          reads=[dq01, dc_], writes=[drt[i]])
                em.op("dve", lambda e: e.tensor_tensor(out=qt[:, :, 0:8], in0=rt[0][:, 0:16, :], in1=rt[1][:, 0:16, :], op=ALU.subtract),
                      reads=[drt[0], drt[1]], writes=[dqt])
                em.op("dve", lambda e: e.tensor_tensor(out=qt[:, :, 8:16], in0=rt[2][:, 0:16, :], in1=rt[3][:, 0:16, :], op=ALU.add),
                      reads=[drt[2], drt[3]], writes=[dqt])

            def a1kv():
                def f(e):
                    for k in range(8):
                        i = e.matmul(qkv[:, 0:512], lhsT=h_[:, k, :], rhs=Wqkv[:, k, 1024:1536], start=(k == 0), stop=(k == 7))
                    return i
                em.op("pe", f, reads=[dhT[b % 2], dWqkv], writes=[dq01])
                kv = qkv[:, 0:256].rearrange("p (h d) -> p h d", d=64)
                for dup in range(2):
                    em.op("act", lambda e, dup=dup: e.activation(out=kd[:, :, dup, dup * 64 + 16:dup * 64 + 64], in_=kv[:, :, 16:64],
                                                                 func=AF.Copy), reads=[dq01], writes=[dkd])
                em.op("act", lambda e: e.activation(out=vA[:, b + 1, :], in_=qkv[:, 256:512], func=AF.Copy),
                      reads=[dq01], writes=[dvA[b + 1]])

            def a2kv():
                kv = qkv[:, 0:256].rearrange("p (h d) -> p h d", d=64)
                cosv = c_[:, 0:160].rearrange("p (h d) -> p h d", d=8)[:, 16:20, :]
                sinv = c_[:, 160:320].rearrange("p (h d) -> p h d", d=8)[:, 16:20, :]
                x1, x2 = kv[:, :, 0:8], kv[:, :, 8:16]
                for i, (xx, tb) in enumerate([(x1, cosv), (x2, sinv), (x2, cosv), (x1, sinv)]):
                    em.op("dve", lambda e, i=i, xx=xx, tb=tb: e.tensor_tensor(out=rt[i][:, 16:20, :], in0=xx, in1=tb, op=ALU.mult),
                          reads=[dq01, dc_], writes=[drt[i]])
                for dup in range(2):
                    em.op("dve", lambda e, dup=dup: e.tensor_tensor(out=kd[:, :, dup, dup * 64:dup * 64 + 8], in0=rt[0][:, 16:20, :],
                                                                    in1=rt[1][:, 16:20, :], op=ALU.subtract),
                          reads=[drt[0], drt[1]], writes=[dkd])
                    em.op("dve", lambda e, dup=dup: e.tensor_tensor(out=kd[:, :, dup, dup * 64 + 8:dup * 64 + 16], in0=rt[2][:, 16:20, :],
                                                                    in1=rt[3][:, 16:20, :], op=ALU.add),
                          reads=[drt[2], drt[3]], writes=[dkd])

            def a3():
                qflat = qt[:, :, :].rearrange("p h d -> p (h d)")

                def tq(e):
                    for k in range(8):
                        i = e.transpose(tp[:, k, :], qflat[:, k * 128:(k + 1) * 128], C.ident[:, :])
                    return i
                em.op("pe", tq, reads=[dqt, C.dident], writes=[dtp])
                em.op("act", lambda e: e.activation(out=qT[b % NQ][:, :, :], in_=tp[:, :, :], func=AF.Copy),
                      reads=[dtp], writes=[dqT[b % NQ]])

            def a4():
                kflat = kd[:, :, :, :].rearrange("p g u d -> p (g u d)")

                def tk(e):
                    for k in range(8):
                        i = e.transpose(tp[:, k, :], kflat[:, k * 128:(k + 1) * 128], C.ident[:, :])
                    return i
                em.op("pe", tk, reads=[dkd, C.dident], writes=[dtp])
                em.op("act", lambda e: e.activation(out=kT[:, :, (b + 1) * 128:(b + 2) * 128], in_=tp[:, :, :], func=AF.Copy),
                      reads=[dtp], writes=[dkT[b + 1]])
            return [a_front, a0, a1q, a2q, a1kv, a2kv, a3, a4, a_frontB]

        def A_loads(b):
            em.dma("sp", xa[b % NX][:, :], src[b * 128:(b + 1) * 128, :], reads=[dsrc[b]], writes=[dxa[b % NX]])
            em.dma("sp", cs[b % 2][:, :], C.acs[b * 128:(b + 1) * 128, :], writes=[dcs[b % 2]])

        def C_steps(b):
            ot = otok[b % 2]

            def c0():
                oflat = ot[:, :, :].rearrange("p h d -> p (h d)")

                def to(e):
                    for k in range(8):
                        i = e.transpose(tp[:, k, :], oflat[:, k * 128:(k + 1) * 128], C.ident[:, :])
                    return i
                em.op("pe", to, reads=dotok[b % 2] + [C.dident], writes=[dtp])
                em.op("act", lambda e: e.activation(out=oT[:, :, :], in_=tp[:, :, :], func=AF.Copy), reads=[dtp], writes=[doT])

            def c1():
                def fw(e):
                    for half in range(2):
                        for k in range(8):
                            i = e.matmul(qkv[:, half * 512:(half + 1) * 512], lhsT=oT[:, k, :], rhs=Wo[:, k, half * 512:(half + 1) * 512],
                                         start=(k == 0), stop=(k == 7))
                    return i
                em.op("pe", fw, reads=[doT, dWo], writes=[dq01])

            def c2():
                si = sctr[0] % 4
                sctr[0] += 1
                csl[b] = si
                emit_post_A(C, em, qkv[:, 0:1024], dq01, sst[si], dsst[si], junk, djunk)

            def c2B():
                si = csl.pop(b)
                emit_post_B(C, em, qkv[:, 0:1024], dq01, xa[b % NX], dxa[b % NX], gpost, dgpost, sst[si], dsst[si],
                            C.y[b * 128:(b + 1) * 128, :], C.dy[b])
            return [c0, c1, c2, c2B]

        def finish_heads(b, h0):
            s_ = stt[b % 2]
            k = h0 // 8
            dsts = dsth[b % 2][h0:h0 + 8]
            df = dfin[b % 2][k]
            hs = slice(h0, h0 + 8)
            em.op("dve", lambda e: e.tensor_tensor(out=s_[:, 3, hs], in0=s_[:, 1, hs], in1=sinkt[:, hs], op=ALU.add),
                  reads=dsts + [dsink], writes=[df])
            em.op("act", lambda e: e.activation(out=s_[:, 4, hs], in_=s_[:, 3, hs], func=AF.Exp), reads=[df], writes=[df])
            em.op("dve", lambda e: e.tensor_tensor(out=s_[:, 4, hs], in0=s_[:, 4, hs], in1=s_[:, 2, hs], op=ALU.add),
                  reads=dsts + [df], writes=[df])
            em.op("dve", lambda e: e.reciprocal(out=s_[:, 5, hs], in_=s_[:, 4, hs]), reads=[df], writes=[df])
            em.op("dve", lambda e: e.tensor_tensor(out=otok[b % 2][:, hs, :], in0=ops[:, :, :],
                                                   in1=s_[:, 5, hs].unsqueeze(2).broadcast_to([128, 8, 64]), op=ALU.mult),
                  reads=[dops, df], writes=[dotok[b % 2][k]])

        def stageB(b, steps):
            m_, dm_ = mk[b % 2], dmk[b % 2]
            em.dma("sp", m_[:, :], C.amask[b], writes=[dm_])
            s_ = stt[b % 2]
            q_ = qT[b % NQ]

            def S(h):
                g, pr, hf = h // 4, h // 2, h % 2
                bk = h % 4
                bank = sps[:, bk, 0:384]

                def f(e):
                    e.matmul(bank, lhsT=q_[:, pr, :], rhs=kT[:, g * 2 + hf, b * 128: b * 128 + 384], start=True, stop=False)
                    return e.matmul(bank, lhsT=C.ident[:, :], rhs=m_[:, :], start=False, stop=True)
                em.op("pe", f, reads=[dqT[b % NQ], dkT[b], dkT[b + 1], dkT[b + 2], dm_, C.dident], writes=[dsps[bk]])
                dst = dsth[b % 2][h]
                em.op("dve", lambda e: e.tensor_reduce(out=s_[:, 0, h:h + 1], in_=bank, op=ALU.max, axis=AX.X, negate=True),
                      reads=[dsps[bk]], writes=[dst])
                em.op("dve", lambda e: e.tensor_tensor(out=s_[:, 1, h:h + 1], in0=s_[:, 0, h:h + 1], in1=nsink[:, h:h + 1], op=ALU.min),
                      reads=[dst, dsink], writes=[dst])
                em.op("act", lambda e: e.activation(out=pb[bk][:, :], in_=bank, func=AF.Exp,
                                                    bias=s_[:, 1, h:h + 1], accum_out=s_[:, 2, h:h + 1]),
                      reads=[dsps[bk], dst], writes=[dpb[bk], dst])

            def T(h):
                bk = h % 4
                tb = sps[:, bk, :].bitcast(BF16)[:, 0:384].rearrange("p (c t) -> p c t", t=128)

                def ft(e):
                    for c in range(3):
                        r = e.transpose(tb[:, c, :], pb[bk][:, c * 128:(c + 1) * 128], C.ident[:, :])
                    return r
                em.op("pe", ft, reads=[dpb[bk], C.dident], writes=[dsps[bk]])
                if h % 2 == 0:
                    em.op("dve", lambda e: e.tensor_copy(out=pT[h % 3][:, :, :], in_=tb), reads=[dsps[bk]], writes=[dpT[h % 3]])
                else:
                    em.op("act", lambda e: e.activation(out=pT[h % 3][:, :, :], in_=tb, func=AF.Copy), reads=[dsps[bk]], writes=[dpT[h % 3]])

            def PV(h):
                g = h // 4

                def fo(e):
                    for c in range(3):
                        r = e.matmul(ops[:, h % 8, :], lhsT=pT[h % 3][:, c, :], rhs=vA[:, b + c, g * 64:(g + 1) * 64],
                                     start=(c == 0), stop=(c == 2))
                    return r
                em.op("pe", fo, reads=[dpT[h % 3], dvA[b], dvA[b + 1], dvA[b + 2]], writes=[dops])
                if h % 8 == 7:
                    finish_heads(b, h - 7)

            for h in range(3):
                S(h)
            for h in range(16):
                T(h)
                if h >= 1:
                    PV(h - 1)
                if h + 3 < 16:
                    S(h + 3)
                if h in (1, 3, 5, 7, 9, 11, 13, 14, 15) and steps:
                    steps.pop(0)()
            PV(15)
            while steps:
                steps.pop(0)()

        nblk = getattr(C, "nblk", NCH)
        nop = lambda: None

        def run_A(b):
            A = A_steps(b)
            for k in (0, 8, 1, 2, 3, 4, 5, 6, 7):
                A[k]()
        for b0 in range(min(2, NCH)):
            A_loads(b0)
        run_A(0)
        if NCH > 2:
            A_loads(2)
        if NCH > 1:
            run_A(1)
        for b in range(nblk):
            Cs = C_steps(b - 1) if b >= 1 else [nop] * 4
            As = A_steps(b + 2) if b + 2 < NCH else [nop] * 9
            ld = (lambda b=b: A_loads(b + 3)) if b + 3 < NCH else nop
            steps = [lambda Cs=Cs, As=As: (Cs[0](), As[0]()),
                     lambda Cs=Cs, As=As: (Cs[1](), As[8]()),
                     lambda Cs=Cs, As=As: (Cs[2](), As[1]()),
                     lambda Cs=Cs, ld=ld: (Cs[3](), ld()),
                     As[2], As[3], lambda As=As: (As[4](), As[6]()), As[5], As[7]]
            stageB(b, steps)
        for s_ in C_steps(nblk - 1):
            s_()
        em.barrier()


def emit_ret(C, l, src, dsrc):
    nc, em = C.nc, C.em
    nch = getattr(C, "nblk", NCH)
    half_b = NCH // 2
    with ExitStack() as st0:
        bnd = _alloc(st0, nc, "bnd", [128, 1], F32)
        kdec = _alloc(st0, nc, "kdec", [128, 8], F32)
        g128 = _alloc(st0, nc, "g128", [128, 8], F32)
        decrow = _alloc(st0, nc, "decrow", [128, 8, 128], F32)
        DT = _alloc(st0, nc, "DT", [128, 4, 128], F32)
        S32 = _alloc(st0, nc, "S32", [128, 4, 2, 512], F32)
        stT = ExitStack()
        rc = _alloc(stT, nc, "rc", [128, 6, 128], F32)
        cpos = _alloc(stT, nc, "cpos", [128, 4], F32)
        dl = _alloc(stT, nc, "dl", [128, 8], F32)
        lg = _alloc(stT, nc, "lg", [128, 8], F32)
        tmpD = _alloc(stT, nc, "tmpD", [128, 2, 128], F32)
        drc, dtab, dS32, dtmp = Dep("rc"), Dep("tab"), Dep("S32"), Dep("tmpD")
        em.dma("sp", rc[:, :, :], C.rconst[:, :, :], writes=[drc])
        em.dma("sp", cpos[:, :], C.rpos[:, :], writes=[drc], owner=drc)
        em.dma("sp", bnd[:, :], C.rbnd[:, :], writes=[drc], owner=drc)
        em.dma("sp", dl[:, 0:4], C.rdf[0, :].partition_broadcast(128), writes=[dtab])
        em.dma("sp", dl[:, 4:8], C.rdb[0, :].partition_broadcast(128), writes=[dtab], owner=dtab)
        em.op("act", lambda e: e.activation(out=lg[:, :], in_=dl[:, :], func=AF.Exp, scale=-1.0), reads=[dtab], writes=[dtab])
        em.op("dve", lambda e: e.tensor_scalar(out=lg[:, :], in0=lg[:, :], scalar1=1.0, scalar2=None, op0=ALU.add), reads=[dtab], writes=[dtab])
        em.op("act", lambda e: e.activation(out=lg[:, :], in_=lg[:, :], func=AF.Ln), reads=[dtab], writes=[dtab])
        em.op("dve", lambda e: e.tensor_scalar(out=lg[:, :], in0=lg[:, :], scalar1=-1.0, scalar2=None, op0=ALU.mult), reads=[dtab], writes=[dtab])
        em.op("act", lambda e: e.activation(out=g128[:, :], in_=lg[:, :], func=AF.Exp, scale=128.0), reads=[dtab], writes=[dtab])
        em.op("act", lambda e: e.activation(out=kdec[:, 0:4], in_=lg[:, 0:4], func=AF.Exp, scale=cpos[:, 2:3]), reads=[dtab, drc], writes=[dtab])
        em.op("act", lambda e: e.activation(out=kdec[:, 4:8], in_=lg[:, 4:8], func=AF.Exp, scale=cpos[:, 3:4]), reads=[dtab, drc], writes=[dtab])
        em.op("dve", lambda e: e.tensor_scalar(out=kdec[:, :], in0=kdec[:, :], scalar1=1.0 / 16, scalar2=None, op0=ALU.mult), reads=[dtab], writes=[dtab])
        for h in range(4):
            em.op("act", lambda e, h=h: e.activation(out=decrow[:, h, :], in_=rc[:, 4, :], func=AF.Exp, scale=lg[:, h:h + 1]), reads=[dtab, drc], writes=[dtab])
            em.op("act", lambda e, h=h: e.activation(out=decrow[:, 4 + h, :], in_=rc[:, 5, :], func=AF.Exp, scale=lg[:, 4 + h:5 + h]), reads=[dtab, drc], writes=[dtab])
            em.op("act", lambda e, h=h: e.activation(out=tmpD[:, 0, :], in_=rc[:, 0, :], func=AF.Exp, scale=lg[:, h:h + 1]), reads=[dtab, drc], writes=[dtmp])
            em.op("act", lambda e, h=h: e.activation(out=tmpD[:, 1, :], in_=rc[:, 1, :], func=AF.Exp, scale=lg[:, 4 + h:5 + h]), reads=[dtab, drc], writes=[dtmp])
            em.op("dve", lambda e, h=h: e.tensor_tensor(out=tmpD[:, :, :], in0=tmpD[:, :, :], in1=rc[:, 2:4, :], op=ALU.mult), reads=[dtmp, drc], writes=[dtmp])
            em.op("dve", lambda e, h=h: e.tensor_tensor(out=DT[:, h, :], in0=tmpD[:, 0, :], in1=tmpD[:, 1, :], op=ALU.add), reads=[dtmp], writes=[dtab])
        em.op("dve", lambda e: e.memset(S32[:, :, :, :], 0.0), writes=[dS32])
        em.barrier()
        stT.close()

        def rotary(ps, dps, c_, dc_, rt, drt, out_bf, dout):
            p4 = ps.rearrange("p (h t f) -> p h t f", h=4, t=2)
            o4 = out_bf[:, :].rearrange("p (h t f) -> p h t f", h=4, t=2)
            cosb = c_[:, 0:128].unsqueeze(1).broadcast_to([128, 4, 128])
            sinb = c_[:, 128:256].unsqueeze(1).broadcast_to([128, 4, 128])
            x1, x2 = p4[:, :, 0, :], p4[:, :, 1, :]
            prods = [(x1, cosb), (x2, sinb), (x2, cosb), (x1, sinb)]
            for i in (0, 1):
                xx, tb = prods[i]
                em.op("dve", lambda e, i=i, xx=xx, tb=tb: e.tensor_tensor(out=rt[i][:, :, :], in0=xx, in1=tb, op=ALU.mult),
                      reads=[dps, dc_], writes=[drt[i]])
            em.op("dve", lambda e: e.tensor_tensor(out=o4[:, :, 0, :], in0=rt[0][:, :, :], in1=rt[1][:, :, :], op=ALU.subtract),
                  reads=[drt[0], drt[1]], writes=[dout])
            for i in (2, 3):
                xx, tb = prods[i]
                em.op("dve", lambda e, i=i, xx=xx, tb=tb: e.tensor_tensor(out=rt[i][:, :, :], in0=xx, in1=tb, op=ALU.mult),
                      reads=[dps, dc_], writes=[drt[i]])
            em.op("dve", lambda e: e.tensor_tensor(out=o4[:, :, 1, :], in0=rt[2][:, :, :], in1=rt[3][:, :, :], op=ALU.add),
                  reads=[drt[2], drt[3]], writes=[dout])

        def state_update(S32, dS32, kd_tok, dkd, v_tok, dv, dsp, dsd, gcol0):
            for h in range(4):
                for c in range(2):
                    def f(e, h=h, c=c):
                        return e.matmul(dsp[:, :], lhsT=kd_tok[:, h * 256 + c * 128: h * 256 + (c + 1) * 128],
                                        rhs=v_tok[:, h * 512:(h + 1) * 512], start=True, stop=True)
                    em.op("pe", f, reads=[dkd, dv], writes=[dsd])
                    em.op("dve", lambda e, h=h, c=c: e.scalar_tensor_tensor(out=S32[:, h, c, :], in0=S32[:, h, c, :],
                                                                          scalar=g128[:, gcol0 + h:gcol0 + h + 1], in1=dsp[:, :],
                                                                          op0=ALU.mult, op1=ALU.add),
                          reads=[dS32, dsd, dtab], writes=[dS32])

        def boundary(S32, dS32):
            em.op("dve", lambda e: e.tensor_scalar(out=S32[:, :, :, :], in0=S32[:, :, :, :], scalar1=bnd[:, 0:1], scalar2=None,
                                                   op0=ALU.mult), reads=[dS32, drc], writes=[dS32])

        with ExitStack() as st:
            Wkv = _alloc(st, nc, "Wkv", [128, 8, 3072], BF16)
            dWkv = [Dep("Wkv%d" % k) for k in range(2)]
            wv = C.rwi[0].rearrange("(k p) f -> p k f", p=128)
            em.dma("pool", Wkv[:, 0:4, :], wv[:, 0:4, 1024:4096], writes=[dWkv[0]])
            em.dma("pool", Wkv[:, 4:8, :], wv[:, 4:8, 1024:4096], writes=[dWkv[1]])
            gpre = _alloc(st, nc, "gpre", [128, D], F32)
            dgpre = Dep("gpre")
            em.dma("sp", gpre[:, :], C.ng[l, 2, :].partition_broadcast(128), writes=[dgpre])
            xa = [_alloc(st, nc, "xa%d" % i, [128, D], F32) for i in range(3)]
            dxa = [Dep("xa%d" % i) for i in range(3)]
            hb2 = [_alloc(st, nc, "hb%d" % i, [128, D], BF16) for i in range(2)]
            dhb2 = [Dep("hb%d" % i) for i in range(2)]
            hT = [_alloc(st, nc, "hT%d" % i, [128, 8, 128], BF16) for i in range(2)]
            dhT = [Dep("hT%d" % i) for i in range(2)]
            NCS = 4
            cs = [_alloc(st, nc, "rcs%d" % i, [128, 256], F32) for i in range(NCS)]
            dcs = [Dep("rcs%d" % i) for i in range(NCS)]
            rt = [_alloc(st, nc, "rrt%d" % i, [128, 4, 128], F32) for i in range(4)]
            drt = [Dep("rrt%d" % i) for i in range(4)]
            krot = [_alloc(st, nc, "krot%d" % i, [128, 1024], BF16) for i in range(2)]
            dkrot = [Dep("krot%d" % i) for i in range(2)]
            kf = [_alloc(st, nc, "kf%d" % i, [128, 1024], BF16) for i in range(2)]
            dkf = [Dep("kf%d" % i) for i in range(2)]
            kb = [_alloc(st, nc, "kb%d" % i, [128, 1024], BF16) for i in range(2)]
            dkb = [Dep("kb%d" % i) for i in range(2)]
            kTb = [_alloc(st, nc, "kTb%d" % i, [128, 8, 128], BF16) for i in range(2)]
            dkTb = [Dep("kTb%d" % i) for i in range(2)]
            vtok = [_alloc(st, nc, "vtok%d" % i, [128, 2048], BF16) for i in range(2)]
            dvtok = [[Dep("vtok%d_%d" % (i, k)) for k in range(2)] for i in range(2)]
            Sbf2 = [_alloc(st, nc, "Sbf%d" % i, [128, 4096], BF16) for i in range(2)]
            dSbfh2 = [[Dep("Sbf%d_%d" % (i, h)) for h in range(4)] for i in range(2)]
            dS32h = [Dep("S32b_%d" % h) for h in range(4)]
            em.op("dve", lambda e: e.memset(S32[:, :, :, :], 0.0), reads=[dS32], writes=dS32h)
            sst = [_alloc(st, nc, "ss%d" % i, [128, 4], F32) for i in range(3)]
            dsst = [Dep("ss%d" % i) for i in range(3)]
            tp = _palloc(st, nc, "tp", [128, 8, 128], BF16)
            dtp = PDep("tp")
            pk = _palloc(st, nc, "pk", [128, 1024], F32)
            dpk = PDep("pk")
            pv1 = _palloc(st, nc, "pv", [128, 1024], F32)
            pv = [pv1, pv1]
            dpv1 = PDep("pv")
            dpv = [dpv1, dpv1]
            dspr = [_palloc(st, nc, "dsp%d" % i, [128, 512], F32) for i in range(2)]
            dsdr = [PDep("dsp%d" % i) for i in range(2)]

            def loads1(n):
                em.dma("sp", xa[n % 3][:, :], src[n * 128:(n + 1) * 128, :], reads=[dsrc[n]], writes=[dxa[n % 3]])
                em.dma("sp", cs[n % NCS][:, :], C.rcs[n * 128:(n + 1) * 128, :], writes=[dcs[n % NCS]])

            def front1A(n):
                r = n % 2
                emit_front_A(C, em, xa[n % 3], dxa[n % 3], hb2[r], dhb2[r], sst[n % 3], dsst[n % 3])

            def front1B(n):
                r = n % 2
                emit_front_B(C, em, xa[n % 3], dxa[n % 3], hb2[r], dhb2[r], sst[n % 3], dsst[n % 3], gpre, dgpre)

            def front1(n):
                front1A(n)
                front1B(n)

            def P1_steps(n):
                r = n % 2
                c_, dc_ = cs[n % NCS], dcs[n % NCS]

                def p0():
                    hbs = hb2[r]

                    def tps(e):
                        for k in range(8):
                            i = e.transpose(tp[:, k, :], hbs[:, k * 128:(k + 1) * 128], C.ident[:, :])
                        return i
                    em.op("pe", tps, reads=[dhb2[r], C.dident], writes=[dtp])
                    em.op("act", lambda e: e.activation(out=hT[r][:, :, :], in_=tp[:, :, :], func=AF.Copy), reads=[dtp], writes=[dhT[r]])

                def p1():
                    for g in range(2):
                        def f(e, g=g):
                            for k in range(8):
                                i = e.matmul(pk[:, g * 512:(g + 1) * 512], lhsT=hT[r][:, k, :], rhs=Wkv[:, k, g * 512:(g + 1) * 512],
                                             start=(k == 0), stop=(k == 7))
                            return i
                        em.op("pe", f, reads=[dhT[r]] + dWkv, writes=[dpk])
                    rotary(pk[:, :], dpk, c_, dc_, rt, drt, krot[r], dkrot[r])

                def p2():
                    for (dst, ddst, col) in ((kf[r], dkf[r], 0), (kb[r], dkb[r], 4)):
                        em.op("dve", lambda e, dst=dst, col=col: e.tensor_tensor(
                            out=dst[:, :].rearrange("p (h f) -> p h f", h=4), in0=krot[r][:, :].rearrange("p (h f) -> p h f", h=4),
                            in1=kdec[:, col:col + 4].unsqueeze(2).broadcast_to([128, 4, 256]), op=ALU.mult),
                            reads=[dkrot[r], dtab], writes=[ddst])

                    def tk(e):
                        for k in range(8):
                            i = e.transpose(tp[:, k, :], krot[r][:, k * 128:(k + 1) * 128], C.ident[:, :])
                        return i
                    em.op("pe", tk, reads=[dkrot[r], C.dident], writes=[dtp])
                    em.op("act", lambda e: e.activation(out=kTb[r][:, :, :], in_=tp[:, :, :], func=AF.Copy), reads=[dtp], writes=[dkTb[r]])
                    em.dma("sp", C.s_kT[n], kTb[r][:, :, :].rearrange("p k t -> p (k t)"), reads=[dkTb[r]], writes=[C.dscr[n]], owner=dkTb[r])
                    em.dma("sp", C.s_kf[n], kf[r][:, :], reads=[dkf[r]], writes=[C.dscr[n]], owner=dkf[r])

                def mkv(hv):
                    def pv_():
                        for g in range(2):
                            def f(e, g=g):
                                col = 1024 + hv * 1024 + g * 512
                                for k in range(8):
                                    i = e.matmul(pv[hv][:, g * 512:(g + 1) * 512], lhsT=hT[r][:, k, :], rhs=Wkv[:, k, col:col + 512],
                                                 start=(k == 0), stop=(k == 7))
                                return i
                            em.op("pe", f, reads=[dhT[r]] + dWkv, writes=[dpv[hv]])
                        em.op("act", lambda e: e.activation(out=vtok[r][:, hv * 1024:(hv + 1) * 1024], in_=pv[hv][:, :], func=AF.Copy),
                              reads=[dpv[hv]], writes=[dvtok[r][hv]])
                    return pv_

                def p5():
                    em.dma("sp", C.s_v[n], vtok[r][:, :], reads=dvtok[r], writes=[C.dscr[n]], owner=dvtok[r][0])
                return [p0, p1, mkv(0), mkv(1), p2, p5]

            def U(n, steps):
                r = n % 2
                if n - 3 >= 0:
                    loads1(n - 3)
                if n - 2 >= 0:
                    front1A(n - 2)
                Sbf, dSbfh = Sbf2[n % 2], dSbfh2[n % 2]
                for h in range(4):
                    if h == 2 and n - 2 >= 0:
                        front1B(n - 2)
                    em.op("act", lambda e, h=h: e.activation(out=Sbf[:, h * 1024:(h + 1) * 1024],
                                                             in_=S32[:, h, :, :].rearrange("p c f -> p (c f)"), func=AF.Copy),
                          reads=[dS32h[h]], writes=[dSbfh[h]])
                    if h == 3:
                        em.dma("sp", C.s_sb[n], Sbf[:, :], reads=dSbfh, writes=[C.dscr[n]], owner=dSbfh[0])
                    for c in range(2):
                        if n > 0:
                            dsp, dsd = dspr[c], dsdr[c]

                            def f(e, h=h, c=c, dsp=dsp):
                                return e.matmul(dsp[:, :], lhsT=kb[r][:, h * 256 + c * 128: h * 256 + (c + 1) * 128],
                                                rhs=vtok[r][:, h * 512:(h + 1) * 512], start=True, stop=True)
                            em.op("pe", f, reads=[dkb[r], dvtok[r][h // 2]], writes=[dsd])
                            em.op("dve", lambda e, h=h, c=c, dsp=dsp: e.scalar_tensor_tensor(out=S32[:, h, c, :], in0=S32[:, h, c, :],
                                                                                           scalar=g128[:, 4 + h:5 + h], in1=dsp[:, :],
                                                                                           op0=ALU.mult, op1=ALU.add),
                                  reads=[dS32h[h], dsd, dtab], writes=[dS32h[h]])
                        if steps:
                            steps.pop(0)()
                while steps:
                    steps.pop(0)()
                if n > 0 and n == half_b:
                    em.op("dve", lambda e: e.tensor_scalar(out=S32[:, :, :, :], in0=S32[:, :, :, :], scalar1=bnd[:, 0:1], scalar2=None,
                                                           op0=ALU.mult), reads=dS32h + [drc], writes=dS32h)

            for i in range(1, 4):
                if nch - i >= 0:
                    loads1(nch - i)
            front1(nch - 1)
            if nch > 1:
                front1(nch - 2)
            for s_ in P1_steps(nch - 1):
                s_()
            for n in range(nch - 1, -1, -1):
                U(n, P1_steps(n - 1) if n > 0 else [])
            em.barrier()

        em.op("dve", lambda e: e.memset(S32[:, :, :, :], 0.0), writes=[dS32])
        with ExitStack() as st:
            Wqg = _alloc(st, nc, "Wqg", [128, 8, 3072], BF16)
            dWqg = [Dep("Wqg%d" % k) for k in range(2)]
            wv = C.rwi[0].rearrange("(k p) f -> p k f", p=128)
            em.dma("pool", Wqg[:, :, 0:1024], wv[:, :, 0:1024], writes=[dWqg[0]])
            em.dma("pool", Wqg[:, :, 1024:3072], wv[:, :, 4096:6144], writes=[dWqg[1]])
            Wro = _alloc(st, nc, "Wro", [128, 16, D], BF16)
            dWro = Dep("Wro")
            em.dma("pool", Wro[:, :, :], C.rwo[0].rearrange("(k p) f -> p k f", p=128), writes=[dWro])
            gpre, dgpre, gpost, dgpost = load_gains(C, em, st, nc, l, 2, 3, 1.0)
            NX = 4
            xa = [_alloc(st, nc, "xa%d" % i, [128, D], F32) for i in range(NX)]
            dxa = [Dep("xa%d" % i) for i in range(NX)]
            hb = _alloc(st, nc, "hb", [128, D], BF16)
            dhb = Dep("hb")
            NH = 3
            hT = [_alloc(st, nc, "hT%d" % i, [128, 8, 128], BF16) for i in range(NH)]
            dhT = [Dep("hT%d" % i) for i in range(NH)]
            cs = [_alloc(st, nc, "rcs%d" % i, [128, 256], F32) for i in range(2)]
            dcs = [Dep("rcs%d" % i) for i in range(2)]
            rt2 = [_alloc(st, nc, "rrt%d" % i, [128, 4, 128], F32) for i in range(2)]
            drt2 = [Dep("rrt%d" % i) for i in range(2)]
            rt = [rt2[0], rt2[1], rt2[0], rt2[1]]
            drt = [drt2[0], drt2[1], drt2[0], drt2[1]]
            qrot = _alloc(st, nc, "qrot", [128, 1024], BF16)
            dqrot = Dep("qrot")
            qT = [[_alloc(st, nc, "qT%d_%d" % (r, i), [128, 8, 128], BF16) for i in range(3)] for r in range(2)]
            dqT = [[Dep("qT%d_%d" % (r, i)) for i in range(3)] for r in range(2)]
            sg = [_alloc(st, nc, "sg%d" % i, [128, 512], BF16) for i in range(3)]
            dsg = [Dep("sg%d" % i) for i in range(3)]
            NR = 2
            kTl = [_alloc(st, nc, "kTl%d" % i, [128, 8, 128], BF16) for i in range(NR)]
            kfl = [_alloc(st, nc, "kfl%d" % i, [128, 1024], BF16) for i in range(NR)]
            vl = [_alloc(st, nc, "vl%d" % i, [128, 2048], BF16) for i in range(NR)]
            sbl1 = _alloc(st, nc, "sbl", [128, 4, 2, 512], BF16)
            sbl = [sbl1, sbl1]
            dkTl = [Dep("kTl%d" % i) for i in range(NR)]
            dkfl = [Dep("kfl%d" % i) for i in range(NR)]
            dvl = [Dep("vl%d" % i) for i in range(NR)]
            dsblh = [Dep("sbl_%d" % h) for h in range(4)]
            Sfb = _alloc(st, nc, "Sfb", [128, 4, 2, 512], BF16)
            dSfb = [Dep("Sfb%d" % h) for h in range(4)]
            dS32h = [Dep("S32_%d" % h) for h in range(4)]
            STb = [_alloc(st, nc, "STb%d" % i, [128, 128], BF16) for i in range(2)]
            dSTb = [Dep("STb%d" % i) for i in range(2)]
            otok = [_alloc(st, nc, "otok%d" % i, [128, 2048], BF16) for i in range(2)]
            dotok = [[Dep("otok%d_%d" % (i, h)) for h in range(4)] for i in range(2)]
            oT = _alloc(st, nc, "oT", [128, 16, 128], BF16)
            doT = Dep("oT")
            junk = _alloc(st, nc, "junk", [128, D], BF16)
            djunk = Dep("junk")
            junk2 = junk
            djunk2 = djunk
            sst = [_alloc(st, nc, "ss%d" % i, [128, 4], F32) for i in range(6)]
            dsst = [Dep("ss%d" % i) for i in range(6)]
            tp = _palloc(st, nc, "tp", [128, 8, 128], BF16)
            dtp = PDep("tp")
            pq = _palloc(st, nc, "pq", [128, 1024], F32)
            dpq = PDep("pq")
            pG = _palloc(st, nc, "pG", [128, 512], F32)
            dpG = PDep("pG")
            pS = _palloc(st, nc, "pS", [128, 512], F32)
            dpS = PDep("pS")
            pY = [_palloc(st, nc, "pY%d" % i, [128, 512], F32) for i in range(2)]
            dpY = [PDep("pY%d" % i) for i in range(2)]
            pD = _palloc(st, nc, "pD", [128, 512], F32)
            dpD = PDep("pD")
            sctr = [0]
            em.op("dve", lambda e: e.memset(Sfb[:, :, :, :], 0.0), writes=dSfb)
            em.op("dve", lambda e: e.memset(S32[:, :, :, :], 0.0), reads=[dS32], writes=dS32h)

            def nss():
                si = sctr[0] % 6
                sctr[0] += 1
                return sst[si], dsst[si]

            osl = {}
            ysl = {}

            def load_sb(n, h):
                em.dma("sp", sbl1[:, h, :, :].rearrange("p c f -> p (c f)"), C.s_sb[n][:, h * 1024:(h + 1) * 1024],
                       reads=[C.dscr[n]], writes=[dsblh[h]])

            hb2 = [hb, _alloc(st, nc, "hb_b", [128, D], BF16)]
            dhb2 = [dhb, Dep("hb_b")]

            fsl = {}

            def frontA(n):
                xs, dxs = xa[n % NX], dxa[n % NX]
                em.dma("sp", xs[:, :], src[n * 128:(n + 1) * 128, :], reads=[dsrc[n]], writes=[dxs])
                em.dma("sp", cs[n % 2][:, :], C.rcs[n * 128:(n + 1) * 128, :], writes=[dcs[n % 2]])
                ss, dss = nss()
                fsl[n] = (ss, dss)
                emit_front_A(C, em, xs, dxs, hb2[n % 2], dhb2[n % 2], ss, dss)

            def frontB(n):
                ss, dss = fsl.pop(n)
                emit_front_B(C, em, xa[n % NX], dxa[n % NX], hb2[n % 2], dhb2[n % 2], ss, dss, gpre, dgpre)

            def front(n):
                frontA(n)
                frontB(n)

            def P_steps(n):
                r = n % 2
                xs, dxs = xa[n % NX], dxa[n % NX]
                c_, dc_ = cs[r], dcs[r]

                def p0():
                    if n == 0:
                        for h in range(4):
                            load_sb(0, h)
                    hbs = hb2[n % 2]

                    def tps(e):
                        for k in range(8):
                            i = e.transpose(tp[:, k, :], hbs[:, k * 128:(k + 1) * 128], C.ident[:, :])
                        return i
                    em.op("pe", tps, reads=[dhb2[n % 2], C.dident], writes=[dtp])
                    em.op("act", lambda e: e.activation(out=hT[n % NH][:, :, :], in_=tp[:, :, :], func=AF.Copy), reads=[dtp], writes=[dhT[n % NH]])

                def p1():
                    em.dma("sp", kTl[r][:, :, :].rearrange("p k t -> p (k t)"), C.s_kT[n], reads=[C.dscr[n]], writes=[dkTl[r]])
                    em.dma("sp", kfl[r][:, :], C.s_kf[n], reads=[C.dscr[n]], writes=[dkfl[r]])
                    em.dma("sp", vl[r][:, :], C.s_v[n], reads=[C.dscr[n]], writes=[dvl[r]])
                    for g in range(2):
                        def f(e, g=g):
                            for k in range(8):
                                i = e.matmul(pq[:, g * 512:(g + 1) * 512], lhsT=hT[n % NH][:, k, :], rhs=Wqg[:, k, g * 512:(g + 1) * 512],
                                             start=(k == 0), stop=(k == 7))
                            return i
                        em.op("pe", f, reads=[dhT[n % NH], dWqg[0]], writes=[dpq])
                    rotary(pq[:, :], dpq, c_, dc_, rt, drt, qrot, dqrot)

                def p2():
                    def tq(e):
                        for k in range(8):
                            i = e.transpose(tp[:, k, :], qrot[:, k * 128:(k + 1) * 128], C.ident[:, :])
                        return i
                    em.op("pe", tq, reads=[dqrot, C.dident], writes=[dtp])
                    em.op("act", lambda e: e.activation(out=qT[r][0][:, :, :], in_=tp[:, :, :], func=AF.Copy), reads=[dtp], writes=[dqT[r][0]])
                    for i in range(2):
                        em.op("dve", lambda e, i=i: e.tensor_tensor(
                            out=qT[r][1 + i][:, :, :].rearrange("p (h c) t -> p h c t", c=2),
                            in0=qT[r][0][:, :, :].rearrange("p (h c) t -> p h c t", c=2),
                            in1=decrow[:, 4 * i:4 * i + 4, :].unsqueeze(2).broadcast_to([128, 4, 2, 128]), op=ALU.mult),
                            reads=[dqT[r][0], dtab], writes=[dqT[r][1 + i]])
                return [p0, p1, p2]

            def O_steps(n):
                r = n % 2
                steps = []
                for half in range(2):
                    def o_t(half=half):
                        def to(e):
                            for k in range(8):
                                i = e.transpose(tp[:, k, :], otok[r][:, (half * 8 + k) * 128:(half * 8 + k + 1) * 128], C.ident[:, :])
                            return i
                        em.op("pe", to, reads=dotok[r] + [C.dident], writes=[dtp])
                        em.op("act", lambda e: e.activation(out=oT[:, half * 8:(half + 1) * 8, :], in_=tp[:, :, :], func=AF.Copy),
                              reads=[dtp], writes=[doT])
                    steps.append(o_t)

                def o_w():
                    def fw(e):
                        for half in range(2):
                            for k in range(16):
                                i = e.matmul(pq[:, half * 512:(half + 1) * 512], lhsT=oT[:, k, :], rhs=Wro[:, k, half * 512:(half + 1) * 512],
                                             start=(k == 0), stop=(k == 15))
                        return i
                    em.op("pe", fw, reads=[doT, dWro], writes=[dpq])

                def o_e():
                    ss, dss = nss()
                    osl[n] = (ss, dss)
                    emit_post_A(C, em, pq[:, :], dpq, ss, dss, junk, djunk)

                def o_eB():
                    ss, dss = osl.pop(n)
                    emit_post_B(C, em, pq[:, :], dpq, xa[n % NX], dxa[n % NX], gpost, dgpost, ss, dss,
                                C.y[n * 128:(n + 1) * 128, :], C.dy[n])
                steps += [o_w, o_e, o_eB]
                return steps

            def H(n, steps):
                r = n % 2
                last = (n + 1 >= nch)

                def GS(h):
                    def fg(e):
                        for k in range(8):
                            i = e.matmul(pG[:, :], lhsT=hT[n % NH][:, k, :], rhs=Wqg[:, k, 1024 + h * 512:1024 + (h + 1) * 512],
                                         start=(k == 0), stop=(k == 7))
                        return i
                    em.op("pe", fg, reads=[dhT[n % NH], dWqg[1]], writes=[dpG])
                    em.op("act", lambda e: e.activation(out=sg[h % 3][:, :], in_=pG[:, :], func=AF.Silu), reads=[dpG], writes=[dsg[h % 3]])

                    def fs(e):
                        for c in range(2):
                            i = e.matmul(pS[:, 0:128], lhsT=kTl[r][:, 2 * h + c, :], rhs=qT[r][0][:, 2 * h + c, :], start=(c == 0), stop=(c == 1))
                        return i
                    em.op("pe", fs, reads=[dkTl[r], dqT[r][0]], writes=[dpS])
                    em.op("dve", lambda e: e.tensor_tensor(out=STb[h % 2][:, :], in0=pS[:, 0:128], in1=DT[:, h, :], op=ALU.mult),
                          reads=[dpS, dtab], writes=[dSTb[h % 2]])

                def Y(h):
                    yh, dyh = pY[h % 2], dpY[h % 2]

                    def fy(e):
                        e.matmul(yh[:, :], lhsT=STb[h % 2][:, :], rhs=vl[r][:, h * 512:(h + 1) * 512], start=True, stop=False)
                        for c in range(2):
                            e.matmul(yh[:, :], lhsT=qT[r][1][:, 2 * h + c, :], rhs=Sfb[:, h, c, :], start=False, stop=False)
                        for c in range(2):
                            i = e.matmul(yh[:, :], lhsT=qT[r][2][:, 2 * h + c, :], rhs=sbl[r][:, h, c, :], start=False, stop=(c == 1))
                        return i
                    em.op("pe", fy, reads=[dSTb[h % 2], dvl[r], dqT[r][1], dqT[r][2], dSfb[h], dsblh[h]], writes=[dyh])
                    ss, dss = nss()
                    em.op("act", lambda e: e.activation(out=junk2[:, 0:512], in_=yh[:, :], func=AF.Square, accum_out=ss[:, 0:1]),
                          reads=[dyh], writes=[djunk2, dss])
                    emit_rstd(em, ss, dss, C.mhalf, C.dmh, 512)
                    ysl[h] = (ss, dss)

                def YB(h):
                    yh, dyh = pY[h % 2], dpY[h % 2]
                    ss, dss = ysl.pop(h)
                    em.op("dve", lambda e: e.scalar_tensor_tensor(out=otok[r][:, h * 512:(h + 1) * 512], in0=yh[:, :], scalar=ss[:, 2:3],
                                                                  in1=sg[h % 3][:, :], op0=ALU.mult, op1=ALU.mult),
                          reads=[dyh, dss, dsg[h % 3]], writes=[dotok[r][h]])

                def UPD(h, cs_):
                    for c in cs_:
                        def f(e, c=c):
                            return e.matmul(pD[:, :], lhsT=kfl[r][:, h * 256 + c * 128: h * 256 + (c + 1) * 128],
                                            rhs=vl[r][:, h * 512:(h + 1) * 512], start=True, stop=True)
                        em.op("pe", f, reads=[dkfl[r], dvl[r]], writes=[dpD])
                        em.op("dve", lambda e, c=c: e.scalar_tensor_tensor(out=S32[:, h, c, :], in0=S32[:, h, c, :],
                                                                         scalar=g128[:, h:h + 1], in1=pD[:, :],
                                                                         op0=ALU.mult, op1=ALU.add),
                              reads=[dS32h[h], dpD, dtab], writes=[dS32h[h]])
                    if 1 not in cs_:
                        return
                    if n + 1 == half_b:
                        em.op("dve", lambda e: e.tensor_scalar(out=S32[:, h, :, :], in0=S32[:, h, :, :], scalar1=bnd[:, 0:1], scalar2=None,
                                                               op0=ALU.mult), reads=[dS32h[h], drc], writes=[dS32h[h]])
                    em.op("act", lambda e: e.activation(out=Sfb[:, h, :, :], in_=S32[:, h, :, :], func=AF.Copy),
                          reads=[dS32h[h]], writes=[dSfb[h]])

                GS(0)
                for h in range(4):
                    if h + 1 < 4:
                        GS(h + 1)
                    if h >= 1 and not last:
                        UPD(h - 1, [1])
                    Y(h)
                    if h >= 1:
                        YB(h - 1)
                    if not last:
                        load_sb(n + 1, h)
                        UPD(h, [0])
                    for _ in range(2):
                        if steps:
                            steps.pop(0)()
                YB(3)
                if not last:
                    UPD(3, [1])
                while steps:
                    steps.pop(0)()

            nop = lambda: None
            front(0)
            if nch > 1:
                front(1)
            for s_ in P_steps(0):
                s_()
            if nch > 1:
                P_steps(1)[0]()
            for n in range(nch):
                O = O_steps(n - 1) if n >= 1 else [nop] * 5
                Opp = O_steps(n - 2)[4] if n >= 2 else nop
                P = P_steps(n + 1) if n + 1 < nch else [nop] * 3
                P0n = P_steps(n + 2)[0] if n + 2 < nch else nop
                frA = (lambda n=n: frontA(n + 2)) if n + 2 < nch else nop
                frB = (lambda n=n: frontB(n + 2)) if n + 2 < nch else nop
                steps = [lambda Opp=Opp, O=O: (Opp(), O[0]()), P[1], O[1], frA, P[2], lambda O=O, frB=frB: (O[2](), frB()), P0n, O[3]]
                H(n, steps)
            if nch >= 2:
                O_steps(nch - 2)[4]()
            for s_ in O_steps(nch - 1):
                s_()
            em.barrier()


def build(nsub=6, subs=None, nblk=NCH, dbg=3):
    nc = bass.Bass("TRN2", target_bir_lowering=False)
    em = Emitter(nc)
    C = Ctx()
    C.nc, C.em = nc, em

    def din(name, shape, dt=F32):
        return nc.dram_tensor(name, shape, dt, kind="ExternalInput").ap()
    C.xin = din("xin", [NTOK, D])
    C.ng = din("norm_gains", [2, 6, D])
    C.fwi = din("ffn_w_in", [2, 2, D, 2 * DFF])
    C.fwo = din("ffn_w_out", [2, 2, DFF, D])
    C.wqkv = din("attn_w_qkv", [1, D, 1536])
    C.wo = din("attn_w_o", [1, D, D])
    C.sink = din("attn_sink", [1, 16])
    C.rwi = din("ret_w_in", [1, D, 6144])
    C.rwo = din("ret_w_o", [1, 2048, D])
    C.rdf = din("ret_decay_fwd", [1, 4])
    C.rdb = din("ret_decay_bwd", [1, 4])
    C.identd = din("c_ident", [128, 128], BF16)
    C.acs = din("c_acs", [NTOK, 320])
    C.amask = din("c_amask", [NCH, 128, 384], BF16)
    C.rconst = din("c_rconst", [128, 6, 128])
    C.rpos = din("c_rpos", [128, 4])
    C.rbnd = din("c_rbnd", [128, 1])
    C.rcs = din("c_rcs", [NTOK, 256])
    C.s_kT = nc.dram_tensor("s_kT", [NCH, 128, 1024], BF16, kind="Internal").ap()
    C.s_kf = nc.dram_tensor("s_kf", [NCH, 128, 1024], BF16, kind="Internal").ap()
    C.s_v = nc.dram_tensor("s_v", [NCH, 128, 2048], BF16, kind="Internal").ap()
    C.s_sb = nc.dram_tensor("s_sb", [NCH, 128, 4096], BF16, kind="Internal").ap()
    C.dscr = [Dep("scr%d" % i) for i in range(NCH)]
    C.y = nc.dram_tensor("y", [NTOK, D], F32, kind="ExternalOutput").ap()
    C.dy = [Dep("y%d" % i) for i in range(NCH)]
    C.dxin = [Dep("xin%d" % i) for i in range(NCH)]

    C.ident = nc.alloc_sbuf_tensor("ident", [128, 128], BF16)
    C.dident = Dep("ident")
    C.mhalf = nc.alloc_sbuf_tensor("mhalf", [128, 1], F32)
    C.dmh = Dep("mhalf")
    em.dma("sp", C.ident[:, :], C.identd[:, :], writes=[C.dident])
    em.op("pool", lambda e: e.memset(C.mhalf[:, :], -0.5), writes=[C.dmh])

    C.nblk = nblk
    C.dbg = dbg
    if subs is None:
        subs = [("ffn", 0, 0), ("attn", 0, 0), ("ffn", 0, 1), ("ffn", 1, 0), ("ret", 1, 0), ("ffn", 1, 1)]
    src, dsrc = C.xin, C.dxin
    for i, (kind, l, which) in enumerate(subs[:nsub]):
        if kind == "ffn":
            emit_ffn(C, l, which, src, dsrc)
        elif kind == "attn":
            emit_attn(C, l, src, dsrc)
        elif kind == "ret":
            emit_ret(C, l, src, dsrc)
        src, dsrc = C.y, C.dy
    em.finish()
    return nc


def make_consts():
    c = {}
    c["c_ident"] = np.eye(128, dtype=np.float32).astype(ml_dtypes.bfloat16)
    j = np.arange(128, dtype=np.float32)[:, None]
    i = np.arange(128, dtype=np.float32)[None, :]
    rc = np.zeros((128, 6, 128), np.float32)
    rc[:, 0] = np.maximum(i - j, 0.0)
    rc[:, 1] = np.maximum(j - i, 0.0)
    rc[:, 2] = (i >= j) / 16.0
    rc[:, 3] = (j > i) / 16.0
    rc[:, 4] = np.broadcast_to(i + 1.0, (128, 128))
    rc[:, 5] = np.broadcast_to(128.0 - i, (128, 128))
    c["c_rconst"] = rc
    p = np.arange(128, dtype=np.float32)
    c["c_rpos"] = np.stack([p + 1, 128 - p, 127 - p, p], axis=1).astype(np.float32)
    return c


def make_core_consts(seqlen):
    c = {}
    tok = np.arange(NTOK)
    pos = (tok % seqlen).astype(np.float32)
    inv = (500000.0 ** (-np.arange(8, dtype=np.float32) / 8)).astype(np.float32)
    ang = pos[:, None] * inv[None, :]
    cos = np.cos(ang).astype(np.float32)
    sin = np.sin(ang).astype(np.float32)
    hs = np.ones((1, 20, 1), np.float32)
    hs[:, :16] = 0.125
    acs = np.concatenate([(np.tile(cos[:, None, :], (1, 20, 1)) * hs).reshape(NTOK, 160),
                          (np.tile(sin[:, None, :], (1, 20, 1)) * hs).reshape(NTOK, 160)], axis=1)
    c["c_acs"] = np.ascontiguousarray(acs, dtype=np.float32)
    b = np.arange(NCH)[:, None, None]
    qi = np.arange(128)[None, :, None]
    kc = np.arange(384)[None, None, :]
    tq = 128 * b + qi
    tk = 128 * (b - 1) + kc
    valid = (tk >= 0) & (tk < NTOK) & ((tk // seqlen) == (tq // seqlen)) & (np.abs(tk - tq) <= 128)
    c["c_amask"] = np.where(valid, 0.0, NEG).astype(np.float32).astype(ml_dtypes.bfloat16)
    invr = (10000.0 ** (-np.arange(128, dtype=np.float32) / 128)).astype(np.float32)
    angr = pos[:, None] * invr[None, :]
    c["c_rcs"] = np.ascontiguousarray(np.concatenate([np.cos(angr), np.sin(angr)], axis=1), dtype=np.float32)
    c["c_rbnd"] = np.full((128, 1), 1.0 if seqlen == NTOK else 0.0, np.float32)
    return c


def kernel(x_prompt, x_sample, norm_gains, ffn_w_in, ffn_w_out, attn_w_qkv, attn_w_o, attn_sink,
           ret_w_in, ret_w_o, ret_decay_fwd, ret_decay_bwd, _nsub=6, _subs=None, _nblk=NCH, _cores=None, _trace=False, _dbg=3):
    f = lambda a: np.ascontiguousarray(np.asarray(a, dtype=np.float32))
    xp = f(x_prompt).reshape(4, NTOK, D)
    xs = f(x_sample).reshape(4, NTOK, D)
    shared = {
        "norm_gains": f(norm_gains), "ffn_w_in": f(ffn_w_in), "ffn_w_out": f(ffn_w_out),
        "attn_w_qkv": f(attn_w_qkv), "attn_w_o": f(attn_w_o), "attn_sink": f(attn_sink),
        "ret_w_in": f(ret_w_in), "ret_w_o": f(ret_w_o), "ret_decay_fwd": f(ret_decay_fwd),
        "ret_decay_bwd": f(ret_decay_bwd),
    }
    shared.update(make_consts())
    in_maps = []
    cc = [make_core_consts(2048), make_core_consts(4096)]
    for c in range(8):
        m = dict(shared)
        m.update(cc[0] if c < 4 else cc[1])
        m["xin"] = xp[c] if c < 4 else xs[c - 4]
        in_maps.append(m)
    nc = build(_nsub, _subs, _nblk, _dbg)
    if _cores is not None:
        res = run_bass_kernel_spmd(nc, [in_maps[c] for c in _cores], core_ids=list(range(len(_cores))), trace=_trace)
        if _trace:
            print("exec_time_ns", res.exec_time_ns)
        return [np.asarray(r["y"], dtype=np.float32) for r in res.results]
    res = run_bass_kernel_spmd(nc, in_maps, core_ids=list(range(8)))
    outs = [np.asarray(r["y"], dtype=np.float32) for r in res.results]
    y_prompt = np.stack(outs[:4]).reshape(8, 2048, D)
    y_sample = np.stack(outs[4:]).reshape(4, 4096, D)
    return (y_prompt, y_sample)
```

```python
import os
import numpy as np
import ml_dtypes
from contextlib import ExitStack
import concourse.bass as bass
import concourse.mybir as mybir
from concourse.bass_utils import run_bass_kernel_spmd

F32 = mybir.dt.float32
BF16 = mybir.dt.bfloat16
AF = mybir.ActivationFunctionType
ALU = mybir.AluOpType
AX = mybir.AxisListType

NTOK = 4096
NCH = 32
D = 1024
DFF = 2816
EPS = 1e-6
EPOCH = 1 << 30
NEG = -30000.0


class Dep:
    __slots__ = ("name", "w", "rs", "dsem", "dcnt", "ex", "retired")

    def __init__(self, name="", ex=False):
        self.name = name
        self.ex = ex
        self.w = None
        self.rs = {}
        self.dsem = None
        self.dcnt = 0
        self.retired = False


class Emitter:
    def __init__(self, nc):
        self.nc = nc
        self.eng = {"pe": nc.tensor, "act": nc.scalar, "dve": nc.vector,
                    "pool": nc.gpsimd, "sp": nc.sync}
        self.sem = {}
        self.cnt = {}
        self.nsem = 0
        for e in self.eng:
            self.sem[e] = self._newsem("e_" + e)
            self.cnt[e] = 0
        self.waited = {}
        self.dma_owners = []
        self.free_dsems = []
        self.no_recycle = set()

    def _newsem(self, name):
        self.nsem += 1
        return self.nc.alloc_semaphore("%s_%d" % (name, self.nsem))

    def _tick(self, e):
        if self.cnt[e] >= EPOCH:
            self.sem[e] = self._newsem("e_" + e)
            self.cnt[e] = 0
        self.cnt[e] += 1
        return self.sem[e], self.cnt[e]

    def _need(self, e, rec, needs):
        if rec is None:
            return
        if rec[0] == "e":
            _, pe, sem, val = rec
            if pe == e and e == "pe":
                return
            needs.append((sem, val))
        else:
            o = rec[1]
            needs.append((o.dsem, o.dcnt))

    def _collect(self, e, reads, writes):
        needs = []
        for d in reads:
            self._need(e, d.w, needs)
        for d in writes:
            if d.w is not None:
                self._need(e, d.w, needs)
            for k, r in d.rs.items():
                self._need(e, r, needs)
        return needs

    def _emit_waits(self, e, needs):
        eng = self.eng[e]
        best = {}
        for sem, val in needs:
            k = id(sem)
            if k not in best or best[k][1] < val:
                best[k] = (sem, val)
        for k, (sem, val) in best.items():
            wk = (e, k)
            if self.waited.get(wk, 0) >= val:
                continue
            self.waited[wk] = val
            eng.wait_ge(sem, val)

    def op(self, e, fn, reads=(), writes=()):
        xr = [d for d in reads if d.ex]
        needs = []
        if xr:
            for d in xr:
                self._need(e, d.w, needs)
            reads = [d for d in reads if not d.ex]
            writes = list(writes) + [d for d in xr if d not in writes]
        needs += self._collect(e, reads, writes)
        self._emit_waits(e, needs)
        inst = fn(self.eng[e])
        sem, val = self._tick(e)
        inst.then_inc(sem, 1)
        rec = ("e", e, sem, val)
        for d in reads:
            d.rs[e] = rec
        for d in writes:
            d.w = rec
            d.rs = {}
        return inst

    def dma(self, q, out, in_, reads=(), writes=(), owner=None):
        needs = self._collect(q, reads, writes)
        if owner is None:
            owner = writes[0] if writes else reads[0]
        if owner.dsem is None or owner.retired:
            if self.free_dsems and q != "pool":
                owner.dsem, owner.dcnt = self.free_dsems.pop()
            else:
                owner.dsem, owner.dcnt = self._newsem("d_" + owner.name), 0
                if q == "pool":
                    self.no_recycle.add(id(owner.dsem))
            owner.retired = False
            self.dma_owners.append(owner)
        self._emit_waits(q, needs)
        inst = self.eng[q].dma_start(out=out, in_=in_)
        owner.dcnt += 16
        inst.then_inc(owner.dsem, 16)
        rec = ("d", owner)
        for d in reads:
            d.rs["dma%d" % id(owner)] = rec
        for d in writes:
            d.w = rec
            d.rs = {}
        return inst

    def barrier(self):
        pts = [(self.sem[e], self.cnt[e]) for e in self.eng if self.cnt[e] > 0]
        pts += [(o.dsem, o.dcnt) for o in self.dma_owners]
        for e in self.eng:
            self._emit_waits(e, pts)
        for o in self.dma_owners:
            o.retired = True
            if id(o.dsem) not in self.no_recycle:
                self.free_dsems.append((o.dsem, o.dcnt))
        self.dma_owners = []

    def finish(self):
        sp = self.eng["sp"]
        pts = [(o.dsem, o.dcnt) for o in self.dma_owners]
        self._emit_waits("sp", pts)


def PDep(name):
    return Dep(name, ex=True)


class Ctx:
    pass


_uid = [0]


def _alloc(st, nc, name, shape, dt):
    _uid[0] += 1
    return st.enter_context(nc.sbuf_tensor("%s_u%d" % (name, _uid[0]), shape, dt))


def _palloc(st, nc, name, shape, dt):
    _uid[0] += 1
    return st.enter_context(nc.psum_tensor("%s_u%d" % (name, _uid[0]), shape, dt))


def emit_rstd(em, ss, dss, mhalf, dmh, n_feat):
    em.op("pool", lambda e: e.tensor_scalar(out=ss[:, 1:2], in0=ss[:, 0:1], scalar1=1.0 / n_feat,
                                            scalar2=EPS, op0=ALU.mult, op1=ALU.add),
          reads=[dss], writes=[dss])
    em.op("pool", lambda e: e.tensor_tensor(out=ss[:, 2:3], in0=ss[:, 1:2], in1=mhalf[:, 0:1], op=ALU.pow),
          reads=[dss, dmh], writes=[dss])


def emit_prenorm_T(C, em, xs, dxs, gpre, dgpre, hbs, dhbs, ss, dss, tp, dtp, hT_dst, dhT):
    em.op("act", lambda e: e.activation(out=hbs[:, :], in_=xs[:, :], func=AF.Square, accum_out=ss[:, 0:1]),
          reads=[dxs], writes=[dhbs, dss])
    emit_rstd(em, ss, dss, C.mhalf, C.dmh, D)
    em.op("dve", lambda e: e.scalar_tensor_tensor(out=hbs[:, :], in0=xs[:, :], scalar=ss[:, 2:3], in1=gpre[:, :],
                                                  op0=ALU.mult, op1=ALU.mult),
          reads=[dxs, dss, dgpre], writes=[dhbs])

    def tps(e):
        for k in range(8):
            i = e.transpose(tp[:, k, :], hbs[:, k * 128:(k + 1) * 128], C.ident[:, :])
        return i
    em.op("pe", tps, reads=[dhbs, C.dident], writes=[dtp])
    em.op("act", lambda e: e.activation(out=hT_dst, in_=tp[:, :, :], func=AF.Copy), reads=[dtp], writes=[dhT])


def load_gains(C, em, st, nc, l, ipre, ipost, post_scale):
    gpre = _alloc(st, nc, "gpre", [128, D], F32)
    gpost = _alloc(st, nc, "gpost", [128, D], F32)
    dgpre = Dep("gpre")
    dgpost = Dep("gpost")
    em.dma("sp", gpre[:, :], C.ng[l, ipre, :].partition_broadcast(128), writes=[dgpre])
    em.dma("sp", gpost[:, :], C.ng[l, ipost, :].partition_broadcast(128), writes=[dgpost])
    if post_scale != 1.0:
        em.op("pool", lambda e: e.tensor_scalar(out=gpost[:, :], in0=gpost[:, :], scalar1=post_scale, scalar2=0.0,
                                                op0=ALU.mult, op1=ALU.add), reads=[dgpost], writes=[dgpost])
    return gpre, dgpre, gpost, dgpost


def emit_post_A(C, em, ps_out, dps, ss, dss, junk, djunk):
    em.op("act", lambda e: e.activation(out=junk[:, :], in_=ps_out, func=AF.Square, accum_out=ss[:, 0:1]),
          reads=[dps], writes=[djunk, dss])
    emit_rstd(em, ss, dss, C.mhalf, C.dmh, D)


def emit_post_B(C, em, ps_out, dps, xs, dxs, gpost, dgpost, ss, dss, dst_ap, ddst):
    em.op("dve", lambda e: e.scalar_tensor_tensor(out=ps_out, in0=ps_out, scalar=ss[:, 2:3], in1=gpost[:, :],
                                                  op0=ALU.mult, op1=ALU.mult),
          reads=[dps, dss, dgpost], writes=[dps])
    em.op("dve", lambda e: e.tensor_tensor(out=xs[:, :], in0=ps_out, in1=xs[:, :], op=ALU.add),
          reads=[dps, dxs], writes=[dxs])
    em.dma("sp", dst_ap, xs[:, :], reads=[dxs], writes=[ddst], owner=dxs)


def emit_post_residual(C, em, ps_out, dps, xs, dxs, gpost, dgpost, ss, dss, junk, djunk, dst_ap, ddst):
    emit_post_A(C, em, ps_out, dps, ss, dss, junk, djunk)
    emit_post_B(C, em, ps_out, dps, xs, dxs, gpost, dgpost, ss, dss, dst_ap, ddst)


def emit_front_A(C, em, xs, dxs, hbs, dhbs, ss, dss):
    em.op("act", lambda e: e.activation(out=hbs[:, :], in_=xs[:, :], func=AF.Square, accum_out=ss[:, 0:1]),
          reads=[dxs], writes=[dhbs, dss])
    emit_rstd(em, ss, dss, C.mhalf, C.dmh, D)


def emit_front_B(C, em, xs, dxs, hbs, dhbs, ss, dss, gpre, dgpre):
    em.op("dve", lambda e: e.scalar_tensor_tensor(out=hbs[:, :], in0=xs[:, :], scalar=ss[:, 2:3], in1=gpre[:, :],
                                                  op0=ALU.mult, op1=ALU.mult),
          reads=[dxs, dss, dgpre], writes=[dhbs])


def emit_ffn(C, l, which, src, dsrc):
    nc, em = C.nc, C.em
    NJ = DFF // 128
    LAG = 3
    with ExitStack() as st:
        Win = _alloc(st, nc, "Win", [128, 8, 2 * DFF], BF16)
        Wout = _alloc(st, nc, "Wout", [128, NJ, D], BF16)
        JB = [0, 6, 12, 17, 22]
        jblk = [0] * 6 + [1] * 6 + [2] * 5 + [3] * 5
        dWin = [[Dep("WinG%d" % k), Dep("WinU%d" % k)] for k in range(4)]
        dWout = [Dep("Wout%d" % k) for k in range(4)]
        w_in = C.fwi[l, which]
        w_out = C.fwo[l, which]
        wi_v = w_in.rearrange("(k p) f -> p k f", p=128)
        wo_v = w_out.rearrange("(j p) d -> p j d", p=128)
        for k in range(4):
            c0_, c1_ = JB[k] * 128, JB[k + 1] * 128
            em.dma("pool", Win[:, :, c0_:c1_], wi_v[:, :, c0_:c1_], writes=[dWin[k][0]])
            em.dma("pool", Win[:, :, DFF + c0_:DFF + c1_], wi_v[:, :, DFF + c0_:DFF + c1_], writes=[dWin[k][1]])
            em.dma("pool", Wout[:, JB[k]:JB[k + 1], :], wo_v[:, JB[k]:JB[k + 1], :], writes=[dWout[k]])
        gpre, dgpre, gpost, dgpost = load_gains(C, em, st, nc, l, 0 if which == 0 else 4, 1 if which == 0 else 5, 0.5)

        xa = [_alloc(st, nc, "xa%d" % i, [128, D], F32) for i in range(3)]
        dxa = [Dep("xa%d" % i) for i in range(3)]
        xb = [_alloc(st, nc, "xb%d" % i, [128, D], F32) for i in range(3)]
        dxb = [Dep("xb%d" % i) for i in range(3)]
        hb = [_alloc(st, nc, "hb%d" % i, [128, D], BF16) for i in range(2)]
        dhb = [Dep("hb%d" % i) for i in range(2)]
        hT = [_alloc(st, nc, "hT%d" % i, [128, 8, 256], BF16) for i in range(2)]
        dhT = [[Dep("hT%d_%d" % (i, c)) for c in range(2)] for i in range(2)]
        NA = 6
        actT = [_alloc(st, nc, "actT%d" % i, [128, 256], BF16) for i in range(NA)]
        dact = [Dep("actT%d" % i) for i in range(NA)]
        sg = [_alloc(st, nc, "sg%d" % i, [128, 256], BF16) for i in range(2)]
        dsg = [Dep("sg%d" % i) for i in range(2)]
        junk = _alloc(st, nc, "junk", [128, D], BF16)
        djunk = Dep("junk")
        sst = [_alloc(st, nc, "ss%d" % i, [128, 4], F32) for i in range(4)]
        dsst = [Dep("ss%d" % i) for i in range(4)]
        tp = [_palloc(st, nc, "tp%d" % i, [128, 8, 128], BF16) for i in range(2)]
        dtp = [PDep("tp%d" % i) for i in range(2)]
        gu = [_palloc(st, nc, "gu%d" % i, [128, 2, 256], F32) for i in range(2)]
        dgu = [PDep("gu%d" % i) for i in range(2)]
        pout = _palloc(st, nc, "pout", [128, 2, D], F32)
        dpout = [PDep("pout%d" % i) for i in range(2)]

        NT = NTOK // 256
        sctr = [0]

        def loads(t):
            for c in range(2):
                ch = 2 * t + c
                em.dma("sp", xa[ch % 3][:, :], src[ch * 128:(ch + 1) * 128, :], reads=[dsrc[ch]], writes=[dxa[ch % 3]])

        fst = {}

        def frontA(t):
            for c in range(2):
                ch = 2 * t + c
                si = sctr[0] % 4
                sctr[0] += 1
                fst[ch] = si
                emit_front_A(C, em, xa[ch % 3], dxa[ch % 3], hb[ch % 2], dhb[ch % 2], sst[si], dsst[si])

        def frontB(t):
            for c in range(2):
                ch = 2 * t + c
                si = fst.pop(ch)
                emit_front_B(C, em, xa[ch % 3], dxa[ch % 3], hb[ch % 2], dhb[ch % 2], sst[si], dsst[si], gpre, dgpre)

        def transp(t, c):
            ch = 2 * t + c
            hbs = hb[ch % 2]

            def tps(e):
                for k in range(8):
                    i = e.transpose(tp[ch % 2][:, k, :], hbs[:, k * 128:(k + 1) * 128], C.ident[:, :])
                return i
            em.op("pe", tps, reads=[dhb[ch % 2], C.dident], writes=[dtp[ch % 2]])
            em.op("act", lambda e: e.activation(out=hT[t % 2][:, :, c * 128:(c + 1) * 128], in_=tp[ch % 2][:, :, :], func=AF.Copy),
                  reads=[dtp[ch % 2]], writes=[dhT[t % 2][c]])

        def xb_loads(t):
            for tc in range(2):
                ch = 2 * t + tc
                em.dma("sp", xb[ch % 3][:, :], src[ch * 128:(ch + 1) * 128, :], reads=[dsrc[ch]], writes=[dxb[ch % 3]])

        def p1(t, j):
            g = gu[j % 2]
            hTt = hT[t % 2]

            def f(e):
                for half in range(2):
                    for k in range(8):
                        i = e.matmul(g[:, half, :], lhsT=Win[:, k, half * DFF + j * 128: half * DFF + (j + 1) * 128],
                                     rhs=hTt[:, k, :], start=(k == 0), stop=(k == 7))
                return i
            em.op("pe", f, reads=dWin[jblk[j]] + dhT[t % 2], writes=[dgu[j % 2]])
            s = sg[j % 2]
            em.op("act", lambda e: e.activation(out=s[:, :], in_=g[:, 0, :], func=AF.Silu),
                  reads=[dgu[j % 2]], writes=[dsg[j % 2]])
            a = actT[j % NA]
            em.op("dve", lambda e: e.tensor_tensor(out=a[:, :], in0=g[:, 1, :], in1=s[:, :], op=ALU.mult),
                  reads=[dgu[j % 2], dsg[j % 2]], writes=[dact[j % NA]])

        def p2(t, j):
            a = actT[j % NA]

            def f(e):
                for tc in range(2):
                    for half in range(2):
                        i = e.matmul(pout[:, tc, half * 512:(half + 1) * 512], lhsT=a[:, tc * 128:(tc + 1) * 128],
                                     rhs=Wout[:, j, half * 512:(half + 1) * 512], start=(j == 0), stop=(j == NJ - 1))
                return i
            em.op("pe", f, reads=[dact[j % NA], dWout[jblk[j]]], writes=dpout)

        est = {}

        def epilogueA(t):
            for tc in range(2):
                si = sctr[0] % 4
                sctr[0] += 1
                est[(t, tc)] = si
                emit_post_A(C, em, pout[:, tc, :], dpout[tc], sst[si], dsst[si], junk, djunk)

        def epilogueB(t):
            for tc in range(2):
                ch = 2 * t + tc
                si = est.pop((t, tc))
                emit_post_B(C, em, pout[:, tc, :], dpout[tc], xb[ch % 3], dxb[ch % 3], gpost, dgpost, sst[si], dsst[si],
                            C.y[ch * 128:(ch + 1) * 128, :], C.dy[ch])

        loads(0)
        frontA(0)
        frontB(0)
        transp(0, 0)
        transp(0, 1)
        if NT > 1:
            loads(1)
        for t in range(NT):
            for j in range(NJ + LAG):
                if j < NJ:
                    p1(t, j)
                if j >= LAG:
                    p2(t, j - LAG)
                if j == 1 and t >= 1:
                    epilogueB(t - 1)
                if t + 1 < NT:
                    if j == 4:
                        frontA(t + 1)
                    elif j == 7:
                        frontB(t + 1)
                    elif j == 11:
                        transp(t + 1, 0)
                    elif j == 15:
                        transp(t + 1, 1)
                    elif j == 18 and t + 2 < NT:
                        loads(t + 2)
                if j == 13:
                    xb_loads(t)
            epilogueA(t)
        epilogueB(NT - 1)
        em.barrier()


def emit_attn(C, l, src, dsrc):
    nc, em = C.nc, C.em
    SCALE = 0.125
    with ExitStack() as st:
        Wqkv = _alloc(st, nc, "Wqkv", [128, 8, 1536], BF16)
        Wo = _alloc(st, nc, "Wo", [128, 8, D], BF16)
        dWqkv, dWo = Dep("Wqkv"), Dep("Wo")
        em.dma("pool", Wqkv[:, :, :], C.wqkv[0].rearrange("(k p) f -> p k f", p=128), writes=[dWqkv])
        em.dma("pool", Wo[:, :, :], C.wo[0].rearrange("(k p) f -> p k f", p=128), writes=[dWo])
        gpre, dgpre, gpost, dgpost = load_gains(C, em, st, nc, l, 2, 3, 1.0)
        sinkt = _alloc(st, nc, "sinkt", [128, 16], F32)
        nsink = _alloc(st, nc, "nsink", [128, 16], F32)
        dsink = Dep("sink")
        em.dma("sp", sinkt[:, :], C.sink[0, :].partition_broadcast(128), writes=[dsink])
        em.op("pool", lambda e: e.tensor_scalar(out=nsink[:, :], in0=sinkt[:, :], scalar1=-1.0, scalar2=0.0,
                                                op0=ALU.mult, op1=ALU.add), reads=[dsink], writes=[dsink])

        kT = _alloc(st, nc, "kT_all", [128, 8, 34 * 128], BF16)
        vA = _alloc(st, nc, "v_all", [128, 34, 256], BF16)
        dkT = [Dep("kT%d" % i) for i in range(34)]
        dvA = [Dep("vA%d" % i) for i in range(34)]
        for i in (0, 33):
            em.op("dve", lambda e, i=i: e.memset(kT[:, :, i * 128:(i + 1) * 128], 0.0), writes=[dkT[i]])
            em.op("dve", lambda e, i=i: e.memset(vA[:, i, :], 0.0), writes=[dvA[i]])

        NX = 5
        xa = [_alloc(st, nc, "xa%d" % i, [128, D], F32) for i in range(NX)]
        dxa = [Dep("xa%d" % i) for i in range(NX)]
        hb = [_alloc(st, nc, "hb%d" % i, [128, D], BF16) for i in range(2)]
        dhb = [Dep("hb%d" % i) for i in range(2)]
        hT = [_alloc(st, nc, "hT%d" % i, [128, 8, 128], BF16) for i in range(2)]
        dhT = [Dep("hT%d" % i) for i in range(2)]
        cs = [_alloc(st, nc, "cs%d" % i, [128, 320], F32) for i in range(2)]
        dcs = [Dep("cs%d" % i) for i in range(2)]
        mk = [_alloc(st, nc, "mk%d" % i, [128, 384], BF16) for i in range(2)]
        dmk = [Dep("mk%d" % i) for i in range(2)]
        qtok = [_alloc(st, nc, "qtok%d" % i, [128, 16, 64], BF16) for i in range(2)]
        dqtok = [Dep("qtok%d" % i) for i in range(2)]
        kdtok = [_alloc(st, nc, "kdtok%d" % i, [128, 4, 2, 128], BF16) for i in range(2)]
        dkdtok = [Dep("kdtok%d" % i) for i in range(2)]
        for i in range(2):
            em.op("dve", lambda e, i=i: e.memset(kdtok[i][:, :, :, :], 0.0), writes=[dkdtok[i]])
        rt = [_alloc(st, nc, "rt%d" % i, [128, 20, 8], F32) for i in range(4)]
        drt = [Dep("rt%d" % i) for i in range(4)]
        NQ = 3
        qT = [_alloc(st, nc, "qT%d" % i, [128, 8, 128], BF16) for i in range(NQ)]
        dqT = [Dep("qT%d" % i) for i in range(NQ)]
        pb = [_alloc(st, nc, "pb%d" % i, [128, 384], BF16) for i in range(4)]
        dpb = [Dep("pb%d" % i) for i in range(4)]
        pT = [_alloc(st, nc, "pT%d" % i, [128, 3, 128], BF16) for i in range(3)]
        dpT = [Dep("pT%d" % i) for i in range(3)]
        stt = [_alloc(st, nc, "stt%d" % i, [128, 6, 16], F32) for i in range(2)]
        dsth = [[Dep("st%d_%d" % (i, h)) for h in range(16)] for i in range(2)]
        dfin = [[Dep("fin%d_%d" % (i, k)) for k in range(2)] for i in range(2)]
        otok = [_alloc(st, nc, "otok%d" % i, [128, 16, 64], BF16) for i in range(2)]
        dotok = [[Dep("otok%d_%d" % (i, k)) for k in range(2)] for i in range(2)]
        oT = _alloc(st, nc, "oT", [128, 8, 128], BF16)
        doT = Dep("oT")
        junk = _alloc(st, nc, "junk", [128, D], BF16)
        djunk = Dep("junk")
        sst = [_alloc(st, nc, "ss%d" % i, [128, 4], F32) for i in range(4)]
        dsst = [Dep("ss%d" % i) for i in range(4)]

        qkv = _palloc(st, nc, "qkv", [128, 1024], F32)
        dq01 = PDep("qkv01")
        tp = _palloc(st, nc, "tp", [128, 8, 128], BF16)
        dtp = PDep("tp")
        sps = _palloc(st, nc, "sps", [128, 4, 512], F32)
        dsps = [PDep("sps%d" % i) for i in range(4)]
        ops = _palloc(st, nc, "ops", [128, 8, 64], F32)
        dops = PDep("ops")
        sctr = [0]

        fsl = {}
        csl = {}
        def A_steps(b):
            xs, dxs = xa[b % NX], dxa[b % NX]
            c_, dc_ = cs[b % 2], dcs[b % 2]
            h_ = hT[b % 2]
            qt, dqt = qtok[b % 2], dqtok[b % 2]
            kd, dkd = kdtok[b % 2], dkdtok[b % 2]

            def a_front():
                si = sctr[0] % 4
                sctr[0] += 1
                fsl[b] = si
                emit_front_A(C, em, xs, dxs, hb[b % 2], dhb[b % 2], sst[si], dsst[si])

            def a_frontB():
                si = fsl.pop(b)
                emit_front_B(C, em, xs, dxs, hb[b % 2], dhb[b % 2], sst[si], dsst[si], gpre, dgpre)

            def a0():
                hbs = hb[b % 2]

                def tps(e):
                    for k in range(8):
                        i = e.transpose(tp[:, k, :], hbs[:, k * 128:(k + 1) * 128], C.ident[:, :])
                    return i
                em.op("pe", tps, reads=[dhb[b % 2], C.dident], writes=[dtp])
                em.op("act", lambda e: e.activation(out=h_[:, :, :], in_=tp[:, :, :], func=AF.Copy), reads=[dtp], writes=[dhT[b % 2]])

            def a1q():
                for g in range(2):
                    def f(e, g=g):
                        for k in range(8):
                            i = e.matmul(qkv[:, g * 512:(g + 1) * 512], lhsT=h_[:, k, :], rhs=Wqkv[:, k, g * 512:(g + 1) * 512],
                                         start=(k == 0), stop=(k == 7))
                        return i
                    em.op("pe", f, reads=[dhT[b % 2], dWqkv], writes=[dq01])
                qv = qkv[:, 0:1024].rearrange("p (h d) -> p h d", d=64)
                em.op("act", lambda e: e.activation(out=qt[:, :, 16:64], in_=qv[:, :, 16:64], func=AF.Copy, scale=SCALE),
                      reads=[dq01], writes=[dqt])

            def a2q():
                qv = qkv[:, 0:1024].rearrange("p (h d) -> p h d", d=64)
                cosv = c_[:, 0:160].rearrange("p (h d) -> p h d", d=8)[:, 0:16, :]
                sinv = c_[:, 160:320].rearrange("p (h d) -> p h d", d=8)[:, 0:16, :]
                x1, x2 = qv[:, :, 0:8], qv[:, :, 8:16]
                for i, (xx, tb) in enumerate([(x1, cosv), (x2, sinv), (x2, cosv), (x1, sinv)]):
                    em.op("dve", lambda e, i=i, xx=xx, tb=tb: e.tensor_tensor(out=rt[i][:, 0:16, :], in0=xx, in1=tb, op=ALU.mult),
                          reads=[dq01, dc_], writes=[drt[i]])
                em.op("dve", lambda e: e.tensor_tensor(out=qt[:, :, 0:8], in0=rt[0][:, 0:16, :], in1=rt[1][:, 0:16, :], op=ALU.subtract),
                      reads=[drt[0], drt[1]], writes=[dqt])
                em.op("dve", lambda e: e.tensor_tensor(out=qt[:, :, 8:16], in0=rt[2][:, 0:16, :], in1=rt[3][:, 0:16, :], op=ALU.add),
                      reads=[drt[2], drt[3]], writes=[dqt])

            def a1kv():
                def f(e):
                    for k in range(8):
                        i = e.matmul(qkv[:, 0:512], lhsT=h_[:, k, :], rhs=Wqkv[:, k, 1024:1536], start=(k == 0), stop=(k == 7))
                    return i
                em.op("pe", f, reads=[dhT[b % 2], dWqkv], writes=[dq01])
                kv = qkv[:, 0:256].rearrange("p (h d) -> p h d", d=64)
                for dup in range(2):
                    em.op("act", lambda e, dup=dup: e.activation(out=kd[:, :, dup, dup * 64 + 16:dup * 64 + 64], in_=kv[:, :, 16:64],
                                                                 func=AF.Copy), reads=[dq01], writes=[dkd])
                em.op("act", lambda e: e.activation(out=vA[:, b + 1, :], in_=qkv[:, 256:512], func=AF.Copy),
                      reads=[dq01], writes=[dvA[b + 1]])

            def a2kv():
                kv = qkv[:, 0:256].rearrange("p (h d) -> p h d", d=64)
                cosv = c_[:, 0:160].rearrange("p (h d) -> p h d", d=8)[:, 16:20, :]
                sinv = c_[:, 160:320].rearrange("p (h d) -> p h d", d=8)[:, 16:20, :]
                x1, x2 = kv[:, :, 0:8], kv[:, :, 8:16]
                for i, (xx, tb) in enumerate([(x1, cosv), (x2, sinv), (x2, cosv), (x1, sinv)]):
                    em.op("dve", lambda e, i=i, xx=xx, tb=tb: e.tensor_tensor(out=rt[i][:, 16:20, :], in0=xx, in1=tb, op=ALU.mult),
                          reads=[dq01, dc_], writes=[drt[i]])
                for dup in range(2):
                    em.op("dve", lambda e, dup=dup: e.tensor_tensor(out=kd[:, :, dup, dup * 64:dup * 64 + 8], in0=rt[0][:, 16:20, :],
                                                                    in1=rt[1][:, 16:20, :], op=ALU.subtract),
                          reads=[drt[0], drt[1]], writes=[dkd])
                    em.op("dve", lambda e, dup=dup: e.tensor_tensor(out=kd[:, :, dup, dup * 64 + 8:dup * 64 + 16], in0=rt[2][:, 16:20, :],
                                                                    in1=rt[3][:, 16:20, :], op=ALU.add),
                          reads=[drt[2], drt[3]], writes=[dkd])

            def a3():
                qflat = qt[:, :, :].rearrange("p h d -> p (h d)")

                def tq(e):
                    for k in range(8):
                        i = e.transpose(tp[:, k, :], qflat[:, k * 128:(k + 1) * 128], C.ident[:, :])
                    return i
                em.op("pe", tq, reads=[dqt, C.dident], writes=[dtp])
                em.op("act", lambda e: e.activation(out=qT[b % NQ][:, :, :], in_=tp[:, :, :], func=AF.Copy),
                      reads=[dtp], writes=[dqT[b % NQ]])

            def a4():
                kflat = kd[:, :, :, :].rearrange("p g u d -> p (g u d)")

                def tk(e):
                    for k in range(8):
                        i = e.transpose(tp[:, k, :], kflat[:, k * 128:(k + 1) * 128], C.ident[:, :])
                    return i
                em.op("pe", tk, reads=[dkd, C.dident], writes=[dtp])
                em.op("act", lambda e: e.activation(out=kT[:, :, (b + 1) * 128:(b + 2) * 128], in_=tp[:, :, :], func=AF.Copy),
                      reads=[dtp], writes=[dkT[b + 1]])
            return [a_front, a0, a1q, a2q, a1kv, a2kv, a3, a4, a_frontB]

        def A_loads(b):
            em.dma("sp", xa[b % NX][:, :], src[b * 128:(b + 1) * 128, :], reads=[dsrc[b]], writes=[dxa[b % NX]])
            em.dma("sp", cs[b % 2][:, :], C.acs[b * 128:(b + 1) * 128, :], writes=[dcs[b % 2]])

        def C_steps(b):
            ot = otok[b % 2]

            def c0():
                oflat = ot[:, :, :].rearrange("p h d -> p (h d)")

                def to(e):
                    for k in range(8):
                        i = e.transpose(tp[:, k, :], oflat[:, k * 128:(k + 1) * 128], C.ident[:, :])
                    return i
                em.op("pe", to, reads=dotok[b % 2] + [C.dident], writes=[dtp])
                em.op("act", lambda e: e.activation(out=oT[:, :, :], in_=tp[:, :, :], func=AF.Copy), reads=[dtp], writes=[doT])

            def c1():
                def fw(e):
                    for half in range(2):
                        for k in range(8):
                            i = e.matmul(qkv[:, half * 512:(half + 1) * 512], lhsT=oT[:, k, :], rhs=Wo[:, k, half * 512:(half + 1) * 512],
                                         start=(k == 0), stop=(k == 7))
                    return i
                em.op("pe", fw, reads=[doT, dWo], writes=[dq01])

            def c2():
                si = sctr[0] % 4
                sctr[0] += 1
                csl[b] = si
                emit_post_A(C, em, qkv[:, 0:1024], dq01, sst[si], dsst[si], junk, djunk)

            def c2B():
                si = csl.pop(b)
                emit_post_B(C, em, qkv[:, 0:1024], dq01, xa[b % NX], dxa[b % NX], gpost, dgpost, sst[si], dsst[si],
                            C.y[b * 128:(b + 1) * 128, :], C.dy[b])
            return [c0, c1, c2, c2B]

        ocp = [_alloc(st, nc, "ocp%d" % i, [128, 8, 64], F32) for i in range(2)]
        docp = [Dep("ocp%d" % i) for i in range(2)]

        def finish_heads(b, h0):
            s_ = stt[b % 2]
            k = h0 // 8
            dsts = dsth[b % 2][h0:h0 + 8]
            df = dfin[b % 2][k]
            hs = slice(h0, h0 + 8)
            em.op("dve", lambda e: e.tensor_copy(out=ocp[k][:, :, :], in_=ops[:, :, :]), reads=[dops], writes=[docp[k]])
            em.op("dve", lambda e: e.tensor_tensor(out=s_[:, 3, hs], in0=s_[:, 1, hs], in1=sinkt[:, hs], op=ALU.add),
                  reads=dsts + [dsink], writes=[df])
            em.op("act", lambda e: e.activation(out=s_[:, 4, hs], in_=s_[:, 3, hs], func=AF.Exp), reads=[df], writes=[df])
            em.op("dve", lambda e: e.tensor_tensor(out=s_[:, 4, hs], in0=s_[:, 4, hs], in1=s_[:, 2, hs], op=ALU.add),
                  reads=dsts + [df], writes=[df])
            em.op("dve", lambda e: e.reciprocal(out=s_[:, 5, hs], in_=s_[:, 4, hs]), reads=[df], writes=[df])
            em.op("dve", lambda e: e.tensor_tensor(out=otok[b % 2][:, hs, :], in0=ocp[k][:, :, :],
                                                   in1=s_[:, 5, hs].unsqueeze(2).broadcast_to([128, 8, 64]), op=ALU.mult),
                  reads=[docp[k], df], writes=[dotok[b % 2][k]])

        def stageB(b, steps):
            m_, dm_ = mk[b % 2], dmk[b % 2]
            em.dma("sp", m_[:, :], C.amask[b], writes=[dm_])
            s_ = stt[b % 2]
            q_ = qT[b % NQ]

            def S(h):
                g, pr, hf = h // 4, h // 2, h % 2
                bk = h % 4
                bank = sps[:, bk, 0:384]

                def f(e):
                    e.matmul(bank, lhsT=q_[:, pr, :], rhs=kT[:, g * 2 + hf, b * 128: b * 128 + 384], start=True, stop=False)
                    return e.matmul(bank, lhsT=C.ident[:, :], rhs=m_[:, :], start=False, stop=True)
                em.op("pe", f, reads=[dqT[b % NQ], dkT[b], dkT[b + 1], dkT[b + 2], dm_, C.dident], writes=[dsps[bk]])
                dst = dsth[b % 2][h]
                em.op("dve", lambda e: e.tensor_reduce(out=s_[:, 0, h:h + 1], in_=bank, op=ALU.max, axis=AX.X, negate=True),
                      reads=[dsps[bk]], writes=[dst])
                em.op("dve", lambda e: e.tensor_tensor(out=s_[:, 1, h:h + 1], in0=s_[:, 0, h:h + 1], in1=nsink[:, h:h + 1], op=ALU.min),
                      reads=[dst, dsink], writes=[dst])
                em.op("act", lambda e: e.activation(out=pb[bk][:, :], in_=bank, func=AF.Exp,
                                                    bias=s_[:, 1, h:h + 1], accum_out=s_[:, 2, h:h + 1]),
                      reads=[dsps[bk], dst], writes=[dpb[bk], dst])

            def T(h):
                bk = h % 4
                tb = sps[:, bk, :].bitcast(BF16)[:, 0:384].rearrange("p (c t) -> p c t", t=128)

                def ft(e):
                    for c in range(3):
                        r = e.transpose(tb[:, c, :], pb[bk][:, c * 128:(c + 1) * 128], C.ident[:, :])
                    return r
                em.op("pe", ft, reads=[dpb[bk], C.dident], writes=[dsps[bk]])
                if h % 2 == 0:
                    em.op("dve", lambda e: e.tensor_copy(out=pT[h % 3][:, :, :], in_=tb), reads=[dsps[bk]], writes=[dpT[h % 3]])
                else:
                    em.op("act", lambda e: e.activation(out=pT[h % 3][:, :, :], in_=tb, func=AF.Copy), reads=[dsps[bk]], writes=[dpT[h % 3]])

            def PV(h):
                g = h // 4

                def fo(e):
                    for c in range(3):
                        r = e.matmul(ops[:, h % 8, :], lhsT=pT[h % 3][:, c, :], rhs=vA[:, b + c, g * 64:(g + 1) * 64],
                                     start=(c == 0), stop=(c == 2))
                    return r
                em.op("pe", fo, reads=[dpT[h % 3], dvA[b], dvA[b + 1], dvA[b + 2]], writes=[dops])
                if h % 8 == 7:
                    finish_heads(b, h - 7)

            for h in range(3):
                S(h)
            for h in range(16):
                T(h)
                if h >= 1:
                    PV(h - 1)
                if h + 3 < 16:
                    S(h + 3)
                if h in (1, 3, 5, 7, 9, 11, 13, 14, 15) and steps:
                    steps.pop(0)()
            PV(15)
            while steps:
                steps.pop(0)()

        nblk = getattr(C, "nblk", NCH)
        nop = lambda: None

        def run_A(b):
            A = A_steps(b)
            for k in (0, 8, 1, 2, 3, 4, 5, 6, 7):
                A[k]()
        for b0 in range(min(2, NCH)):
            A_loads(b0)
        run_A(0)
        if NCH > 2:
            A_loads(2)
        if NCH > 1:
            run_A(1)
        for b in range(nblk):
            Cs = C_steps(b - 1) if b >= 1 else [nop] * 4
            As = A_steps(b + 2) if b + 2 < NCH else [nop] * 9
            ld = (lambda b=b: A_loads(b + 3)) if b + 3 < NCH else nop
            steps = [lambda Cs=Cs, As=As: (Cs[0](), As[0]()),
                     lambda Cs=Cs, As=As: (Cs[1](), As[8]()),
                     lambda Cs=Cs, As=As: (Cs[2](), As[1]()),
                     lambda Cs=Cs, ld=ld: (Cs[3](), ld()),
                     As[2], As[3], lambda As=As: (As[4](), As[6]()), As[5], As[7]]
            stageB(b, steps)
        for s_ in C_steps(nblk - 1):
            s_()
        em.barrier()


def emit_ret(C, l, src, dsrc):
    nc, em = C.nc, C.em
    nch = getattr(C, "nblk", NCH)
    half_b = NCH // 2
    with ExitStack() as st0:
        bnd = _alloc(st0, nc, "bnd", [128, 1], F32)
        kdec = _alloc(st0, nc, "kdec", [128, 8], F32)
        g128 = _alloc(st0, nc, "g128", [128, 8], F32)
        decrow = _alloc(st0, nc, "decrow", [128, 8, 128], F32)
        DT = _alloc(st0, nc, "DT", [128, 4, 128], F32)
        S32 = _alloc(st0, nc, "S32", [128, 4, 2, 512], F32)
        stT = ExitStack()
        rc = _alloc(stT, nc, "rc", [128, 6, 128], F32)
        cpos = _alloc(stT, nc, "cpos", [128, 4], F32)
        dl = _alloc(stT, nc, "dl", [128, 8], F32)
        lg = _alloc(stT, nc, "lg", [128, 8], F32)
        tmpD = _alloc(stT, nc, "tmpD", [128, 2, 128], F32)
        drc, dtab, dS32, dtmp = Dep("rc"), Dep("tab"), Dep("S32"), Dep("tmpD")
        em.dma("sp", rc[:, :, :], C.rconst[:, :, :], writes=[drc])
        em.dma("sp", cpos[:, :], C.rpos[:, :], writes=[drc], owner=drc)
        em.dma("sp", bnd[:, :], C.rbnd[:, :], writes=[drc], owner=drc)
        em.dma("sp", dl[:, 0:4], C.rdf[0, :].partition_broadcast(128), writes=[dtab])
        em.dma("sp", dl[:, 4:8], C.rdb[0, :].partition_broadcast(128), writes=[dtab], owner=dtab)
        em.op("act", lambda e: e.activation(out=lg[:, :], in_=dl[:, :], func=AF.Exp, scale=-1.0), reads=[dtab], writes=[dtab])
        em.op("dve", lambda e: e.tensor_scalar(out=lg[:, :], in0=lg[:, :], scalar1=1.0, scalar2=None, op0=ALU.add), reads=[dtab], writes=[dtab])
        em.op("act", lambda e: e.activation(out=lg[:, :], in_=lg[:, :], func=AF.Ln), reads=[dtab], writes=[dtab])
        em.op("dve", lambda e: e.tensor_scalar(out=lg[:, :], in0=lg[:, :], scalar1=-1.0, scalar2=None, op0=ALU.mult), reads=[dtab], writes=[dtab])
        em.op("act", lambda e: e.activation(out=g128[:, :], in_=lg[:, :], func=AF.Exp, scale=128.0), reads=[dtab], writes=[dtab])
        em.op("act", lambda e: e.activation(out=kdec[:, 0:4], in_=lg[:, 0:4], func=AF.Exp, scale=cpos[:, 2:3]), reads=[dtab, drc], writes=[dtab])
        em.op("act", lambda e: e.activation(out=kdec[:, 4:8], in_=lg[:, 4:8], func=AF.Exp, scale=cpos[:, 3:4]), reads=[dtab, drc], writes=[dtab])
        em.op("dve", lambda e: e.tensor_scalar(out=kdec[:, :], in0=kdec[:, :], scalar1=1.0 / 16, scalar2=None, op0=ALU.mult), reads=[dtab], writes=[dtab])
        for h in range(4):
            em.op("act", lambda e, h=h: e.activation(out=decrow[:, h, :], in_=rc[:, 4, :], func=AF.Exp, scale=lg[:, h:h + 1]), reads=[dtab, drc], writes=[dtab])
            em.op("act", lambda e, h=h: e.activation(out=decrow[:, 4 + h, :], in_=rc[:, 5, :], func=AF.Exp, scale=lg[:, 4 + h:5 + h]), reads=[dtab, drc], writes=[dtab])
            em.op("act", lambda e, h=h: e.activation(out=tmpD[:, 0, :], in_=rc[:, 0, :], func=AF.Exp, scale=lg[:, h:h + 1]), reads=[dtab, drc], writes=[dtmp])
            em.op("act", lambda e, h=h: e.activation(out=tmpD[:, 1, :], in_=rc[:, 1, :], func=AF.Exp, scale=lg[:, 4 + h:5 + h]), reads=[dtab, drc], writes=[dtmp])
            em.op("dve", lambda e, h=h: e.tensor_tensor(out=tmpD[:, :, :], in0=tmpD[:, :, :], in1=rc[:, 2:4, :], op=ALU.mult), reads=[dtmp, drc], writes=[dtmp])
            em.op("dve", lambda e, h=h: e.tensor_tensor(out=DT[:, h, :], in0=tmpD[:, 0, :], in1=tmpD[:, 1, :], op=ALU.add), reads=[dtmp], writes=[dtab])
        em.op("dve", lambda e: e.memset(S32[:, :, :, :], 0.0), writes=[dS32])
        em.barrier()
        stT.close()

        def rotary(ps, dps, c_, dc_, rt, drt, out_bf, dout):
            p4 = ps.rearrange("p (h t f) -> p h t f", h=4, t=2)
            o4 = out_bf[:, :].rearrange("p (h t f) -> p h t f", h=4, t=2)
            cosb = c_[:, 0:128].unsqueeze(1).broadcast_to([128, 4, 128])
            sinb = c_[:, 128:256].unsqueeze(1).broadcast_to([128, 4, 128])
            x1, x2 = p4[:, :, 0, :], p4[:, :, 1, :]
            prods = [(x1, cosb), (x2, sinb), (x2, cosb), (x1, sinb)]
            for i in (0, 1):
                xx, tb = prods[i]
                em.op("dve", lambda e, i=i, xx=xx, tb=tb: e.tensor_tensor(out=rt[i][:, :, :], in0=xx, in1=tb, op=ALU.mult),
                      reads=[dps, dc_], writes=[drt[i]])
            em.op("dve", lambda e: e.tensor_tensor(out=o4[:, :, 0, :], in0=rt[0][:, :, :], in1=rt[1][:, :, :], op=ALU.subtract),
                  reads=[drt[0], drt[1]], writes=[dout])
            for i in (2, 3):
                xx, tb = prods[i]
                em.op("dve", lambda e, i=i, xx=xx, tb=tb: e.tensor_tensor(out=rt[i][:, :, :], in0=xx, in1=tb, op=ALU.mult),
                      reads=[dps, dc_], writes=[drt[i]])
            em.op("dve", lambda e: e.tensor_tensor(out=o4[:, :, 1, :], in0=rt[2][:, :, :], in1=rt[3][:, :, :], op=ALU.add),
                  reads=[drt[2], drt[3]], writes=[dout])

        def state_update(S32, dS32, kd_tok, dkd, v_tok, dv, dsp, dsd, gcol0):
            for h in range(4):
                for c in range(2):
                    def f(e, h=h, c=c):
                        return e.matmul(dsp[:, :], lhsT=kd_tok[:, h * 256 + c * 128: h * 256 + (c + 1) * 128],
                                        rhs=v_tok[:, h * 512:(h + 1) * 512], start=True, stop=True)
                    em.op("pe", f, reads=[dkd, dv], writes=[dsd])
                    em.op("dve", lambda e, h=h, c=c: e.scalar_tensor_tensor(out=S32[:, h, c, :], in0=S32[:, h, c, :],
                                                                          scalar=g128[:, gcol0 + h:gcol0 + h + 1], in1=dsp[:, :],
                                                                          op0=ALU.mult, op1=ALU.add),
                          reads=[dS32, dsd, dtab], writes=[dS32])

        def boundary(S32, dS32):
            em.op("dve", lambda e: e.tensor_scalar(out=S32[:, :, :, :], in0=S32[:, :, :, :], scalar1=bnd[:, 0:1], scalar2=None,
                                                   op0=ALU.mult), reads=[dS32, drc], writes=[dS32])

        with ExitStack() as st:
            Wkv = _alloc(st, nc, "Wkv", [128, 8, 3072], BF16)
            dWkv = [Dep("Wkv%d" % k) for k in range(2)]
            wv = C.rwi[0].rearrange("(k p) f -> p k f", p=128)
            em.dma("pool", Wkv[:, 0:4, :], wv[:, 0:4, 1024:4096], writes=[dWkv[0]])
            em.dma("pool", Wkv[:, 4:8, :], wv[:, 4:8, 1024:4096], writes=[dWkv[1]])
            gpre = _alloc(st, nc, "gpre", [128, D], F32)
            dgpre = Dep("gpre")
            em.dma("sp", gpre[:, :], C.ng[l, 2, :].partition_broadcast(128), writes=[dgpre])
            xa = [_alloc(st, nc, "xa%d" % i, [128, D], F32) for i in range(3)]
            dxa = [Dep("xa%d" % i) for i in range(3)]
            hb2 = [_alloc(st, nc, "hb%d" % i, [128, D], BF16) for i in range(2)]
            dhb2 = [Dep("hb%d" % i) for i in range(2)]
            hT = [_alloc(st, nc, "hT%d" % i, [128, 8, 128], BF16) for i in range(2)]
            dhT = [Dep("hT%d" % i) for i in range(2)]
            NCS = 4
            cs = [_alloc(st, nc, "rcs%d" % i, [128, 256], F32) for i in range(NCS)]
            dcs = [Dep("rcs%d" % i) for i in range(NCS)]
            rt = [_alloc(st, nc, "rrt%d" % i, [128, 4, 128], F32) for i in range(4)]
            drt = [Dep("rrt%d" % i) for i in range(4)]
            krot = [_alloc(st, nc, "krot%d" % i, [128, 1024], BF16) for i in range(2)]
            dkrot = [Dep("krot%d" % i) for i in range(2)]
            kf = [_alloc(st, nc, "kf%d" % i, [128, 1024], BF16) for i in range(2)]
            dkf = [Dep("kf%d" % i) for i in range(2)]
            kb = [_alloc(st, nc, "kb%d" % i, [128, 1024], BF16) for i in range(2)]
            dkb = [Dep("kb%d" % i) for i in range(2)]
            kTb = [_alloc(st, nc, "kTb%d" % i, [128, 8, 128], BF16) for i in range(2)]
            dkTb = [Dep("kTb%d" % i) for i in range(2)]
            vtok = [_alloc(st, nc, "vtok%d" % i, [128, 2048], BF16) for i in range(2)]
            dvtok = [[Dep("vtok%d_%d" % (i, k)) for k in range(2)] for i in range(2)]
            Sbf2 = [_alloc(st, nc, "Sbf%d" % i, [128, 4096], BF16) for i in range(2)]
            dSbfh2 = [[Dep("Sbf%d_%d" % (i, h)) for h in range(4)] for i in range(2)]
            dS32h = [Dep("S32b_%d" % h) for h in range(4)]
            em.op("dve", lambda e: e.memset(S32[:, :, :, :], 0.0), reads=[dS32], writes=dS32h)
            sst = [_alloc(st, nc, "ss%d" % i, [128, 4], F32) for i in range(3)]
            dsst = [Dep("ss%d" % i) for i in range(3)]
            tp = _palloc(st, nc, "tp", [128, 8, 128], BF16)
            dtp = PDep("tp")
            pk = _palloc(st, nc, "pk", [128, 1024], F32)
            dpk = PDep("pk")
            pv1 = _palloc(st, nc, "pv", [128, 1024], F32)
            pv = [pv1, pv1]
            dpv1 = PDep("pv")
            dpv = [dpv1, dpv1]
            dspr = [_palloc(st, nc, "dsp%d" % i, [128, 512], F32) for i in range(2)]
            dsdr = [PDep("dsp%d" % i) for i in range(2)]

            def loads1(n):
                em.dma("sp", xa[n % 3][:, :], src[n * 128:(n + 1) * 128, :], reads=[dsrc[n]], writes=[dxa[n % 3]])
                em.dma("sp", cs[n % NCS][:, :], C.rcs[n * 128:(n + 1) * 128, :], writes=[dcs[n % NCS]])

            def front1A(n):
                r = n % 2
                emit_front_A(C, em, xa[n % 3], dxa[n % 3], hb2[r], dhb2[r], sst[n % 3], dsst[n % 3])

            def front1B(n):
                r = n % 2
                emit_front_B(C, em, xa[n % 3], dxa[n % 3], hb2[r], dhb2[r], sst[n % 3], dsst[n % 3], gpre, dgpre)

            def front1(n):
                front1A(n)
                front1B(n)

            def P1_steps(n):
                r = n % 2
                c_, dc_ = cs[n % NCS], dcs[n % NCS]

                def p0():
                    hbs = hb2[r]

                    def tps(e):
                        for k in range(8):
                            i = e.transpose(tp[:, k, :], hbs[:, k * 128:(k + 1) * 128], C.ident[:, :])
                        return i
                    em.op("pe", tps, reads=[dhb2[r], C.dident], writes=[dtp])
                    em.op("act", lambda e: e.activation(out=hT[r][:, :, :], in_=tp[:, :, :], func=AF.Copy), reads=[dtp], writes=[dhT[r]])

                def p1():
                    for g in range(2):
                        def f(e, g=g):
                            for k in range(8):
                                i = e.matmul(pk[:, g * 512:(g + 1) * 512], lhsT=hT[r][:, k, :], rhs=Wkv[:, k, g * 512:(g + 1) * 512],
                                             start=(k == 0), stop=(k == 7))
                            return i
                        em.op("pe", f, reads=[dhT[r]] + dWkv, writes=[dpk])
                    rotary(pk[:, :], dpk, c_, dc_, rt, drt, krot[r], dkrot[r])

                def p2():
                    for (dst, ddst, col) in ((kf[r], dkf[r], 0), (kb[r], dkb[r], 4)):
                        em.op("dve", lambda e, dst=dst, col=col: e.tensor_tensor(
                            out=dst[:, :].rearrange("p (h f) -> p h f", h=4), in0=krot[r][:, :].rearrange("p (h f) -> p h f", h=4),
                            in1=kdec[:, col:col + 4].unsqueeze(2).broadcast_to([128, 4, 256]), op=ALU.mult),
                            reads=[dkrot[r], dtab], writes=[ddst])

                    def tk(e):
                        for k in range(8):
                            i = e.transpose(tp[:, k, :], krot[r][:, k * 128:(k + 1) * 128], C.ident[:, :])
                        return i
                    em.op("pe", tk, reads=[dkrot[r], C.dident], writes=[dtp])
                    em.op("act", lambda e: e.activation(out=kTb[r][:, :, :], in_=tp[:, :, :], func=AF.Copy), reads=[dtp], writes=[dkTb[r]])
                    em.dma("sp", C.s_kT[n], kTb[r][:, :, :].rearrange("p k t -> p (k t)"), reads=[dkTb[r]], writes=[C.dscr[n]], owner=dkTb[r])
                    em.dma("sp", C.s_kf[n], kf[r][:, :], reads=[dkf[r]], writes=[C.dscr[n]], owner=dkf[r])

                def mkv(hv):
                    def pv_():
                        for g in range(2):
                            def f(e, g=g):
                                col = 1024 + hv * 1024 + g * 512
                                for k in range(8):
                                    i = e.matmul(pv[hv][:, g * 512:(g + 1) * 512], lhsT=hT[r][:, k, :], rhs=Wkv[:, k, col:col + 512],
                                                 start=(k == 0), stop=(k == 7))
                                return i
                            em.op("pe", f, reads=[dhT[r]] + dWkv, writes=[dpv[hv]])
                        em.op("act", lambda e: e.activation(out=vtok[r][:, hv * 1024:(hv + 1) * 1024], in_=pv[hv][:, :], func=AF.Copy),
                              reads=[dpv[hv]], writes=[dvtok[r][hv]])
                    return pv_

                def p5():
                    em.dma("sp", C.s_v[n], vtok[r][:, :], reads=dvtok[r], writes=[C.dscr[n]], owner=dvtok[r][0])
                return [p0, p1, mkv(0), mkv(1), p2, p5]

            def U(n, steps):
                r = n % 2
                if n - 3 >= 0:
                    loads1(n - 3)
                if n - 2 >= 0:
                    front1A(n - 2)
                Sbf, dSbfh = Sbf2[n % 2], dSbfh2[n % 2]
                for h in range(4):
                    if h == 2 and n - 2 >= 0:
                        front1B(n - 2)
                    em.op("act", lambda e, h=h: e.activation(out=Sbf[:, h * 1024:(h + 1) * 1024],
                                                             in_=S32[:, h, :, :].rearrange("p c f -> p (c f)"), func=AF.Copy),
                          reads=[dS32h[h]], writes=[dSbfh[h]])
                    if h == 3:
                        em.dma("sp", C.s_sb[n], Sbf[:, :], reads=dSbfh, writes=[C.dscr[n]], owner=dSbfh[0])
                    for c in range(2):
                        if n > 0:
                            dsp, dsd = dspr[c], dsdr[c]

                            def f(e, h=h, c=c, dsp=dsp):
                                return e.matmul(dsp[:, :], lhsT=kb[r][:, h * 256 + c * 128: h * 256 + (c + 1) * 128],
                                                rhs=vtok[r][:, h * 512:(h + 1) * 512], start=True, stop=True)
                            em.op("pe", f, reads=[dkb[r], dvtok[r][h // 2]], writes=[dsd])
                            em.op("dve", lambda e, h=h, c=c, dsp=dsp: e.scalar_tensor_tensor(out=S32[:, h, c, :], in0=S32[:, h, c, :],
                                                                                           scalar=g128[:, 4 + h:5 + h], in1=dsp[:, :],
                                                                                           op0=ALU.mult, op1=ALU.add),
                                  reads=[dS32h[h], dsd, dtab], writes=[dS32h[h]])
                        if steps:
                            steps.pop(0)()
                while steps:
                    steps.pop(0)()
                if n > 0 and n == half_b:
                    em.op("dve", lambda e: e.tensor_scalar(out=S32[:, :, :, :], in0=S32[:, :, :, :], scalar1=bnd[:, 0:1], scalar2=None,
                                                           op0=ALU.mult), reads=dS32h + [drc], writes=dS32h)

            for i in range(1, 4):
                if nch - i >= 0:
                    loads1(nch - i)
            front1(nch - 1)
            if nch > 1:
                front1(nch - 2)
            for s_ in P1_steps(nch - 1):
                s_()
            for n in range(nch - 1, -1, -1):
                U(n, P1_steps(n - 1) if n > 0 else [])
            em.barrier()

        em.op("dve", lambda e: e.memset(S32[:, :, :, :], 0.0), writes=[dS32])
        with ExitStack() as st:
            Wqg = _alloc(st, nc, "Wqg", [128, 8, 3072], BF16)
            dWqg = [Dep("Wqg%d" % k) for k in range(2)]
            wv = C.rwi[0].rearrange("(k p) f -> p k f", p=128)
            em.dma("pool", Wqg[:, :, 0:1024], wv[:, :, 0:1024], writes=[dWqg[0]])
            em.dma("pool", Wqg[:, :, 1024:3072], wv[:, :, 4096:6144], writes=[dWqg[1]])
            Wro = _alloc(st, nc, "Wro", [128, 16, D], BF16)
            dWro = Dep("Wro")
            em.dma("pool", Wro[:, :, :], C.rwo[0].rearrange("(k p) f -> p k f", p=128), writes=[dWro])
            gpre, dgpre, gpost, dgpost = load_gains(C, em, st, nc, l, 2, 3, 1.0)
            NX = 4
            xa = [_alloc(st, nc, "xa%d" % i, [128, D], F32) for i in range(NX)]
            dxa = [Dep("xa%d" % i) for i in range(NX)]
            hb = _alloc(st, nc, "hb", [128, D], BF16)
            dhb = Dep("hb")
            NH = 3
            hT = [_alloc(st, nc, "hT%d" % i, [128, 8, 128], BF16) for i in range(NH)]
            dhT = [Dep("hT%d" % i) for i in range(NH)]
            cs = [_alloc(st, nc, "rcs%d" % i, [128, 256], F32) for i in range(2)]
            dcs = [Dep("rcs%d" % i) for i in range(2)]
            rt2 = [_alloc(st, nc, "rrt%d" % i, [128, 4, 128], F32) for i in range(2)]
            drt2 = [Dep("rrt%d" % i) for i in range(2)]
            rt = [rt2[0], rt2[1], rt2[0], rt2[1]]
            drt = [drt2[0], drt2[1], drt2[0], drt2[1]]
            qrot = _alloc(st, nc, "qrot", [128, 1024], BF16)
            dqrot = Dep("qrot")
            qT = [[_alloc(st, nc, "qT%d_%d" % (r, i), [128, 8, 128], BF16) for i in range(3)] for r in range(2)]
            dqT = [[Dep("qT%d_%d" % (r, i)) for i in range(3)] for r in range(2)]
            sg = [_alloc(st, nc, "sg%d" % i, [128, 512], BF16) for i in range(3)]
            dsg = [Dep("sg%d" % i) for i in range(3)]
            NR = 2
            kTl = [_alloc(st, nc, "kTl%d" % i, [128, 8, 128], BF16) for i in range(NR)]
            kfl = [_alloc(st, nc, "kfl%d" % i, [128, 1024], BF16) for i in range(NR)]
            vl = [_alloc(st, nc, "vl%d" % i, [128, 2048], BF16) for i in range(NR)]
            sbl1 = _alloc(st, nc, "sbl", [128, 4, 2, 512], BF16)
            sbl = [sbl1, sbl1]
            dkTl = [Dep("kTl%d" % i) for i in range(NR)]
            dkfl = [Dep("kfl%d" % i) for i in range(NR)]
            dvl = [Dep("vl%d" % i) for i in range(NR)]
            dsblh = [Dep("sbl_%d" % h) for h in range(4)]
            Sfb = _alloc(st, nc, "Sfb", [128, 4, 2, 512], BF16)
            dSfb = [Dep("Sfb%d" % h) for h in range(4)]
            dS32h = [Dep("S32_%d" % h) for h in range(4)]
            STb = [_alloc(st, nc, "STb%d" % i, [128, 128], BF16) for i in range(2)]
            dSTb = [Dep("STb%d" % i) for i in range(2)]
            otok = [_alloc(st, nc, "otok%d" % i, [128, 2048], BF16) for i in range(2)]
            dotok = [[Dep("otok%d_%d" % (i, h)) for h in range(4)] for i in range(2)]
            oT = _alloc(st, nc, "oT", [128, 16, 128], BF16)
            doT = Dep("oT")
            junk = _alloc(st, nc, "junk", [128, D], BF16)
            djunk = Dep("junk")
            junk2 = junk
            djunk2 = djunk
            sst = [_alloc(st, nc, "ss%d" % i, [128, 4], F32) for i in range(6)]
            dsst = [Dep("ss%d" % i) for i in range(6)]
            tp = _palloc(st, nc, "tp", [128, 8, 128], BF16)
            dtp = PDep("tp")
            pq = _palloc(st, nc, "pq", [128, 1024], F32)
            dpq = PDep("pq")
            pG = _palloc(st, nc, "pG", [128, 512], F32)
            dpG = PDep("pG")
            pS = _palloc(st, nc, "pS", [128, 512], F32)
            dpS = PDep("pS")
            pY = [_palloc(st, nc, "pY%d" % i, [128, 512], F32) for i in range(2)]
            dpY = [PDep("pY%d" % i) for i in range(2)]
            pD = _palloc(st, nc, "pD", [128, 512], F32)
            dpD = PDep("pD")
            sctr = [0]
            em.op("dve", lambda e: e.memset(Sfb[:, :, :, :], 0.0), writes=dSfb)
            em.op("dve", lambda e: e.memset(S32[:, :, :, :], 0.0), reads=[dS32], writes=dS32h)

            def nss():
                si = sctr[0] % 6
                sctr[0] += 1
                return sst[si], dsst[si]

            osl = {}
            ysl = {}

            def load_sb(n, h):
                em.dma("sp", sbl1[:, h, :, :].rearrange("p c f -> p (c f)"), C.s_sb[n][:, h * 1024:(h + 1) * 1024],
                       reads=[C.dscr[n]], writes=[dsblh[h]])

            hb2 = [hb, _alloc(st, nc, "hb_b", [128, D], BF16)]
            dhb2 = [dhb, Dep("hb_b")]

            fsl = {}

            def frontA(n):
                xs, dxs = xa[n % NX], dxa[n % NX]
                em.dma("sp", xs[:, :], src[n * 128:(n + 1) * 128, :], reads=[dsrc[n]], writes=[dxs])
                em.dma("sp", cs[n % 2][:, :], C.rcs[n * 128:(n + 1) * 128, :], writes=[dcs[n % 2]])
                ss, dss = nss()
                fsl[n] = (ss, dss)
                emit_front_A(C, em, xs, dxs, hb2[n % 2], dhb2[n % 2], ss, dss)

            def frontB(n):
                ss, dss = fsl.pop(n)
                emit_front_B(C, em, xa[n % NX], dxa[n % NX], hb2[n % 2], dhb2[n % 2], ss, dss, gpre, dgpre)

            def front(n):
                frontA(n)
                frontB(n)

            def P_steps(n):
                r = n % 2
                xs, dxs = xa[n % NX], dxa[n % NX]
                c_, dc_ = cs[r], dcs[r]

                def p0():
                    if n == 0:
                        for h in range(4):
                            load_sb(0, h)
                    hbs = hb2[n % 2]

                    def tps(e):
                        for k in range(8):
                            i = e.transpose(tp[:, k, :], hbs[:, k * 128:(k + 1) * 128], C.ident[:, :])
                        return i
                    em.op("pe", tps, reads=[dhb2[n % 2], C.dident], writes=[dtp])
                    em.op("act", lambda e: e.activation(out=hT[n % NH][:, :, :], in_=tp[:, :, :], func=AF.Copy), reads=[dtp], writes=[dhT[n % NH]])

                def p1():
                    em.dma("sp", kTl[r][:, :, :].rearrange("p k t -> p (k t)"), C.s_kT[n], reads=[C.dscr[n]], writes=[dkTl[r]])
                    em.dma("sp", kfl[r][:, :], C.s_kf[n], reads=[C.dscr[n]], writes=[dkfl[r]])
                    em.dma("sp", vl[r][:, :], C.s_v[n], reads=[C.dscr[n]], writes=[dvl[r]])
                    for g in range(2):
                        def f(e, g=g):
                            for k in range(8):
                                i = e.matmul(pq[:, g * 512:(g + 1) * 512], lhsT=hT[n % NH][:, k, :], rhs=Wqg[:, k, g * 512:(g + 1) * 512],
                                             start=(k == 0), stop=(k == 7))
                            return i
                        em.op("pe", f, reads=[dhT[n % NH], dWqg[0]], writes=[dpq])
                    rotary(pq[:, :], dpq, c_, dc_, rt, drt, qrot, dqrot)

                def p2():
                    def tq(e):
                        for k in range(8):
                            i = e.transpose(tp[:, k, :], qrot[:, k * 128:(k + 1) * 128], C.ident[:, :])
                        return i
                    em.op("pe", tq, reads=[dqrot, C.dident], writes=[dtp])
                    em.op("act", lambda e: e.activation(out=qT[r][0][:, :, :], in_=tp[:, :, :], func=AF.Copy), reads=[dtp], writes=[dqT[r][0]])
                    for i in range(2):
                        em.op("dve", lambda e, i=i: e.tensor_tensor(
                            out=qT[r][1 + i][:, :, :].rearrange("p (h c) t -> p h c t", c=2),
                            in0=qT[r][0][:, :, :].rearrange("p (h c) t -> p h c t", c=2),
                            in1=decrow[:, 4 * i:4 * i + 4, :].unsqueeze(2).broadcast_to([128, 4, 2, 128]), op=ALU.mult),
                            reads=[dqT[r][0], dtab], writes=[dqT[r][1 + i]])
                return [p0, p1, p2]

            def O_steps(n):
                r = n % 2
                steps = []
                for half in range(2):
                    def o_t(half=half):
                        def to(e):
                            for k in range(8):
                                i = e.transpose(tp[:, k, :], otok[r][:, (half * 8 + k) * 128:(half * 8 + k + 1) * 128], C.ident[:, :])
                            return i
                        em.op("pe", to, reads=dotok[r] + [C.dident], writes=[dtp])
                        em.op("act", lambda e: e.activation(out=oT[:, half * 8:(half + 1) * 8, :], in_=tp[:, :, :], func=AF.Copy),
                              reads=[dtp], writes=[doT])
                    steps.append(o_t)

                def o_w():
                    def fw(e):
                        for half in range(2):
                            for k in range(16):
                                i = e.matmul(pq[:, half * 512:(half + 1) * 512], lhsT=oT[:, k, :], rhs=Wro[:, k, half * 512:(half + 1) * 512],
                                             start=(k == 0), stop=(k == 15))
                        return i
                    em.op("pe", fw, reads=[doT, dWro], writes=[dpq])

                def o_e():
                    ss, dss = nss()
                    osl[n] = (ss, dss)
                    emit_post_A(C, em, pq[:, :], dpq, ss, dss, junk, djunk)

                def o_eB():
                    ss, dss = osl.pop(n)
                    emit_post_B(C, em, pq[:, :], dpq, xa[n % NX], dxa[n % NX], gpost, dgpost, ss, dss,
                                C.y[n * 128:(n + 1) * 128, :], C.dy[n])
                steps += [o_w, o_e, o_eB]
                return steps

            def H(n, steps):
                r = n % 2
                last = (n + 1 >= nch)

                def GS(h):
                    def fg(e):
                        for k in range(8):
                            i = e.matmul(pG[:, :], lhsT=hT[n % NH][:, k, :], rhs=Wqg[:, k, 1024 + h * 512:1024 + (h + 1) * 512],
                                         start=(k == 0), stop=(k == 7))
                        return i
                    em.op("pe", fg, reads=[dhT[n % NH], dWqg[1]], writes=[dpG])
                    em.op("act", lambda e: e.activation(out=sg[h % 3][:, :], in_=pG[:, :], func=AF.Silu), reads=[dpG], writes=[dsg[h % 3]])

                    def fs(e):
                        for c in range(2):
                            i = e.matmul(pS[:, 0:128], lhsT=kTl[r][:, 2 * h + c, :], rhs=qT[r][0][:, 2 * h + c, :], start=(c == 0), stop=(c == 1))
                        return i
                    em.op("pe", fs, reads=[dkTl[r], dqT[r][0]], writes=[dpS])
                    em.op("dve", lambda e: e.tensor_tensor(out=STb[h % 2][:, :], in0=pS[:, 0:128], in1=DT[:, h, :], op=ALU.mult),
                          reads=[dpS, dtab], writes=[dSTb[h % 2]])

                def Y(h):
                    yh, dyh = pY[h % 2], dpY[h % 2]

                    def fy(e):
                        e.matmul(yh[:, :], lhsT=STb[h % 2][:, :], rhs=vl[r][:, h * 512:(h + 1) * 512], start=True, stop=False)
                        for c in range(2):
                            e.matmul(yh[:, :], lhsT=qT[r][1][:, 2 * h + c, :], rhs=Sfb[:, h, c, :], start=False, stop=False)
                        for c in range(2):
                            i = e.matmul(yh[:, :], lhsT=qT[r][2][:, 2 * h + c, :], rhs=sbl[r][:, h, c, :], start=False, stop=(c == 1))
                        return i
                    em.op("pe", fy, reads=[dSTb[h % 2], dvl[r], dqT[r][1], dqT[r][2], dSfb[h], dsblh[h]], writes=[dyh])
                    ss, dss = nss()
                    em.op("act", lambda e: e.activation(out=junk2[:, 0:512], in_=yh[:, :], func=AF.Square, accum_out=ss[:, 0:1]),
                          reads=[dyh], writes=[djunk2, dss])
                    emit_rstd(em, ss, dss, C.mhalf, C.dmh, 512)
                    ysl[h] = (ss, dss)

                def YB(h):
                    yh, dyh = pY[h % 2], dpY[h % 2]
                    ss, dss = ysl.pop(h)
                    em.op("dve", lambda e: e.scalar_tensor_tensor(out=otok[r][:, h * 512:(h + 1) * 512], in0=yh[:, :], scalar=ss[:, 2:3],
                                                                  in1=sg[h % 3][:, :], op0=ALU.mult, op1=ALU.mult),
                          reads=[dyh, dss, dsg[h % 3]], writes=[dotok[r][h]])

                def UPD(h, cs_):
                    for c in cs_:
                        def f(e, c=c):
                            return e.matmul(pD[:, :], lhsT=kfl[r][:, h * 256 + c * 128: h * 256 + (c + 1) * 128],
                                            rhs=vl[r][:, h * 512:(h + 1) * 512], start=True, stop=True)
                        em.op("pe", f, reads=[dkfl[r], dvl[r]], writes=[dpD])
                        em.op("dve", lambda e, c=c: e.scalar_tensor_tensor(out=S32[:, h, c, :], in0=S32[:, h, c, :],
                                                                         scalar=g128[:, h:h + 1], in1=pD[:, :],
                                                                         op0=ALU.mult, op1=ALU.add),
                              reads=[dS32h[h], dpD, dtab], writes=[dS32h[h]])
                    if 1 not in cs_:
                        return
                    if n + 1 == half_b:
                        em.op("dve", lambda e: e.tensor_scalar(out=S32[:, h, :, :], in0=S32[:, h, :, :], scalar1=bnd[:, 0:1], scalar2=None,
                                                               op0=ALU.mult), reads=[dS32h[h], drc], writes=[dS32h[h]])
                    em.op("act", lambda e: e.activation(out=Sfb[:, h, :, :], in_=S32[:, h, :, :], func=AF.Copy),
                          reads=[dS32h[h]], writes=[dSfb[h]])

                GS(0)
                for h in range(4):
                    if h + 1 < 4:
                        GS(h + 1)
                    if h >= 1 and not last:
                        UPD(h - 1, [1])
                    Y(h)
                    if h >= 1:
                        YB(h - 1)
                    if not last:
                        load_sb(n + 1, h)
                        UPD(h, [0])
                    for _ in range(2):
                        if steps:
                            steps.pop(0)()
                YB(3)
                if not last:
                    UPD(3, [1])
                while steps:
                    steps.pop(0)()

            nop = lambda: None
            front(0)
            if nch > 1:
                front(1)
            for s_ in P_steps(0):
                s_()
            if nch > 1:
                P_steps(1)[0]()
            for n in range(nch):
                O = O_steps(n - 1) if n >= 1 else [nop] * 5
                Opp = O_steps(n - 2)[4] if n >= 2 else nop
                P = P_steps(n + 1) if n + 1 < nch else [nop] * 3
                P0n = P_steps(n + 2)[0] if n + 2 < nch else nop
                frA = (lambda n=n: frontA(n + 2)) if n + 2 < nch else nop
                frB = (lambda n=n: frontB(n + 2)) if n + 2 < nch else nop
                steps = [lambda Opp=Opp, O=O: (Opp(), O[0]()), P[1], O[1], frA, P[2], lambda O=O, frB=frB: (O[2](), frB()), P0n, O[3]]
                H(n, steps)
            if nch >= 2:
                O_steps(nch - 2)[4]()
            for s_ in O_steps(nch - 1):
                s_()
            em.barrier()


def build(nsub=6, subs=None, nblk=NCH, dbg=3):
    nc = bass.Bass("TRN2", target_bir_lowering=False)
    em = Emitter(nc)
    C = Ctx()
    C.nc, C.em = nc, em

    def din(name, shape, dt=F32):
        return nc.dram_tensor(name, shape, dt, kind="ExternalInput").ap()
    C.xin = din("xin", [NTOK, D])
    C.ng = din("norm_gains", [2, 6, D])
    C.fwi = din("ffn_w_in", [2, 2, D, 2 * DFF])
    C.fwo = din("ffn_w_out", [2, 2, DFF, D])
    C.wqkv = din("attn_w_qkv", [1, D, 1536])
    C.wo = din("attn_w_o", [1, D, D])
    C.sink = din("attn_sink", [1, 16])
    C.rwi = din("ret_w_in", [1, D, 6144])
    C.rwo = din("ret_w_o", [1, 2048, D])
    C.rdf = din("ret_decay_fwd", [1, 4])
    C.rdb = din("ret_decay_bwd", [1, 4])
    C.identd = din("c_ident", [128, 128], BF16)
    C.acs = din("c_acs", [NTOK, 320])
    C.amask = din("c_amask", [NCH, 128, 384], BF16)
    C.rconst = din("c_rconst", [128, 6, 128])
    C.rpos = din("c_rpos", [128, 4])
    C.rbnd = din("c_rbnd", [128, 1])
    C.rcs = din("c_rcs", [NTOK, 256])
    C.s_kT = nc.dram_tensor("s_kT", [NCH, 128, 1024], BF16, kind="Internal").ap()
    C.s_kf = nc.dram_tensor("s_kf", [NCH, 128, 1024], BF16, kind="Internal").ap()
    C.s_v = nc.dram_tensor("s_v", [NCH, 128, 2048], BF16, kind="Internal").ap()
    C.s_sb = nc.dram_tensor("s_sb", [NCH, 128, 4096], BF16, kind="Internal").ap()
    C.dscr = [Dep("scr%d" % i) for i in range(NCH)]
    C.y = nc.dram_tensor("y", [NTOK, D], F32, kind="ExternalOutput").ap()
    C.dy = [Dep("y%d" % i) for i in range(NCH)]
    C.dxin = [Dep("xin%d" % i) for i in range(NCH)]

    C.ident = nc.alloc_sbuf_tensor("ident", [128, 128], BF16)
    C.dident = Dep("ident")
    C.mhalf = nc.alloc_sbuf_tensor("mhalf", [128, 1], F32)
    C.dmh = Dep("mhalf")
    em.dma("sp", C.ident[:, :], C.identd[:, :], writes=[C.dident])
    em.op("pool", lambda e: e.memset(C.mhalf[:, :], -0.5), writes=[C.dmh])

    C.nblk = nblk
    C.dbg = dbg
    if subs is None:
        subs = [("ffn", 0, 0), ("attn", 0, 0), ("ffn", 0, 1), ("ffn", 1, 0), ("ret", 1, 0), ("ffn", 1, 1)]
    src, dsrc = C.xin, C.dxin
    for i, (kind, l, which) in enumerate(subs[:nsub]):
        if kind == "ffn":
            emit_ffn(C, l, which, src, dsrc)
        elif kind == "attn":
            emit_attn(C, l, src, dsrc)
        elif kind == "ret":
            emit_ret(C, l, src, dsrc)
        src, dsrc = C.y, C.dy
    em.finish()
    return nc


def make_consts():
    c = {}
    c["c_ident"] = np.eye(128, dtype=np.float32).astype(ml_dtypes.bfloat16)
    j = np.arange(128, dtype=np.float32)[:, None]
    i = np.arange(128, dtype=np.float32)[None, :]
    rc = np.zeros((128, 6, 128), np.float32)
    rc[:, 0] = np.maximum(i - j, 0.0)
    rc[:, 1] = np.maximum(j - i, 0.0)
    rc[:, 2] = (i >= j) / 16.0
    rc[:, 3] = (j > i) / 16.0
    rc[:, 4] = np.broadcast_to(i + 1.0, (128, 128))
    rc[:, 5] = np.broadcast_to(128.0 - i, (128, 128))
    c["c_rconst"] = rc
    p = np.arange(128, dtype=np.float32)
    c["c_rpos"] = np.stack([p + 1, 128 - p, 127 - p, p], axis=1).astype(np.float32)
    return c


def make_core_consts(seqlen):
    c = {}
    tok = np.arange(NTOK)
    pos = (tok % seqlen).astype(np.float32)
    inv = (500000.0 ** (-np.arange(8, dtype=np.float32) / 8)).astype(np.float32)
    ang = pos[:, None] * inv[None, :]
    cos = np.cos(ang).astype(np.float32)
    sin = np.sin(ang).astype(np.float32)
    hs = np.ones((1, 20, 1), np.float32)
    hs[:, :16] = 0.125
    acs = np.concatenate([(np.tile(cos[:, None, :], (1, 20, 1)) * hs).reshape(NTOK, 160),
                          (np.tile(sin[:, None, :], (1, 20, 1)) * hs).reshape(NTOK, 160)], axis=1)
    c["c_acs"] = np.ascontiguousarray(acs, dtype=np.float32)
    b = np.arange(NCH)[:, None, None]
    qi = np.arange(128)[None, :, None]
    kc = np.arange(384)[None, None, :]
    tq = 128 * b + qi
    tk = 128 * (b - 1) + kc
    valid = (tk >= 0) & (tk < NTOK) & ((tk // seqlen) == (tq // seqlen)) & (np.abs(tk - tq) <= 128)
    c["c_amask"] = np.where(valid, 0.0, NEG).astype(np.float32).astype(ml_dtypes.bfloat16)
    invr = (10000.0 ** (-np.arange(128, dtype=np.float32) / 128)).astype(np.float32)
    angr = pos[:, None] * invr[None, :]
    c["c_rcs"] = np.ascontiguousarray(np.concatenate([np.cos(angr), np.sin(angr)], axis=1), dtype=np.float32)
    c["c_rbnd"] = np.full((128, 1), 1.0 if seqlen == NTOK else 0.0, np.float32)
    return c


def kernel(x_prompt, x_sample, norm_gains, ffn_w_in, ffn_w_out, attn_w_qkv, attn_w_o, attn_sink,
           ret_w_in, ret_w_o, ret_decay_fwd, ret_decay_bwd, _nsub=6, _subs=None, _nblk=NCH, _cores=None, _trace=False, _dbg=3):
    f = lambda a: np.ascontiguousarray(np.asarray(a, dtype=np.float32))
    xp = f(x_prompt).reshape(4, NTOK, D)
    xs = f(x_sample).reshape(4, NTOK, D)
    shared = {
        "norm_gains": f(norm_gains), "ffn_w_in": f(ffn_w_in), "ffn_w_out": f(ffn_w_out),
        "attn_w_qkv": f(attn_w_qkv), "attn_w_o": f(attn_w_o), "attn_sink": f(attn_sink),
        "ret_w_in": f(ret_w_in), "ret_w_o": f(ret_w_o), "ret_decay_fwd": f(ret_decay_fwd),
        "ret_decay_bwd": f(ret_decay_bwd),
    }
    shared.update(make_consts())
    in_maps = []
    cc = [make_core_consts(2048), make_core_consts(4096)]
    for c in range(8):
        m = dict(shared)
        m.update(cc[0] if c < 4 else cc[1])
        m["xin"] = xp[c] if c < 4 else xs[c - 4]
        in_maps.append(m)
    nc = build(_nsub, _subs, _nblk, _dbg)
    if _cores is not None:
        res = run_bass_kernel_spmd(nc, [in_maps[c] for c in _cores], core_ids=list(range(len(_cores))), trace=_trace)
        if _trace:
            print("exec_time_ns", res.exec_time_ns)
        return [np.asarray(r["y"], dtype=np.float32) for r in res.results]
    res = run_bass_kernel_spmd(nc, in_maps, core_ids=list(range(8)))
    outs = [np.asarray(r["y"], dtype=np.float32) for r in res.results]
    y_prompt = np.stack(outs[:4]).reshape(8, 2048, D)
    y_sample = np.stack(outs[4:]).reshape(4, 4096, D)
    return (y_prompt, y_sample)
```

```python
import os
import numpy as np
import ml_dtypes
from contextlib import ExitStack
import concourse.bass as bass
import concourse.mybir as mybir
from concourse.bass_utils import run_bass_kernel_spmd

F32 = mybir.dt.float32
BF16 = mybir.dt.bfloat16
AF = mybir.ActivationFunctionType
ALU = mybir.AluOpType
AX = mybir.AxisListType

NTOK = 4096
NCH = 32
D = 1024
DFF = 2816
EPS = 1e-6
EPOCH = 1 << 30
NEG = -30000.0


class Dep:
    __slots__ = ("name", "w", "rs", "dsem", "dcnt", "ex", "retired")

    def __init__(self, name="", ex=False):
        self.name = name
        self.ex = ex
        self.w = None
        self.rs = {}
        self.dsem = None
        self.dcnt = 0
        self.retired = False


class Emitter:
    def __init__(self, nc):
        self.nc = nc
        self.eng = {"pe": nc.tensor, "act": nc.scalar, "dve": nc.vector,
                    "pool": nc.gpsimd, "sp": nc.sync}
        self.sem = {}
        self.cnt = {}
        self.nsem = 0
        for e in self.eng:
            self.sem[e] = self._newsem("e_" + e)
            self.cnt[e] = 0
        self.waited = {}
        self.dma_owners = []
        self.free_dsems = []
        self.no_recycle = set()

    def _newsem(self, name):
        self.nsem += 1
        return self.nc.alloc_semaphore("%s_%d" % (name, self.nsem))

    def _tick(self, e):
        if self.cnt[e] >= EPOCH:
            self.sem[e] = self._newsem("e_" + e)
            self.cnt[e] = 0
        self.cnt[e] += 1
        return self.sem[e], self.cnt[e]

    def _need(self, e, rec, needs):
        if rec is None:
            return
        if rec[0] == "e":
            _, pe, sem, val = rec
            if pe == e and e == "pe":
                return
            needs.append((sem, val))
        else:
            o = rec[1]
            needs.append((o.dsem, o.dcnt))

    def _collect(self, e, reads, writes):
        needs = []
        for d in reads:
            self._need(e, d.w, needs)
        for d in writes:
            if d.w is not None:
                self._need(e, d.w, needs)
            for k, r in d.rs.items():
                self._need(e, r, needs)
        return needs

    def _emit_waits(self, e, needs):
        eng = self.eng[e]
        best = {}
        for sem, val in needs:
            k = id(sem)
            if k not in best or best[k][1] < val:
                best[k] = (sem, val)
        for k, (sem, val) in best.items():
            wk = (e, k)
            if self.waited.get(wk, 0) >= val:
                continue
            self.waited[wk] = val
            eng.wait_ge(sem, val)

    def op(self, e, fn, reads=(), writes=()):
        xr = [d for d in reads if d.ex]
        needs = []
        if xr:
            for d in xr:
                self._need(e, d.w, needs)
            reads = [d for d in reads if not d.ex]
            writes = list(writes) + [d for d in xr if d not in writes]
        needs += self._collect(e, reads, writes)
        self._emit_waits(e, needs)
        inst = fn(self.eng[e])
        sem, val = self._tick(e)
        inst.then_inc(sem, 1)
        rec = ("e", e, sem, val)
        for d in reads:
            d.rs[e] = rec
        for d in writes:
            d.w = rec
            d.rs = {}
        return inst

    def dma(self, q, out, in_, reads=(), writes=(), owner=None):
        needs = self._collect(q, reads, writes)
        if owner is None:
            owner = writes[0] if writes else reads[0]
        if owner.dsem is None or owner.retired:
            if self.free_dsems and q != "pool":
                owner.dsem, owner.dcnt = self.free_dsems.pop()
            else:
                owner.dsem, owner.dcnt = self._newsem("d_" + owner.name), 0
                if q == "pool":
                    self.no_recycle.add(id(owner.dsem))
            owner.retired = False
            self.dma_owners.append(owner)
        self._emit_waits(q, needs)
        inst = self.eng[q].dma_start(out=out, in_=in_)
        owner.dcnt += 16
        inst.then_inc(owner.dsem, 16)
        rec = ("d", owner)
        for d in reads:
            d.rs["dma%d" % id(owner)] = rec
        for d in writes:
            d.w = rec
            d.rs = {}
        return inst

    def barrier(self):
        pts = [(self.sem[e], self.cnt[e]) for e in self.eng if self.cnt[e] > 0]
        pts += [(o.dsem, o.dcnt) for o in self.dma_owners]
        for e in self.eng:
            self._emit_waits(e, pts)
        for o in self.dma_owners:
            o.retired = True
            if id(o.dsem) not in self.no_recycle:
                self.free_dsems.append((o.dsem, o.dcnt))
        self.dma_owners = []

    def finish(self):
        sp = self.eng["sp"]
        pts = [(o.dsem, o.dcnt) for o in self.dma_owners]
        self._emit_waits("sp", pts)


def PDep(name):
    return Dep(name, ex=True)


class Ctx:
    pass


_uid = [0]


def _alloc(st, nc, name, shape, dt):
    _uid[0] += 1
    return st.enter_context(nc.sbuf_tensor("%s_u%d" % (name, _uid[0]), shape, dt))


def _palloc(st, nc, name, shape, dt):
    _uid[0] += 1
    return st.enter_context(nc.psum_tensor("%s_u%d" % (name, _uid[0]), shape, dt))


def emit_rstd(em, ss, dss, mhalf, dmh, n_feat):
    em.op("pool", lambda e: e.tensor_scalar(out=ss[:, 1:2], in0=ss[:, 0:1], scalar1=1.0 / n_feat,
                                            scalar2=EPS, op0=ALU.mult, op1=ALU.add),
          reads=[dss], writes=[dss])
    em.op("pool", lambda e: e.tensor_tensor(out=ss[:, 2:3], in0=ss[:, 1:2], in1=mhalf[:, 0:1], op=ALU.pow),
          reads=[dss, dmh], writes=[dss])


def emit_prenorm_T(C, em, xs, dxs, gpre, dgpre, hbs, dhbs, ss, dss, tp, dtp, hT_dst, dhT):
    em.op("act", lambda e: e.activation(out=hbs[:, :], in_=xs[:, :], func=AF.Square, accum_out=ss[:, 0:1]),
          reads=[dxs], writes=[dhbs, dss])
    emit_rstd(em, ss, dss, C.mhalf, C.dmh, D)
    em.op("dve", lambda e: e.scalar_tensor_tensor(out=hbs[:, :], in0=xs[:, :], scalar=ss[:, 2:3], in1=gpre[:, :],
                                                  op0=ALU.mult, op1=ALU.mult),
          reads=[dxs, dss, dgpre], writes=[dhbs])

    def tps(e):
        for k in range(8):
            i = e.transpose(tp[:, k, :], hbs[:, k * 128:(k + 1) * 128], C.ident[:, :])
        return i
    em.op("pe", tps, reads=[dhbs, C.dident], writes=[dtp])
    em.op("act", lambda e: e.activation(out=hT_dst, in_=tp[:, :, :], func=AF.Copy), reads=[dtp], writes=[dhT])


def load_gains(C, em, st, nc, l, ipre, ipost, post_scale):
    gpre = _alloc(st, nc, "gpre", [128, D], F32)
    gpost = _alloc(st, nc, "gpost", [128, D], F32)
    dgpre = Dep("gpre")
    dgpost = Dep("gpost")
    em.dma("sp", gpre[:, :], C.ng[l, ipre, :].partition_broadcast(128), writes=[dgpre])
    em.dma("sp", gpost[:, :], C.ng[l, ipost, :].partition_broadcast(128), writes=[dgpost])
    if post_scale != 1.0:
        em.op("pool", lambda e: e.tensor_scalar(out=gpost[:, :], in0=gpost[:, :], scalar1=post_scale, scalar2=0.0,
                                                op0=ALU.mult, op1=ALU.add), reads=[dgpost], writes=[dgpost])
    return gpre, dgpre, gpost, dgpost


def emit_post_A(C, em, ps_out, dps, ss, dss, junk, djunk):
    em.op("act", lambda e: e.activation(out=junk[:, :], in_=ps_out, func=AF.Square, accum_out=ss[:, 0:1]),
          reads=[dps], writes=[djunk, dss])
    emit_rstd(em, ss, dss, C.mhalf, C.dmh, D)


def emit_post_B(C, em, ps_out, dps, xs, dxs, gpost, dgpost, ss, dss, dst_ap, ddst):
    em.op("dve", lambda e: e.scalar_tensor_tensor(out=ps_out, in0=ps_out, scalar=ss[:, 2:3], in1=gpost[:, :],
                                                  op0=ALU.mult, op1=ALU.mult),
          reads=[dps, dss, dgpost], writes=[dps])
    em.op("dve", lambda e: e.tensor_tensor(out=xs[:, :], in0=ps_out, in1=xs[:, :], op=ALU.add),
          reads=[dps, dxs], writes=[dxs])
    em.dma("sp", dst_ap, xs[:, :], reads=[dxs], writes=[ddst], owner=dxs)


def emit_post_residual(C, em, ps_out, dps, xs, dxs, gpost, dgpost, ss, dss, junk, djunk, dst_ap, ddst):
    emit_post_A(C, em, ps_out, dps, ss, dss, junk, djunk)
    emit_post_B(C, em, ps_out, dps, xs, dxs, gpost, dgpost, ss, dss, dst_ap, ddst)


def emit_front_A(C, em, xs, dxs, hbs, dhbs, ss, dss):
    em.op("act", lambda e: e.activation(out=hbs[:, :], in_=xs[:, :], func=AF.Square, accum_out=ss[:, 0:1]),
          reads=[dxs], writes=[dhbs, dss])
    emit_rstd(em, ss, dss, C.mhalf, C.dmh, D)


def emit_front_B(C, em, xs, dxs, hbs, dhbs, ss, dss, gpre, dgpre):
    em.op("dve", lambda e: e.scalar_tensor_tensor(out=hbs[:, :], in0=xs[:, :], scalar=ss[:, 2:3], in1=gpre[:, :],
                                                  op0=ALU.mult, op1=ALU.mult),
          reads=[dxs, dss, dgpre], writes=[dhbs])


def emit_ffn(C, l, which, src, dsrc):
    nc, em = C.nc, C.em
    NJ = DFF // 128
    LAG = 3
    with ExitStack() as st:
        Win = _alloc(st, nc, "Win", [128, 8, 2 * DFF], BF16)
        Wout = _alloc(st, nc, "Wout", [128, NJ, D], BF16)
        JB = [0, 6, 12, 17, 22]
        jblk = [0] * 6 + [1] * 6 + [2] * 5 + [3] * 5
        dWin = [[Dep("WinG%d" % k), Dep("WinU%d" % k)] for k in range(4)]
        dWout = [Dep("Wout%d" % k) for k in range(4)]
        w_in = C.fwi[l, which]
        w_out = C.fwo[l, which]
        wi_v = w_in.rearrange("(k p) f -> p k f", p=128)
        wo_v = w_out.rearrange("(j p) d -> p j d", p=128)
        for k in range(4):
            c0_, c1_ = JB[k] * 128, JB[k + 1] * 128
            em.dma("pool", Win[:, :, c0_:c1_], wi_v[:, :, c0_:c1_], writes=[dWin[k][0]])
            em.dma("pool", Win[:, :, DFF + c0_:DFF + c1_], wi_v[:, :, DFF + c0_:DFF + c1_], writes=[dWin[k][1]])
            em.dma("pool", Wout[:, JB[k]:JB[k + 1], :], wo_v[:, JB[k]:JB[k + 1], :], writes=[dWout[k]])
        gpre, dgpre, gpost, dgpost = load_gains(C, em, st, nc, l, 0 if which == 0 else 4, 1 if which == 0 else 5, 0.5)

        xa = [_alloc(st, nc, "xa%d" % i, [128, D], F32) for i in range(3)]
        dxa = [Dep("xa%d" % i) for i in range(3)]
        xb = [_alloc(st, nc, "xb%d" % i, [128, D], F32) for i in range(3)]
        dxb = [Dep("xb%d" % i) for i in range(3)]
        hb = [_alloc(st, nc, "hb%d" % i, [128, D], BF16) for i in range(2)]
        dhb = [Dep("hb%d" % i) for i in range(2)]
        hT = [_alloc(st, nc, "hT%d" % i, [128, 8, 256], BF16) for i in range(2)]
        dhT = [[Dep("hT%d_%d" % (i, c)) for c in range(2)] for i in range(2)]
        NA = 6
        actT = [_alloc(st, nc, "actT%d" % i, [128, 256], BF16) for i in range(NA)]
        dact = [Dep("actT%d" % i) for i in range(NA)]
        sg = [_alloc(st, nc, "sg%d" % i, [128, 256], BF16) for i in range(2)]
        dsg = [Dep("sg%d" % i) for i in range(2)]
        junk = _alloc(st, nc, "junk", [128, D], BF16)
        djunk = Dep("junk")
        sst = [_alloc(st, nc, "ss%d" % i, [128, 4], F32) for i in range(4)]
        dsst = [Dep("ss%d" % i) for i in range(4)]
        tp = [_palloc(st, nc, "tp%d" % i, [128, 8, 128], BF16) for i in range(2)]
        dtp = [PDep("tp%d" % i) for i in range(2)]
        gu = [_palloc(st, nc, "gu%d" % i, [128, 2, 256], F32) for i in range(2)]
        dgu = [PDep("gu%d" % i) for i in range(2)]
        pout = _palloc(st, nc, "pout", [128, 2, D], F32)
        dpout = [PDep("pout%d" % i) for i in range(2)]

        NT = NTOK // 256
        sctr = [0]

        def loads(t):
            for c in range(2):
                ch = 2 * t + c
                em.dma("sp", xa[ch % 3][:, :], src[ch * 128:(ch + 1) * 128, :], reads=[dsrc[ch]], writes=[dxa[ch % 3]])

        fst = {}

        def frontA(t):
            for c in range(2):
                ch = 2 * t + c
                si = sctr[0] % 4
                sctr[0] += 1
                fst[ch] = si
                emit_front_A(C, em, xa[ch % 3], dxa[ch % 3], hb[ch % 2], dhb[ch % 2], sst[si], dsst[si])

        def frontB(t):
            for c in range(2):
                ch = 2 * t + c
                si = fst.pop(ch)
                emit_front_B(C, em, xa[ch % 3], dxa[ch % 3], hb[ch % 2], dhb[ch % 2], sst[si], dsst[si], gpre, dgpre)

        def transp(t, c):
            ch = 2 * t + c
            hbs = hb[ch % 2]

            def tps(e):
                for k in range(8):
                    i = e.transpose(tp[ch % 2][:, k, :], hbs[:, k * 128:(k + 1) * 128], C.ident[:, :])
                return i
            em.op("pe", tps, reads=[dhb[ch % 2], C.dident], writes=[dtp[ch % 2]])
            em.op("act", lambda e: e.activation(out=hT[t % 2][:, :, c * 128:(c + 1) * 128], in_=tp[ch % 2][:, :, :], func=AF.Copy),
                  reads=[dtp[ch % 2]], writes=[dhT[t % 2][c]])

        def xb_loads(t):
            for tc in range(2):
                ch = 2 * t + tc
                em.dma("sp", xb[ch % 3][:, :], src[ch * 128:(ch + 1) * 128, :], reads=[dsrc[ch]], writes=[dxb[ch % 3]])

        def p1(t, j):
            g = gu[j % 2]
            hTt = hT[t % 2]

            def f(e):
                for half in range(2):
                    for k in range(8):
                        i = e.matmul(g[:, half, :], lhsT=Win[:, k, half * DFF + j * 128: half * DFF + (j + 1) * 128],
                                     rhs=hTt[:, k, :], start=(k == 0), stop=(k == 7))
                return i
            em.op("pe", f, reads=dWin[jblk[j]] + dhT[t % 2], writes=[dgu[j % 2]])
            s = sg[j % 2]
            em.op("act", lambda e: e.activation(out=s[:, :], in_=g[:, 0, :], func=AF.Silu),
                  reads=[dgu[j % 2]], writes=[dsg[j % 2]])
            a = actT[j % NA]
            em.op("dve", lambda e: e.tensor_tensor(out=a[:, :], in0=g[:, 1, :], in1=s[:, :], op=ALU.mult),
                  reads=[dgu[j % 2], dsg[j % 2]], writes=[dact[j % NA]])

        def p2(t, j):
            a = actT[j % NA]

            def f(e):
                for tc in range(2):
                    for half in range(2):
                        i = e.matmul(pout[:, tc, half * 512:(half + 1) * 512], lhsT=a[:, tc * 128:(tc + 1) * 128],
                                     rhs=Wout[:, j, half * 512:(half + 1) * 512], start=(j == 0), stop=(j == NJ - 1))
                return i
            em.op("pe", f, reads=[dact[j % NA], dWout[jblk[j]]], writes=dpout)

        est = {}

        def epilogueA(t):
            for tc in range(2):
                si = sctr[0] % 4
                sctr[0] += 1
                est[(t, tc)] = si
                emit_post_A(C, em, pout[:, tc, :], dpout[tc], sst[si], dsst[si], junk, djunk)

        def epilogueB(t):
            for tc in range(2):
                ch = 2 * t + tc
                si = est.pop((t, tc))
                emit_post_B(C, em, pout[:, tc, :], dpout[tc], xb[ch % 3], dxb[ch % 3], gpost, dgpost, sst[si], dsst[si],
                            C.y[ch * 128:(ch + 1) * 128, :], C.dy[ch])

        loads(0)
        frontA(0)
        frontB(0)
        transp(0, 0)
        transp(0, 1)
        if NT > 1:
            loads(1)
        for t in range(NT):
            for j in range(NJ + LAG):
                if j < NJ:
                    p1(t, j)
                if j >= LAG:
                    p2(t, j - LAG)
                if j == 1 and t >= 1:
                    epilogueB(t - 1)
                if t + 1 < NT:
                    if j == 4:
                        frontA(t + 1)
                    elif j == 7:
                        frontB(t + 1)
                    elif j == 11:
                        transp(t + 1, 0)
                    elif j == 15:
                        transp(t + 1, 1)
                    elif j == 18 and t + 2 < NT:
                        loads(t + 2)
                if j == 13:
                    xb_loads(t)
            epilogueA(t)
        epilogueB(NT - 1)
        em.barrier()


def emit_attn(C, l, src, dsrc):
    nc, em = C.nc, C.em
    SCALE = 0.125
    with ExitStack() as st:
        Wqkv = _alloc(st, nc, "Wqkv", [128, 8, 1536], BF16)
        Wo = _alloc(st, nc, "Wo", [128, 8, D], BF16)
        dWqkv, dWo = Dep("Wqkv"), Dep("Wo")
        em.dma("pool", Wqkv[:, :, :], C.wqkv[0].rearrange("(k p) f -> p k f", p=128), writes=[dWqkv])
        em.dma("pool", Wo[:, :, :], C.wo[0].rearrange("(k p) f -> p k f", p=128), writes=[dWo])
        gpre, dgpre, gpost, dgpost = load_gains(C, em, st, nc, l, 2, 3, 1.0)
        sinkt = _alloc(st, nc, "sinkt", [128, 16], F32)
        nsink = _alloc(st, nc, "nsink", [128, 16], F32)
        dsink = Dep("sink")
        em.dma("sp", sinkt[:, :], C.sink[0, :].partition_broadcast(128), writes=[dsink])
        em.op("pool", lambda e: e.tensor_scalar(out=nsink[:, :], in0=sinkt[:, :], scalar1=-1.0, scalar2=0.0,
                                                op0=ALU.mult, op1=ALU.add), reads=[dsink], writes=[dsink])

        kT = _alloc(st, nc, "kT_all", [128, 8, 34 * 128], BF16)
        vA = _alloc(st, nc, "v_all", [128, 34, 256], BF16)
        dkT = [Dep("kT%d" % i) for i in range(34)]
        dvA = [Dep("vA%d" % i) for i in range(34)]
        for i in (0, 33):
            em.op("dve", lambda e, i=i: e.memset(kT[:, :, i * 128:(i + 1) * 128], 0.0), writes=[dkT[i]])
            em.op("dve", lambda e, i=i: e.memset(vA[:, i, :], 0.0), writes=[dvA[i]])

        NX = 5
        xa = [_alloc(st, nc, "xa%d" % i, [128, D], F32) for i in range(NX)]
        dxa = [Dep("xa%d" % i) for i in range(NX)]
        hb = [_alloc(st, nc, "hb%d" % i, [128, D], BF16) for i in range(2)]
        dhb = [Dep("hb%d" % i) for i in range(2)]
        hT = [_alloc(st, nc, "hT%d" % i, [128, 8, 128], BF16) for i in range(2)]
        dhT = [Dep("hT%d" % i) for i in range(2)]
        cs = [_alloc(st, nc, "cs%d" % i, [128, 320], F32) for i in range(2)]
        dcs = [Dep("cs%d" % i) for i in range(2)]
        mk = [_alloc(st, nc, "mk%d" % i, [128, 384], BF16) for i in range(2)]
        dmk = [Dep("mk%d" % i) for i in range(2)]
        qtok = [_alloc(st, nc, "qtok%d" % i, [128, 16, 64], BF16) for i in range(2)]
        dqtok = [Dep("qtok%d" % i) for i in range(2)]
        kdtok = [_alloc(st, nc, "kdtok%d" % i, [128, 4, 2, 128], BF16) for i in range(2)]
        dkdtok = [Dep("kdtok%d" % i) for i in range(2)]
        for i in range(2):
            em.op("dve", lambda e, i=i: e.memset(kdtok[i][:, :, :, :], 0.0), writes=[dkdtok[i]])
        rt = [_alloc(st, nc, "rt%d" % i, [128, 20, 8], F32) for i in range(4)]
        drt = [Dep("rt%d" % i) for i in range(4)]
        NQ = 3
        qT = [_alloc(st, nc, "qT%d" % i, [128, 8, 128], BF16) for i in range(NQ)]
        dqT = [Dep("qT%d" % i) for i in range(NQ)]
        pb = [_alloc(st, nc, "pb%d" % i, [128, 384], BF16) for i in range(4)]
        dpb = [Dep("pb%d" % i) for i in range(4)]
        pT = [_alloc(st, nc, "pT%d" % i, [128, 3, 128], BF16) for i in range(3)]
        dpT = [Dep("pT%d" % i) for i in range(3)]
        stt = [_alloc(st, nc, "stt%d" % i, [128, 6, 16], F32) for i in range(2)]
        dsth = [[Dep("st%d_%d" % (i, h)) for h in range(16)] for i in range(2)]
        dfin = [[Dep("fin%d_%d" % (i, k)) for k in range(2)] for i in range(2)]
        otok = [_alloc(st, nc, "otok%d" % i, [128, 16, 64], BF16) for i in range(2)]
        dotok = [[Dep("otok%d_%d" % (i, k)) for k in range(2)] for i in range(2)]
        oT = _alloc(st, nc, "oT", [128, 8, 128], BF16)
        doT = Dep("oT")
        junk = _alloc(st, nc, "junk", [128, D], BF16)
        djunk = Dep("junk")
        sst = [_alloc(st, nc, "ss%d" % i, [128, 4], F32) for i in range(4)]
        dsst = [Dep("ss%d" % i) for i in range(4)]

        qkv = _palloc(st, nc, "qkv", [128, 1024], F32)
        dq01 = PDep("qkv01")
        tp = _palloc(st, nc, "tp", [128, 8, 128], BF16)
        dtp = PDep("tp")
        sps = _palloc(st, nc, "sps", [128, 4, 512], F32)
        dsps = [PDep("sps%d" % i) for i in range(4)]
        ops = _palloc(st, nc, "ops", [128, 8, 64], F32)
        dops = PDep("ops")
        sctr = [0]

        fsl = {}
        csl = {}
        def A_steps(b):
            xs, dxs = xa[b % NX], dxa[b % NX]
            c_, dc_ = cs[b % 2], dcs[b % 2]
            h_ = hT[b % 2]
            qt, dqt = qtok[b % 2], dqtok[b % 2]
            kd, dkd = kdtok[b % 2], dkdtok[b % 2]

            def a_front():
                si = sctr[0] % 4
                sctr[0] += 1
                fsl[b] = si
                emit_front_A(C, em, xs, dxs, hb[b % 2], dhb[b % 2], sst[si], dsst[si])

            def a_frontB():
                si = fsl.pop(b)
                emit_front_B(C, em, xs, dxs, hb[b % 2], dhb[b % 2], sst[si], dsst[si], gpre, dgpre)

            def a0():
                hbs = hb[b % 2]

                def tps(e):
                    for k in range(8):
                        i = e.transpose(tp[:, k, :], hbs[:, k * 128:(k + 1) * 128], C.ident[:, :])
                    return i
                em.op("pe", tps, reads=[dhb[b % 2], C.dident], writes=[dtp])
                em.op("act", lambda e: e.activation(out=h_[:, :, :], in_=tp[:, :, :], func=AF.Copy), reads=[dtp], writes=[dhT[b % 2]])

            def a1q():
                for g in range(2):
                    def f(e, g=g):
                        for k in range(8):
                            i = e.matmul(qkv[:, g * 512:(g + 1) * 512], lhsT=h_[:, k, :], rhs=Wqkv[:, k, g * 512:(g + 1) * 512],
                                         start=(k == 0), stop=(k == 7))
                        return i
                    em.op("pe", f, reads=[dhT[b % 2], dWqkv], writes=[dq01])
                qv = qkv[:, 0:1024].rearrange("p (h d) -> p h d", d=64)
                em.op("act", lambda e: e.activation(out=qt[:, :, 16:64], in_=qv[:, :, 16:64], func=AF.Copy, scale=SCALE),
                      reads=[dq01], writes=[dqt])

            def a2q():
                qv = qkv[:, 0:1024].rearrange("p (h d) -> p h d", d=64)
                cosv = c_[:, 0:160].rearrange("p (h d) -> p h d", d=8)[:, 0:16, :]
                sinv = c_[:, 160:320].rearrange("p (h d) -> p h d", d=8)[:, 0:16, :]
                x1, x2 = qv[:, :, 0:8], qv[:, :, 8:16]
                for i, (xx, tb) in enumerate([(x1, cosv), (x2, sinv), (x2, cosv), (x1, sinv)]):
                    em.op("dve", lambda e, i=i, xx=xx, tb=tb: e.tensor_tensor(out=rt[i][:, 0:16, :], in0=xx, in1=tb, op=ALU.mult),
                          reads=[dq01, dc_], writes=[drt[i]])
                em.op("dve", lambda e: e.tensor_tensor(out=qt[:, :, 0:8], in0=rt[0][:, 0:16, :], in1=rt[1][:, 0:16, :], op=ALU.subtract),
                      reads=[drt[0], drt[1]], writes=[dqt])
                em.op("dve", lambda e: e.tensor_tensor(out=qt[:, :, 8:16], in0=rt[2][:, 0:16, :], in1=rt[3][:, 0:16, :], op=ALU.add),
                      reads=[drt[2], drt[3]], writes=[dqt])

            def a1kv():
                def f(e):
                    for k in range(8):
                        i = e.matmul(qkv[:, 0:512], lhsT=h_[:, k, :], rhs=Wqkv[:, k, 1024:1536], start=(k == 0), stop=(k == 7))
                    return i
                em.op("pe", f, reads=[dhT[b % 2], dWqkv], writes=[dq01])
                kv = qkv[:, 0:256].rearrange("p (h d) -> p h d", d=64)
                for dup in range(2):
                    em.op("act", lambda e, dup=dup: e.activation(out=kd[:, :, dup, dup * 64 + 16:dup * 64 + 64], in_=kv[:, :, 16:64],
                                                                 func=AF.Copy), reads=[dq01], writes=[dkd])
                em.op("act", lambda e: e.activation(out=vA[:, b + 1, :], in_=qkv[:, 256:512], func=AF.Copy),
                      reads=[dq01], writes=[dvA[b + 1]])

            def a2kv():
                kv = qkv[:, 0:256].rearrange("p (h d) -> p h d", d=64)
                cosv = c_[:, 0:160].rearrange("p (h d) -> p h d", d=8)[:, 16:20, :]
                sinv = c_[:, 160:320].rearrange("p (h d) -> p h d", d=8)[:, 16:20, :]
                x1, x2 = kv[:, :, 0:8], kv[:, :, 8:16]
                for i, (xx, tb) in enumerate([(x1, cosv), (x2, sinv), (x2, cosv), (x1, sinv)]):
                    em.op("dve", lambda e, i=i, xx=xx, tb=tb: e.tensor_tensor(out=rt[i][:, 16:20, :], in0=xx, in1=tb, op=ALU.mult),
                          reads=[dq01, dc_], writes=[drt[i]])
                for dup in range(2):
                    em.op("dve", lambda e, dup=dup: e.tensor_tensor(out=kd[:, :, dup, dup * 64:dup * 64 + 8], in0=rt[0][:, 16:20, :],
                                                                    in1=rt[1][:, 16:20, :], op=ALU.subtract),
                          reads=[drt[0], drt[1]], writes=[dkd])
                    em.op("dve", lambda e, dup=dup: e.tensor_tensor(out=kd[:, :, dup, dup * 64 + 8:dup * 64 + 16], in0=rt[2][:, 16:20, :],
                                                                    in1=rt[3][:, 16:20, :], op=ALU.add),
                          reads=[drt[2], drt[3]], writes=[dkd])

            def a3():
                qflat = qt[:, :, :].rearrange("p h d -> p (h d)")

                def tq(e):
                    for k in range(8):
                        i = e.transpose(tp[:, k, :], qflat[:, k * 128:(k + 1) * 128], C.ident[:, :])
                    return i
                em.op("pe", tq, reads=[dqt, C.dident], writes=[dtp])
                em.op("act", lambda e: e.activation(out=qT[b % NQ][:, :, :], in_=tp[:, :, :], func=AF.Copy),
                      reads=[dtp], writes=[dqT[b % NQ]])

            def a4():
                kflat = kd[:, :, :, :].rearrange("p g u d -> p (g u d)")

                def tk(e):
                    for k in range(8):
                        i = e.transpose(tp[:, k, :], kflat[:, k * 128:(k + 1) * 128], C.ident[:, :])
                    return i
                em.op("pe", tk, reads=[dkd, C.dident], writes=[dtp])
                em.op("act", lambda e: e.activation(out=kT[:, :, (b + 1) * 128:(b + 2) * 128], in_=tp[:, :, :], func=AF.Copy),
                      reads=[dtp], writes=[dkT[b + 1]])
            return [a_front, a0, a1q, a2q, a1kv, a2kv, a3, a4, a_frontB]

        def A_loads(b):
            em.dma("sp", xa[b % NX][:, :], src[b * 128:(b + 1) * 128, :], reads=[dsrc[b]], writes=[dxa[b % NX]])
            em.dma("sp", cs[b % 2][:, :], C.acs[b * 128:(b + 1) * 128, :], writes=[dcs[b % 2]])

        def C_steps(b):
            ot = otok[b % 2]

            def c0():
                oflat = ot[:, :, :].rearrange("p h d -> p (h d)")

                def to(e):
                    for k in range(8):
                        i = e.transpose(tp[:, k, :], oflat[:, k * 128:(k + 1) * 128], C.ident[:, :])
                    return i
                em.op("pe", to, reads=dotok[b % 2] + [C.dident], writes=[dtp])
                em.op("act", lambda e: e.activation(out=oT[:, :, :], in_=tp[:, :, :], func=AF.Copy), reads=[dtp], writes=[doT])

            def c1():
                def fw(e):
                    for half in range(2):
                        for k in range(8):
                            i = e.matmul(qkv[:, half * 512:(half + 1) * 512], lhsT=oT[:, k, :], rhs=Wo[:, k, half * 512:(half + 1) * 512],
                                         start=(k == 0), stop=(k == 7))
                    return i
                em.op("pe", fw, reads=[doT, dWo], writes=[dq01])

            def c2():
                si = sctr[0] % 4
                sctr[0] += 1
                csl[b] = si
                emit_post_A(C, em, qkv[:, 0:1024], dq01, sst[si], dsst[si], junk, djunk)

            def c2B():
                si = csl.pop(b)
                emit_post_B(C, em, qkv[:, 0:1024], dq01, xa[b % NX], dxa[b % NX], gpost, dgpost, sst[si], dsst[si],
                            C.y[b * 128:(b + 1) * 128, :], C.dy[b])
            return [c0, c1, c2, c2B]

        ocp = [_alloc(st, nc, "ocp%d" % i, [128, 8, 64], F32) for i in range(2)]
        docp = [Dep("ocp%d" % i) for i in range(2)]

        def finish_heads(b, h0):
            s_ = stt[b % 2]
            k = h0 // 8
            dsts = dsth[b % 2][h0:h0 + 8]
            df = dfin[b % 2][k]
            hs = slice(h0, h0 + 8)
            em.op("dve", lambda e: e.tensor_copy(out=ocp[k][:, :, :], in_=ops[:, :, :]), reads=[dops], writes=[docp[k]])
            em.op("dve", lambda e: e.tensor_tensor(out=s_[:, 3, hs], in0=s_[:, 1, hs], in1=sinkt[:, hs], op=ALU.add),
                  reads=dsts + [dsink], writes=[df])
            em.op("act", lambda e: e.activation(out=s_[:, 4, hs], in_=s_[:, 3, hs], func=AF.Exp), reads=[df], writes=[df])
            em.op("dve", lambda e: e.tensor_tensor(out=s_[:, 4, hs], in0=s_[:, 4, hs], in1=s_[:, 2, hs], op=ALU.add),
                  reads=dsts + [df], writes=[df])
            em.op("dve", lambda e: e.reciprocal(out=s_[:, 5, hs], in_=s_[:, 4, hs]), reads=[df], writes=[df])
            em.op("pool", lambda e: e.tensor_tensor(out=otok[b % 2][:, hs, :], in0=ocp[k][:, :, :],
                                                   in1=s_[:, 5, hs].unsqueeze(2).broadcast_to([128, 8, 64]), op=ALU.mult),
                  reads=[docp[k], df], writes=[dotok[b % 2][k]])

        def stageB(b, steps):
            m_, dm_ = mk[b % 2], dmk[b % 2]
            em.dma("sp", m_[:, :], C.amask[b], writes=[dm_])
            s_ = stt[b % 2]
            q_ = qT[b % NQ]

            def S(h):
                g, pr, hf = h // 4, h // 2, h % 2
                bk = h % 4
                bank = sps[:, bk, 0:384]

                def f(e):
                    e.matmul(bank, lhsT=q_[:, pr, :], rhs=kT[:, g * 2 + hf, b * 128: b * 128 + 384], start=True, stop=False)
                    return e.matmul(bank, lhsT=C.ident[:, :], rhs=m_[:, :], start=False, stop=True)
                em.op("pe", f, reads=[dqT[b % NQ], dkT[b], dkT[b + 1], dkT[b + 2], dm_, C.dident], writes=[dsps[bk]])
                dst = dsth[b % 2][h]
                em.op("dve", lambda e: e.tensor_reduce(out=s_[:, 0, h:h + 1], in_=bank, op=ALU.max, axis=AX.X, negate=True),
                      reads=[dsps[bk]], writes=[dst])
                em.op("dve", lambda e: e.tensor_tensor(out=s_[:, 1, h:h + 1], in0=s_[:, 0, h:h + 1], in1=nsink[:, h:h + 1], op=ALU.min),
                      reads=[dst, dsink], writes=[dst])
                em.op("act", lambda e: e.activation(out=pb[bk][:, :], in_=bank, func=AF.Exp,
                                                    bias=s_[:, 1, h:h + 1], accum_out=s_[:, 2, h:h + 1]),
                      reads=[dsps[bk], dst], writes=[dpb[bk], dst])

            def T(h):
                bk = h % 4
                tb = sps[:, bk, :].bitcast(BF16)[:, 0:384].rearrange("p (c t) -> p c t", t=128)

                def ft(e):
                    for c in range(3):
                        r = e.transpose(tb[:, c, :], pb[bk][:, c * 128:(c + 1) * 128], C.ident[:, :])
                    return r
                em.op("pe", ft, reads=[dpb[bk], C.dident], writes=[dsps[bk]])
                if h % 2 == 0:
                    em.op("dve", lambda e: e.tensor_copy(out=pT[h % 3][:, :, :], in_=tb), reads=[dsps[bk]], writes=[dpT[h % 3]])
                else:
                    em.op("act", lambda e: e.activation(out=pT[h % 3][:, :, :], in_=tb, func=AF.Copy), reads=[dsps[bk]], writes=[dpT[h % 3]])

            def PV(h):
                g = h // 4

                def fo(e):
                    for c in range(3):
                        r = e.matmul(ops[:, h % 8, :], lhsT=pT[h % 3][:, c, :], rhs=vA[:, b + c, g * 64:(g + 1) * 64],
                                     start=(c == 0), stop=(c == 2))
                    return r
                em.op("pe", fo, reads=[dpT[h % 3], dvA[b], dvA[b + 1], dvA[b + 2]], writes=[dops])
                if h % 8 == 7:
                    finish_heads(b, h - 7)

            for h in range(3):
                S(h)
            for h in range(16):
                T(h)
                if h >= 1:
                    PV(h - 1)
                if h + 3 < 16:
                    S(h + 3)
                if h in (1, 3, 5, 7, 9, 11, 13, 14, 15) and steps:
                    steps.pop(0)()
            PV(15)
            while steps:
                steps.pop(0)()

        nblk = getattr(C, "nblk", NCH)
        nop = lambda: None

        def run_A(b):
            A = A_steps(b)
            for k in (0, 8, 1, 2, 3, 4, 5, 6, 7):
                A[k]()
        for b0 in range(min(2, NCH)):
            A_loads(b0)
        run_A(0)
        if NCH > 2:
            A_loads(2)
        if NCH > 1:
            run_A(1)
        for b in range(nblk):
            Cs = C_steps(b - 1) if b >= 1 else [nop] * 4
            As = A_steps(b + 2) if b + 2 < NCH else [nop] * 9
            ld = (lambda b=b: A_loads(b + 3)) if b + 3 < NCH else nop
            steps = [lambda Cs=Cs, As=As: (Cs[0](), As[0]()),
                     lambda Cs=Cs, As=As: (Cs[1](), As[8]()),
                     lambda Cs=Cs, As=As: (Cs[2](), As[1]()),
                     lambda Cs=Cs, ld=ld: (Cs[3](), ld()),
                     As[2], As[3], lambda As=As: (As[4](), As[6]()), As[5], As[7]]
            stageB(b, steps)
        for s_ in C_steps(nblk - 1):
            s_()
        em.barrier()


def emit_ret(C, l, src, dsrc):
    nc, em = C.nc, C.em
    nch = getattr(C, "nblk", NCH)
    half_b = NCH // 2
    with ExitStack() as st0:
        bnd = _alloc(st0, nc, "bnd", [128, 1], F32)
        kdec = _alloc(st0, nc, "kdec", [128, 8], F32)
        g128 = _alloc(st0, nc, "g128", [128, 8], F32)
        decrow = _alloc(st0, nc, "decrow", [128, 8, 128], F32)
        DT = _alloc(st0, nc, "DT", [128, 4, 128], F32)
        S32 = _alloc(st0, nc, "S32", [128, 4, 2, 512], F32)
        stT = ExitStack()
        rc = _alloc(stT, nc, "rc", [128, 6, 128], F32)
        cpos = _alloc(stT, nc, "cpos", [128, 4], F32)
        dl = _alloc(stT, nc, "dl", [128, 8], F32)
        lg = _alloc(stT, nc, "lg", [128, 8], F32)
        tmpD = _alloc(stT, nc, "tmpD", [128, 2, 128], F32)
        drc, dtab, dS32, dtmp = Dep("rc"), Dep("tab"), Dep("S32"), Dep("tmpD")
        em.dma("sp", rc[:, :, :], C.rconst[:, :, :], writes=[drc])
        em.dma("sp", cpos[:, :], C.rpos[:, :], writes=[drc], owner=drc)
        em.dma("sp", bnd[:, :], C.rbnd[:, :], writes=[drc], owner=drc)
        em.dma("sp", dl[:, 0:4], C.rdf[0, :].partition_broadcast(128), writes=[dtab])
        em.dma("sp", dl[:, 4:8], C.rdb[0, :].partition_broadcast(128), writes=[dtab], owner=dtab)
        em.op("act", lambda e: e.activation(out=lg[:, :], in_=dl[:, :], func=AF.Exp, scale=-1.0), reads=[dtab], writes=[dtab])
        em.op("dve", lambda e: e.tensor_scalar(out=lg[:, :], in0=lg[:, :], scalar1=1.0, scalar2=None, op0=ALU.add), reads=[dtab], writes=[dtab])
        em.op("act", lambda e: e.activation(out=lg[:, :], in_=lg[:, :], func=AF.Ln), reads=[dtab], writes=[dtab])
        em.op("dve", lambda e: e.tensor_scalar(out=lg[:, :], in0=lg[:, :], scalar1=-1.0, scalar2=None, op0=ALU.mult), reads=[dtab], writes=[dtab])
        em.op("act", lambda e: e.activation(out=g128[:, :], in_=lg[:, :], func=AF.Exp, scale=128.0), reads=[dtab], writes=[dtab])
        em.op("act", lambda e: e.activation(out=kdec[:, 0:4], in_=lg[:, 0:4], func=AF.Exp, scale=cpos[:, 2:3]), reads=[dtab, drc], writes=[dtab])
        em.op("act", lambda e: e.activation(out=kdec[:, 4:8], in_=lg[:, 4:8], func=AF.Exp, scale=cpos[:, 3:4]), reads=[dtab, drc], writes=[dtab])
        em.op("dve", lambda e: e.tensor_scalar(out=kdec[:, :], in0=kdec[:, :], scalar1=1.0 / 16, scalar2=None, op0=ALU.mult), reads=[dtab], writes=[dtab])
        for h in range(4):
            em.op("act", lambda e, h=h: e.activation(out=decrow[:, h, :], in_=rc[:, 4, :], func=AF.Exp, scale=lg[:, h:h + 1]), reads=[dtab, drc], writes=[dtab])
            em.op("act", lambda e, h=h: e.activation(out=decrow[:, 4 + h, :], in_=rc[:, 5, :], func=AF.Exp, scale=lg[:, 4 + h:5 + h]), reads=[dtab, drc], writes=[dtab])
            em.op("act", lambda e, h=h: e.activation(out=tmpD[:, 0, :], in_=rc[:, 0, :], func=AF.Exp, scale=lg[:, h:h + 1]), reads=[dtab, drc], writes=[dtmp])
            em.op("act", lambda e, h=h: e.activation(out=tmpD[:, 1, :], in_=rc[:, 1, :], func=AF.Exp, scale=lg[:, 4 + h:5 + h]), reads=[dtab, drc], writes=[dtmp])
            em.op("dve", lambda e, h=h: e.tensor_tensor(out=tmpD[:, :, :], in0=tmpD[:, :, :], in1=rc[:, 2:4, :], op=ALU.mult), reads=[dtmp, drc], writes=[dtmp])
            em.op("dve", lambda e, h=h: e.tensor_tensor(out=DT[:, h, :], in0=tmpD[:, 0, :], in1=tmpD[:, 1, :], op=ALU.add), reads=[dtmp], writes=[dtab])
        em.op("dve", lambda e: e.memset(S32[:, :, :, :], 0.0), writes=[dS32])
        em.barrier()
        stT.close()

        Wqg = _alloc(st0, nc, "Wqg", [128, 8, 3072], BF16)
        dWqg = [Dep("Wqg%d" % k) for k in range(2)]
        wv_ = C.rwi[0].rearrange("(k p) f -> p k f", p=128)

        def rotary(ps, dps, c_, dc_, rt, drt, out_bf, dout):
            p4 = ps.rearrange("p (h t f) -> p h t f", h=4, t=2)
            o4 = out_bf[:, :].rearrange("p (h t f) -> p h t f", h=4, t=2)
            cosb = c_[:, 0:128].unsqueeze(1).broadcast_to([128, 4, 128])
            sinb = c_[:, 128:256].unsqueeze(1).broadcast_to([128, 4, 128])
            x1, x2 = p4[:, :, 0, :], p4[:, :, 1, :]
            prods = [(x1, cosb), (x2, sinb), (x2, cosb), (x1, sinb)]
            for i in (0, 1):
                xx, tb = prods[i]
                em.op("dve", lambda e, i=i, xx=xx, tb=tb: e.tensor_tensor(out=rt[i][:, :, :], in0=xx, in1=tb, op=ALU.mult),
                      reads=[dps, dc_], writes=[drt[i]])
            em.op("dve", lambda e: e.tensor_tensor(out=o4[:, :, 0, :], in0=rt[0][:, :, :], in1=rt[1][:, :, :], op=ALU.subtract),
                  reads=[drt[0], drt[1]], writes=[dout])
            for i in (2, 3):
                xx, tb = prods[i]
                em.op("dve", lambda e, i=i, xx=xx, tb=tb: e.tensor_tensor(out=rt[i][:, :, :], in0=xx, in1=tb, op=ALU.mult),
                      reads=[dps, dc_], writes=[drt[i]])
            em.op("dve", lambda e: e.tensor_tensor(out=o4[:, :, 1, :], in0=rt[2][:, :, :], in1=rt[3][:, :, :], op=ALU.add),
                  reads=[drt[2], drt[3]], writes=[dout])

        def state_update(S32, dS32, kd_tok, dkd, v_tok, dv, dsp, dsd, gcol0):
            for h in range(4):
                for c in range(2):
                    def f(e, h=h, c=c):
                        return e.matmul(dsp[:, :], lhsT=kd_tok[:, h * 256 + c * 128: h * 256 + (c + 1) * 128],
                                        rhs=v_tok[:, h * 512:(h + 1) * 512], start=True, stop=True)
                    em.op("pe", f, reads=[dkd, dv], writes=[dsd])
                    em.op("dve", lambda e, h=h, c=c: e.scalar_tensor_tensor(out=S32[:, h, c, :], in0=S32[:, h, c, :],
                                                                          scalar=g128[:, gcol0 + h:gcol0 + h + 1], in1=dsp[:, :],
                                                                          op0=ALU.mult, op1=ALU.add),
                          reads=[dS32, dsd, dtab], writes=[dS32])

        def boundary(S32, dS32):
            em.op("dve", lambda e: e.tensor_scalar(out=S32[:, :, :, :], in0=S32[:, :, :, :], scalar1=bnd[:, 0:1], scalar2=None,
                                                   op0=ALU.mult), reads=[dS32, drc], writes=[dS32])

        with ExitStack() as st:
            Wkv = _alloc(st, nc, "Wkv", [128, 8, 3072], BF16)
            dWkv = [Dep("Wkv%d" % k) for k in range(2)]
            wv = C.rwi[0].rearrange("(k p) f -> p k f", p=128)
            em.dma("pool", Wkv[:, 0:4, :], wv[:, 0:4, 1024:4096], writes=[dWkv[0]])
            em.dma("pool", Wkv[:, 4:8, :], wv[:, 4:8, 1024:4096], writes=[dWkv[1]])
            em.dma("pool", Wqg[:, :, 0:1024], wv_[:, :, 0:1024], writes=[dWqg[0]])
            em.dma("pool", Wqg[:, :, 1024:3072], wv_[:, :, 4096:6144], writes=[dWqg[1]])
            gpre = _alloc(st, nc, "gpre", [128, D], F32)
            dgpre = Dep("gpre")
            em.dma("sp", gpre[:, :], C.ng[l, 2, :].partition_broadcast(128), writes=[dgpre])
            xa = [_alloc(st, nc, "xa%d" % i, [128, D], F32) for i in range(3)]
            dxa = [Dep("xa%d" % i) for i in range(3)]
            hb2 = [_alloc(st, nc, "hb%d" % i, [128, D], BF16) for i in range(2)]
            dhb2 = [Dep("hb%d" % i) for i in range(2)]
            hT = [_alloc(st, nc, "hT%d" % i, [128, 8, 128], BF16) for i in range(2)]
            dhT = [Dep("hT%d" % i) for i in range(2)]
            NCS = 4
            cs = [_alloc(st, nc, "rcs%d" % i, [128, 256], F32) for i in range(NCS)]
            dcs = [Dep("rcs%d" % i) for i in range(NCS)]
            rt = [_alloc(st, nc, "rrt%d" % i, [128, 4, 128], F32) for i in range(4)]
            drt = [Dep("rrt%d" % i) for i in range(4)]
            krot = [_alloc(st, nc, "krot%d" % i, [128, 1024], BF16) for i in range(2)]
            dkrot = [Dep("krot%d" % i) for i in range(2)]
            kf = [_alloc(st, nc, "kf%d" % i, [128, 1024], BF16) for i in range(2)]
            dkf = [Dep("kf%d" % i) for i in range(2)]
            kb = [_alloc(st, nc, "kb%d" % i, [128, 1024], BF16) for i in range(2)]
            dkb = [Dep("kb%d" % i) for i in range(2)]
            kTb = [_alloc(st, nc, "kTb%d" % i, [128, 8, 128], BF16) for i in range(2)]
            dkTb = [Dep("kTb%d" % i) for i in range(2)]
            vtok = [_alloc(st, nc, "vtok%d" % i, [128, 2048], BF16) for i in range(2)]
            dvtok = [[Dep("vtok%d_%d" % (i, k)) for k in range(2)] for i in range(2)]
            Sbf2 = [_alloc(st, nc, "Sbf%d" % i, [128, 4096], BF16) for i in range(2)]
            dSbfh2 = [[Dep("Sbf%d_%d" % (i, h)) for h in range(4)] for i in range(2)]
            dS32h = [Dep("S32b_%d" % h) for h in range(4)]
            em.op("dve", lambda e: e.memset(S32[:, :, :, :], 0.0), reads=[dS32], writes=dS32h)
            sst = [_alloc(st, nc, "ss%d" % i, [128, 4], F32) for i in range(3)]
            dsst = [Dep("ss%d" % i) for i in range(3)]
            tp = _palloc(st, nc, "tp", [128, 8, 128], BF16)
            dtp = PDep("tp")
            pk = _palloc(st, nc, "pk", [128, 1024], F32)
            dpk = PDep("pk")
            pv1 = _palloc(st, nc, "pv", [128, 1024], F32)
            pv = [pv1, pv1]
            dpv1 = PDep("pv")
            dpv = [dpv1, dpv1]
            dspr = [_palloc(st, nc, "dsp%d" % i, [128, 512], F32) for i in range(2)]
            dsdr = [PDep("dsp%d" % i) for i in range(2)]

            def loads1(n):
                em.dma("sp", xa[n % 3][:, :], src[n * 128:(n + 1) * 128, :], reads=[dsrc[n]], writes=[dxa[n % 3]])
                em.dma("sp", cs[n % NCS][:, :], C.rcs[n * 128:(n + 1) * 128, :], writes=[dcs[n % NCS]])

            def front1A(n):
                r = n % 2
                emit_front_A(C, em, xa[n % 3], dxa[n % 3], hb2[r], dhb2[r], sst[n % 3], dsst[n % 3])

            def front1B(n):
                r = n % 2
                emit_front_B(C, em, xa[n % 3], dxa[n % 3], hb2[r], dhb2[r], sst[n % 3], dsst[n % 3], gpre, dgpre)

            def front1(n):
                front1A(n)
                front1B(n)

            def P1_steps(n):
                r = n % 2
                c_, dc_ = cs[n % NCS], dcs[n % NCS]

                def p0():
                    hbs = hb2[r]

                    def tps(e):
                        for k in range(8):
                            i = e.transpose(tp[:, k, :], hbs[:, k * 128:(k + 1) * 128], C.ident[:, :])
                        return i
                    em.op("pe", tps, reads=[dhb2[r], C.dident], writes=[dtp])
                    em.op("act", lambda e: e.activation(out=hT[r][:, :, :], in_=tp[:, :, :], func=AF.Copy), reads=[dtp], writes=[dhT[r]])

                def p1():
                    for g in range(2):
                        def f(e, g=g):
                            for k in range(8):
                                i = e.matmul(pk[:, g * 512:(g + 1) * 512], lhsT=hT[r][:, k, :], rhs=Wkv[:, k, g * 512:(g + 1) * 512],
                                             start=(k == 0), stop=(k == 7))
                            return i
                        em.op("pe", f, reads=[dhT[r]] + dWkv, writes=[dpk])
                    rotary(pk[:, :], dpk, c_, dc_, rt, drt, krot[r], dkrot[r])

                def p2():
                    for (dst, ddst, col) in ((kf[r], dkf[r], 0), (kb[r], dkb[r], 4)):
                        em.op("pool", lambda e, dst=dst, col=col: e.tensor_tensor(
                            out=dst[:, :].rearrange("p (h f) -> p h f", h=4), in0=krot[r][:, :].rearrange("p (h f) -> p h f", h=4),
                            in1=kdec[:, col:col + 4].unsqueeze(2).broadcast_to([128, 4, 256]), op=ALU.mult),
                            reads=[dkrot[r], dtab], writes=[ddst])

                    def tk(e):
                        for k in range(8):
                            i = e.transpose(tp[:, k, :], krot[r][:, k * 128:(k + 1) * 128], C.ident[:, :])
                        return i
                    em.op("pe", tk, reads=[dkrot[r], C.dident], writes=[dtp])
                    em.op("act", lambda e: e.activation(out=kTb[r][:, :, :], in_=tp[:, :, :], func=AF.Copy), reads=[dtp], writes=[dkTb[r]])
                    em.dma("sp", C.s_kT[n], kTb[r][:, :, :].rearrange("p k t -> p (k t)"), reads=[dkTb[r]], writes=[C.dscr[n]], owner=dkTb[r])
                    em.dma("sp", C.s_kf[n], kf[r][:, :], reads=[dkf[r]], writes=[C.dscr[n]], owner=dkf[r])

                def mkv(hv):
                    def pv_():
                        for g in range(2):
                            def f(e, g=g):
                                col = 1024 + hv * 1024 + g * 512
                                for k in range(8):
                                    i = e.matmul(pv[hv][:, g * 512:(g + 1) * 512], lhsT=hT[r][:, k, :], rhs=Wkv[:, k, col:col + 512],
                                                 start=(k == 0), stop=(k == 7))
                                return i
                            em.op("pe", f, reads=[dhT[r]] + dWkv, writes=[dpv[hv]])
                        em.op("act", lambda e: e.activation(out=vtok[r][:, hv * 1024:(hv + 1) * 1024], in_=pv[hv][:, :], func=AF.Copy),
                              reads=[dpv[hv]], writes=[dvtok[r][hv]])
                    return pv_

                def p5():
                    em.dma("sp", C.s_v[n], vtok[r][:, :], reads=dvtok[r], writes=[C.dscr[n]], owner=dvtok[r][0])
                return [p0, p1, mkv(0), mkv(1), p2, p5]

            def U(n, steps):
                r = n % 2
                if n - 3 >= 0:
                    loads1(n - 3)
                if n - 2 >= 0:
                    front1A(n - 2)
                Sbf, dSbfh = Sbf2[n % 2], dSbfh2[n % 2]
                for h in range(4):
                    if h == 2 and n - 2 >= 0:
                        front1B(n - 2)
                    em.op("act", lambda e, h=h: e.activation(out=Sbf[:, h * 1024:(h + 1) * 1024],
                                                             in_=S32[:, h, :, :].rearrange("p c f -> p (c f)"), func=AF.Copy),
                          reads=[dS32h[h]], writes=[dSbfh[h]])
                    if h == 3:
                        em.dma("sp", C.s_sb[n], Sbf[:, :], reads=dSbfh, writes=[C.dscr[n]], owner=dSbfh[0])
                    for c in range(2):
                        if n > 0:
                            dsp, dsd = dspr[c], dsdr[c]

                            def f(e, h=h, c=c, dsp=dsp):
                                return e.matmul(dsp[:, :], lhsT=kb[r][:, h * 256 + c * 128: h * 256 + (c + 1) * 128],
                                                rhs=vtok[r][:, h * 512:(h + 1) * 512], start=True, stop=True)
                            em.op("pe", f, reads=[dkb[r], dvtok[r][h // 2]], writes=[dsd])
                            em.op("dve", lambda e, h=h, c=c, dsp=dsp: e.scalar_tensor_tensor(out=S32[:, h, c, :], in0=S32[:, h, c, :],
                                                                                           scalar=g128[:, 4 + h:5 + h], in1=dsp[:, :],
                                                                                           op0=ALU.mult, op1=ALU.add),
                                  reads=[dS32h[h], dsd, dtab], writes=[dS32h[h]])
                        if steps:
                            steps.pop(0)()
                while steps:
                    steps.pop(0)()
                if n > 0 and n == half_b:
                    em.op("dve", lambda e: e.tensor_scalar(out=S32[:, :, :, :], in0=S32[:, :, :, :], scalar1=bnd[:, 0:1], scalar2=None,
                                                           op0=ALU.mult), reads=dS32h + [drc], writes=dS32h)

            for i in range(1, 4):
                if nch - i >= 0:
                    loads1(nch - i)
            front1(nch - 1)
            if nch > 1:
                front1(nch - 2)
            for s_ in P1_steps(nch - 1):
                s_()
            for n in range(nch - 1, -1, -1):
                U(n, P1_steps(n - 1) if n > 0 else [])
            em.barrier()

        em.op("dve", lambda e: e.memset(S32[:, :, :, :], 0.0), writes=[dS32])
        with ExitStack() as st:
            Wro = _alloc(st, nc, "Wro", [128, 16, D], BF16)
            dWro = Dep("Wro")
            em.dma("pool", Wro[:, :, :], C.rwo[0].rearrange("(k p) f -> p k f", p=128), writes=[dWro])
            gpre, dgpre, gpost, dgpost = load_gains(C, em, st, nc, l, 2, 3, 1.0)
            NX = 4
            xa = [_alloc(st, nc, "xa%d" % i, [128, D], F32) for i in range(NX)]
            dxa = [Dep("xa%d" % i) for i in range(NX)]
            hb = _alloc(st, nc, "hb", [128, D], BF16)
            dhb = Dep("hb")
            NH = 3
            hT = [_alloc(st, nc, "hT%d" % i, [128, 8, 128], BF16) for i in range(NH)]
            dhT = [Dep("hT%d" % i) for i in range(NH)]
            cs = [_alloc(st, nc, "rcs%d" % i, [128, 256], F32) for i in range(2)]
            dcs = [Dep("rcs%d" % i) for i in range(2)]
            rt2 = [_alloc(st, nc, "rrt%d" % i, [128, 4, 128], F32) for i in range(2)]
            drt2 = [Dep("rrt%d" % i) for i in range(2)]
            rt = [rt2[0], rt2[1], rt2[0], rt2[1]]
            drt = [drt2[0], drt2[1], drt2[0], drt2[1]]
            qrot = _alloc(st, nc, "qrot", [128, 1024], BF16)
            dqrot = Dep("qrot")
            qT = [[_alloc(st, nc, "qT%d_%d" % (r, i), [128, 8, 128], BF16) for i in range(3)] for r in range(2)]
            dqT = [[Dep("qT%d_%d" % (r, i)) for i in range(3)] for r in range(2)]
            sg = [_alloc(st, nc, "sg%d" % i, [128, 512], BF16) for i in range(3)]
            dsg = [Dep("sg%d" % i) for i in range(3)]
            NR = 2
            kTl = [_alloc(st, nc, "kTl%d" % i, [128, 8, 128], BF16) for i in range(NR)]
            kfl = [_alloc(st, nc, "kfl%d" % i, [128, 1024], BF16) for i in range(NR)]
            vl = [_alloc(st, nc, "vl%d" % i, [128, 2048], BF16) for i in range(NR)]
            sbl1 = _alloc(st, nc, "sbl", [128, 4, 2, 512], BF16)
            sbl = [sbl1, sbl1]
            dkTl = [Dep("kTl%d" % i) for i in range(NR)]
            dkfl = [Dep("kfl%d" % i) for i in range(NR)]
            dvl = [Dep("vl%d" % i) for i in range(NR)]
            dsblh = [Dep("sbl_%d" % h) for h in range(4)]
            Sfb = _alloc(st, nc, "Sfb", [128, 4, 2, 512], BF16)
            dSfb = [Dep("Sfb%d" % h) for h in range(4)]
            dS32h = [Dep("S32_%d" % h) for h in range(4)]
            STb = [_alloc(st, nc, "STb%d" % i, [128, 128], BF16) for i in range(2)]
            dSTb = [Dep("STb%d" % i) for i in range(2)]
            otok = [_alloc(st, nc, "otok%d" % i, [128, 2048], BF16) for i in range(2)]
            dotok = [[Dep("otok%d_%d" % (i, h)) for h in range(4)] for i in range(2)]
            oT = _alloc(st, nc, "oT", [128, 16, 128], BF16)
            doT = Dep("oT")
            junk = _alloc(st, nc, "junk", [128, D], BF16)
            djunk = Dep("junk")
            junk2 = junk
            djunk2 = djunk
            sst = [_alloc(st, nc, "ss%d" % i, [128, 4], F32) for i in range(6)]
            dsst = [Dep("ss%d" % i) for i in range(6)]
            tp = _palloc(st, nc, "tp", [128, 8, 128], BF16)
            dtp = PDep("tp")
            pq = _palloc(st, nc, "pq", [128, 1024], F32)
            dpq = PDep("pq")
            pG = _palloc(st, nc, "pG", [128, 512], F32)
            dpG = PDep("pG")
            pS = _palloc(st, nc, "pS", [128, 512], F32)
            dpS = PDep("pS")
            pY = [_palloc(st, nc, "pY%d" % i, [128, 512], F32) for i in range(2)]
            dpY = [PDep("pY%d" % i) for i in range(2)]
            pD = _palloc(st, nc, "pD", [128, 512], F32)
            dpD = PDep("pD")
            sctr = [0]
            em.op("dve", lambda e: e.memset(Sfb[:, :, :, :], 0.0), writes=dSfb)
            em.op("dve", lambda e: e.memset(S32[:, :, :, :], 0.0), reads=[dS32], writes=dS32h)

            def nss():
                si = sctr[0] % 6
                sctr[0] += 1
                return sst[si], dsst[si]

            osl = {}
            ysl = {}

            def load_sb(n, h):
                em.dma("sp", sbl1[:, h, :, :].rearrange("p c f -> p (c f)"), C.s_sb[n][:, h * 1024:(h + 1) * 1024],
                       reads=[C.dscr[n]], writes=[dsblh[h]])

            hb2 = [hb, _alloc(st, nc, "hb_b", [128, D], BF16)]
            dhb2 = [dhb, Dep("hb_b")]

            fsl = {}

            def frontA(n):
                xs, dxs = xa[n % NX], dxa[n % NX]
                em.dma("sp", xs[:, :], src[n * 128:(n + 1) * 128, :], reads=[dsrc[n]], writes=[dxs])
                em.dma("sp", cs[n % 2][:, :], C.rcs[n * 128:(n + 1) * 128, :], writes=[dcs[n % 2]])
                ss, dss = nss()
                fsl[n] = (ss, dss)
                emit_front_A(C, em, xs, dxs, hb2[n % 2], dhb2[n % 2], ss, dss)

            def frontB(n):
                ss, dss = fsl.pop(n)
                emit_front_B(C, em, xa[n % NX], dxa[n % NX], hb2[n % 2], dhb2[n % 2], ss, dss, gpre, dgpre)

            def front(n):
                frontA(n)
                frontB(n)

            def P_steps(n):
                r = n % 2
                xs, dxs = xa[n % NX], dxa[n % NX]
                c_, dc_ = cs[r], dcs[r]

                def p0():
                    if n == 0:
                        for h in range(4):
                            load_sb(0, h)
                    hbs = hb2[n % 2]

                    def tps(e):
                        for k in range(8):
                            i = e.transpose(tp[:, k, :], hbs[:, k * 128:(k + 1) * 128], C.ident[:, :])
                        return i
                    em.op("pe", tps, reads=[dhb2[n % 2], C.dident], writes=[dtp])
                    em.op("act", lambda e: e.activation(out=hT[n % NH][:, :, :], in_=tp[:, :, :], func=AF.Copy), reads=[dtp], writes=[dhT[n % NH]])

                def p1():
                    em.dma("sp", kTl[r][:, :, :].rearrange("p k t -> p (k t)"), C.s_kT[n], reads=[C.dscr[n]], writes=[dkTl[r]])
                    em.dma("sp", kfl[r][:, :], C.s_kf[n], reads=[C.dscr[n]], writes=[dkfl[r]])
                    em.dma("sp", vl[r][:, :], C.s_v[n], reads=[C.dscr[n]], writes=[dvl[r]])
                    for g in range(2):
                        def f(e, g=g):
                            for k in range(8):
                                i = e.matmul(pq[:, g * 512:(g + 1) * 512], lhsT=hT[n % NH][:, k, :], rhs=Wqg[:, k, g * 512:(g + 1) * 512],
                                             start=(k == 0), stop=(k == 7))
                            return i
                        em.op("pe", f, reads=[dhT[n % NH], dWqg[0]], writes=[dpq])
                    rotary(pq[:, :], dpq, c_, dc_, rt, drt, qrot, dqrot)

                def p2():
                    def tq(e):
                        for k in range(8):
                            i = e.transpose(tp[:, k, :], qrot[:, k * 128:(k + 1) * 128], C.ident[:, :])
                        return i
                    em.op("pe", tq, reads=[dqrot, C.dident], writes=[dtp])
                    em.op("act", lambda e: e.activation(out=qT[r][0][:, :, :], in_=tp[:, :, :], func=AF.Copy), reads=[dtp], writes=[dqT[r][0]])
                    for i in range(2):
                        em.op("pool", lambda e, i=i: e.tensor_tensor(
                            out=qT[r][1 + i][:, :, :].rearrange("p (h c) t -> p h c t", c=2),
                            in0=qT[r][0][:, :, :].rearrange("p (h c) t -> p h c t", c=2),
                            in1=decrow[:, 4 * i:4 * i + 4, :].unsqueeze(2).broadcast_to([128, 4, 2, 128]), op=ALU.mult),
                            reads=[dqT[r][0], dtab], writes=[dqT[r][1 + i]])
                return [p0, p1, p2]

            def O_steps(n):
                r = n % 2
                steps = []
                for half in range(2):
                    def o_t(half=half):
                        def to(e):
                            for k in range(8):
                                i = e.transpose(tp[:, k, :], otok[r][:, (half * 8 + k) * 128:(half * 8 + k + 1) * 128], C.ident[:, :])
                            return i
                        em.op("pe", to, reads=dotok[r] + [C.dident], writes=[dtp])
                        em.op("act", lambda e: e.activation(out=oT[:, half * 8:(half + 1) * 8, :], in_=tp[:, :, :], func=AF.Copy),
                              reads=[dtp], writes=[doT])
                    steps.append(o_t)

                def o_w():
                    def fw(e):
                        for half in range(2):
                            for k in range(16):
                                i = e.matmul(pq[:, half * 512:(half + 1) * 512], lhsT=oT[:, k, :], rhs=Wro[:, k, half * 512:(half + 1) * 512],
                                             start=(k == 0), stop=(k == 15))
                        return i
                    em.op("pe", fw, reads=[doT, dWro], writes=[dpq])

                def o_e():
                    ss, dss = nss()
                    osl[n] = (ss, dss)
                    emit_post_A(C, em, pq[:, :], dpq, ss, dss, junk, djunk)

                def o_eB():
                    ss, dss = osl.pop(n)
                    emit_post_B(C, em, pq[:, :], dpq, xa[n % NX], dxa[n % NX], gpost, dgpost, ss, dss,
                                C.y[n * 128:(n + 1) * 128, :], C.dy[n])
                steps += [o_w, o_e, o_eB]
                return steps

            def H(n, steps):
                r = n % 2
                last = (n + 1 >= nch)

                def GS(h):
                    def fg(e):
                        for k in range(8):
                            i = e.matmul(pG[:, :], lhsT=hT[n % NH][:, k, :], rhs=Wqg[:, k, 1024 + h * 512:1024 + (h + 1) * 512],
                                         start=(k == 0), stop=(k == 7))
                        return i
                    em.op("pe", fg, reads=[dhT[n % NH], dWqg[1]], writes=[dpG])
                    em.op("act", lambda e: e.activation(out=sg[h % 3][:, :], in_=pG[:, :], func=AF.Silu), reads=[dpG], writes=[dsg[h % 3]])

                    def fs(e):
                        for c in range(2):
                            i = e.matmul(pS[:, 0:128], lhsT=kTl[r][:, 2 * h + c, :], rhs=qT[r][0][:, 2 * h + c, :], start=(c == 0), stop=(c == 1))
                        return i
                    em.op("pe", fs, reads=[dkTl[r], dqT[r][0]], writes=[dpS])
                    em.op("dve", lambda e: e.tensor_tensor(out=STb[h % 2][:, :], in0=pS[:, 0:128], in1=DT[:, h, :], op=ALU.mult),
                          reads=[dpS, dtab], writes=[dSTb[h % 2]])

                def Y(h):
                    yh, dyh = pY[h % 2], dpY[h % 2]

                    def fy(e):
                        e.matmul(yh[:, :], lhsT=STb[h % 2][:, :], rhs=vl[r][:, h * 512:(h + 1) * 512], start=True, stop=False)
                        for c in range(2):
                            e.matmul(yh[:, :], lhsT=qT[r][1][:, 2 * h + c, :], rhs=Sfb[:, h, c, :], start=False, stop=False)
                        for c in range(2):
                            i = e.matmul(yh[:, :], lhsT=qT[r][2][:, 2 * h + c, :], rhs=sbl[r][:, h, c, :], start=False, stop=(c == 1))
                        return i
                    em.op("pe", fy, reads=[dSTb[h % 2], dvl[r], dqT[r][1], dqT[r][2], dSfb[h], dsblh[h]], writes=[dyh])
                    ss, dss = nss()
                    em.op("act", lambda e: e.activation(out=junk2[:, 0:512], in_=yh[:, :], func=AF.Square, accum_out=ss[:, 0:1]),
                          reads=[dyh], writes=[djunk2, dss])
                    emit_rstd(em, ss, dss, C.mhalf, C.dmh, 512)
                    ysl[h] = (ss, dss)

                def YB(h):
                    yh, dyh = pY[h % 2], dpY[h % 2]
                    ss, dss = ysl.pop(h)
                    em.op("dve", lambda e: e.scalar_tensor_tensor(out=otok[r][:, h * 512:(h + 1) * 512], in0=yh[:, :], scalar=ss[:, 2:3],
                                                                  in1=sg[h % 3][:, :], op0=ALU.mult, op1=ALU.mult),
                          reads=[dyh, dss, dsg[h % 3]], writes=[dotok[r][h]])

                def UPD(h, cs_):
                    for c in cs_:
                        def f(e, c=c):
                            return e.matmul(pD[:, :], lhsT=kfl[r][:, h * 256 + c * 128: h * 256 + (c + 1) * 128],
                                            rhs=vl[r][:, h * 512:(h + 1) * 512], start=True, stop=True)
                        em.op("pe", f, reads=[dkfl[r], dvl[r]], writes=[dpD])
                        em.op("dve", lambda e, c=c: e.scalar_tensor_tensor(out=S32[:, h, c, :], in0=S32[:, h, c, :],
                                                                         scalar=g128[:, h:h + 1], in1=pD[:, :],
                                                                         op0=ALU.mult, op1=ALU.add),
                              reads=[dS32h[h], dpD, dtab], writes=[dS32h[h]])
                    if 1 not in cs_:
                        return
                    if n + 1 == half_b:
                        em.op("dve", lambda e: e.tensor_scalar(out=S32[:, h, :, :], in0=S32[:, h, :, :], scalar1=bnd[:, 0:1], scalar2=None,
                                                               op0=ALU.mult), reads=[dS32h[h], drc], writes=[dS32h[h]])
                    em.op("act", lambda e: e.activation(out=Sfb[:, h, :, :], in_=S32[:, h, :, :], func=AF.Copy),
                          reads=[dS32h[h]], writes=[dSfb[h]])

                GS(0)
                for h in range(4):
                    if h + 1 < 4:
                        GS(h + 1)
                    if h >= 1 and not last:
                        UPD(h - 1, [1])
                    Y(h)
                    if h >= 1:
                        YB(h - 1)
                    if not last:
                        load_sb(n + 1, h)
                        UPD(h, [0])
                    for _ in range(2):
                        if steps:
                            steps.pop(0)()
                YB(3)
                if not last:
                    UPD(3, [1])
                while steps:
                    steps.pop(0)()

            nop = lambda: None
            front(0)
            if nch > 1:
                front(1)
            for s_ in P_steps(0):
                s_()
            if nch > 1:
                P_steps(1)[0]()
            for n in range(nch):
                O = O_steps(n - 1) if n >= 1 else [nop] * 5
                Opp = O_steps(n - 2)[4] if n >= 2 else nop
                P = P_steps(n + 1) if n + 1 < nch else [nop] * 3
                P0n = P_steps(n + 2)[0] if n + 2 < nch else nop
                frA = (lambda n=n: frontA(n + 2)) if n + 2 < nch else nop
                frB = (lambda n=n: frontB(n + 2)) if n + 2 < nch else nop
                steps = [lambda Opp=Opp, O=O: (Opp(), O[0]()), P[1], O[1], frA, P[2], lambda O=O, frB=frB: (O[2](), frB()), P0n, O[3]]
                H(n, steps)
            if nch >= 2:
                O_steps(nch - 2)[4]()
            for s_ in O_steps(nch - 1):
                s_()
            em.barrier()


def build(nsub=6, subs=None, nblk=NCH, dbg=3):
    nc = bass.Bass("TRN2", target_bir_lowering=False)
    em = Emitter(nc)
    C = Ctx()
    C.nc, C.em = nc, em

    def din(name, shape, dt=F32):
        return nc.dram_tensor(name, shape, dt, kind="ExternalInput").ap()
    C.xin = din("xin", [NTOK, D])
    C.ng = din("norm_gains", [2, 6, D])
    C.fwi = din("ffn_w_in", [2, 2, D, 2 * DFF])
    C.fwo = din("ffn_w_out", [2, 2, DFF, D])
    C.wqkv = din("attn_w_qkv", [1, D, 1536])
    C.wo = din("attn_w_o", [1, D, D])
    C.sink = din("attn_sink", [1, 16])
    C.rwi = din("ret_w_in", [1, D, 6144])
    C.rwo = din("ret_w_o", [1, 2048, D])
    C.rdf = din("ret_decay_fwd", [1, 4])
    C.rdb = din("ret_decay_bwd", [1, 4])
    C.identd = din("c_ident", [128, 128], BF16)
    C.acs = din("c_acs", [NTOK, 320])
    C.amask = din("c_amask", [NCH, 128, 384], BF16)
    C.rconst = din("c_rconst", [128, 6, 128])
    C.rpos = din("c_rpos", [128, 4])
    C.rbnd = din("c_rbnd", [128, 1])
    C.rcs = din("c_rcs", [NTOK, 256])
    C.s_kT = nc.dram_tensor("s_kT", [NCH, 128, 1024], BF16, kind="Internal").ap()
    C.s_kf = nc.dram_tensor("s_kf", [NCH, 128, 1024], BF16, kind="Internal").ap()
    C.s_v = nc.dram_tensor("s_v", [NCH, 128, 2048], BF16, kind="Internal").ap()
    C.s_sb = nc.dram_tensor("s_sb", [NCH, 128, 4096], BF16, kind="Internal").ap()
    C.dscr = [Dep("scr%d" % i) for i in range(NCH)]
    C.y = nc.dram_tensor("y", [NTOK, D], F32, kind="ExternalOutput").ap()
    C.dy = [Dep("y%d" % i) for i in range(NCH)]
    C.dxin = [Dep("xin%d" % i) for i in range(NCH)]

    C.ident = nc.alloc_sbuf_tensor("ident", [128, 128], BF16)
    C.dident = Dep("ident")
    C.mhalf = nc.alloc_sbuf_tensor("mhalf", [128, 1], F32)
    C.dmh = Dep("mhalf")
    em.dma("sp", C.ident[:, :], C.identd[:, :], writes=[C.dident])
    em.op("pool", lambda e: e.memset(C.mhalf[:, :], -0.5), writes=[C.dmh])

    C.nblk = nblk
    C.dbg = dbg
    if subs is None:
        subs = [("ffn", 0, 0), ("attn", 0, 0), ("ffn", 0, 1), ("ffn", 1, 0), ("ret", 1, 0), ("ffn", 1, 1)]
    src, dsrc = C.xin, C.dxin
    for i, (kind, l, which) in enumerate(subs[:nsub]):
        if kind == "ffn":
            emit_ffn(C, l, which, src, dsrc)
        elif kind == "attn":
            emit_attn(C, l, src, dsrc)
        elif kind == "ret":
            emit_ret(C, l, src, dsrc)
        src, dsrc = C.y, C.dy
    em.finish()
    return nc


def make_consts():
    c = {}
    c["c_ident"] = np.eye(128, dtype=np.float32).astype(ml_dtypes.bfloat16)
    j = np.arange(128, dtype=np.float32)[:, None]
    i = np.arange(128, dtype=np.float32)[None, :]
    rc = np.zeros((128, 6, 128), np.float32)
    rc[:, 0] = np.maximum(i - j, 0.0)
    rc[:, 1] = np.maximum(j - i, 0.0)
    rc[:, 2] = (i >= j) / 16.0
    rc[:, 3] = (j > i) / 16.0
    rc[:, 4] = np.broadcast_to(i + 1.0, (128, 128))
    rc[:, 5] = np.broadcast_to(128.0 - i, (128, 128))
    c["c_rconst"] = rc
    p = np.arange(128, dtype=np.float32)
    c["c_rpos"] = np.stack([p + 1, 128 - p, 127 - p, p], axis=1).astype(np.float32)
    return c


def make_core_consts(seqlen):
    c = {}
    tok = np.arange(NTOK)
    pos = (tok % seqlen).astype(np.float32)
    inv = (500000.0 ** (-np.arange(8, dtype=np.float32) / 8)).astype(np.float32)
    ang = pos[:, None] * inv[None, :]
    cos = np.cos(ang).astype(np.float32)
    sin = np.sin(ang).astype(np.float32)
    hs = np.ones((1, 20, 1), np.float32)
    hs[:, :16] = 0.125
    acs = np.concatenate([(np.tile(cos[:, None, :], (1, 20, 1)) * hs).reshape(NTOK, 160),
                          (np.tile(sin[:, None, :], (1, 20, 1)) * hs).reshape(NTOK, 160)], axis=1)
    c["c_acs"] = np.ascontiguousarray(acs, dtype=np.float32)
    b = np.arange(NCH)[:, None, None]
    qi = np.arange(128)[None, :, None]
    kc = np.arange(384)[None, None, :]
    tq = 128 * b + qi
    tk = 128 * (b - 1) + kc
    valid = (tk >= 0) & (tk < NTOK) & ((tk // seqlen) == (tq // seqlen)) & (np.abs(tk - tq) <= 128)
    c["c_amask"] = np.where(valid, 0.0, NEG).astype(np.float32).astype(ml_dtypes.bfloat16)
    invr = (10000.0 ** (-np.arange(128, dtype=np.float32) / 128)).astype(np.float32)
    angr = pos[:, None] * invr[None, :]
    c["c_rcs"] = np.ascontiguousarray(np.concatenate([np.cos(angr), np.sin(angr)], axis=1), dtype=np.float32)
    c["c_rbnd"] = np.full((128, 1), 1.0 if seqlen == NTOK else 0.0, np.float32)
    return c


def kernel(x_prompt, x_sample, norm_gains, ffn_w_in, ffn_w_out, attn_w_qkv, attn_w_o, attn_sink,
           ret_w_in, ret_w_o, ret_decay_fwd, ret_decay_bwd, _nsub=6, _subs=None, _nblk=NCH, _cores=None, _trace=False, _dbg=3):
    f = lambda a: np.ascontiguousarray(np.asarray(a, dtype=np.float32))
    xp = f(x_prompt).reshape(4, NTOK, D)
    xs = f(x_sample).reshape(4, NTOK, D)
    shared = {
        "norm_gains": f(norm_gains), "ffn_w_in": f(ffn_w_in), "ffn_w_out": f(ffn_w_out),
        "attn_w_qkv": f(attn_w_qkv), "attn_w_o": f(attn_w_o), "attn_sink": f(attn_sink),
        "ret_w_in": f(ret_w_in), "ret_w_o": f(ret_w_o), "ret_decay_fwd": f(ret_decay_fwd),
        "ret_decay_bwd": f(ret_decay_bwd),
    }
    shared.update(make_consts())
    in_maps = []
    cc = [make_core_consts(2048), make_core_consts(4096)]
    for c in range(8):
        m = dict(shared)
        m.update(cc[0] if c < 4 else cc[1])
        m["xin"] = xp[c] if c < 4 else xs[c - 4]
        in_maps.append(m)
    nc = build(_nsub, _subs, _nblk, _dbg)
    if _cores is not None:
        res = run_bass_kernel_spmd(nc, [in_maps[c] for c in _cores], core_ids=list(range(len(_cores))), trace=_trace)
        if _trace:
            print("exec_time_ns", res.exec_time_ns)
        return [np.asarray(r["y"], dtype=np.float32) for r in res.results]
    res = run_bass_kernel_spmd(nc, in_maps, core_ids=list(range(8)))
    outs = [np.asarray(r["y"], dtype=np.float32) for r in res.results]
    y_prompt = np.stack(outs[:4]).reshape(8, 2048, D)
    y_sample = np.stack(outs[4:]).reshape(4, 4096, D)
    return (y_prompt, y_sample)
```

```python
import os
import numpy as np
import ml_dtypes
from contextlib import ExitStack
import concourse.bass as bass
import concourse.mybir as mybir
from concourse.bass_utils import run_bass_kernel_spmd

F32 = mybir.dt.float32
BF16 = mybir.dt.bfloat16
AF = mybir.ActivationFunctionType
ALU = mybir.AluOpType
AX = mybir.AxisListType

NTOK = 4096
NCH = 32
D = 1024
DFF = 2816
EPS = 1e-6
EPOCH = 1 << 30
NEG = -30000.0


class Dep:
    __slots__ = ("name", "w", "rs", "dsem", "dcnt", "ex", "retired")

    def __init__(self, name="", ex=False):
        self.name = name
        self.ex = ex
        self.w = None
        self.rs = {}
        self.dsem = None
        self.dcnt = 0
        self.retired = False


class Emitter:
    def __init__(self, nc):
        self.nc = nc
        self.eng = {"pe": nc.tensor, "act": nc.scalar, "dve": nc.vector,
                    "pool": nc.gpsimd, "sp": nc.sync}
        self.sem = {}
        self.cnt = {}
        self.nsem = 0
        for e in self.eng:
            self.sem[e] = self._newsem("e_" + e)
            self.cnt[e] = 0
        self.waited = {}
        self.dma_owners = []
        self.free_dsems = []
        self.no_recycle = set()

    def _newsem(self, name):
        self.nsem += 1
        return self.nc.alloc_semaphore("%s_%d" % (name, self.nsem))

    def _tick(self, e):
        if self.cnt[e] >= EPOCH:
            self.sem[e] = self._newsem("e_" + e)
            self.cnt[e] = 0
        self.cnt[e] += 1
        return self.sem[e], self.cnt[e]

    def _need(self, e, rec, needs):
        if rec is None:
            return
        if rec[0] == "e":
            _, pe, sem, val = rec
            if pe == e and e == "pe":
                return
            needs.append((sem, val))
        else:
            o = rec[1]
            needs.append((o.dsem, o.dcnt))

    def _collect(self, e, reads, writes):
        needs = []
        for d in reads:
            self._need(e, d.w, needs)
        for d in writes:
            if d.w is not None:
                self._need(e, d.w, needs)
            for k, r in d.rs.items():
                self._need(e, r, needs)
        return needs

    def _emit_waits(self, e, needs):
        eng = self.eng[e]
        best = {}
        for sem, val in needs:
            k = id(sem)
            if k not in best or best[k][1] < val:
                best[k] = (sem, val)
        for k, (sem, val) in best.items():
            wk = (e, k)
            if self.waited.get(wk, 0) >= val:
                continue
            self.waited[wk] = val
            eng.wait_ge(sem, val)

    def op(self, e, fn, reads=(), writes=()):
        xr = [d for d in reads if d.ex]
        needs = []
        if xr:
            for d in xr:
                self._need(e, d.w, needs)
            reads = [d for d in reads if not d.ex]
            writes = list(writes) + [d for d in xr if d not in writes]
        needs += self._collect(e, reads, writes)
        self._emit_waits(e, needs)
        inst = fn(self.eng[e])
        sem, val = self._tick(e)
        inst.then_inc(sem, 1)
        rec = ("e", e, sem, val)
        for d in reads:
            d.rs[e] = rec
        for d in writes:
            d.w = rec
            d.rs = {}
        return inst

    def dma(self, q, out, in_, reads=(), writes=(), owner=None):
        needs = self._collect(q, reads, writes)
        if owner is None:
            owner = writes[0] if writes else reads[0]
        if owner.dsem is None or owner.retired:
            if self.free_dsems and q != "pool":
                owner.dsem, owner.dcnt = self.free_dsems.pop()
            else:
                owner.dsem, owner.dcnt = self._newsem("d_" + owner.name), 0
                if q == "pool":
                    self.no_recycle.add(id(owner.dsem))
            owner.retired = False
            self.dma_owners.append(owner)
        self._emit_waits(q, needs)
        inst = self.eng[q].dma_start(out=out, in_=in_)
        owner.dcnt += 16
        inst.then_inc(owner.dsem, 16)
        rec = ("d", owner)
        for d in reads:
            d.rs["dma%d" % id(owner)] = rec
        for d in writes:
            d.w = rec
            d.rs = {}
        return inst

    def barrier(self):
        pts = [(self.sem[e], self.cnt[e]) for e in self.eng if self.cnt[e] > 0]
        pts += [(o.dsem, o.dcnt) for o in self.dma_owners]
        for e in self.eng:
            self._emit_waits(e, pts)
        for o in self.dma_owners:
            o.retired = True
            if id(o.dsem) not in self.no_recycle:
                self.free_dsems.append((o.dsem, o.dcnt))
        self.dma_owners = []

    def finish(self):
        sp = self.eng["sp"]
        pts = [(o.dsem, o.dcnt) for o in self.dma_owners]
        self._emit_waits("sp", pts)


def PDep(name):
    return Dep(name, ex=True)


class Ctx:
    pass


_uid = [0]


def _alloc(st, nc, name, shape, dt):
    _uid[0] += 1
    return st.enter_context(nc.sbuf_tensor("%s_u%d" % (name, _uid[0]), shape, dt))


def _palloc(st, nc, name, shape, dt):
    _uid[0] += 1
    return st.enter_context(nc.psum_tensor("%s_u%d" % (name, _uid[0]), shape, dt))


def emit_rstd(em, ss, dss, mhalf, dmh, n_feat):
    em.op("pool", lambda e: e.tensor_scalar(out=ss[:, 1:2], in0=ss[:, 0:1], scalar1=1.0 / n_feat,
                                            scalar2=EPS, op0=ALU.mult, op1=ALU.add),
          reads=[dss], writes=[dss])
    em.op("pool", lambda e: e.tensor_tensor(out=ss[:, 2:3], in0=ss[:, 1:2], in1=mhalf[:, 0:1], op=ALU.pow),
          reads=[dss, dmh], writes=[dss])


def emit_prenorm_T(C, em, xs, dxs, gpre, dgpre, hbs, dhbs, ss, dss, tp, dtp, hT_dst, dhT):
    em.op("act", lambda e: e.activation(out=hbs[:, :], in_=xs[:, :], func=AF.Square, accum_out=ss[:, 0:1]),
          reads=[dxs], writes=[dhbs, dss])
    emit_rstd(em, ss, dss, C.mhalf, C.dmh, D)
    em.op("dve", lambda e: e.scalar_tensor_tensor(out=hbs[:, :], in0=xs[:, :], scalar=ss[:, 2:3], in1=gpre[:, :],
                                                  op0=ALU.mult, op1=ALU.mult),
          reads=[dxs, dss, dgpre], writes=[dhbs])

    def tps(e):
        for k in range(8):
            i = e.transpose(tp[:, k, :], hbs[:, k * 128:(k + 1) * 128], C.ident[:, :])
        return i
    em.op("pe", tps, reads=[dhbs, C.dident], writes=[dtp])
    em.op("act", lambda e: e.activation(out=hT_dst, in_=tp[:, :, :], func=AF.Copy), reads=[dtp], writes=[dhT])


def load_gains(C, em, st, nc, l, ipre, ipost, post_scale):
    gpre = _alloc(st, nc, "gpre", [128, D], F32)
    gpost = _alloc(st, nc, "gpost", [128, D], F32)
    dgpre = Dep("gpre")
    dgpost = Dep("gpost")
    em.dma("sp", gpre[:, :], C.ng[l, ipre, :].partition_broadcast(128), writes=[dgpre])
    em.dma("sp", gpost[:, :], C.ng[l, ipost, :].partition_broadcast(128), writes=[dgpost])
    if post_scale != 1.0:
        em.op("pool", lambda e: e.tensor_scalar(out=gpost[:, :], in0=gpost[:, :], scalar1=post_scale, scalar2=0.0,
                                                op0=ALU.mult, op1=ALU.add), reads=[dgpost], writes=[dgpost])
    return gpre, dgpre, gpost, dgpost


def emit_post_A(C, em, ps_out, dps, ss, dss, junk, djunk):
    em.op("act", lambda e: e.activation(out=junk[:, :], in_=ps_out, func=AF.Square, accum_out=ss[:, 0:1]),
          reads=[dps], writes=[djunk, dss])
    emit_rstd(em, ss, dss, C.mhalf, C.dmh, D)


def emit_post_B(C, em, ps_out, dps, xs, dxs, gpost, dgpost, ss, dss, dst_ap, ddst):
    em.op("dve", lambda e: e.scalar_tensor_tensor(out=ps_out, in0=ps_out, scalar=ss[:, 2:3], in1=gpost[:, :],
                                                  op0=ALU.mult, op1=ALU.mult),
          reads=[dps, dss, dgpost], writes=[dps])
    em.op("dve", lambda e: e.tensor_tensor(out=xs[:, :], in0=ps_out, in1=xs[:, :], op=ALU.add),
          reads=[dps, dxs], writes=[dxs])
    em.dma("sp", dst_ap, xs[:, :], reads=[dxs], writes=[ddst], owner=dxs)


def emit_post_residual(C, em, ps_out, dps, xs, dxs, gpost, dgpost, ss, dss, junk, djunk, dst_ap, ddst):
    emit_post_A(C, em, ps_out, dps, ss, dss, junk, djunk)
    emit_post_B(C, em, ps_out, dps, xs, dxs, gpost, dgpost, ss, dss, dst_ap, ddst)


def emit_front_A(C, em, xs, dxs, hbs, dhbs, ss, dss):
    em.op("act", lambda e: e.activation(out=hbs[:, :], in_=xs[:, :], func=AF.Square, accum_out=ss[:, 0:1]),
          reads=[dxs], writes=[dhbs, dss])
    emit_rstd(em, ss, dss, C.mhalf, C.dmh, D)


def emit_front_B(C, em, xs, dxs, hbs, dhbs, ss, dss, gpre, dgpre):
    em.op("dve", lambda e: e.scalar_tensor_tensor(out=hbs[:, :], in0=xs[:, :], scalar=ss[:, 2:3], in1=gpre[:, :],
                                                  op0=ALU.mult, op1=ALU.mult),
          reads=[dxs, dss, dgpre], writes=[dhbs])


def emit_ffn(C, l, which, src, dsrc, wcast=None, precast=None):
    nc, em = C.nc, C.em
    NJ = DFF // 128
    LAG = 3
    with ExitStack() as st:
        Win = _alloc(st, nc, "Win", [128, 8, 2 * DFF], BF16)
        Wout = _alloc(st, nc, "Wout", [128, NJ, D], BF16)
        JB = [0, 6, 12, 17, 22]
        jblk = [0] * 6 + [1] * 6 + [2] * 5 + [3] * 5
        dWin = [[Dep("WinG%d" % k), Dep("WinU%d" % k)] for k in range(4)]
        dWout = [Dep("Wout%d" % k) for k in range(4)]
        w_in = C.fwi[l, which]
        w_out = C.fwo[l, which]
        wi_v = w_in.rearrange("(k p) f -> p k f", p=128)
        wo_v = w_out.rearrange("(j p) d -> p j d", p=128)
        if wcast is not None:
            s_in, s_out, dsc = wcast
            si_v = s_in.rearrange("(k p) f -> p k f", p=128)
            so_v = s_out.rearrange("(j p) d -> p j d", p=128)

        def wload(k):
            c0_, c1_ = JB[k] * 128, JB[k + 1] * 128
            if wcast is not None:
                em.dma("sp", Win[:, :, c0_:c1_], si_v[:, :, c0_:c1_], reads=dsc[0:4], writes=[dWin[k][0]])
                em.dma("sp", Win[:, :, DFF + c0_:DFF + c1_], si_v[:, :, DFF + c0_:DFF + c1_], reads=dsc[0:4], writes=[dWin[k][1]])
                em.dma("sp", Wout[:, JB[k]:JB[k + 1], :], so_v[:, JB[k]:JB[k + 1], :], reads=dsc[4:6], writes=[dWout[k]])
            else:
                em.dma("pool", Win[:, :, c0_:c1_], wi_v[:, :, c0_:c1_], writes=[dWin[k][0]])
                em.dma("pool", Win[:, :, DFF + c0_:DFF + c1_], wi_v[:, :, DFF + c0_:DFF + c1_], writes=[dWin[k][1]])
                em.dma("pool", Wout[:, JB[k]:JB[k + 1], :], wo_v[:, JB[k]:JB[k + 1], :], writes=[dWout[k]])

        def wprecast(i):
            if precast is None or i >= 6:
                return
            (nl, nw), (s_in2, s_out2, dsc2) = precast
            if i < 4:
                em.dma("pool", s_in2[i * 256:(i + 1) * 256, :], C.fwi[nl, nw][i * 256:(i + 1) * 256, :], writes=[dsc2[i]])
            else:
                i2 = i - 4
                em.dma("pool", s_out2[i2 * 1408:(i2 + 1) * 1408, :], C.fwo[nl, nw][i2 * 1408:(i2 + 1) * 1408, :], writes=[dsc2[i]])
        gpre, dgpre, gpost, dgpost = load_gains(C, em, st, nc, l, 0 if which == 0 else 4, 1 if which == 0 else 5, 0.5)

        xa = [_alloc(st, nc, "xa%d" % i, [128, D], F32) for i in range(3)]
        dxa = [Dep("xa%d" % i) for i in range(3)]
        xb = [_alloc(st, nc, "xb%d" % i, [128, D], F32) for i in range(3)]
        dxb = [Dep("xb%d" % i) for i in range(3)]
        hb = [_alloc(st, nc, "hb%d" % i, [128, D], BF16) for i in range(2)]
        dhb = [Dep("hb%d" % i) for i in range(2)]
        hT = [_alloc(st, nc, "hT%d" % i, [128, 8, 256], BF16) for i in range(2)]
        dhT = [[Dep("hT%d_%d" % (i, c)) for c in range(2)] for i in range(2)]
        NA = 6
        actT = [_alloc(st, nc, "actT%d" % i, [128, 256], BF16) for i in range(NA)]
        dact = [Dep("actT%d" % i) for i in range(NA)]
        sg = [_alloc(st, nc, "sg%d" % i, [128, 256], BF16) for i in range(2)]
        dsg = [Dep("sg%d" % i) for i in range(2)]
        junk = _alloc(st, nc, "junk", [128, D], BF16)
        djunk = Dep("junk")
        sst = [_alloc(st, nc, "ss%d" % i, [128, 4], F32) for i in range(4)]
        dsst = [Dep("ss%d" % i) for i in range(4)]
        tp = [_palloc(st, nc, "tp%d" % i, [128, 8, 128], BF16) for i in range(2)]
        dtp = [PDep("tp%d" % i) for i in range(2)]
        gu = [_palloc(st, nc, "gu%d" % i, [128, 2, 256], F32) for i in range(2)]
        dgu = [PDep("gu%d" % i) for i in range(2)]
        pout = _palloc(st, nc, "pout", [128, 2, D], F32)
        dpout = [PDep("pout%d" % i) for i in range(2)]

        NT = NTOK // 256
        sctr = [0]

        def loads(t):
            for c in range(2):
                ch = 2 * t + c
                em.dma("sp", xa[ch % 3][:, :], src[ch * 128:(ch + 1) * 128, :], reads=[dsrc[ch]], writes=[dxa[ch % 3]])

        fst = {}

        def frontA(t):
            for c in range(2):
                ch = 2 * t + c
                si = sctr[0] % 4
                sctr[0] += 1
                fst[ch] = si
                emit_front_A(C, em, xa[ch % 3], dxa[ch % 3], hb[ch % 2], dhb[ch % 2], sst[si], dsst[si])

        def frontB(t):
            for c in range(2):
                ch = 2 * t + c
                si = fst.pop(ch)
                emit_front_B(C, em, xa[ch % 3], dxa[ch % 3], hb[ch % 2], dhb[ch % 2], sst[si], dsst[si], gpre, dgpre)

        def transp(t, c):
            ch = 2 * t + c
            hbs = hb[ch % 2]

            def tps(e):
                for k in range(8):
                    i = e.transpose(tp[ch % 2][:, k, :], hbs[:, k * 128:(k + 1) * 128], C.ident[:, :])
                return i
            em.op("pe", tps, reads=[dhb[ch % 2], C.dident], writes=[dtp[ch % 2]])
            em.op("act", lambda e: e.activation(out=hT[t % 2][:, :, c * 128:(c + 1) * 128], in_=tp[ch % 2][:, :, :], func=AF.Copy),
                  reads=[dtp[ch % 2]], writes=[dhT[t % 2][c]])

        def xb_loads(t):
            for tc in range(2):
                ch = 2 * t + tc
                em.dma("sp", xb[ch % 3][:, :], src[ch * 128:(ch + 1) * 128, :], reads=[dsrc[ch]], writes=[dxb[ch % 3]])

        def p1(t, j):
            g = gu[j % 2]
            hTt = hT[t % 2]

            def f(e):
                for half in range(2):
                    for k in range(8):
                        i = e.matmul(g[:, half, :], lhsT=Win[:, k, half * DFF + j * 128: half * DFF + (j + 1) * 128],
                                     rhs=hTt[:, k, :], start=(k == 0), stop=(k == 7))
                return i
            em.op("pe", f, reads=dWin[jblk[j]] + dhT[t % 2], writes=[dgu[j % 2]])
            s = sg[j % 2]
            em.op("act", lambda e: e.activation(out=s[:, :], in_=g[:, 0, :], func=AF.Silu),
                  reads=[dgu[j % 2]], writes=[dsg[j % 2]])
            a = actT[j % NA]
            em.op("dve", lambda e: e.tensor_tensor(out=a[:, :], in0=g[:, 1, :], in1=s[:, :], op=ALU.mult),
                  reads=[dgu[j % 2], dsg[j % 2]], writes=[dact[j % NA]])

        def p2(t, j):
            a = actT[j % NA]

            def f(e):
                for tc in range(2):
                    for half in range(2):
                        i = e.matmul(pout[:, tc, half * 512:(half + 1) * 512], lhsT=a[:, tc * 128:(tc + 1) * 128],
                                     rhs=Wout[:, j, half * 512:(half + 1) * 512], start=(j == 0), stop=(j == NJ - 1))
                return i
            em.op("pe", f, reads=[dact[j % NA], dWout[jblk[j]]], writes=dpout)

        est = {}

        def epilogueA(t):
            for tc in range(2):
                si = sctr[0] % 4
                sctr[0] += 1
                est[(t, tc)] = si
                emit_post_A(C, em, pout[:, tc, :], dpout[tc], sst[si], dsst[si], junk, djunk)

        def epilogueB(t):
            for tc in range(2):
                ch = 2 * t + tc
                si = est.pop((t, tc))
                emit_post_B(C, em, pout[:, tc, :], dpout[tc], xb[ch % 3], dxb[ch % 3], gpost, dgpost, sst[si], dsst[si],
                            C.y[ch * 128:(ch + 1) * 128, :], C.dy[ch])

        loads(0)
        wload(0)
        frontA(0)
        frontB(0)
        for k in range(1, 4):
            wload(k)
        transp(0, 0)
        transp(0, 1)
        if NT > 1:
            loads(1)
        for t in range(NT):
            for j in range(NJ + LAG):
                if j < NJ:
                    p1(t, j)
                if j >= LAG:
                    p2(t, j - LAG)
                if j == 1 and t >= 1:
                    epilogueB(t - 1)
                if t + 1 < NT:
                    if j == 4:
                        frontA(t + 1)
                    elif j == 7:
                        frontB(t + 1)
                    elif j == 11:
                        transp(t + 1, 0)
                    elif j == 15:
                        transp(t + 1, 1)
                    elif j == 18 and t + 2 < NT:
                        loads(t + 2)
                if j == 13:
                    xb_loads(t)
                if j == 20 and t >= 2:
                    wprecast(t - 2)
            epilogueA(t)
        epilogueB(NT - 1)
        em.barrier()


def emit_attn(C, l, src, dsrc):
    nc, em = C.nc, C.em
    SCALE = 0.125
    with ExitStack() as st:
        Wqkv = _alloc(st, nc, "Wqkv", [128, 8, 1536], BF16)
        Wo = _alloc(st, nc, "Wo", [128, 8, D], BF16)
        dWqkv, dWo = Dep("Wqkv"), Dep("Wo")
        em.dma("pool", Wqkv[:, :, :], C.wqkv[0].rearrange("(k p) f -> p k f", p=128), writes=[dWqkv])
        em.dma("pool", Wo[:, :, :], C.wo[0].rearrange("(k p) f -> p k f", p=128), writes=[dWo])
        gpre, dgpre, gpost, dgpost = load_gains(C, em, st, nc, l, 2, 3, 1.0)
        sinkt = _alloc(st, nc, "sinkt", [128, 16], F32)
        nsink = _alloc(st, nc, "nsink", [128, 16], F32)
        dsink = Dep("sink")
        em.dma("sp", sinkt[:, :], C.sink[0, :].partition_broadcast(128), writes=[dsink])
        em.op("pool", lambda e: e.tensor_scalar(out=nsink[:, :], in0=sinkt[:, :], scalar1=-1.0, scalar2=0.0,
                                                op0=ALU.mult, op1=ALU.add), reads=[dsink], writes=[dsink])

        kT = _alloc(st, nc, "kT_all", [128, 8, 34 * 128], BF16)
        vA = _alloc(st, nc, "v_all", [128, 34, 256], BF16)
        dkT = [Dep("kT%d" % i) for i in range(34)]
        dvA = [Dep("vA%d" % i) for i in range(34)]
        for i in (0, 33):
            em.op("dve", lambda e, i=i: e.memset(kT[:, :, i * 128:(i + 1) * 128], 0.0), writes=[dkT[i]])
            em.op("dve", lambda e, i=i: e.memset(vA[:, i, :], 0.0), writes=[dvA[i]])

        NX = 5
        xa = [_alloc(st, nc, "xa%d" % i, [128, D], F32) for i in range(NX)]
        dxa = [Dep("xa%d" % i) for i in range(NX)]
        hb = [_alloc(st, nc, "hb%d" % i, [128, D], BF16) for i in range(2)]
        dhb = [Dep("hb%d" % i) for i in range(2)]
        hT = [_alloc(st, nc, "hT%d" % i, [128, 8, 128], BF16) for i in range(2)]
        dhT = [Dep("hT%d" % i) for i in range(2)]
        cs = [_alloc(st, nc, "cs%d" % i, [128, 320], F32) for i in range(2)]
        dcs = [Dep("cs%d" % i) for i in range(2)]
        mk = [_alloc(st, nc, "mk%d" % i, [128, 384], BF16) for i in range(2)]
        dmk = [Dep("mk%d" % i) for i in range(2)]
        qtok = [_alloc(st, nc, "qtok%d" % i, [128, 16, 64], BF16) for i in range(2)]
        dqtok = [Dep("qtok%d" % i) for i in range(2)]
        kdtok = [_alloc(st, nc, "kdtok%d" % i, [128, 4, 2, 128], BF16) for i in range(2)]
        dkdtok = [Dep("kdtok%d" % i) for i in range(2)]
        for i in range(2):
            em.op("dve", lambda e, i=i: e.memset(kdtok[i][:, :, :, :], 0.0), writes=[dkdtok[i]])
        rt = [_alloc(st, nc, "rt%d" % i, [128, 20, 8], F32) for i in range(4)]
        drt = [Dep("rt%d" % i) for i in range(4)]
        NQ = 3
        qT = [_alloc(st, nc, "qT%d" % i, [128, 8, 128], BF16) for i in range(NQ)]
        dqT = [Dep("qT%d" % i) for i in range(NQ)]
        pb = [_alloc(st, nc, "pb%d" % i, [128, 384], BF16) for i in range(4)]
        dpb = [Dep("pb%d" % i) for i in range(4)]
        pT = [_alloc(st, nc, "pT%d" % i, [128, 3, 128], BF16) for i in range(3)]
        dpT = [Dep("pT%d" % i) for i in range(3)]
        stt = [_alloc(st, nc, "stt%d" % i, [128, 6, 16], F32) for i in range(2)]
        dsth = [[Dep("st%d_%d" % (i, h)) for h in range(16)] for i in range(2)]
        dfin = [[Dep("fin%d_%d" % (i, k)) for k in range(2)] for i in range(2)]
        otok = [_alloc(st, nc, "otok%d" % i, [128, 16, 64], BF16) for i in range(2)]
        dotok = [[Dep("otok%d_%d" % (i, k)) for k in range(2)] for i in range(2)]
        oT = _alloc(st, nc, "oT", [128, 8, 128], BF16)
        doT = Dep("oT")
        junk = _alloc(st, nc, "junk", [128, D], BF16)
        djunk = Dep("junk")
        sst = [_alloc(st, nc, "ss%d" % i, [128, 4], F32) for i in range(4)]
        dsst = [Dep("ss%d" % i) for i in range(4)]

        qkv = _palloc(st, nc, "qkv", [128, 1024], F32)
        dq01 = PDep("qkv01")
        tp = _palloc(st, nc, "tp", [128, 8, 128], BF16)
        dtp = PDep("tp")
        sps = _palloc(st, nc, "sps", [128, 4, 512], F32)
        dsps = [PDep("sps%d" % i) for i in range(4)]
        ops = _palloc(st, nc, "ops", [128, 8, 64], F32)
        dops = PDep("ops")
        sctr = [0]

        fsl = {}
        csl = {}
        def A_steps(b):
            xs, dxs = xa[b % NX], dxa[b % NX]
            c_, dc_ = cs[b % 2], dcs[b % 2]
            h_ = hT[b % 2]
            qt, dqt = qtok[b % 2], dqtok[b % 2]
            kd, dkd = kdtok[b % 2], dkdtok[b % 2]

            def a_front():
                si = sctr[0] % 4
                sctr[0] += 1
                fsl[b] = si
                emit_front_A(C, em, xs, dxs, hb[b % 2], dhb[b % 2], sst[si], dsst[si])

            def a_frontB():
                si = fsl.pop(b)
                emit_front_B(C, em, xs, dxs, hb[b % 2], dhb[b % 2], sst[si], dsst[si], gpre, dgpre)

            def a0():
                hbs = hb[b % 2]

                def tps(e):
                    for k in range(8):
                        i = e.transpose(tp[:, k, :], hbs[:, k * 128:(k + 1) * 128], C.ident[:, :])
                    return i
                em.op("pe", tps, reads=[dhb[b % 2], C.dident], writes=[dtp])
                em.op("act", lambda e: e.activation(out=h_[:, :, :], in_=tp[:, :, :], func=AF.Copy), reads=[dtp], writes=[dhT[b % 2]])

            def a1q():
                for g in range(2):
                    def f(e, g=g):
                        for k in range(8):
                            i = e.matmul(qkv[:, g * 512:(g + 1) * 512], lhsT=h_[:, k, :], rhs=Wqkv[:, k, g * 512:(g + 1) * 512],
                                         start=(k == 0), stop=(k == 7))
                        return i
                    em.op("pe", f, reads=[dhT[b % 2], dWqkv], writes=[dq01])
                qv = qkv[:, 0:1024].rearrange("p (h d) -> p h d", d=64)
                em.op("act", lambda e: e.activation(out=qt[:, :, 16:64], in_=qv[:, :, 16:64], func=AF.Copy, scale=SCALE),
                      reads=[dq01], writes=[dqt])

            def a2q():
                qv = qkv[:, 0:1024].rearrange("p (h d) -> p h d", d=64)
                cosv = c_[:, 0:160].rearrange("p (h d) -> p h d", d=8)[:, 0:16, :]
                sinv = c_[:, 160:320].rearrange("p (h d) -> p h d", d=8)[:, 0:16, :]
                x1, x2 = qv[:, :, 0:8], qv[:, :, 8:16]
                for i, (xx, tb) in enumerate([(x1, cosv), (x2, sinv), (x2, cosv), (x1, sinv)]):
                    em.op("dve", lambda e, i=i, xx=xx, tb=tb: e.tensor_tensor(out=rt[i][:, 0:16, :], in0=xx, in1=tb, op=ALU.mult),
                          reads=[dq01, dc_], writes=[drt[i]])
                em.op("dve", lambda e: e.tensor_tensor(out=qt[:, :, 0:8], in0=rt[0][:, 0:16, :], in1=rt[1][:, 0:16, :], op=ALU.subtract),
                      reads=[drt[0], drt[1]], writes=[dqt])
                em.op("dve", lambda e: e.tensor_tensor(out=qt[:, :, 8:16], in0=rt[2][:, 0:16, :], in1=rt[3][:, 0:16, :], op=ALU.add),
                      reads=[drt[2], drt[3]], writes=[dqt])

            def a1kv():
                def f(e):
                    for k in range(8):
                        i = e.matmul(qkv[:, 0:512], lhsT=h_[:, k, :], rhs=Wqkv[:, k, 1024:1536], start=(k == 0), stop=(k == 7))
                    return i
                em.op("pe", f, reads=[dhT[b % 2], dWqkv], writes=[dq01])
                kv = qkv[:, 0:256].rearrange("p (h d) -> p h d", d=64)
                for dup in range(2):
                    em.op("act", lambda e, dup=dup: e.activation(out=kd[:, :, dup, dup * 64 + 16:dup * 64 + 64], in_=kv[:, :, 16:64],
                                                                 func=AF.Copy), reads=[dq01], writes=[dkd])
                em.op("act", lambda e: e.activation(out=vA[:, b + 1, :], in_=qkv[:, 256:512], func=AF.Copy),
                      reads=[dq01], writes=[dvA[b + 1]])

            def a2kv():
                kv = qkv[:, 0:256].rearrange("p (h d) -> p h d", d=64)
                cosv = c_[:, 0:160].rearrange("p (h d) -> p h d", d=8)[:, 16:20, :]
                sinv = c_[:, 160:320].rearrange("p (h d) -> p h d", d=8)[:, 16:20, :]
                x1, x2 = kv[:, :, 0:8], kv[:, :, 8:16]
                for i, (xx, tb) in enumerate([(x1, cosv), (x2, sinv), (x2, cosv), (x1, sinv)]):
                    em.op("dve", lambda e, i=i, xx=xx, tb=tb: e.tensor_tensor(out=rt[i][:, 16:20, :], in0=xx, in1=tb, op=ALU.mult),
                          reads=[dq01, dc_], writes=[drt[i]])
                for dup in range(2):
                    em.op("dve", lambda e, dup=dup: e.tensor_tensor(out=kd[:, :, dup, dup * 64:dup * 64 + 8], in0=rt[0][:, 16:20, :],
                                                                    in1=rt[1][:, 16:20, :], op=ALU.subtract),
                          reads=[drt[0], drt[1]], writes=[dkd])
                    em.op("dve", lambda e, dup=dup: e.tensor_tensor(out=kd[:, :, dup, dup * 64 + 8:dup * 64 + 16], in0=rt[2][:, 16:20, :],
                                                                    in1=rt[3][:, 16:20, :], op=ALU.add),
                          reads=[drt[2], drt[3]], writes=[dkd])

            def a3():
                qflat = qt[:, :, :].rearrange("p h d -> p (h d)")

                def tq(e):
                    for k in range(8):
                        i = e.transpose(tp[:, k, :], qflat[:, k * 128:(k + 1) * 128], C.ident[:, :])
                    return i
                em.op("pe", tq, reads=[dqt, C.dident], writes=[dtp])
                em.op("act", lambda e: e.activation(out=qT[b % NQ][:, :, :], in_=tp[:, :, :], func=AF.Copy),
                      reads=[dtp], writes=[dqT[b % NQ]])

            def a4():
                kflat = kd[:, :, :, :].rearrange("p g u d -> p (g u d)")

                def tk(e):
                    for k in range(8):
                        i = e.transpose(tp[:, k, :], kflat[:, k * 128:(k + 1) * 128], C.ident[:, :])
                    return i
                em.op("pe", tk, reads=[dkd, C.dident], writes=[dtp])
                em.op("act", lambda e: e.activation(out=kT[:, :, (b + 1) * 128:(b + 2) * 128], in_=tp[:, :, :], func=AF.Copy),
                      reads=[dtp], writes=[dkT[b + 1]])
            return [a_front, a0, a1q, a2q, a1kv, a2kv, a3, a4, a_frontB]

        def A_loads(b):
            em.dma("sp", xa[b % NX][:, :], src[b * 128:(b + 1) * 128, :], reads=[dsrc[b]], writes=[dxa[b % NX]])
            em.dma("sp", cs[b % 2][:, :], C.acs[b * 128:(b + 1) * 128, :], writes=[dcs[b % 2]])

        def C_steps(b):
            ot = otok[b % 2]

            def c0():
                oflat = ot[:, :, :].rearrange("p h d -> p (h d)")

                def to(e):
                    for k in range(8):
                        i = e.transpose(tp[:, k, :], oflat[:, k * 128:(k + 1) * 128], C.ident[:, :])
                    return i
                em.op("pe", to, reads=dotok[b % 2] + [C.dident], writes=[dtp])
                em.op("act", lambda e: e.activation(out=oT[:, :, :], in_=tp[:, :, :], func=AF.Copy), reads=[dtp], writes=[doT])

            def c1():
                def fw(e):
                    for half in range(2):
                        for k in range(8):
                            i = e.matmul(qkv[:, half * 512:(half + 1) * 512], lhsT=oT[:, k, :], rhs=Wo[:, k, half * 512:(half + 1) * 512],
                                         start=(k == 0), stop=(k == 7))
                    return i
                em.op("pe", fw, reads=[doT, dWo], writes=[dq01])

            def c2():
                si = sctr[0] % 4
                sctr[0] += 1
                csl[b] = si
                emit_post_A(C, em, qkv[:, 0:1024], dq01, sst[si], dsst[si], junk, djunk)

            def c2B():
                si = csl.pop(b)
                emit_post_B(C, em, qkv[:, 0:1024], dq01, xa[b % NX], dxa[b % NX], gpost, dgpost, sst[si], dsst[si],
                            C.y[b * 128:(b + 1) * 128, :], C.dy[b])
            return [c0, c1, c2, c2B]

        ocp = [_alloc(st, nc, "ocp%d" % i, [128, 8, 64], F32) for i in range(2)]
        docp = [Dep("ocp%d" % i) for i in range(2)]

        def finish_heads(b, h0):
            s_ = stt[b % 2]
            k = h0 // 8
            dsts = dsth[b % 2][h0:h0 + 8]
            df = dfin[b % 2][k]
            hs = slice(h0, h0 + 8)
            em.op("dve", lambda e: e.tensor_copy(out=ocp[k][:, :, :], in_=ops[:, :, :]), reads=[dops], writes=[docp[k]])
            em.op("dve", lambda e: e.tensor_tensor(out=s_[:, 3, hs], in0=s_[:, 1, hs], in1=sinkt[:, hs], op=ALU.add),
                  reads=dsts + [dsink], writes=[df])
            em.op("act", lambda e: e.activation(out=s_[:, 4, hs], in_=s_[:, 3, hs], func=AF.Exp), reads=[df], writes=[df])
            em.op("dve", lambda e: e.tensor_tensor(out=s_[:, 4, hs], in0=s_[:, 4, hs], in1=s_[:, 2, hs], op=ALU.add),
                  reads=dsts + [df], writes=[df])
            em.op("dve", lambda e: e.reciprocal(out=s_[:, 5, hs], in_=s_[:, 4, hs]), reads=[df], writes=[df])
            em.op("pool", lambda e: e.tensor_tensor(out=otok[b % 2][:, hs, :], in0=ocp[k][:, :, :],
                                                   in1=s_[:, 5, hs].unsqueeze(2).broadcast_to([128, 8, 64]), op=ALU.mult),
                  reads=[docp[k], df], writes=[dotok[b % 2][k]])

        def stageB(b, steps):
            m_, dm_ = mk[b % 2], dmk[b % 2]
            em.dma("sp", m_[:, :], C.amask[b], writes=[dm_])
            s_ = stt[b % 2]
            q_ = qT[b % NQ]

            def S(h):
                g, pr, hf = h // 4, h // 2, h % 2
                bk = h % 4
                bank = sps[:, bk, 0:384]

                def f(e):
                    e.matmul(bank, lhsT=q_[:, pr, :], rhs=kT[:, g * 2 + hf, b * 128: b * 128 + 384], start=True, stop=False)
                    return e.matmul(bank, lhsT=C.ident[:, :], rhs=m_[:, :], start=False, stop=True)
                em.op("pe", f, reads=[dqT[b % NQ], dkT[b], dkT[b + 1], dkT[b + 2], dm_, C.dident], writes=[dsps[bk]])
                dst = dsth[b % 2][h]
                em.op("dve", lambda e: e.tensor_reduce(out=s_[:, 0, h:h + 1], in_=bank, op=ALU.max, axis=AX.X, negate=True),
                      reads=[dsps[bk]], writes=[dst])
                em.op("dve", lambda e: e.tensor_tensor(out=s_[:, 1, h:h + 1], in0=s_[:, 0, h:h + 1], in1=nsink[:, h:h + 1], op=ALU.min),
                      reads=[dst, dsink], writes=[dst])
                em.op("act", lambda e: e.activation(out=pb[bk][:, :], in_=bank, func=AF.Exp,
                                                    bias=s_[:, 1, h:h + 1], accum_out=s_[:, 2, h:h + 1]),
                      reads=[dsps[bk], dst], writes=[dpb[bk], dst])

            def T(h):
                bk = h % 4
                tb = sps[:, bk, :].bitcast(BF16)[:, 0:384].rearrange("p (c t) -> p c t", t=128)

                def ft(e):
                    for c in range(3):
                        r = e.transpose(tb[:, c, :], pb[bk][:, c * 128:(c + 1) * 128], C.ident[:, :])
                    return r
                em.op("pe", ft, reads=[dpb[bk], C.dident], writes=[dsps[bk]])
                if h % 2 == 0:
                    em.op("dve", lambda e: e.tensor_copy(out=pT[h % 3][:, :, :], in_=tb), reads=[dsps[bk]], writes=[dpT[h % 3]])
                else:
                    em.op("act", lambda e: e.activation(out=pT[h % 3][:, :, :], in_=tb, func=AF.Copy), reads=[dsps[bk]], writes=[dpT[h % 3]])

            def PV(h):
                g = h // 4

                def fo(e):
                    for c in range(3):
                        r = e.matmul(ops[:, h % 8, :], lhsT=pT[h % 3][:, c, :], rhs=vA[:, b + c, g * 64:(g + 1) * 64],
                                     start=(c == 0), stop=(c == 2))
                    return r
                em.op("pe", fo, reads=[dpT[h % 3], dvA[b], dvA[b + 1], dvA[b + 2]], writes=[dops])
                if h % 8 == 7:
                    finish_heads(b, h - 7)

            for h in range(3):
                S(h)
            for h in range(16):
                T(h)
                if h >= 1:
                    PV(h - 1)
                if h + 3 < 16:
                    S(h + 3)
                if h in (1, 3, 5, 7, 9, 11, 13, 14, 15) and steps:
                    steps.pop(0)()
            PV(15)
            while steps:
                steps.pop(0)()

        nblk = getattr(C, "nblk", NCH)
        nop = lambda: None

        def run_A(b):
            A = A_steps(b)
            for k in (0, 8, 1, 2, 3, 4, 5, 6, 7):
                A[k]()
        for b0 in range(min(2, NCH)):
            A_loads(b0)
        run_A(0)
        if NCH > 2:
            A_loads(2)
        if NCH > 1:
            run_A(1)
        for b in range(nblk):
            Cs = C_steps(b - 1) if b >= 1 else [nop] * 4
            As = A_steps(b + 2) if b + 2 < NCH else [nop] * 9
            ld = (lambda b=b: A_loads(b + 3)) if b + 3 < NCH else nop
            steps = [lambda Cs=Cs, As=As: (Cs[0](), As[0]()),
                     lambda Cs=Cs, As=As: (Cs[1](), As[8]()),
                     lambda Cs=Cs, As=As: (Cs[2](), As[1]()),
                     lambda Cs=Cs, ld=ld: (Cs[3](), ld()),
                     As[2], As[3], lambda As=As: (As[4](), As[6]()), As[5], As[7]]
            stageB(b, steps)
        for s_ in C_steps(nblk - 1):
            s_()
        em.barrier()


def emit_ret(C, l, src, dsrc):
    nc, em = C.nc, C.em
    nch = getattr(C, "nblk", NCH)
    half_b = NCH // 2
    with ExitStack() as st0:
        bnd = _alloc(st0, nc, "bnd", [128, 1], F32)
        kdec = _alloc(st0, nc, "kdec", [128, 8], F32)
        g128 = _alloc(st0, nc, "g128", [128, 8], F32)
        decrow = _alloc(st0, nc, "decrow", [128, 8, 128], F32)
        DT = _alloc(st0, nc, "DT", [128, 4, 128], F32)
        S32 = _alloc(st0, nc, "S32", [128, 4, 2, 512], F32)
        stT = ExitStack()
        rc = _alloc(stT, nc, "rc", [128, 6, 128], F32)
        cpos = _alloc(stT, nc, "cpos", [128, 4], F32)
        dl = _alloc(stT, nc, "dl", [128, 8], F32)
        lg = _alloc(stT, nc, "lg", [128, 8], F32)
        tmpD = _alloc(stT, nc, "tmpD", [128, 2, 128], F32)
        drc, dtab, dS32, dtmp = Dep("rc"), Dep("tab"), Dep("S32"), Dep("tmpD")
        em.dma("sp", rc[:, :, :], C.rconst[:, :, :], writes=[drc])
        em.dma("sp", cpos[:, :], C.rpos[:, :], writes=[drc], owner=drc)
        em.dma("sp", bnd[:, :], C.rbnd[:, :], writes=[drc], owner=drc)
        em.dma("sp", dl[:, 0:4], C.rdf[0, :].partition_broadcast(128), writes=[dtab])
        em.dma("sp", dl[:, 4:8], C.rdb[0, :].partition_broadcast(128), writes=[dtab], owner=dtab)
        em.op("act", lambda e: e.activation(out=lg[:, :], in_=dl[:, :], func=AF.Exp, scale=-1.0), reads=[dtab], writes=[dtab])
        em.op("dve", lambda e: e.tensor_scalar(out=lg[:, :], in0=lg[:, :], scalar1=1.0, scalar2=None, op0=ALU.add), reads=[dtab], writes=[dtab])
        em.op("act", lambda e: e.activation(out=lg[:, :], in_=lg[:, :], func=AF.Ln), reads=[dtab], writes=[dtab])
        em.op("dve", lambda e: e.tensor_scalar(out=lg[:, :], in0=lg[:, :], scalar1=-1.0, scalar2=None, op0=ALU.mult), reads=[dtab], writes=[dtab])
        em.op("act", lambda e: e.activation(out=g128[:, :], in_=lg[:, :], func=AF.Exp, scale=128.0), reads=[dtab], writes=[dtab])
        em.op("act", lambda e: e.activation(out=kdec[:, 0:4], in_=lg[:, 0:4], func=AF.Exp, scale=cpos[:, 2:3]), reads=[dtab, drc], writes=[dtab])
        em.op("act", lambda e: e.activation(out=kdec[:, 4:8], in_=lg[:, 4:8], func=AF.Exp, scale=cpos[:, 3:4]), reads=[dtab, drc], writes=[dtab])
        em.op("dve", lambda e: e.tensor_scalar(out=kdec[:, :], in0=kdec[:, :], scalar1=1.0 / 16, scalar2=None, op0=ALU.mult), reads=[dtab], writes=[dtab])
        for h in range(4):
            em.op("act", lambda e, h=h: e.activation(out=decrow[:, h, :], in_=rc[:, 4, :], func=AF.Exp, scale=lg[:, h:h + 1]), reads=[dtab, drc], writes=[dtab])
            em.op("act", lambda e, h=h: e.activation(out=decrow[:, 4 + h, :], in_=rc[:, 5, :], func=AF.Exp, scale=lg[:, 4 + h:5 + h]), reads=[dtab, drc], writes=[dtab])
            em.op("act", lambda e, h=h: e.activation(out=tmpD[:, 0, :], in_=rc[:, 0, :], func=AF.Exp, scale=lg[:, h:h + 1]), reads=[dtab, drc], writes=[dtmp])
            em.op("act", lambda e, h=h: e.activation(out=tmpD[:, 1, :], in_=rc[:, 1, :], func=AF.Exp, scale=lg[:, 4 + h:5 + h]), reads=[dtab, drc], writes=[dtmp])
            em.op("dve", lambda e, h=h: e.tensor_tensor(out=tmpD[:, :, :], in0=tmpD[:, :, :], in1=rc[:, 2:4, :], op=ALU.mult), reads=[dtmp, drc], writes=[dtmp])
            em.op("dve", lambda e, h=h: e.tensor_tensor(out=DT[:, h, :], in0=tmpD[:, 0, :], in1=tmpD[:, 1, :], op=ALU.add), reads=[dtmp], writes=[dtab])
        em.op("dve", lambda e: e.memset(S32[:, :, :, :], 0.0), writes=[dS32])
        em.barrier()
        stT.close()

        Wqg = _alloc(st0, nc, "Wqg", [128, 8, 3072], BF16)
        dWqg = [Dep("Wqg%d" % k) for k in range(2)]
        wv_ = C.rwi[0].rearrange("(k p) f -> p k f", p=128)

        def rotary(ps, dps, c_, dc_, rt, drt, out_bf, dout):
            p4 = ps.rearrange("p (h t f) -> p h t f", h=4, t=2)
            o4 = out_bf[:, :].rearrange("p (h t f) -> p h t f", h=4, t=2)
            cosb = c_[:, 0:128].unsqueeze(1).broadcast_to([128, 4, 128])
            sinb = c_[:, 128:256].unsqueeze(1).broadcast_to([128, 4, 128])
            x1, x2 = p4[:, :, 0, :], p4[:, :, 1, :]
            prods = [(x1, cosb), (x2, sinb), (x2, cosb), (x1, sinb)]
            for i in (0, 1):
                xx, tb = prods[i]
                em.op("dve", lambda e, i=i, xx=xx, tb=tb: e.tensor_tensor(out=rt[i][:, :, :], in0=xx, in1=tb, op=ALU.mult),
                      reads=[dps, dc_], writes=[drt[i]])
            em.op("dve", lambda e: e.tensor_tensor(out=o4[:, :, 0, :], in0=rt[0][:, :, :], in1=rt[1][:, :, :], op=ALU.subtract),
                  reads=[drt[0], drt[1]], writes=[dout])
            for i in (2, 3):
                xx, tb = prods[i]
                em.op("dve", lambda e, i=i, xx=xx, tb=tb: e.tensor_tensor(out=rt[i][:, :, :], in0=xx, in1=tb, op=ALU.mult),
                      reads=[dps, dc_], writes=[drt[i]])
            em.op("dve", lambda e: e.tensor_tensor(out=o4[:, :, 1, :], in0=rt[2][:, :, :], in1=rt[3][:, :, :], op=ALU.add),
                  reads=[drt[2], drt[3]], writes=[dout])

        def state_update(S32, dS32, kd_tok, dkd, v_tok, dv, dsp, dsd, gcol0):
            for h in range(4):
                for c in range(2):
                    def f(e, h=h, c=c):
                        return e.matmul(dsp[:, :], lhsT=kd_tok[:, h * 256 + c * 128: h * 256 + (c + 1) * 128],
                                        rhs=v_tok[:, h * 512:(h + 1) * 512], start=True, stop=True)
                    em.op("pe", f, reads=[dkd, dv], writes=[dsd])
                    em.op("dve", lambda e, h=h, c=c: e.scalar_tensor_tensor(out=S32[:, h, c, :], in0=S32[:, h, c, :],
                                                                          scalar=g128[:, gcol0 + h:gcol0 + h + 1], in1=dsp[:, :],
                                                                          op0=ALU.mult, op1=ALU.add),
                          reads=[dS32, dsd, dtab], writes=[dS32])

        def boundary(S32, dS32):
            em.op("dve", lambda e: e.tensor_scalar(out=S32[:, :, :, :], in0=S32[:, :, :, :], scalar1=bnd[:, 0:1], scalar2=None,
                                                   op0=ALU.mult), reads=[dS32, drc], writes=[dS32])

        with ExitStack() as st:
            Wkv = _alloc(st, nc, "Wkv", [128, 8, 3072], BF16)
            dWkv = [Dep("Wkv%d" % k) for k in range(2)]
            wv = C.rwi[0].rearrange("(k p) f -> p k f", p=128)
            em.dma("pool", Wkv[:, 0:4, :], wv[:, 0:4, 1024:4096], writes=[dWkv[0]])
            em.dma("pool", Wkv[:, 4:8, :], wv[:, 4:8, 1024:4096], writes=[dWkv[1]])
            em.dma("pool", Wqg[:, :, 0:1024], wv_[:, :, 0:1024], writes=[dWqg[0]])
            em.dma("pool", Wqg[:, :, 1024:3072], wv_[:, :, 4096:6144], writes=[dWqg[1]])
            gpre = _alloc(st, nc, "gpre", [128, D], F32)
            dgpre = Dep("gpre")
            em.dma("sp", gpre[:, :], C.ng[l, 2, :].partition_broadcast(128), writes=[dgpre])
            xa = [_alloc(st, nc, "xa%d" % i, [128, D], F32) for i in range(3)]
            dxa = [Dep("xa%d" % i) for i in range(3)]
            hb2 = [_alloc(st, nc, "hb%d" % i, [128, D], BF16) for i in range(2)]
            dhb2 = [Dep("hb%d" % i) for i in range(2)]
            hT = [_alloc(st, nc, "hT%d" % i, [128, 8, 128], BF16) for i in range(2)]
            dhT = [Dep("hT%d" % i) for i in range(2)]
            NCS = 4
            cs = [_alloc(st, nc, "rcs%d" % i, [128, 256], F32) for i in range(NCS)]
            dcs = [Dep("rcs%d" % i) for i in range(NCS)]
            rt = [_alloc(st, nc, "rrt%d" % i, [128, 4, 128], F32) for i in range(4)]
            drt = [Dep("rrt%d" % i) for i in range(4)]
            krot = [_alloc(st, nc, "krot%d" % i, [128, 1024], BF16) for i in range(2)]
            dkrot = [Dep("krot%d" % i) for i in range(2)]
            kf = [_alloc(st, nc, "kf%d" % i, [128, 1024], BF16) for i in range(2)]
            dkf = [Dep("kf%d" % i) for i in range(2)]
            kb = [_alloc(st, nc, "kb%d" % i, [128, 1024], BF16) for i in range(2)]
            dkb = [Dep("kb%d" % i) for i in range(2)]
            kTb = [_alloc(st, nc, "kTb%d" % i, [128, 8, 128], BF16) for i in range(2)]
            dkTb = [Dep("kTb%d" % i) for i in range(2)]
            vtok = [_alloc(st, nc, "vtok%d" % i, [128, 2048], BF16) for i in range(2)]
            dvtok = [[Dep("vtok%d_%d" % (i, k)) for k in range(2)] for i in range(2)]
            Sbf2 = [_alloc(st, nc, "Sbf%d" % i, [128, 4096], BF16) for i in range(2)]
            dSbfh2 = [[Dep("Sbf%d_%d" % (i, h)) for h in range(4)] for i in range(2)]
            dS32h = [Dep("S32b_%d" % h) for h in range(4)]
            em.op("dve", lambda e: e.memset(S32[:, :, :, :], 0.0), reads=[dS32], writes=dS32h)
            sst = [_alloc(st, nc, "ss%d" % i, [128, 4], F32) for i in range(3)]
            dsst = [Dep("ss%d" % i) for i in range(3)]
            tp = _palloc(st, nc, "tp", [128, 8, 128], BF16)
            dtp = PDep("tp")
            pk = _palloc(st, nc, "pk", [128, 1024], F32)
            dpk = PDep("pk")
            pv1 = _palloc(st, nc, "pv", [128, 1024], F32)
            pv = [pv1, pv1]
            dpv1 = PDep("pv")
            dpv = [dpv1, dpv1]
            dspr = [_palloc(st, nc, "dsp%d" % i, [128, 512], F32) for i in range(2)]
            dsdr = [PDep("dsp%d" % i) for i in range(2)]

            def loads1(n):
                em.dma("sp", xa[n % 3][:, :], src[n * 128:(n + 1) * 128, :], reads=[dsrc[n]], writes=[dxa[n % 3]])
                em.dma("sp", cs[n % NCS][:, :], C.rcs[n * 128:(n + 1) * 128, :], writes=[dcs[n % NCS]])

            def front1A(n):
                r = n % 2
                emit_front_A(C, em, xa[n % 3], dxa[n % 3], hb2[r], dhb2[r], sst[n % 3], dsst[n % 3])

            def front1B(n):
                r = n % 2
                emit_front_B(C, em, xa[n % 3], dxa[n % 3], hb2[r], dhb2[r], sst[n % 3], dsst[n % 3], gpre, dgpre)

            def front1(n):
                front1A(n)
                front1B(n)

            def P1_steps(n):
                r = n % 2
                c_, dc_ = cs[n % NCS], dcs[n % NCS]

                def p0():
                    hbs = hb2[r]

                    def tps(e):
                        for k in range(8):
                            i = e.transpose(tp[:, k, :], hbs[:, k * 128:(k + 1) * 128], C.ident[:, :])
                        return i
                    em.op("pe", tps, reads=[dhb2[r], C.dident], writes=[dtp])
                    em.op("act", lambda e: e.activation(out=hT[r][:, :, :], in_=tp[:, :, :], func=AF.Copy), reads=[dtp], writes=[dhT[r]])

                def p1():
                    for g in range(2):
                        def f(e, g=g):
                            for k in range(8):
                                i = e.matmul(pk[:, g * 512:(g + 1) * 512], lhsT=hT[r][:, k, :], rhs=Wkv[:, k, g * 512:(g + 1) * 512],
                                             start=(k == 0), stop=(k == 7))
                            return i
                        em.op("pe", f, reads=[dhT[r]] + dWkv, writes=[dpk])
                    rotary(pk[:, :], dpk, c_, dc_, rt, drt, krot[r], dkrot[r])

                def p2():
                    for (dst, ddst, col) in ((kf[r], dkf[r], 0), (kb[r], dkb[r], 4)):
                        em.op("pool", lambda e, dst=dst, col=col: e.tensor_tensor(
                            out=dst[:, :].rearrange("p (h f) -> p h f", h=4), in0=krot[r][:, :].rearrange("p (h f) -> p h f", h=4),
                            in1=kdec[:, col:col + 4].unsqueeze(2).broadcast_to([128, 4, 256]), op=ALU.mult),
                            reads=[dkrot[r], dtab], writes=[ddst])

                    def tk(e):
                        for k in range(8):
                            i = e.transpose(tp[:, k, :], krot[r][:, k * 128:(k + 1) * 128], C.ident[:, :])
                        return i
                    em.op("pe", tk, reads=[dkrot[r], C.dident], writes=[dtp])
                    em.op("act", lambda e: e.activation(out=kTb[r][:, :, :], in_=tp[:, :, :], func=AF.Copy), reads=[dtp], writes=[dkTb[r]])
                    em.dma("sp", C.s_kT[n], kTb[r][:, :, :].rearrange("p k t -> p (k t)"), reads=[dkTb[r]], writes=[C.dscr[n]], owner=dkTb[r])
                    em.dma("sp", C.s_kf[n], kf[r][:, :], reads=[dkf[r]], writes=[C.dscr[n]], owner=dkf[r])

                def mkv(hv):
                    def pv_():
                        for g in range(2):
                            def f(e, g=g):
                                col = 1024 + hv * 1024 + g * 512
                                for k in range(8):
                                    i = e.matmul(pv[hv][:, g * 512:(g + 1) * 512], lhsT=hT[r][:, k, :], rhs=Wkv[:, k, col:col + 512],
                                                 start=(k == 0), stop=(k == 7))
                                return i
                            em.op("pe", f, reads=[dhT[r]] + dWkv, writes=[dpv[hv]])
                        em.op("act", lambda e: e.activation(out=vtok[r][:, hv * 1024:(hv + 1) * 1024], in_=pv[hv][:, :], func=AF.Copy),
                              reads=[dpv[hv]], writes=[dvtok[r][hv]])
                    return pv_

                def p5():
                    em.dma("sp", C.s_v[n], vtok[r][:, :], reads=dvtok[r], writes=[C.dscr[n]], owner=dvtok[r][0])
                return [p0, p1, mkv(0), mkv(1), p2, p5]

            def U(n, steps):
                r = n % 2
                if n - 3 >= 0:
                    loads1(n - 3)
                if n - 2 >= 0:
                    front1A(n - 2)
                Sbf, dSbfh = Sbf2[n % 2], dSbfh2[n % 2]
                for h in range(4):
                    if h == 2 and n - 2 >= 0:
                        front1B(n - 2)
                    em.op("act", lambda e, h=h: e.activation(out=Sbf[:, h * 1024:(h + 1) * 1024],
                                                             in_=S32[:, h, :, :].rearrange("p c f -> p (c f)"), func=AF.Copy),
                          reads=[dS32h[h]], writes=[dSbfh[h]])
                    if h == 3:
                        em.dma("sp", C.s_sb[n], Sbf[:, :], reads=dSbfh, writes=[C.dscr[n]], owner=dSbfh[0])
                    for c in range(2):
                        if n > 0:
                            dsp, dsd = dspr[c], dsdr[c]

                            def f(e, h=h, c=c, dsp=dsp):
                                return e.matmul(dsp[:, :], lhsT=kb[r][:, h * 256 + c * 128: h * 256 + (c + 1) * 128],
                                                rhs=vtok[r][:, h * 512:(h + 1) * 512], start=True, stop=True)
                            em.op("pe", f, reads=[dkb[r], dvtok[r][h // 2]], writes=[dsd])
                            em.op("dve", lambda e, h=h, c=c, dsp=dsp: e.scalar_tensor_tensor(out=S32[:, h, c, :], in0=S32[:, h, c, :],
                                                                                           scalar=g128[:, 4 + h:5 + h], in1=dsp[:, :],
                                                                                           op0=ALU.mult, op1=ALU.add),
                                  reads=[dS32h[h], dsd, dtab], writes=[dS32h[h]])
                        if steps:
                            steps.pop(0)()
                while steps:
                    steps.pop(0)()
                if n > 0 and n == half_b:
                    em.op("dve", lambda e: e.tensor_scalar(out=S32[:, :, :, :], in0=S32[:, :, :, :], scalar1=bnd[:, 0:1], scalar2=None,
                                                           op0=ALU.mult), reads=dS32h + [drc], writes=dS32h)

            for i in range(1, 4):
                if nch - i >= 0:
                    loads1(nch - i)
            front1(nch - 1)
            if nch > 1:
                front1(nch - 2)
            for s_ in P1_steps(nch - 1):
                s_()
            for n in range(nch - 1, -1, -1):
                U(n, P1_steps(n - 1) if n > 0 else [])
            em.barrier()

        em.op("dve", lambda e: e.memset(S32[:, :, :, :], 0.0), writes=[dS32])
        with ExitStack() as st:
            Wro = _alloc(st, nc, "Wro", [128, 16, D], BF16)
            dWro = Dep("Wro")
            em.dma("pool", Wro[:, :, :], C.rwo[0].rearrange("(k p) f -> p k f", p=128), writes=[dWro])
            gpre, dgpre, gpost, dgpost = load_gains(C, em, st, nc, l, 2, 3, 1.0)
            NX = 4
            xa = [_alloc(st, nc, "xa%d" % i, [128, D], F32) for i in range(NX)]
            dxa = [Dep("xa%d" % i) for i in range(NX)]
            hb = _alloc(st, nc, "hb", [128, D], BF16)
            dhb = Dep("hb")
            NH = 3
            hT = [_alloc(st, nc, "hT%d" % i, [128, 8, 128], BF16) for i in range(NH)]
            dhT = [Dep("hT%d" % i) for i in range(NH)]
            cs = [_alloc(st, nc, "rcs%d" % i, [128, 256], F32) for i in range(2)]
            dcs = [Dep("rcs%d" % i) for i in range(2)]
            rt2 = [_alloc(st, nc, "rrt%d" % i, [128, 4, 128], F32) for i in range(2)]
            drt2 = [Dep("rrt%d" % i) for i in range(2)]
            rt = [rt2[0], rt2[1], rt2[0], rt2[1]]
            drt = [drt2[0], drt2[1], drt2[0], drt2[1]]
            qrot = _alloc(st, nc, "qrot", [128, 1024], BF16)
            dqrot = Dep("qrot")
            qT = [[_alloc(st, nc, "qT%d_%d" % (r, i), [128, 8, 128], BF16) for i in range(3)] for r in range(2)]
            dqT = [[Dep("qT%d_%d" % (r, i)) for i in range(3)] for r in range(2)]
            sg = [_alloc(st, nc, "sg%d" % i, [128, 512], BF16) for i in range(3)]
            dsg = [Dep("sg%d" % i) for i in range(3)]
            NR = 2
            kTl = [_alloc(st, nc, "kTl%d" % i, [128, 8, 128], BF16) for i in range(NR)]
            kfl = [_alloc(st, nc, "kfl%d" % i, [128, 1024], BF16) for i in range(NR)]
            vl = [_alloc(st, nc, "vl%d" % i, [128, 2048], BF16) for i in range(NR)]
            sbl1 = _alloc(st, nc, "sbl", [128, 4, 2, 512], BF16)
            sbl = [sbl1, sbl1]
            dkTl = [Dep("kTl%d" % i) for i in range(NR)]
            dkfl = [Dep("kfl%d" % i) for i in range(NR)]
            dvl = [Dep("vl%d" % i) for i in range(NR)]
            dsblh = [Dep("sbl_%d" % h) for h in range(4)]
            Sfb = _alloc(st, nc, "Sfb", [128, 4, 2, 512], BF16)
            dSfb = [Dep("Sfb%d" % h) for h in range(4)]
            dS32h = [Dep("S32_%d" % h) for h in range(4)]
            STb = [_alloc(st, nc, "STb%d" % i, [128, 128], BF16) for i in range(2)]
            dSTb = [Dep("STb%d" % i) for i in range(2)]
            otok = [_alloc(st, nc, "otok%d" % i, [128, 2048], BF16) for i in range(2)]
            dotok = [[Dep("otok%d_%d" % (i, h)) for h in range(4)] for i in range(2)]
            oT = _alloc(st, nc, "oT", [128, 16, 128], BF16)
            doT = Dep("oT")
            junk = _alloc(st, nc, "junk", [128, D], BF16)
            djunk = Dep("junk")
            junk2 = junk
            djunk2 = djunk
            sst = [_alloc(st, nc, "ss%d" % i, [128, 4], F32) for i in range(6)]
            dsst = [Dep("ss%d" % i) for i in range(6)]
            tp = _palloc(st, nc, "tp", [128, 8, 128], BF16)
            dtp = PDep("tp")
            pq = _palloc(st, nc, "pq", [128, 1024], F32)
            dpq = PDep("pq")
            pG = _palloc(st, nc, "pG", [128, 512], F32)
            dpG = PDep("pG")
            pS = _palloc(st, nc, "pS", [128, 512], F32)
            dpS = PDep("pS")
            pY = [_palloc(st, nc, "pY%d" % i, [128, 512], F32) for i in range(2)]
            dpY = [PDep("pY%d" % i) for i in range(2)]
            pD = _palloc(st, nc, "pD", [128, 512], F32)
            dpD = PDep("pD")
            sctr = [0]
            em.op("dve", lambda e: e.memset(Sfb[:, :, :, :], 0.0), writes=dSfb)
            em.op("dve", lambda e: e.memset(S32[:, :, :, :], 0.0), reads=[dS32], writes=dS32h)

            def nss():
                si = sctr[0] % 6
                sctr[0] += 1
                return sst[si], dsst[si]

            osl = {}
            ysl = {}

            def load_sb(n, h):
                em.dma("sp", sbl1[:, h, :, :].rearrange("p c f -> p (c f)"), C.s_sb[n][:, h * 1024:(h + 1) * 1024],
                       reads=[C.dscr[n]], writes=[dsblh[h]])

            hb2 = [hb, _alloc(st, nc, "hb_b", [128, D], BF16)]
            dhb2 = [dhb, Dep("hb_b")]

            fsl = {}

            def frontA(n):
                xs, dxs = xa[n % NX], dxa[n % NX]
                em.dma("sp", xs[:, :], src[n * 128:(n + 1) * 128, :], reads=[dsrc[n]], writes=[dxs])
                em.dma("sp", cs[n % 2][:, :], C.rcs[n * 128:(n + 1) * 128, :], writes=[dcs[n % 2]])
                ss, dss = nss()
                fsl[n] = (ss, dss)
                emit_front_A(C, em, xs, dxs, hb2[n % 2], dhb2[n % 2], ss, dss)

            def frontB(n):
                ss, dss = fsl.pop(n)
                emit_front_B(C, em, xa[n % NX], dxa[n % NX], hb2[n % 2], dhb2[n % 2], ss, dss, gpre, dgpre)

            def front(n):
                frontA(n)
                frontB(n)

            def P_steps(n):
                r = n % 2
                xs, dxs = xa[n % NX], dxa[n % NX]
                c_, dc_ = cs[r], dcs[r]

                def p0():
                    if n == 0:
                        for h in range(4):
                            load_sb(0, h)
                    hbs = hb2[n % 2]

                    def tps(e):
                        for k in range(8):
                            i = e.transpose(tp[:, k, :], hbs[:, k * 128:(k + 1) * 128], C.ident[:, :])
                        return i
                    em.op("pe", tps, reads=[dhb2[n % 2], C.dident], writes=[dtp])
                    em.op("act", lambda e: e.activation(out=hT[n % NH][:, :, :], in_=tp[:, :, :], func=AF.Copy), reads=[dtp], writes=[dhT[n % NH]])

                def p1():
                    em.dma("sp", kTl[r][:, :, :].rearrange("p k t -> p (k t)"), C.s_kT[n], reads=[C.dscr[n]], writes=[dkTl[r]])
                    em.dma("sp", kfl[r][:, :], C.s_kf[n], reads=[C.dscr[n]], writes=[dkfl[r]])
                    em.dma("sp", vl[r][:, :], C.s_v[n], reads=[C.dscr[n]], writes=[dvl[r]])
                    for g in range(2):
                        def f(e, g=g):
                            for k in range(8):
                                i = e.matmul(pq[:, g * 512:(g + 1) * 512], lhsT=hT[n % NH][:, k, :], rhs=Wqg[:, k, g * 512:(g + 1) * 512],
                                             start=(k == 0), stop=(k == 7))
                            return i
                        em.op("pe", f, reads=[dhT[n % NH], dWqg[0]], writes=[dpq])
                    rotary(pq[:, :], dpq, c_, dc_, rt, drt, qrot, dqrot)

                def p2():
                    def tq(e):
                        for k in range(8):
                            i = e.transpose(tp[:, k, :], qrot[:, k * 128:(k + 1) * 128], C.ident[:, :])
                        return i
                    em.op("pe", tq, reads=[dqrot, C.dident], writes=[dtp])
                    em.op("act", lambda e: e.activation(out=qT[r][0][:, :, :], in_=tp[:, :, :], func=AF.Copy), reads=[dtp], writes=[dqT[r][0]])
                    for i in range(2):
                        em.op("pool", lambda e, i=i: e.tensor_tensor(
                            out=qT[r][1 + i][:, :, :].rearrange("p (h c) t -> p h c t", c=2),
                            in0=qT[r][0][:, :, :].rearrange("p (h c) t -> p h c t", c=2),
                            in1=decrow[:, 4 * i:4 * i + 4, :].unsqueeze(2).broadcast_to([128, 4, 2, 128]), op=ALU.mult),
                            reads=[dqT[r][0], dtab], writes=[dqT[r][1 + i]])
                return [p0, p1, p2]

            def O_steps(n):
                r = n % 2
                steps = []
                for half in range(2):
                    def o_t(half=half):
                        def to(e):
                            for k in range(8):
                                i = e.transpose(tp[:, k, :], otok[r][:, (half * 8 + k) * 128:(half * 8 + k + 1) * 128], C.ident[:, :])
                            return i
                        em.op("pe", to, reads=dotok[r] + [C.dident], writes=[dtp])
                        em.op("act", lambda e: e.activation(out=oT[:, half * 8:(half + 1) * 8, :], in_=tp[:, :, :], func=AF.Copy),
                              reads=[dtp], writes=[doT])
                    steps.append(o_t)

                def o_w():
                    def fw(e):
                        for half in range(2):
                            for k in range(16):
                                i = e.matmul(pq[:, half * 512:(half + 1) * 512], lhsT=oT[:, k, :], rhs=Wro[:, k, half * 512:(half + 1) * 512],
                                             start=(k == 0), stop=(k == 15))
                        return i
                    em.op("pe", fw, reads=[doT, dWro], writes=[dpq])

                def o_e():
                    ss, dss = nss()
                    osl[n] = (ss, dss)
                    emit_post_A(C, em, pq[:, :], dpq, ss, dss, junk, djunk)

                def o_eB():
                    ss, dss = osl.pop(n)
                    emit_post_B(C, em, pq[:, :], dpq, xa[n % NX], dxa[n % NX], gpost, dgpost, ss, dss,
                                C.y[n * 128:(n + 1) * 128, :], C.dy[n])
                steps += [o_w, o_e, o_eB]
                return steps

            def H(n, steps):
                r = n % 2
                last = (n + 1 >= nch)

                def GS(h):
                    def fg(e):
                        for k in range(8):
                            i = e.matmul(pG[:, :], lhsT=hT[n % NH][:, k, :], rhs=Wqg[:, k, 1024 + h * 512:1024 + (h + 1) * 512],
                                         start=(k == 0), stop=(k == 7))
                        return i
                    em.op("pe", fg, reads=[dhT[n % NH], dWqg[1]], writes=[dpG])
                    em.op("act", lambda e: e.activation(out=sg[h % 3][:, :], in_=pG[:, :], func=AF.Silu), reads=[dpG], writes=[dsg[h % 3]])

                    def fs(e):
                        for c in range(2):
                            i = e.matmul(pS[:, 0:128], lhsT=kTl[r][:, 2 * h + c, :], rhs=qT[r][0][:, 2 * h + c, :], start=(c == 0), stop=(c == 1))
                        return i
                    em.op("pe", fs, reads=[dkTl[r], dqT[r][0]], writes=[dpS])
                    em.op("dve", lambda e: e.tensor_tensor(out=STb[h % 2][:, :], in0=pS[:, 0:128], in1=DT[:, h, :], op=ALU.mult),
                          reads=[dpS, dtab], writes=[dSTb[h % 2]])

                def Y(h):
                    yh, dyh = pY[h % 2], dpY[h % 2]

                    def fy(e):
                        e.matmul(yh[:, :], lhsT=STb[h % 2][:, :], rhs=vl[r][:, h * 512:(h + 1) * 512], start=True, stop=False)
                        for c in range(2):
                            e.matmul(yh[:, :], lhsT=qT[r][1][:, 2 * h + c, :], rhs=Sfb[:, h, c, :], start=False, stop=False)
                        for c in range(2):
                            i = e.matmul(yh[:, :], lhsT=qT[r][2][:, 2 * h + c, :], rhs=sbl[r][:, h, c, :], start=False, stop=(c == 1))
                        return i
                    em.op("pe", fy, reads=[dSTb[h % 2], dvl[r], dqT[r][1], dqT[r][2], dSfb[h], dsblh[h]], writes=[dyh])
                    ss, dss = nss()
                    em.op("act", lambda e: e.activation(out=junk2[:, 0:512], in_=yh[:, :], func=AF.Square, accum_out=ss[:, 0:1]),
                          reads=[dyh], writes=[djunk2, dss])
                    emit_rstd(em, ss, dss, C.mhalf, C.dmh, 512)
                    ysl[h] = (ss, dss)

                def YB(h):
                    yh, dyh = pY[h % 2], dpY[h % 2]
                    ss, dss = ysl.pop(h)
                    em.op("dve", lambda e: e.scalar_tensor_tensor(out=otok[r][:, h * 512:(h + 1) * 512], in0=yh[:, :], scalar=ss[:, 2:3],
                                                                  in1=sg[h % 3][:, :], op0=ALU.mult, op1=ALU.mult),
                          reads=[dyh, dss, dsg[h % 3]], writes=[dotok[r][h]])

                def UPD(h, cs_):
                    for c in cs_:
                        def f(e, c=c):
                            return e.matmul(pD[:, :], lhsT=kfl[r][:, h * 256 + c * 128: h * 256 + (c + 1) * 128],
                                            rhs=vl[r][:, h * 512:(h + 1) * 512], start=True, stop=True)
                        em.op("pe", f, reads=[dkfl[r], dvl[r]], writes=[dpD])
                        em.op("dve", lambda e, c=c: e.scalar_tensor_tensor(out=S32[:, h, c, :], in0=S32[:, h, c, :],
                                                                         scalar=g128[:, h:h + 1], in1=pD[:, :],
                                                                         op0=ALU.mult, op1=ALU.add),
                              reads=[dS32h[h], dpD, dtab], writes=[dS32h[h]])
                    if 1 not in cs_:
                        return
                    if n + 1 == half_b:
                        em.op("dve", lambda e: e.tensor_scalar(out=S32[:, h, :, :], in0=S32[:, h, :, :], scalar1=bnd[:, 0:1], scalar2=None,
                                                               op0=ALU.mult), reads=[dS32h[h], drc], writes=[dS32h[h]])
                    em.op("act", lambda e: e.activation(out=Sfb[:, h, :, :], in_=S32[:, h, :, :], func=AF.Copy),
                          reads=[dS32h[h]], writes=[dSfb[h]])

                GS(0)
                for h in range(4):
                    if h + 1 < 4:
                        GS(h + 1)
                    if h >= 1 and not last:
                        UPD(h - 1, [1])
                    Y(h)
                    if h >= 1:
                        YB(h - 1)
                    if not last:
                        load_sb(n + 1, h)
                        UPD(h, [0])
                    for _ in range(2):
                        if steps:
                            steps.pop(0)()
                YB(3)
                if not last:
                    UPD(3, [1])
                while steps:
                    steps.pop(0)()

            nop = lambda: None
            front(0)
            if nch > 1:
                front(1)
            for s_ in P_steps(0):
                s_()
            if nch > 1:
                P_steps(1)[0]()
            for n in range(nch):
                O = O_steps(n - 1) if n >= 1 else [nop] * 5
                Opp = O_steps(n - 2)[4] if n >= 2 else nop
                P = P_steps(n + 1) if n + 1 < nch else [nop] * 3
                P0n = P_steps(n + 2)[0] if n + 2 < nch else nop
                frA = (lambda n=n: frontA(n + 2)) if n + 2 < nch else nop
                frB = (lambda n=n: frontB(n + 2)) if n + 2 < nch else nop
                steps = [lambda Opp=Opp, O=O: (Opp(), O[0]()), P[1], O[1], frA, P[2], lambda O=O, frB=frB: (O[2](), frB()), P0n, O[3]]
                H(n, steps)
            if nch >= 2:
                O_steps(nch - 2)[4]()
            for s_ in O_steps(nch - 1):
                s_()
            em.barrier()


def build(nsub=6, subs=None, nblk=NCH, dbg=3):
    nc = bass.Bass("TRN2", target_bir_lowering=False)
    em = Emitter(nc)
    C = Ctx()
    C.nc, C.em = nc, em

    def din(name, shape, dt=F32):
        return nc.dram_tensor(name, shape, dt, kind="ExternalInput").ap()
    C.xin = din("xin", [NTOK, D])
    C.ng = din("norm_gains", [2, 6, D])
    C.fwi = din("ffn_w_in", [2, 2, D, 2 * DFF])
    C.fwo = din("ffn_w_out", [2, 2, DFF, D])
    C.wqkv = din("attn_w_qkv", [1, D, 1536])
    C.wo = din("attn_w_o", [1, D, D])
    C.sink = din("attn_sink", [1, 16])
    C.rwi = din("ret_w_in", [1, D, 6144])
    C.rwo = din("ret_w_o", [1, 2048, D])
    C.rdf = din("ret_decay_fwd", [1, 4])
    C.rdb = din("ret_decay_bwd", [1, 4])
    C.identd = din("c_ident", [128, 128], BF16)
    C.acs = din("c_acs", [NTOK, 320])
    C.amask = din("c_amask", [NCH, 128, 384], BF16)
    C.rconst = din("c_rconst", [128, 6, 128])
    C.rpos = din("c_rpos", [128, 4])
    C.rbnd = din("c_rbnd", [128, 1])
    C.rcs = din("c_rcs", [NTOK, 256])
    C.s_kT = nc.dram_tensor("s_kT", [NCH, 128, 1024], BF16, kind="Internal").ap()
    C.s_kf = nc.dram_tensor("s_kf", [NCH, 128, 1024], BF16, kind="Internal").ap()
    C.s_v = nc.dram_tensor("s_v", [NCH, 128, 2048], BF16, kind="Internal").ap()
    C.s_sb = nc.dram_tensor("s_sb", [NCH, 128, 4096], BF16, kind="Internal").ap()
    C.dscr = [Dep("scr%d" % i) for i in range(NCH)]
    C.y = nc.dram_tensor("y", [NTOK, D], F32, kind="ExternalOutput").ap()
    C.dy = [Dep("y%d" % i) for i in range(NCH)]
    C.dxin = [Dep("xin%d" % i) for i in range(NCH)]

    C.ident = nc.alloc_sbuf_tensor("ident", [128, 128], BF16)
    C.dident = Dep("ident")
    C.mhalf = nc.alloc_sbuf_tensor("mhalf", [128, 1], F32)
    C.dmh = Dep("mhalf")
    em.dma("sp", C.ident[:, :], C.identd[:, :], writes=[C.dident])
    em.op("pool", lambda e: e.memset(C.mhalf[:, :], -0.5), writes=[C.dmh])

    C.nblk = nblk
    C.dbg = dbg
    if subs is None:
        subs = [("ffn", 0, 0), ("attn", 0, 0), ("ffn", 0, 1), ("ffn", 1, 0), ("ret", 1, 0), ("ffn", 1, 1)]
    ffn_ids = [i for i, s_ in enumerate(subs[:nsub]) if s_[0] == "ffn"]
    wsc = {}
    for n_, i in enumerate(ffn_ids[1:]):
        wsc[i] = (nc.dram_tensor("s_wi%d" % n_, [D, 2 * DFF], BF16, kind="Internal").ap(),
                  nc.dram_tensor("s_wo%d" % n_, [DFF, D], BF16, kind="Internal").ap(),
                  [Dep("sw%d_%d" % (n_, q_)) for q_ in range(6)])
    src, dsrc = C.xin, C.dxin
    for i, (kind, l, which) in enumerate(subs[:nsub]):
        if kind == "ffn":
            nxt = [j for j in ffn_ids if j > i]
            pre = None
            if nxt:
                j = nxt[0]
                pre = ((subs[j][1], subs[j][2]), wsc[j])
            emit_ffn(C, l, which, src, dsrc, wcast=wsc.get(i), precast=pre)
        elif kind == "attn":
            emit_attn(C, l, src, dsrc)
        elif kind == "ret":
            emit_ret(C, l, src, dsrc)
        src, dsrc = C.y, C.dy
    em.finish()
    return nc


def make_consts():
    c = {}
    c["c_ident"] = np.eye(128, dtype=np.float32).astype(ml_dtypes.bfloat16)
    j = np.arange(128, dtype=np.float32)[:, None]
    i = np.arange(128, dtype=np.float32)[None, :]
    rc = np.zeros((128, 6, 128), np.float32)
    rc[:, 0] = np.maximum(i - j, 0.0)
    rc[:, 1] = np.maximum(j - i, 0.0)
    rc[:, 2] = (i >= j) / 16.0
    rc[:, 3] = (j > i) / 16.0
    rc[:, 4] = np.broadcast_to(i + 1.0, (128, 128))
    rc[:, 5] = np.broadcast_to(128.0 - i, (128, 128))
    c["c_rconst"] = rc
    p = np.arange(128, dtype=np.float32)
    c["c_rpos"] = np.stack([p + 1, 128 - p, 127 - p, p], axis=1).astype(np.float32)
    return c


def make_core_consts(seqlen):
    c = {}
    tok = np.arange(NTOK)
    pos = (tok % seqlen).astype(np.float32)
    inv = (500000.0 ** (-np.arange(8, dtype=np.float32) / 8)).astype(np.float32)
    ang = pos[:, None] * inv[None, :]
    cos = np.cos(ang).astype(np.float32)
    sin = np.sin(ang).astype(np.float32)
    hs = np.ones((1, 20, 1), np.float32)
    hs[:, :16] = 0.125
    acs = np.concatenate([(np.tile(cos[:, None, :], (1, 20, 1)) * hs).reshape(NTOK, 160),
                          (np.tile(sin[:, None, :], (1, 20, 1)) * hs).reshape(NTOK, 160)], axis=1)
    c["c_acs"] = np.ascontiguousarray(acs, dtype=np.float32)
    b = np.arange(NCH)[:, None, None]
    qi = np.arange(128)[None, :, None]
    kc = np.arange(384)[None, None, :]
    tq = 128 * b + qi
    tk = 128 * (b - 1) + kc
    valid = (tk >= 0) & (tk < NTOK) & ((tk // seqlen) == (tq // seqlen)) & (np.abs(tk - tq) <= 128)
    c["c_amask"] = np.where(valid, 0.0, NEG).astype(np.float32).astype(ml_dtypes.bfloat16)
    invr = (10000.0 ** (-np.arange(128, dtype=np.float32) / 128)).astype(np.float32)
    angr = pos[:, None] * invr[None, :]
    c["c_rcs"] = np.ascontiguousarray(np.concatenate([np.cos(angr), np.sin(angr)], axis=1), dtype=np.float32)
    c["c_rbnd"] = np.full((128, 1), 1.0 if seqlen == NTOK else 0.0, np.float32)
    return c


def kernel(x_prompt, x_sample, norm_gains, ffn_w_in, ffn_w_out, attn_w_qkv, attn_w_o, attn_sink,
           ret_w_in, ret_w_o, ret_decay_fwd, ret_decay_bwd, _nsub=6, _subs=None, _nblk=NCH, _cores=None, _trace=False, _dbg=3):
    f = lambda a: np.ascontiguousarray(np.asarray(a, dtype=np.float32))
    xp = f(x_prompt).reshape(4, NTOK, D)
    xs = f(x_sample).reshape(4, NTOK, D)
    shared = {
        "norm_gains": f(norm_gains), "ffn_w_in": f(ffn_w_in), "ffn_w_out": f(ffn_w_out),
        "attn_w_qkv": f(attn_w_qkv), "attn_w_o": f(attn_w_o), "attn_sink": f(attn_sink),
        "ret_w_in": f(ret_w_in), "ret_w_o": f(ret_w_o), "ret_decay_fwd": f(ret_decay_fwd),
        "ret_decay_bwd": f(ret_decay_bwd),
    }
    shared.update(make_consts())
    in_maps = []
    cc = [make_core_consts(2048), make_core_consts(4096)]
    for c in range(8):
        m = dict(shared)
        m.update(cc[0] if c < 4 else cc[1])
        m["xin"] = xp[c] if c < 4 else xs[c - 4]
        in_maps.append(m)
    nc = build(_nsub, _subs, _nblk, _dbg)
    if _cores is not None:
        res = run_bass_kernel_spmd(nc, [in_maps[c] for c in _cores], core_ids=list(range(len(_cores))), trace=_trace)
        if _trace:
            print("exec_time_ns", res.exec_time_ns)
        return [np.asarray(r["y"], dtype=np.float32) for r in res.results]
    res = run_bass_kernel_spmd(nc, in_maps, core_ids=list(range(8)))
    outs = [np.asarray(r["y"], dtype=np.float32) for r in res.results]
    y_prompt = np.stack(outs[:4]).reshape(8, 2048, D)
    y_sample = np.stack(outs[4:]).reshape(4, 4096, D)
    return (y_prompt, y_sample)
```
